# Optimizing a Trainium2 kernel written in Bass

```python
import jax, jax.numpy as jnp
from jax import lax
import numpy as np

D_MODEL = 1024
BATCH = 4
SEQ = 4096
DEPTH = 1

CHUNK = 64
N_META = 16
EPS = 1e-6
NEG = -1e30
ML_HEADS = 4
ML_DQK = 128
ML_DV = 256
ML_CONV = 4
GATE_CAP = 15.0
RT_HEADS = 4
RT_DQK = 128
RT_DV = 256
ROPE_BASE = 10000.0
D_MIX = ML_HEADS * ML_DV + RT_HEADS * RT_DV
COLUMN_SIZES = (
    ML_HEADS * ML_DQK,
    ML_HEADS * ML_DQK,
    ML_HEADS * ML_DV,
    ML_HEADS * ML_DV,
    ML_HEADS,
    ML_HEADS,
    RT_HEADS * RT_DQK,
    RT_HEADS * RT_DQK,
    RT_HEADS * RT_DV,
    RT_HEADS * RT_DV,
)
D_IN = sum(COLUMN_SIZES)
D_FF = 2816
FFN_CONV = 3

kernel_name = "hymba_mlstm_retention_convffn"


def rmsnorm(x, g):
    xf = x.astype(jnp.float32)
    y = xf * lax.rsqrt(jnp.mean(xf * xf, axis=-1, keepdims=True) + EPS)
    return (y * g.astype(jnp.float32)).astype(x.dtype)


def head_norm(h, g):
    mu = jnp.mean(h, axis=-1, keepdims=True)
    var = jnp.mean(jnp.square(h - mu), axis=-1, keepdims=True)
    y = (h - mu) * lax.rsqrt(var + EPS)
    return y * g.astype(jnp.float32).reshape(h.shape[-2], h.shape[-1])


def causal_dwconv(x, w):
    k_w, c = w.shape
    return lax.conv_general_dilated(
        x, w.astype(x.dtype)[:, None, :], window_strides=(1,), padding=[(k_w - 1, 0)],
        dimension_numbers=("NWC", "WIO", "NWC"), feature_group_count=c)


def rotary(t, pos):
    d = t.shape[-1]
    inv_freq = ROPE_BASE ** (-jnp.arange(0, d, 2, dtype=jnp.float32) / d)
    ang = pos[:, None] * inv_freq[None, :]
    cos, sin = jnp.cos(ang)[None, :, None, :], jnp.sin(ang)[None, :, None, :]
    t1, t2 = t[..., : d // 2], t[..., d // 2:]
    return jnp.concatenate([t1 * cos - t2 * sin, t1 * sin + t2 * cos], axis=-1)


def to_chunks(t):
    b, lp, hh, d = t.shape
    return t.reshape(b, lp // CHUNK, CHUNK, hh, d).transpose(1, 0, 3, 2, 4)


def from_chunks(t):
    nc, b, hh, c, d = t.shape
    return t.transpose(1, 0, 3, 2, 4).reshape(b, nc * c, hh, d)


def mlstm_chunkwise(q, k, v, li, lf):
    _, b, hh, _, dqk = q.shape
    dv = v.shape[-1]
    causal = jnp.tril(jnp.ones((CHUNK, CHUNK), dtype=bool))

    def step(carry, inp):
        c_st, n_st, m_st = carry
        qc, kc, vc, lic, lfc = inp
        bcum = jnp.cumsum(lfc, axis=-1)
        log_d = bcum[..., :, None] - bcum[..., None, :] + lic[..., None, :]
        log_d = jnp.where(causal, log_d, -jnp.inf)
        log_inter = bcum + m_st[..., None]
        m_row = jnp.maximum(log_inter, jnp.max(log_d, axis=-1))
        w_intra = jnp.exp(log_d - m_row[..., None])
        w_inter = jnp.exp(log_inter - m_row)
        s = jnp.einsum("bhid,bhjd->bhij", qc, kc) * w_intra
        num = (jnp.einsum("bhij,bhjv->bhiv", s, vc)
               + w_inter[..., None] * jnp.einsum("bhid,bhdv->bhiv", qc, c_st))
        den = jnp.sum(s, axis=-1) + w_inter * jnp.einsum("bhid,bhd->bhi", qc, n_st)
        h = num / jnp.maximum(jnp.abs(den), jnp.exp(-m_row))[..., None]
        b_end = bcum[..., -1]
        log_to_end = b_end[..., None] - bcum + lic
        m_new = jnp.maximum(b_end + m_st, jnp.max(log_to_end, axis=-1))
        w_src = jnp.exp(log_to_end - m_new[..., None])
        decay = jnp.exp(b_end + m_st - m_new)
        c_new = decay[..., None, None] * c_st + jnp.einsum("bhj,bhjd,bhjv->bhdv", w_src, kc, vc)
        n_new = decay[..., None] * n_st + jnp.einsum("bhj,bhjd->bhd", w_src, kc)
        return (c_new, n_new, m_new), h

    init = (jnp.zeros((b, hh, dqk, dv), jnp.float32),
            jnp.zeros((b, hh, dqk), jnp.float32),
            jnp.full((b, hh), NEG, jnp.float32))
    _, h = lax.scan(step, init, (q, k, v, li, lf))
    return h


def retention_chunkwise(q, k, v, log_gamma):
    _, b, hh, _, dqk = q.shape
    dv = v.shape[-1]
    idx = jnp.arange(CHUNK, dtype=jnp.float32)
    rel = idx[:, None] - idx[None, :]
    decay_intra = jnp.where(rel >= 0, jnp.exp(log_gamma[:, None, None] * jnp.maximum(rel, 0.0)), 0.0)
    q_decay = jnp.exp(log_gamma[:, None] * (idx + 1.0))
    k_decay = jnp.exp(log_gamma[:, None] * (CHUNK - 1.0 - idx))
    chunk_decay = jnp.exp(log_gamma * CHUNK)
    scores = jnp.einsum("nbhid,nbhjd->nbhij", q, k) * decay_intra[None, None]
    intra = jnp.einsum("nbhij,nbhjv->nbhiv", scores, v)
    q_d = q * q_decay[None, None, :, :, None]
    k_d = k * k_decay[None, None, :, :, None]

    def step(state, inp):
        qc, kc, vc = inp
        inter = jnp.einsum("bhid,bhdv->bhiv", qc, state)
        state = chunk_decay[None, :, None, None] * state + jnp.einsum("bhjd,bhjv->bhdv", kc, vc)
        return state, inter

    _, inter = lax.scan(step, jnp.zeros((b, hh, dqk, dv), jnp.float32), (q_d, k_d, v))
    return intra + inter


def token_mixer(u, w_in, ml_conv_w, ml_b_i, ml_b_f, ml_norm_g, rt_norm_g, w_out):
    b, seq_len, _ = u.shape
    n_pad = CHUNK - N_META
    up = jnp.pad(u, ((0, 0), (n_pad, 0), (0, 0)))
    lp = up.shape[1]
    valid = jnp.arange(lp) >= n_pad
    proj = up @ w_in
    offs = [int(o) for o in np.cumsum(COLUMN_SIZES)[:-1]]
    (ml_q, ml_k, ml_v, ml_o, ml_i, ml_f, rt_q, rt_k, rt_v, rt_g) = jnp.split(proj, offs, axis=-1)

    qk = jax.nn.silu(causal_dwconv(jnp.concatenate([ml_q, ml_k], axis=-1), ml_conv_w))
    qk = qk.astype(jnp.float32)
    q = qk[..., : ML_HEADS * ML_DQK].reshape(b, lp, ML_HEADS, ML_DQK)
    k = qk[..., ML_HEADS * ML_DQK:].reshape(b, lp, ML_HEADS, ML_DQK) * (ML_DQK ** -0.5)
    v = ml_v.astype(jnp.float32).reshape(b, lp, ML_HEADS, ML_DV)
    i_pre = ml_i.astype(jnp.float32) + ml_b_i.astype(jnp.float32)
    li = GATE_CAP * jnp.tanh(i_pre / GATE_CAP)
    li = jnp.where(valid[None, :, None], li, NEG)
    lf = jax.nn.log_sigmoid(ml_f.astype(jnp.float32) + ml_b_f.astype(jnp.float32))
    h_ml = mlstm_chunkwise(to_chunks(q), to_chunks(k), to_chunks(v),
                           to_chunks(li[..., None])[..., 0], to_chunks(lf[..., None])[..., 0])
    h_ml = head_norm(from_chunks(h_ml), ml_norm_g).reshape(b, lp, ML_HEADS * ML_DV)
    h_ml = jax.nn.sigmoid(ml_o.astype(jnp.float32)) * h_ml

    pos = jnp.arange(lp, dtype=jnp.float32)
    rq = rotary(rt_q.astype(jnp.float32).reshape(b, lp, RT_HEADS, RT_DQK), pos)
    rk = rotary(rt_k.astype(jnp.float32).reshape(b, lp, RT_HEADS, RT_DQK), pos)
    rk = rk * (RT_DQK ** -0.5) * valid[None, :, None, None]
    rv = rt_v.astype(jnp.float32).reshape(b, lp, RT_HEADS, RT_DV)
    log_gamma = jnp.log1p(-jnp.exp2(-(5.0 + jnp.arange(RT_HEADS, dtype=jnp.float32))))
    h_rt = retention_chunkwise(to_chunks(rq), to_chunks(rk), to_chunks(rv), log_gamma)
    h_rt = head_norm(from_chunks(h_rt), rt_norm_g).reshape(b, lp, RT_HEADS * RT_DV)
    h_rt = jax.nn.silu(rt_g.astype(jnp.float32)) * h_rt

    mixed = jnp.concatenate([h_ml, h_rt], axis=-1).astype(u.dtype)
    return (mixed @ w_out)[:, n_pad:]


def channel_mixer(u, w_up, w_gate, ffn_conv_w, w_down):
    a = causal_dwconv(u @ w_up, ffn_conv_w)
    return (jax.nn.silu(a) * (u @ w_gate)) @ w_down


def setup_inputs(seed: int = 0) -> dict:
    key = jax.random.key(seed)
    ks = jax.random.split(key, 20)
    f32 = jnp.float32
    nrm = lambda k, s, sc: jax.random.normal(k, s, f32) * sc
    return {
        "x": nrm(ks[0], (BATCH, SEQ, D_MODEL), 1.0),
        "meta_tokens": nrm(ks[1], (N_META, D_MODEL), 1.0),
        "norm_mix_g": 1.0 + nrm(ks[2], (DEPTH, D_MODEL), 0.02),
        "w_in": nrm(ks[3], (DEPTH, D_MODEL, D_IN), D_MODEL ** -0.5),
        "ml_conv_w": nrm(ks[4], (DEPTH, ML_CONV, 2 * ML_HEADS * ML_DQK), ML_CONV ** -0.5),
        "ml_b_i": nrm(ks[5], (DEPTH, ML_HEADS), 0.1),
        "ml_b_f": jnp.linspace(3.0, 6.0, ML_HEADS, dtype=f32)[None, :] + nrm(ks[6], (DEPTH, ML_HEADS), 0.1),
        "ml_norm_g": 1.0 + nrm(ks[7], (DEPTH, ML_HEADS * ML_DV), 0.02),
        "rt_norm_g": 1.0 + nrm(ks[8], (DEPTH, RT_HEADS * RT_DV), 0.02),
        "w_out": nrm(ks[9], (DEPTH, D_MIX, D_MODEL), D_MIX ** -0.5),
        "norm_ffn_g": 1.0 + nrm(ks[10], (DEPTH, D_MODEL), 0.02),
        "w_up": nrm(ks[11], (DEPTH, D_MODEL, D_FF), D_MODEL ** -0.5),
        "w_gate": nrm(ks[12], (DEPTH, D_MODEL, D_FF), D_MODEL ** -0.5),
        "ffn_conv_w": nrm(ks[13], (DEPTH, FFN_CONV, D_FF), FFN_CONV ** -0.5),
        "w_down": nrm(ks[14], (DEPTH, D_FF, D_MODEL), D_FF ** -0.5),
        "norm_final_g": 1.0 + nrm(ks[15], (D_MODEL,), 0.02),
    }


def reference(x, meta_tokens, norm_mix_g, w_in, ml_conv_w, ml_b_i, ml_b_f, ml_norm_g, rt_norm_g,
              w_out, norm_ffn_g, w_up, w_gate, ffn_conv_w, w_down, norm_final_g):
    b, _, d = x.shape
    meta = jnp.broadcast_to(meta_tokens.astype(x.dtype)[None], (b, N_META, d))
    h = jnp.concatenate([meta, x], axis=1)
    for l in range(DEPTH):
        h = h + token_mixer(rmsnorm(h, norm_mix_g[l]), w_in[l], ml_conv_w[l], ml_b_i[l], ml_b_f[l],
                            ml_norm_g[l], rt_norm_g[l], w_out[l]).astype(h.dtype)
        h = h + channel_mixer(rmsnorm(h, norm_ffn_g[l]), w_up[l], w_gate[l], ffn_conv_w[l],
                              w_down[l]).astype(h.dtype)
    return rmsnorm(h, norm_final_g)[:, N_META:]
```

```python
import os
from contextlib import ExitStack

import numpy as np
import ml_dtypes

import concourse.bass as bass
import concourse.mybir as mybir
from concourse.bass_utils import run_bass_kernel_spmd

F32 = mybir.dt.float32
BF16 = mybir.dt.bfloat16
ALU = mybir.AluOpType
AF = mybir.ActivationFunctionType

NCH_P = 16
NCH_F = 17
NCH = NCH_P + NCH_F
CH = 128
NPOS = NCH * CH
D = 1024
DFF = 2816
NJ = DFF // 128
WCOLS = 6152
EPS = 1e-6
NEG = -1e30
GATE_CAP = 15.0
SBUF_BASE = 16640
SBUF_END = 229376
S_ML = 128.0 ** -0.5
LOG_GAMMA = [float(np.log1p(-2.0 ** (-(5.0 + h)))) for h in range(4)]
CD = [float(np.exp(lg * CH)) for lg in LOG_GAMMA]

ENGS = ("pe", "act", "dve", "pool", "sp")
DEBUG = bool(int(os.environ.get("MK_DEBUG", "0")))


class _Op:
    __slots__ = ("eng", "fn", "deps", "dma_sem", "dma_cnt", "sig", "idx", "need_sig", "wk")

    def __init__(self, eng, fn):
        self.eng = eng
        self.fn = fn
        self.deps = []
        self.dma_sem = None
        self.dma_cnt = 0
        self.sig = 0
        self.need_sig = False


class Sched:
    def __init__(self):
        self.ops = []
        self.last_w = {}
        self.readers = {}
        self.dma_counts = {}
        self.fence = None

    fake_par = None
    FAKE_KEEP = ("bk", "W", "Cml_f", "Cml_bf", "Crt_f", "Crt_bf", "asb", "mixed_d", "cst", "vmt", "uT")

    def _fk(self, keys):
        if self.fake_par is None:
            return keys
        out = []
        for k in keys:
            base = k[0] if isinstance(k, tuple) else k
            if base == "bk" and os.environ.get("MK_FAKEBK"):
                out.append((k, self.fake_par))
                continue
            if base in self.FAKE_KEEP or base in ("mst", "ident_bf", "mhalf", "i4", "ones4", "zeros4", "convml", "dqk", "cmask",
                                                   "maskneg4", "ident_f", "g_mix_b", "bsc0", "bsc1", "bif"):
                out.append(k)
            else:
                out.append((k, self.fake_par))
        return out

    def add(self, eng, fn, reads=(), writes=(), dma=None):
        reads = self._fk(reads)
        writes = self._fk(writes)
        op = _Op(eng, fn)
        op.wk = list(writes)[:2]
        op.idx = len(self.ops)
        deps = {}

        def dep(o, raw):
            val = self.dma_counts[o.dma_sem] if o.dma_sem is not None else 0
            if o.idx in deps:
                if raw and not deps[o.idx][1]:
                    deps[o.idx] = (o, True, val)
            else:
                deps[o.idx] = (o, raw, val)

        for r in reads:
            w = self.last_w.get(r)
            if w is not None:
                dep(w, True)
        for k in writes:
            w = self.last_w.get(k)
            if w is not None:
                dep(w, False)
            for rd in self.readers.get(k, ()):
                dep(rd, False)
        if self.fence is not None:
            dep(self.fence[eng], False)
        op.deps = list(deps.values())
        for r in reads:
            self.readers.setdefault(r, []).append(op)
        for k in writes:
            self.last_w[k] = op
            self.readers[k] = []
        if dma is not None:
            op.dma_sem = dma
            self.dma_counts[dma] = self.dma_counts.get(dma, 0) + 16
            op.dma_cnt = self.dma_counts[dma]
        self.ops.append(op)
        return op

    def all_keys(self):
        return list(set(self.last_w.keys()) | set(self.readers.keys()))

    def barrier(self):
        keys = self.all_keys()
        fence = {}
        for e in ENGS:
            fence[e] = self.add(e, lambda eng: eng.nop(), writes=keys)
        self.fence = fence

    @staticmethod
    def _skip(d, op, raw):
        if d.dma_sem is not None or op.dma_sem is not None:
            return False
        if d.eng != op.eng:
            return False
        return d.eng == "pe"

    def schedule(self, window=int(os.environ.get("MK_WIN", "48")), lat=float(os.environ.get("MK_LAT", "180"))):
        ops = self.ops
        n = len(ops)
        ndeps = [0] * n
        users = [[] for _ in range(n)]
        for op in ops:
            ndeps[op.idx] = len(op.deps)
            for (d, raw, val) in op.deps:
                users[d.idx].append(op.idx)
        fin = [0.0] * n
        ready_t = [0.0] * n
        pend = {e: [op.idx for op in ops if op.eng == e] for e in ENGS}
        pos = {e: 0 for e in ENGS}
        done = [False] * n
        efree = {e: 0.0 for e in ENGS}
        order = {e: [] for e in ENGS}
        remaining = n
        while remaining:
            best = None
            for e in ENGS:
                lst = pend[e]
                i = pos[e]
                while i < len(lst) and done[lst[i]]:
                    i += 1
                pos[e] = i
                if i >= len(lst):
                    continue
                w = 1 if e == "sp" else window
                seen = 0
                j = i
                cand = None
                dma_blocked = False
                while j < len(lst) and seen < w:
                    k = lst[j]
                    j += 1
                    if done[k]:
                        continue
                    seen += 1
                    isdma = ops[k].dma_sem is not None
                    if isdma and dma_blocked:
                        continue
                    if isdma:
                        dma_blocked = True
                    if ndeps[k] > 0:
                        continue
                    st = max(ready_t[k], efree[e])
                    key = (st, k)
                    if cand is None or key < cand:
                        cand = key
                        if ready_t[k] <= efree[e]:
                            break
                if cand is not None and (best is None or cand < best[0]):
                    best = (cand, e)
            assert best is not None, "scheduler deadlock"
            (st, k), e = best
            op = ops[k]
            c = getattr(op.fn, "cost", 150.0)
            if op.dma_sem is not None:
                efree[e] = st + 60.0
                fin[k] = st + c
            else:
                efree[e] = st + c
                fin[k] = st + c
            done[k] = True
            remaining -= 1
            order[e].append(op)
            for u in users[k]:
                ndeps[u] -= 1
                t = fin[k] + lat
                if t > ready_t[u]:
                    ready_t[u] = t
        self.est_ns = max(fin) if n else 0.0
        if os.environ.get("MK_CRIT"):
            dist = [0.0] * n
            pred = [-1] * n
            for op in ops:
                c = getattr(op.fn, "cost", 150.0)
                best_t, best_p = 0.0, -1
                for (d, raw, val) in op.deps:
                    t = dist[d.idx] + lat
                    if t > best_t:
                        best_t, best_p = t, d.idx
                dist[op.idx] = best_t + c
                pred[op.idx] = best_p
            lim = self.fence["pe"].idx if self.fence else n
            k = max(range(lim), key=lambda i: dist[i])
            print("[crit] dependency-only critical path before fence:", round(dist[k] / 1000), "us")
            path = []
            while k >= 0:
                path.append(k)
                k = pred[k]
            path.reverse()
            import collections
            cnt = collections.Counter(ops[i].eng for i in path)
            print("[crit] path len", len(path), dict(cnt))
            self.crit_path = path
            mid = int(len(path) * float(os.environ.get("MK_CRITPOS", "0.5")))
            for i in path[mid:mid + 70]:
                print("[crit]  ", ops[i].eng, ops[i].wk, round(getattr(ops[i].fn, "cost", 150.0)))
        if os.environ.get("MK_SCHED_DBG"):
            B = 100000.0
            nb = int(self.est_ns // B) + 1
            busy = {e: [0.0] * nb for e in ENGS}
            for op in ops:
                c = getattr(op.fn, "cost", 150.0)
                if op.dma_sem is not None:
                    continue
                b = int((fin[op.idx] - c) // B)
                busy[op.eng][b] += c
            for b in range(nb):
                print(f"[sched] {b*100:6d}us " + " ".join(f"{e}:{busy[e][b]/B*100:5.1f}%" for e in ENGS if e != "sp"))
            if self.fence:
                print("[sched] fence done at", {e: round(fin[o.idx] / 1000) for e, o in self.fence.items()})
        return order

    def emit(self, block, sems, dma_sems, reorder=True):
        for op in self.ops:
            for (d, raw, val) in op.deps:
                if d.dma_sem is None and not self._skip(d, op, raw):
                    d.need_sig = True
        if reorder:
            per_eng = self.schedule()
        else:
            per_eng = {e: [] for e in ENGS}
            for op in self.ops:
                per_eng[op.eng].append(op)
        cnt = {e: 0 for e in ENGS}
        for e in ENGS:
            for op in per_eng[e]:
                if op.dma_sem is None and op.need_sig:
                    cnt[e] += 1
                    op.sig = cnt[e]
        handles = {"pe": "tensor", "act": "scalar", "dve": "vector", "pool": "gpsimd", "sp": "sync"}

        def body(eng_name):
            def _f(eng):
                waited = {}
                for op in per_eng[eng_name]:
                    for (d, raw, val) in op.deps:
                        if d.dma_sem is not None:
                            key = ("dma", d.dma_sem)
                            sem = dma_sems[d.dma_sem]
                        else:
                            if self._skip(d, op, raw):
                                continue
                            key = d.eng
                            val = d.sig
                            sem = sems[d.eng]
                        if waited.get(key, 0) >= val:
                            continue
                        waited[key] = val
                        eng.wait_ge(sem, val)
                    ins = op.fn(eng)
                    if op.dma_sem is not None:
                        ins.then_inc(dma_sems[op.dma_sem], 16)
                    elif op.need_sig:
                        ins.then_inc(sems[eng_name], 1)
            return _f

        for e in ENGS:
            if per_eng[e]:
                getattr(block, handles[e])(body(e))


def _mmcost(lhsT, rhs):
    n = rhs.free_size()
    c = max(lhsT.free_size() / 1.2, n / 2.37, 30.0)
    if rhs.dtype == F32:
        c *= 4.0
    return c


def _fsz(ap):
    return ap.free_size()


def _mm(out, lhsT, rhs, start=True, stop=True):
    f = lambda e: e.matmul(out, lhsT=lhsT, rhs=rhs, start=start, stop=stop)
    f.cost = _mmcost(lhsT, rhs)
    return f


def _mmk(out, pairs):
    def f(e):
        n = len(pairs)
        ins = None
        for i, (l, r) in enumerate(pairs):
            ins = e.matmul(out, lhsT=l, rhs=r, start=(i == 0), stop=(i == n - 1))
        return ins
    f.cost = sum(_mmcost(l, r) for (l, r) in pairs)
    return f


def _trs(items, ident):
    def f(e):
        ins = None
        for (o, i) in items:
            ins = e.transpose(out=o, in_=i, identity=ident)
        return ins
    f.cost = 120.0 * len(items)
    return f


def _act(out, in_, func, bias=None, scale=None, accum=None):
    def f(e):
        kw = {}
        if bias is not None:
            kw["bias"] = bias
        if scale is not None:
            kw["scale"] = scale
        if accum is not None:
            kw["accum_out"] = accum
        return e.activation(out=out, in_=in_, func=func, **kw)
    f.cost = (224.0 + _fsz(out)) / 1.2
    return f


def _ts(out, in0, s1, op0, s2=None, op1=None):
    def f(e):
        if op1 is None:
            return e.tensor_scalar(out=out, in0=in0, scalar1=s1, scalar2=None, op0=op0)
        return e.tensor_scalar(out=out, in0=in0, scalar1=s1, scalar2=s2, op0=op0, op1=op1)
    f.cost = (100.0 + _fsz(out)) / 0.96
    return f


def _tt(out, in0, in1, op):
    f = lambda e: e.tensor_tensor(out=out, in0=in0, in1=in1, op=op)
    f.cost = (100.0 + _fsz(out)) / 0.96
    return f


def _stt(out, in0, scalar, in1, op0, op1):
    f = lambda e: e.scalar_tensor_tensor(out=out, in0=in0, scalar=scalar, in1=in1, op0=op0, op1=op1)
    f.cost = (100.0 + _fsz(out)) / 0.96
    return f


def _cp(out, in_):
    f = lambda e: e.tensor_copy(out=out, in_=in_)
    f.cost = (100.0 + _fsz(out)) / 0.96
    return f


def _dma(out, in_):
    f = lambda e: e.dma_start(out=out, in_=in_)
    f.cost = 2000.0 + 128.0 * _fsz(out) * 4 / 200.0
    return f


def _scan(out, d0, d1, init, op0, op1):
    f = lambda e: e.tensor_tensor_scan(out=out, data0=d0, data1=d1, initial=init, op0=op0, op1=op1)
    f.cost = (100.0 + 2 * _fsz(out)) / 0.96
    return f


class _Arena:
    def __init__(self, nc, base, end):
        self.nc = nc
        self.off = base
        self.end = end

    def alloc(self, name, shape, dt):
        size = 1
        for s in shape[1:]:
            size *= s
        size *= 2 if dt == BF16 else 4
        off = (self.off + 31) // 32 * 32
        assert off + size <= self.end, f"SBUF overflow at {name}: {off + size} > {self.end}"
        t = self.nc.alloc_sbuf_tensor_at(name, list(shape), dt, offset=off)
        self.off = off + size
        return t.ap()


WGROUPS = [(1024, 2056), (512, 1024), (5640, 6152), (3080, 4104), (0, 512), (5128, 5640), (2056, 3080), (4104, 5128)]


def _wgroup_of(col):
    for g, (a, b) in enumerate(WGROUPS):
        if a <= col < b:
            return g
    raise ValueError(col)


def build_program():
    nc = bass.Bass("TRN2", target_bir_lowering=False)

    def din(name, shape, dt=F32):
        return nc.dram_tensor(name, list(shape), dt, kind="ExternalInput").ap()

    xs_d = din("xs", [NPOS, D])
    win_d = din("w_in_r", [D, WCOLS])
    wout_d = din("w_out", [2048, D])
    wup_d = din("w_up", [D, DFF])
    wgate_d = din("w_gate", [D, DFF])
    wdn_d = din("w_down", [DFF, D])
    cs_d = din("cs_tab", [NCH, 128, 1024])
    vm_d = din("vm_tab", [NCH, 4, 2, 128])
    dqk_d = din("dqk", [128, 12])
    cmask_d = din("cmask", [128, 128])
    mneg_d = din("maskneg4", [128, 128])
    identb_d = din("ident_bf", [128, 128], BF16)
    identf_d = din("ident_f", [128, 128])
    i4_d = din("i4", [4, 8])
    convml_d = din("convw_ml", [128, 32])
    convff_d = din("convw_ffn", [128, NJ * 3])
    bif_d = din("b_if", [4, 2])
    gcol_d = din("gcol", [128, 16])
    gmix_d = din("g_mix_b", [128, D])
    gffn_d = din("g_ffn_b", [128, D])
    gfin_d = din("g_fin_b", [128, D])
    out_d = nc.dram_tensor("out", [2048, D], F32, kind="ExternalOutput").ap()
    mixed_d = nc.dram_tensor("mixed_d", [NCH_F * CH, 2048], BF16,
                             kind="ExternalOutput" if DEBUG else "Internal").ap()

    S = Sched()
    banks = [nc.alloc_psum_tensor(f"bank{i}", [128, 512], F32).ap() for i in range(8)]

    per = _Arena(nc, SBUF_BASE, SBUF_END)
    ident_bf = per.alloc("ident_bf", [128, 128], BF16)
    mhalf = per.alloc("mhalf", [128, 4], F32)
    stat = per.alloc("stat", [128, 8], F32)
    PH_BASE = per.off

    S.add("sp", _dma(ident_bf, identb_d), writes=["ident_bf"], dma="cst")
    S.add("pool", lambda e: e.memset(mhalf, -0.5), writes=["mhalf"])

    A1 = _Arena(nc, PH_BASE, SBUF_END)
    W = A1.alloc("W", [128, 8, WCOLS], BF16)
    ident_f = A1.alloc("ident_f", [128, 128], F32)
    cmask = A1.alloc("cmask", [128, 128], F32)
    maskneg4 = A1.alloc("maskneg4", [128, 128], F32)
    dqk = A1.alloc("dqk", [128, 12], F32)
    g_mix_b = A1.alloc("g_mix_b", [128, D], F32)
    convml = A1.alloc("convml", [128, 32], F32)
    i4 = A1.alloc("i4", [4, 8], F32)
    bif = A1.alloc("bif", [4, 2], F32)
    bsc = A1.alloc("bsc", [4, 2], F32)
    ones4 = A1.alloc("ones4", [4, 128], F32)
    zeros4 = A1.alloc("zeros4", [4, 128], F32)
    xin = A1.alloc("xin", [128, D], F32)
    u = A1.alloc("u", [128, D], BF16)
    uT = [A1.alloc(f"uT{i}", [128, 8, 128], BF16) for i in range(2)]
    asb = A1.alloc("asb", [128, 8, 131], F32)
    cacc = [A1.alloc(f"cacc{i}", [128, 128], F32) for i in range(2)]
    qTmls = [A1.alloc(f"qTml{i}", [128, 4, 128], BF16) for i in range(2)]
    kTmls = [A1.alloc(f"kTml{i}", [128, 4, 128], BF16) for i in range(2)]
    rX = [A1.alloc("rX0", [128, 512], F32)] * 2
    rM = [A1.alloc(f"rM{i}", [128, 256], F32) for i in range(4)]
    qtok = A1.alloc("qtok", [128, 4, 128], BF16)
    ktok = A1.alloc("ktok", [128, 4, 128], BF16)
    cst = [A1.alloc("cst0", [128, 1024], F32)] * 2
    vmt = [A1.alloc("vmt0", [4, 2, 128], F32)] * 2
    qTrts = [A1.alloc(f"qTrt{i}", [128, 4, 128], BF16) for i in range(2)]
    kTrts = [A1.alloc(f"kTrt{i}", [128, 4, 128], BF16) for i in range(2)]
    kw = A1.alloc("kw", [128, 4, 128], BF16)
    Vmls = [A1.alloc(f"Vml{i}", [128, 4, 257], BF16) for i in range(2)]
    Vrts = [A1.alloc(f"Vrt{i}", [128, 4, 256], BF16) for i in range(2)]
    ogs = [A1.alloc(f"og{i}", [128, D], F32) for i in range(2)]
    ggs = [A1.alloc(f"gg{i}", [128, D], F32) for i in range(2)]
    mixed = A1.alloc("mixed", [128, 2048], BF16)
    Cml_f = A1.alloc("Cml_f", [128, 4, 257], F32)
    Cml_bfs = [A1.alloc(f"Cml_bf{i}", [128, 4, 257], BF16) for i in range(2)]
    Crt_f = A1.alloc("Crt_f", [128, 4, 256], F32)
    Crt_bfs = [A1.alloc(f"Crt_bf{i}", [128, 4, 256], BF16) for i in range(2)]
    WT = A1.alloc("WT", [128, 4, 128], F32)
    PT = A1.alloc("PT", [128, 4, 128], BF16)
    Wint = A1.alloc("Wint", [128, 512], F32)
    qsT = A1.alloc("qsT", [128, 4, 128], BF16)
    hrt = A1.alloc("hrt", [128, 4, 256], F32)
    hraw = A1.alloc("hraw", [128, 4, 257], F32)
    rows = {n: A1.alloc("row_" + n, [4, 128], F32) for n in
            ("li0", "li1", "li", "ef", "sp", "nbcum", "B", "M", "R2", "R3")}
    rhs_bd = A1.alloc("rhs_bd", [4, 4, 128], F32)
    smr = A1.alloc("smr", [4, 16], F32)
    sm = A1.alloc("sm", [128, 64], F32)
    smps = A1.alloc("smps", [128, 20], F32)
    st6 = A1.alloc("st6", [128, 4, 6], F32)
    mv = A1.alloc("mv", [128, 4, 2], F32)

    EX8 = sm[:, 0:8]
    BT = sm[:, 8:12]
    BTS = sm[:, 12:16]
    WSARG = sm[:, 16:20]
    WSRC = sm[:, 20:24]
    DEC = sm[:, 24:28]
    DD = sm[:, 28:32]
    RDEN = sm[:, 32:36]
    T1 = sm[:, 36:40]
    T2 = sm[:, 40:44]
    RSTD = sm[:, 44:48]
    SC = sm[:, 48:52]
    BI = sm[:, 52:56]

    bT = banks[0]
    bT_bf = bT.bitcast(BF16)

    for nm, dst, src in (("ident_f", ident_f, identf_d), ("cmask", cmask, cmask_d), ("maskneg4", maskneg4, mneg_d),
                         ("dqk", dqk, dqk_d), ("g_mix_b", g_mix_b, gmix_d),
                         ("convml", convml, convml_d), ("i4", i4, i4_d), ("bif", bif, bif_d)):
        S.add("sp", _dma(dst, src), writes=[nm], dma="cst")
    for g, (c0, c1) in enumerate(WGROUPS):
        for k in range(8):
            S.add("pool", _dma(W[:, k, c0:c1], win_d[k * 128:(k + 1) * 128, c0:c1]),
                  writes=[("W", g, k)], dma=f"W{g}")

    def wk(col):
        g = _wgroup_of(col)
        return [("W", g, k) for k in range(8)]

    S.add("pool", lambda e: e.memset(ones4, 1.0), writes=["ones4"])
    S.add("pool", lambda e: e.memset(zeros4, 0.0), writes=["zeros4"])
    S.add("pool", lambda e: e.memset(asb, 0.0), writes=[("asb", t) for t in range(8)])
    S.add("pool", lambda e: e.memset(Vmls[0], 1.0), writes=[("Vml", 0, 0), ("Vml", 0, 1)])
    S.add("pool", lambda e: e.memset(Vmls[1], 1.0), writes=[("Vml", 1, 0), ("Vml", 1, 1)])
    S.add("pool", lambda e: e.memset(Cml_f, 0.0), writes=[("Cml_f", h) for h in range(4)])
    S.add("pool", lambda e: e.memset(Cml_bfs[0], 0.0), writes=[("Cml_bf", 0, h) for h in range(4)])
    S.add("pool", lambda e: e.memset(Crt_f, 0.0), writes=[("Crt_f", h) for h in range(4)])
    S.add("pool", lambda e: e.memset(Crt_bfs[0], 0.0), writes=[("Crt_bf", 0, h) for h in range(4)])
    S.add("pool", lambda e: e.memset(smr, 0.0), writes=["mst", "D1", "diagM", "diagD"])
    S.add("pool", lambda e: e.memset(smr[:, 0:1], NEG), writes=["mst"])
    S.add("dve", _ts(bsc[:, 0:1], bif[:, 0:1], 1.0 / GATE_CAP, ALU.mult), reads=["bif"], writes=["bsc0"])
    S.add("dve", _ts(bsc[:, 1:2], bif[:, 1:2], -1.0, ALU.mult), reads=["bif"], writes=["bsc1"])

    MST = smr[:, 0:1]
    D1 = smr[:, 1:2]
    DIAGM = smr[:, 4:8]
    DIAGD = smr[:, 8:12]

    def loads(c):
        s = c % 2
        S.add("sp", _dma(xin, xs_d[c * 128:(c + 1) * 128, :]), writes=["xin"], dma="xin")

    def loads2(c):
        S.add("sp", _dma(cst[0], cs_d[c]), writes=[("cst", 0)], dma="cst0")
        S.add("sp", _dma(vmt[0], vm_d[c]), writes=[("vmt", 0)], dma="vmt0")

    aslot = [0]

    def next_aslot():
        i = aslot[0] % 4
        aslot[0] += 1
        return i

    tmslot = [0]

    def rmsnorm_T(src, gb, dstT, dst_keys, src_key, gkey):
        S.add("act", _act(u, src, AF.Square, accum=stat[:, 0:1]), reads=[src_key], writes=["u", "ss"])
        S.add("dve", _ts(stat[:, 1:2], stat[:, 0:1], 1.0 / D, ALU.mult, EPS, ALU.add), reads=["ss"], writes=["ms"])
        S.add("pool", _tt(stat[:, 2:3], stat[:, 1:2], mhalf[:, 0:1], ALU.pow), reads=["ms", "mhalf"], writes=["rstd"])
        S.add("dve", _stt(u, src, stat[:, 2:3], gb, ALU.mult, ALU.mult), reads=[src_key, "rstd", gkey], writes=["u"])
        S.add("pe", _trs([(bT_bf[:, k * 128:(k + 1) * 128], u[:, k * 128:(k + 1) * 128]) for k in range(8)], ident_bf),
              reads=["u", "ident_bf"], writes=[("bk", 0)])
        S.add("act", _act(dstT, bT_bf.rearrange("p (k t) -> p k t", k=8), AF.Copy), reads=[("bk", 0)], writes=dst_keys)


    fmb = [0]

    def chunk(c, full):
        rp, wp = c % 2, (c + 1) % 2
        qTml, kTml, qTrt, kTrt = qTmls[rp], kTmls[rp], qTrts[rp], kTrts[rp]
        og, gg = ogs[rp], ggs[rp]
        Vml, Vrt = Vmls[rp], Vrts[rp]
        Cml_bf, Crt_bf = Cml_bfs[rp], Crt_bfs[rp]
        Cml_bfw, Crt_bfw = Cml_bfs[wp], Crt_bfs[wp]
        if os.environ.get("MK_FAKE2"):
            S.fake_par = c % int(os.environ["MK_FAKE2"])
        s = c % 2
        uTs = uT[s]
        rmsnorm_T(xin, g_mix_b, uTs, [("uT", s)], "xin", "g_mix_b")
        if c + 1 < NCH:
            loads(c + 1)

        def fm_group(cols_ms):
            b = 1 + fmb[0] % 2
            fmb[0] += 1
            bk = banks[b]
            for sl, (col, m) in enumerate(cols_ms):
                S.add("pe", _mmk(bk[0:m, sl * 128:(sl + 1) * 128], [(W[:, k, col:col + m], uTs[:, k, :]) for k in range(8)]),
                      reads=wk(col) + [("uT", s)], writes=[("bk", b)])
            return bk, ("bk", b)

        for grp in ((0, 1) if full else (1,)):
            bk, bkey = fm_group([((grp * 4 + t) * 128, 128) for t in range(4)])
            S.add("act", _act(asb[:, grp * 4:grp * 4 + 4, 3:131], bk.rearrange("p (a b) -> p a b", a=4), AF.Copy),
                  reads=[bkey], writes=[("asb", grp * 4 + t) for t in range(4)])
            for t in range(grp * 4, grp * 4 + 4):
                acc = cacc[t % 2]
                ak = ("cacc", t % 2)
                S.add("dve", _ts(acc, asb[:, t, 3:131], convml[:, t * 4 + 3:t * 4 + 4], ALU.mult),
                      reads=[("asb", t), "convml"], writes=[ak])
                for kk in (2, 1, 0):
                    S.add("dve", _stt(acc, asb[:, t, kk:kk + 128], convml[:, t * 4 + kk:t * 4 + kk + 1], acc, ALU.mult, ALU.add),
                          reads=[("asb", t), "convml", ak], writes=[ak])
                S.add("pool", _cp(asb[:, t, 0:3], asb[:, t, 128:131]), reads=[("asb", t)], writes=[("asb", t)])
                if t < 4:
                    S.add("act", _act(qTml[:, t, :], acc, AF.Silu), reads=[ak], writes=[("qTml", rp, t)])
                else:
                    S.add("act", _act(kTml[:, t - 4, :], acc, AF.Silu), reads=[ak], writes=[("kTml", rp, t - 4)])
        bk, gkey = fm_group([(1024, 4), (1028, 4)])
        gi_reg = bk[0:4, 0:128]
        gf_reg = bk[0:4, 128:256]
        R = rows
        vs = vmt[s]
        S.add("act", _act(R["li0"], gi_reg, AF.Tanh, bias=bsc[:, 0:1], scale=1.0 / GATE_CAP), reads=[gkey, "bsc0"], writes=["li0"])
        S.add("act", _act(R["ef"], gf_reg, AF.Exp, bias=bsc[:, 1:2], scale=-1.0), reads=[gkey, "bsc1"], writes=["ef"])
        S.add("dve", _stt(R["li1"], R["li0"], GATE_CAP, vs[:, 0, :], ALU.mult, ALU.mult), reads=["li0", ("vmt", 0)], writes=["li1"])
        S.add("dve", _tt(R["li"], R["li1"], vs[:, 1, :], ALU.add), reads=["li1", ("vmt", 0)], writes=["li"])
        S.add("act", _act(R["sp"], R["ef"], AF.Ln, bias=1.0), reads=["ef"], writes=["sp"])
        S.add("dve", _scan(R["nbcum"], R["sp"], zeros4, 0.0, ALU.add, ALU.add), reads=["sp", "zeros4"], writes=["nbcum"])
        S.add("dve", _tt(R["B"], R["li"], R["nbcum"], ALU.add), reads=["li", "nbcum"], writes=["B"])
        S.add("dve", _scan(R["M"], R["B"], R["B"], MST, ALU.max, ALU.max), reads=["B", "mst"], writes=["M"])
        S.add("dve", _ts(DIAGM, i4[:, 0:4], R["M"][:, 127:128], ALU.mult), reads=["i4", "M"], writes=["diagM"])
        S.add("dve", _tt(D1, MST, R["M"][:, 127:128], ALU.subtract), reads=["mst", "M"], writes=["D1"])
        S.add("dve", _ts(DIAGD, i4[:, 0:4], D1, ALU.mult), reads=["i4", "D1"], writes=["diagD"])
        if full:
            S.add("dve", _tt(R["R2"], R["nbcum"], R["M"], ALU.subtract), reads=["nbcum", "M"], writes=["R2"])
            S.add("dve", _ts(R["R2"], R["R2"], 80.0, ALU.min), reads=["R2"], writes=["R2"])
            S.add("dve", _ts(R["R3"], R["M"], MST, ALU.subtract, -1.0, ALU.mult), reads=["M", "mst"], writes=["R3"])
            for h in range(4):
                S.add("dve", _ts(rhs_bd[:, h, :], R["M"], i4[:, 4 + h:5 + h], ALU.mult), reads=["M", "i4"], writes=[("rhs_bd", h)])
        S.add("dve", _tt(MST, R["M"][:, 127:128], R["nbcum"][:, 127:128], ALU.subtract), reads=["M", "nbcum"], writes=["mst"])

        sb_ = 5
        SMP = banks[sb_][:, 0:32]
        skey = ("bk", sb_)

        def smp_mm(e):
            ins = e.matmul(SMP[:, 0:4], lhsT=R["B"], rhs=i4[:, 0:4], start=True, stop=True)
            if full:
                e.matmul(SMP[:, 4:8], lhsT=R["R2"], rhs=i4[:, 0:4], start=True, stop=True)
                e.matmul(SMP[:, 8:12], lhsT=R["R3"], rhs=i4[:, 0:4], start=True, stop=True)
            e.matmul(SMP[:, 12:16], lhsT=ones4, rhs=DIAGM, start=True, stop=True)
            ins = e.matmul(SMP[:, 16:20], lhsT=ones4, rhs=DIAGD, start=True, stop=True)
            return ins
        S.add("pe", smp_mm, reads=["B", "R2", "R3", "i4", "ones4", "diagM", "diagD"], writes=[skey])
        if full:
            S.add("act", _act(smps, SMP[:, 0:20], AF.Copy), reads=[skey], writes=["smps"])
        else:
            S.add("act", _act(smps[:, 0:4], SMP[:, 0:4], AF.Copy), reads=[skey], writes=["smps"])
            S.add("act", _act(smps[:, 12:20], SMP[:, 12:20], AF.Copy), reads=[skey], writes=["smps"])
        if full:
            S.add("act", _act(EX8, smps[:, 4:12], AF.Exp), reads=["smps"], writes=["ex8"])
        S.add("dve", _tt(WSARG, smps[:, 0:4], smps[:, 12:16], ALU.subtract), reads=["smps"], writes=["wsarg"])
        S.add("act", _act(WSRC, WSARG, AF.Exp, bias=float(np.log(S_ML))), reads=["wsarg"], writes=["wsrc"])
        S.add("act", _act(DEC, smps[:, 16:20], AF.Exp), reads=["smps"], writes=["dec"])
        b5 = banks[5]
        k5 = ("bk", 5)
        if full:
            S.add("dve", _ts(BTS, smps[:, 0:4], float(np.log(S_ML)), ALU.add), reads=["smps"], writes=["BTS"])
            S.add("pe", _mmk(b5, [(ones4, rhs_bd.rearrange("p h t -> p (h t)")), (ident_f, maskneg4.unsqueeze(1).broadcast_to([128, 4, 128]))]),
                  reads=["ones4", "ident_f", "maskneg4"] + [("rhs_bd", h) for h in range(4)], writes=[k5])
            for h in range(4):
                S.add("act", _act(WT[:, h, :], b5[:, h * 128:(h + 1) * 128], AF.Exp, bias=BTS[:, h:h + 1]),
                      reads=[k5, "BTS"], writes=[("WT", h)])
            for h in range(4):
                S.add("dve", _ts(rhs_bd[:, h, :], R["R3"], i4[:, h:h + 1], ALU.mult), reads=["R3", "i4"], writes=[("rhs_bd", h)])
            S.add("pe", _mm(b5, ones4, rhs_bd.rearrange("p h t -> p (h t)")),
                  reads=["ones4"] + [("rhs_bd", h) for h in range(4)], writes=[k5])
            S.add("act", _act(Wint, b5, AF.Exp), reads=[k5], writes=["Wint"])
            S.add("dve", _tt(qsT.rearrange("p h t -> p (h t)"), qTml.rearrange("p h t -> p (h t)"), Wint, ALU.mult),
                  reads=["Wint"] + [("qTml", rp, h) for h in range(4)], writes=["qsT"])

        def tm_tile(col):
            b = 3 + tmslot[0] % 2
            tmslot[0] += 1
            S.add("pe", _mmk(banks[b], [(uTs[:, k, :], W[:, k, col:col + 512]) for k in range(8)]),
                  reads=wk(col) + [("uT", s)], writes=[("bk", b)])
            return banks[b], ("bk", b)

        for hh in range(2):
            reg, rk = tm_tile(1032 + hh * 512)
            S.add("act", _act(Vml[:, 2 * hh:2 * hh + 2, 0:256], reg.rearrange("p (a b) -> p a b", a=2), AF.Copy),
                  reads=[rk], writes=[("Vml", rp, hh)])
        for hh in range(2):
            reg, rk = tm_tile(3080 + hh * 512)
            S.add("dve", _cp(Vrt[:, 2 * hh:2 * hh + 2, :], reg.rearrange("p (a b) -> p a b", a=2)),
                  reads=[rk], writes=[("Vrt", rp, hh)])
        if full:
            for hh in range(2):
                reg, rk = tm_tile(2056 + hh * 512)
                S.add("act", _act(og[:, hh * 512:(hh + 1) * 512], reg, AF.Tanh, scale=0.5), reads=[rk], writes=[("og", rp, hh)])
            for hh in range(2):
                reg, rk = tm_tile(4104 + hh * 512)
                S.add("act", _act(gg[:, hh * 512:(hh + 1) * 512], reg, AF.Silu), reads=[rk], writes=[("gg", rp, hh)])

        def rotary(col, xi, qk, dst, dkey):
            reg, rk = tm_tile(col)
            X = rX[xi]
            xk = ("rX", 0)
            S.add("act", _act(X, reg, AF.Copy), reads=[rk], writes=[xk])
            Xv = X.rearrange("p (h a t) -> p h a t", h=4, a=2)
            Tc = cst[0][:, (qk * 2) * 256:(qk * 2 + 1) * 256].rearrange("p (h t) -> p h t", h=4)
            Ts = cst[0][:, (qk * 2 + 1) * 256:(qk * 2 + 2) * 256].rearrange("p (h t) -> p h t", h=4)
            Mv = [m.rearrange("p (h t) -> p h t", h=4) for m in rM]
            Dv = dst.rearrange("p h (a t) -> p h a t", a=2)
            S.add("dve", _tt(Mv[0], Xv[:, :, 0, :], Tc, ALU.mult), reads=[xk, ("cst", 0)], writes=[("rM", 0)])
            S.add("dve", _tt(Mv[1], Xv[:, :, 1, :], Ts, ALU.mult), reads=[xk, ("cst", 0)], writes=[("rM", 1)])
            S.add("pool", _tt(Mv[2], Xv[:, :, 0, :], Ts, ALU.mult), reads=[xk, ("cst", 0)], writes=[("rM", 2)])
            S.add("pool", _tt(Mv[3], Xv[:, :, 1, :], Tc, ALU.mult), reads=[xk, ("cst", 0)], writes=[("rM", 3)])
            S.add("dve", _tt(Dv[:, :, 0, :], Mv[0], Mv[1], ALU.subtract), reads=[("rM", 0), ("rM", 1)], writes=[(dkey, 0)])
            S.add("pool", _tt(Dv[:, :, 1, :], Mv[2], Mv[3], ALU.add), reads=[("rM", 2), ("rM", 3)], writes=[(dkey, 1)])

        rotary(5640, 0, 1, ktok, "ktok")
        if full and not os.environ.get("MK_X1"):
            rotary(5128, 1, 0, qtok, "qtok")
        if c + 1 < NCH:
            loads2(c + 1)

        ob = [6]

        def next_ob():
            b = 6 + ob[0] % 2
            ob[0] += 1
            return banks[b], ("bk", b)

        kb_, kbk = next_ob()
        b0bf = kb_.bitcast(BF16)
        S.add("pe", _trs([(b0bf[:, t * 128:(t + 1) * 128], kTml[:, t, :]) for t in range(4)], ident_bf),
              reads=[("kTml", rp, h) for h in range(4)] + ["ident_bf"], writes=[kbk])
        for t in range(4):
            S.add("act", _act(kw[:, t, :], b0bf[:, t * 128:(t + 1) * 128], AF.Copy, scale=WSRC[:, t:t + 1]),
                  reads=[kbk, "wsrc"], writes=[("kw", t)])
        if full:
            qb_, qbk = next_ob()
            qbbf = qb_.bitcast(BF16)
            S.add("pe", _trs([(qbbf[:, h * 128:(h + 1) * 128], ktok[:, h, :]) for h in range(4)] +
                             [(qbbf[:, (4 + h) * 128:(5 + h) * 128], qtok[:, h, :]) for h in range(4)], ident_bf),
                  reads=[("ktok", 0), ("ktok", 1), ("qtok", 0), ("qtok", 1), "ident_bf"], writes=[qbk])
            S.add("act", _act(kTrt, qbbf[:, 0:512].rearrange("p (h t) -> p h t", h=4), AF.Copy), reads=[qbk],
                  writes=[("kTrt", rp, h) for h in range(4)])
            S.add("act", _act(qTrt, qbbf[:, 512:1024].rearrange("p (h t) -> p h t", h=4), AF.Copy), reads=[qbk],
                  writes=[("qTrt", rp, h) for h in range(4)])

        for h in range(4):
            bo, ko = next_ob()
            S.add("pe", _mm(bo[:, 0:257], kw[:, h, :], Vml[:, h, :]), reads=[("kw", h), ("Vml", rp, h // 2)], writes=[ko])
            S.add("dve", _stt(Cml_f[:, h, :], Cml_f[:, h, :], DEC[:, h:h + 1], bo[:, 0:257], ALU.mult, ALU.add),
                  reads=[("Cml_f", h), "dec", ko], writes=[("Cml_f", h)])
            S.add("pool", _cp(Cml_bfw[:, h, :], Cml_f[:, h, :]), reads=[("Cml_f", h)], writes=[("Cml_bf", wp, h)])
        for pr in range(2):
            bo, ko = next_ob()
            for hh in range(2):
                h = 2 * pr + hh
                S.add("pe", _mm(bo[:, hh * 256:(hh + 1) * 256], ktok[:, h, :], Vrt[:, h, :]), reads=[("ktok", 0), ("ktok", 1), ("Vrt", rp, pr)], writes=[ko])
            for hh in range(2):
                h = 2 * pr + hh
                S.add("dve", _stt(Crt_f[:, h, :], Crt_f[:, h, :], CD[h], bo[:, hh * 256:(hh + 1) * 256], ALU.mult, ALU.add),
                      reads=[("Crt_f", h), ko], writes=[("Crt_f", h)])
            for hh in range(2):
                h = 2 * pr + hh
                S.add("pool", _ts(Crt_bfw[:, h, :], Crt_f[:, h, :], CD[h], ALU.mult, 0.0, ALU.add),
                      reads=[("Crt_f", h)], writes=[("Crt_bf", wp, h)])
        if full:
            S.add("pe", lambda e: [e.matmul(b5[:, h * 128:(h + 1) * 128], lhsT=kTml[:, h, :], rhs=qTml[:, h, :], start=True, stop=True)
                                   for h in range(4)][-1],
                  reads=[("kTml", rp, h) for h in range(4)] + [("qTml", rp, h) for h in range(4)], writes=[k5])
            S.add("dve", _tt(PT.rearrange("p h t -> p (h t)"), b5, WT.rearrange("p h t -> p (h t)"), ALU.mult),
                  reads=[k5] + [("WT", h) for h in range(4)], writes=["PT"])
            for h in range(4):
                bo, ko = next_ob()
                S.add("pe", _mmk(bo[:, 0:257], [(PT[:, h, :], Vml[:, h, :]), (qsT[:, h, :], Cml_bf[:, h, :])]),
                      reads=["PT", ("Vml", rp, h // 2), "qsT", ("Cml_bf", rp, h)], writes=[ko])
                S.add("act", _act(hraw[:, h, :], bo[:, 0:257], AF.Copy), reads=[ko], writes=[("hraw", h)])
                S.add("dve", lambda e, h=h: e.bn_stats(out=st6[:, h, :], in_=hraw[:, h, 0:256]), reads=[("hraw", h)], writes=[("st6", h)])
                S.add("dve", lambda e, h=h: e.bn_aggr(out=mv[:, h, :], in_=st6[:, h, :]), reads=[("st6", h)], writes=[("mv", h)])
            allh = [("hraw", h) for h in range(4)]
            allmv = [("mv", h) for h in range(4)]
            S.add("dve", _ts(T2, hraw[:, :, 256], -1.0, ALU.mult), reads=allh, writes=["t2"])
            S.add("dve", _tt(DD, T2, hraw[:, :, 256], ALU.max), reads=allh + ["t2"], writes=["dd"])
            S.add("dve", _tt(DD, DD, EX8[:, 0:4], ALU.max), reads=["dd", "ex8"], writes=["dd"])
            S.add("dve", lambda e: e.reciprocal(out=RDEN, in_=DD), reads=["dd"], writes=["rden"])
            S.add("dve", _tt(T1, RDEN, RDEN, ALU.mult), reads=["rden"], writes=["t1"])
            S.add("dve", _tt(T2, T1, mv[:, :, 1], ALU.mult), reads=["t1"] + allmv, writes=["t2"])
            S.add("dve", _ts(T1, T2, EPS, ALU.add), reads=["t2"], writes=["t1"])
            S.add("pool", _tt(RSTD, T1, mhalf, ALU.pow), reads=["t1", "mhalf"], writes=["rstdh"])
            S.add("dve", _tt(SC, RDEN, RSTD, ALU.mult), reads=["rden", "rstdh"], writes=["sc"])
            S.add("dve", _stt(BI, mv[:, :, 0], -1.0, SC, ALU.mult, ALU.mult), reads=allmv + ["sc"], writes=["bi"])
            for h in range(4):
                S.add("act", _act(hraw[:, h, 0:256], hraw[:, h, 0:256], AF.Identity, bias=BI[:, h:h + 1], scale=SC[:, h:h + 1]),
                      reads=[("hraw", h), "sc", "bi"], writes=[("hraw", h)])
            S.add("dve", _stt(mixed[:, 0:1024].rearrange("p (h v) -> p h v", h=4), og.rearrange("p (h v) -> p h v", h=4), 1.0,
                              hraw[:, :, 0:256], ALU.add, ALU.mult),
                  reads=allh + [("og", rp, 0), ("og", rp, 1)], writes=[("mixed", 0)])
            S.add("pe", lambda e: [e.matmul(b5[:, h * 128:(h + 1) * 128], lhsT=kTrt[:, h, :], rhs=qTrt[:, h, :], start=True, stop=True)
                                   for h in range(4)][-1],
                  reads=[("kTrt", rp, h) for h in range(4)] + [("qTrt", rp, h) for h in range(4)], writes=[k5])
            S.add("dve", _tt(PT, b5.rearrange("p (h t) -> p h t", h=4), cmask.unsqueeze(1).broadcast_to([128, 4, 128]), ALU.mult),
                  reads=[k5, "cmask"], writes=["PT"])
            for pr in range(2):
                bo, ko = next_ob()
                for hh in range(2):
                    h = 2 * pr + hh
                    S.add("pe", _mmk(bo[:, hh * 256:(hh + 1) * 256], [(PT[:, h, :], Vrt[:, h, :]), (qTrt[:, h, :], Crt_bf[:, h, :])]),
                          reads=["PT", ("Vrt", rp, pr), ("qTrt", rp, h), ("Crt_bf", rp, h)], writes=[ko])
                S.add("act", _act(hrt[:, 2 * pr:2 * pr + 2, :], bo.rearrange("p (a b) -> p a b", a=2), AF.Copy),
                      reads=[ko], writes=[("hrt", 2 * pr), ("hrt", 2 * pr + 1)])
                for hh in range(2):
                    h = 2 * pr + hh
                    S.add("dve", lambda e, h=h: e.bn_stats(out=st6[:, h, :], in_=hrt[:, h, :]), reads=[("hrt", h)], writes=[("st6", h)])
                    S.add("dve", lambda e, h=h: e.bn_aggr(out=mv[:, h, :], in_=st6[:, h, :]), reads=[("st6", h)], writes=[("mv", h)])
            allr = [("hrt", h) for h in range(4)]
            S.add("dve", _ts(T1, mv[:, :, 1], EPS, ALU.add), reads=allmv, writes=["t1"])
            S.add("pool", _tt(RSTD, T1, mhalf, ALU.pow), reads=["t1", "mhalf"], writes=["rstdh"])
            S.add("dve", _stt(BI, mv[:, :, 0], -1.0, RSTD, ALU.mult, ALU.mult), reads=allmv + ["rstdh"], writes=["bi"])
            for h in range(4):
                S.add("act", _act(hrt[:, h, :], hrt[:, h, :], AF.Identity, bias=BI[:, h:h + 1], scale=RSTD[:, h:h + 1]),
                      reads=[("hrt", h), "rstdh", "bi"], writes=[("hrt", h)])
            S.add("dve", _tt(mixed[:, 1024:2048], hrt.rearrange("p h v -> p (h v)"), gg, ALU.mult),
                  reads=allr + [("gg", rp, 0), ("gg", rp, 1)], writes=[("mixed", 1)])
        if full:
            f = c - NCH_P
            S.add("act", _dma(mixed_d[f * 128:(f + 1) * 128, :], mixed), reads=[("mixed", 0), ("mixed", 1)],
                  writes=[("mixed_d", f)], dma="mxst")


    loads(0)
    loads2(0)
    _stop = int(os.environ.get("MK_STOP_AFTER", str(NCH)))
    for c in range(min(NCH, _stop)):
        chunk(c, c >= NCH_P)
    if _stop < 100 and os.environ.get("MK_STOP_AFTER"):
        S.fake_par = None
        S.add("sp", lambda e: e.nop(), reads=S.all_keys())
        with ExitStack() as es:
            sems = {e: es.enter_context(nc.semaphore("s_" + e)) for e in ENGS}
            dsems = {k: es.enter_context(nc.semaphore("d_" + k)) for k in S.dma_counts}
            block = es.enter_context(nc.Block())
            S.emit(block, sems, dsems)
        return nc

    S.fake_par = None
    S.barrier()
    A3 = _Arena(nc, PH_BASE, SBUF_END)
    NWD = 10
    Wout = A3.alloc("Wout", [128, 16, D], BF16)
    Wup = A3.alloc("Wup", [128, 8, DFF], BF16)
    Wgt = A3.alloc("Wgt", [128, 8, DFF], BF16)
    Wring = A3.alloc("Wring", [128, NWD, D], BF16)
    xh = A3.alloc("xh", [128, 4, D], F32)
    mtoks = [A3.alloc(f"mtok{i}", [128, 2048], BF16) for i in range(2)]
    mixedT = A3.alloc("mixedT", [128, 16, 256], BF16)
    u2Ts = [A3.alloc(f"u2T{i}", [128, 8, 256], BF16) for i in range(2)]
    u3s = [A3.alloc(f"u3_{i}", [128, D], BF16) for i in range(2)]
    NB3 = int(os.environ.get("MK_NB3", "4"))
    asb3s = [A3.alloc(f"asb3_{i}", [128, 258], F32) for i in range(NB3)]
    acc3 = [A3.alloc(f"acc3_{i}", [128, 256], F32) for i in range(NB3)]
    actT = [A3.alloc(f"actT{i}", [128, 256], BF16) for i in range(NB3)]
    halo = A3.alloc("halo", [128, NJ, 2], F32)
    convff = A3.alloc("convff", [128, NJ * 3], F32)
    g_ffn_b = A3.alloc("g_ffn_b", [128, D], F32)
    g_fin_b = A3.alloc("g_fin_b", [128, D], F32)
    gcol = A3.alloc("gcol", [128, 16], F32)
    stat2 = A3.alloc("stat2", [128, 16], F32)

    print("[kernel] sbuf A1 end", A1.off, "A3 end", A3.off, "limit", SBUF_END)
    S.add("sp", _dma(convff, convff_d), writes=["convff"], dma="cst")
    S.add("sp", _dma(g_ffn_b, gffn_d), writes=["g_ffn_b"], dma="cst")
    S.add("sp", _dma(g_fin_b, gfin_d), writes=["g_fin_b"], dma="cst")
    S.add("sp", _dma(gcol, gcol_d), writes=["gcol"], dma="cst")
    S.add("dve", _ts(gcol[:, 0:8], gcol[:, 0:8], 0.5, ALU.mult), reads=["gcol"], writes=["gcol"])
    S.add("pool", lambda e: e.memset(halo, 0.0), writes=[("halo", j) for j in range(NJ)])
    for k in range(16):
        sl = k % 4
        S.add("sp", _dma(xh[:, sl, :], wout_d[k * 128:(k + 1) * 128, :]), writes=[("xh", sl)], dma=f"xh{sl}")
        S.add("act" if k % 2 else "dve",
              (_act(Wout[:, k, :], xh[:, sl, :], AF.Copy, scale=gcol[:, k:k + 1]) if k % 2 else
               _ts(Wout[:, k, :], xh[:, sl, :], gcol[:, k:k + 1], ALU.mult)),
              reads=[("xh", sl), "gcol"], writes=[("Wout", k)])
    for k in range(8):
        S.add("pool", _dma(Wup[:, k, :], wup_d[k * 128:(k + 1) * 128, :]), writes=[("Wup", k)], dma="Wup")
    for k in range(8):
        S.add("pool", _dma(Wgt[:, k, :], wgate_d[k * 128:(k + 1) * 128, :]), writes=[("Wgt", k)], dma="Wgt")

    b_acc = [banks[0], banks[1], banks[2], banks[3]]
    ybk = None
    agrot = [0]
    NAG = int(os.environ.get("MK_NAG", "3"))
    b_y = [banks[4 + NAG], banks[7]] if NAG < 3 else [banks[7], banks[7]]
    ybk = [4 + NAG, 7] if NAG < 3 else [7, 7]
    wdc = [0]
    strot = [0]
    mtc = [0]

    def rms_small(src, skey):
        i = strot[0] % 4
        strot[0] += 1
        return stat2[:, 4 * i:4 * i + 1], stat2[:, 4 * i + 1:4 * i + 2], stat2[:, 4 * i + 2:4 * i + 3], i

    def f3_pre(f0, ntt, is_halo, bp):
        T = ntt * 128
        u2T = u2Ts[bp]
        xsl = [bp * 2 + tt for tt in range(ntt)]
        for tt in range(ntt):
            f = f0 + tt
            p0 = (NCH_P + f) * 128
            S.add("sp", _dma(xh[:, xsl[tt], :], xs_d[p0:p0 + 128, :]), writes=[("xh", xsl[tt])], dma=f"xh{xsl[tt]}")
        for tt in range(ntt):
            f = f0 + tt
            mi = mtc[0] % 2
            mtc[0] += 1
            mtok = mtoks[mi]
            S.add("sp", _dma(mtok, mixed_d[f * 128:(f + 1) * 128, :]), reads=[("mixed_d", f)], writes=[("mtok", mi)], dma=f"mtok{mi}")
            for half in range(2):
                yb = b_y[half].bitcast(BF16)
                S.add("pe", _trs([(yb[:, kk * 128:(kk + 1) * 128], mtok[:, (half * 8 + kk) * 128:(half * 8 + kk + 1) * 128])
                                  for kk in range(8)], ident_bf), reads=[("mtok", mi), "ident_bf"], writes=[("bk", ybk[half])])
                S.add("act" if half else "dve",
                      (_act(mixedT[:, half * 8:half * 8 + 8, tt * 128:(tt + 1) * 128], yb.rearrange("p (k t) -> p k t", k=8), AF.Copy)
                       if half else _cp(mixedT[:, half * 8:half * 8 + 8, tt * 128:(tt + 1) * 128], yb.rearrange("p (k t) -> p k t", k=8))),
                      reads=[("bk", ybk[half])], writes=[("mixedT", tt, half)])
        for tt in range(ntt):
            xv = xh[:, xsl[tt], :]
            for half in range(2):
                S.add("pe", _mmk(b_y[half], [(mixedT[:, k, tt * 128:(tt + 1) * 128], Wout[:, k, half * 512:(half + 1) * 512])
                                             for k in range(16)]),
                      reads=[("mixedT", tt, 0), ("mixedT", tt, 1)] + [("Wout", k) for k in range(16)], writes=[("bk", ybk[half])])
                S.add("dve", _tt(xv[:, half * 512:(half + 1) * 512], b_y[half], xv[:, half * 512:(half + 1) * 512], ALU.add),
                      reads=[("bk", ybk[half]), ("xh", xsl[tt])], writes=[("xh", xsl[tt])])
        for tt in range(ntt):
            xv = xh[:, xsl[tt], :]
            xk = ("xh", xsl[tt])
            ss, ms, rs, si = rms_small(None, None)
            u3 = u3s[tt % 2]
            uk = ("u3", tt % 2)
            S.add("act", _act(u3, xv, AF.Square, accum=ss), reads=[xk], writes=[uk, ("ss", si)])
            S.add("dve", _ts(ms, ss, 1.0 / D, ALU.mult, EPS, ALU.add), reads=[("ss", si)], writes=[("ms", si)])
            S.add("pool", _tt(rs, ms, mhalf[:, 0:1], ALU.pow), reads=[("ms", si), "mhalf"], writes=[("rs", si)])
            S.add("dve", _stt(u3, xv, rs, g_ffn_b, ALU.mult, ALU.mult), reads=[xk, ("rs", si), "g_ffn_b"], writes=[uk])
            yb = b_y[tt % 2].bitcast(BF16)
            S.add("pe", _trs([(yb[:, k * 128:(k + 1) * 128], u3[:, k * 128:(k + 1) * 128]) for k in range(8)], ident_bf),
                  reads=[uk, "ident_bf"], writes=[("bk", ybk[tt % 2])])
            S.add("act", _act(u2T[:, :, tt * 128:(tt + 1) * 128], yb.rearrange("p (k t) -> p k t", k=8), AF.Copy),
                  reads=[("bk", ybk[tt % 2])], writes=[("u2T", bp, tt)])

    def f3_main(f0, ntt, is_halo, bp):
        T = ntt * 128
        u2T = u2Ts[bp]
        xsl = [bp * 2 + tt for tt in range(ntt)]
        u2k = [("u2T", bp, tt) for tt in range(ntt)]
        for j in range(NJ):
            par = agrot[0] % NB3
            agb = 4 + (agrot[0] % NAG)
            agrot[0] += 1
            aT = banks[agb][:, 0:T]
            gT = banks[agb][:, 256:256 + T]
            if not is_halo:
                ws = wdc[0] % NWD
                wdc[0] += 1
                S.add("pool", _dma(Wring[:, ws, :], wdn_d[j * 128:(j + 1) * 128, :]), writes=[("Wring", ws)], dma=f"Wd{ws}")
            S.add("pe", _mmk(aT, [(Wup[:, k, j * 128:(j + 1) * 128], u2T[:, k, 0:T]) for k in range(8)]),
                  reads=u2k + [("Wup", k) for k in range(8)], writes=[("bk", agb)])
            if not is_halo:
                S.add("pe", _mmk(gT, [(Wgt[:, k, j * 128:(j + 1) * 128], u2T[:, k, 0:T]) for k in range(8)]),
                      reads=u2k + [("Wgt", k) for k in range(8)], writes=[("bk", agb)])
            asb3 = asb3s[par]
            S.add("pool", _cp(asb3[:, 0:2], halo[:, j, :]), reads=[("halo", j)], writes=[("asb3h", par)])
            S.add("act", _act(asb3[:, 2:2 + T], aT, AF.Copy), reads=[("bk", agb)], writes=[("asb3", par)])
            S.add("pool", _cp(halo[:, j, :], asb3[:, T:T + 2]), reads=[("asb3", par)], writes=[("halo", j)])
            if is_halo:
                continue
            acc = acc3[par][:, 0:T]
            ak = ("acc3", par)
            S.add("dve", _ts(acc, asb3[:, 2:2 + T], convff[:, j * 3 + 2:j * 3 + 3], ALU.mult), reads=[("asb3", par), "convff"], writes=[ak])
            S.add("dve", _stt(acc, asb3[:, 1:1 + T], convff[:, j * 3 + 1:j * 3 + 2], acc, ALU.mult, ALU.add),
                  reads=[("asb3", par), ("asb3h", par), "convff", ak], writes=[ak])
            S.add("dve", _stt(acc, asb3[:, 0:T], convff[:, j * 3:j * 3 + 1], acc, ALU.mult, ALU.add),
                  reads=[("asb3", par), ("asb3h", par), "convff", ak], writes=[ak])
            S.add("act", _act(acc, acc, AF.Silu), reads=[ak], writes=[ak])
            at = actT[par][:, 0:T]
            S.add("dve", _tt(at, acc, gT, ALU.mult), reads=[ak, ("bk", agb)], writes=[("actT", par)])
            for tt in range(ntt):
                for half in range(2):
                    S.add("pe", _mm(b_acc[tt * 2 + half], at[:, tt * 128:(tt + 1) * 128], Wring[:, ws, half * 512:(half + 1) * 512],
                                    start=(j == 0), stop=(j == NJ - 1)),
                          reads=[("actT", par), ("Wring", ws)], writes=[("bk", tt * 2 + half)])
        if is_halo:
            return
        for tt in range(ntt):
            f = f0 + tt
            xv = xh[:, xsl[tt], :]
            xk = ("xh", xsl[tt])
            for half in range(2):
                S.add("dve", _tt(xv[:, half * 512:(half + 1) * 512], b_acc[tt * 2 + half], xv[:, half * 512:(half + 1) * 512], ALU.add),
                      reads=[("bk", tt * 2 + half), xk], writes=[xk])
            ss, ms, rs, si = rms_small(None, None)
            u3 = u3s[tt % 2]
            uk = ("u3", tt % 2)
            S.add("act", _act(u3, xv, AF.Square, accum=ss), reads=[xk], writes=[uk, ("ss", si)])
            S.add("dve", _ts(ms, ss, 1.0 / D, ALU.mult, EPS, ALU.add), reads=[("ss", si)], writes=[("ms", si)])
            S.add("pool", _tt(rs, ms, mhalf[:, 0:1], ALU.pow), reads=[("ms", si), "mhalf"], writes=[("rs", si)])
            S.add("dve", _stt(xv, xv, rs, g_fin_b, ALU.mult, ALU.mult), reads=[xk, ("rs", si), "g_fin_b"], writes=[xk])
            S.add("act", _dma(out_d[(f - 1) * 128:f * 128, :], xv), reads=[xk], writes=[("out", f)], dma=f"ost{xsl[tt]}")

    blocks = [(0, 1, True, 1)] + [(1 + 2 * b, 2, False, b % 2) for b in range(8)]
    f3_pre(*blocks[0])
    for bi, blk in enumerate(blocks):
        if bi + 1 < len(blocks):
            f3_pre(*blocks[bi + 1])
        f3_main(*blk)

    fin = S.add("sp", lambda e: e.nop(), reads=[("out", f) for f in range(1, NCH_F)])

    with ExitStack() as es:
        sems = {e: es.enter_context(nc.semaphore("s_" + e)) for e in ENGS}
        dsems = {k: es.enter_context(nc.semaphore("d_" + k)) for k in S.dma_counts}
        block = es.enter_context(nc.Block())
        S.emit(block, sems, dsems, reorder=bool(int(os.environ.get("MK_REORDER", "1"))))
        print("[kernel] est_ns", getattr(S, "est_ns", None), "ops", len(S.ops))
    return nc


def _host_consts():
    idx = np.arange(128, dtype=np.float64)
    dqk = np.zeros((128, 12), np.float32)
    for h in range(4):
        dqk[:, h] = np.exp(LOG_GAMMA[h] * (idx + 1.0))
        dqk[:, 4 + h] = S_ML * np.exp(-LOG_GAMMA[h] * (idx + 1.0))
        dqk[:, 8 + h] = S_ML * np.exp(-LOG_GAMMA[h] * (idx + 1.0)) * np.exp(LOG_GAMMA[h] * CH)
    jj, ii = np.meshgrid(np.arange(128), np.arange(128), indexing="ij")
    cm = (jj <= ii).astype(np.float32)
    mneg = np.where(jj <= ii, 0.0, NEG).astype(np.float32)
    i4 = np.concatenate([np.eye(4), -np.eye(4)], axis=1).astype(np.float32)
    return dict(dqk=dqk, cmask=cm, maskneg4=mneg,
                ident_bf=np.eye(128).astype(ml_dtypes.bfloat16), ident_f=np.eye(128, dtype=np.float32), i4=i4)


def _rope_tables(n_null):
    p = np.arange(NPOS, dtype=np.float64)
    pos = np.where(p >= n_null, 48.0 + (p - n_null), 0.0)
    inv = 10000.0 ** (-np.arange(0, 128, 2, dtype=np.float64) / 128.0)
    ang = pos[:, None] * inv[None, :]
    cosr = np.cos(ang).reshape(NCH, 128, 1, 64)
    sinr = np.sin(ang).reshape(NCH, 128, 1, 64)
    idx = np.arange(128, dtype=np.float64)
    dq = np.stack([np.exp(LOG_GAMMA[h] * (idx + 1.0)) for h in range(4)], axis=1)[None, :, :, None]
    dk = np.stack([S_ML * np.exp(-LOG_GAMMA[h] * (idx + 1.0)) for h in range(4)], axis=1)[None, :, :, None]
    tab = np.stack([cosr * dq, sinr * dq, cosr * dk, sinr * dk], axis=2)
    tab = np.ascontiguousarray(tab.reshape(NCH, 128, 1024)).astype(np.float32)
    valid = (p >= n_null).astype(np.float32).reshape(NCH, 128)
    vm = np.stack([valid, (valid - 1.0) * 1e30], axis=1)
    vm = np.ascontiguousarray(np.broadcast_to(vm[:, None], (NCH, 4, 2, 128))).astype(np.float32)
    return tab, vm


def _prep_inputs(inputs):
    f = lambda k: np.asarray(inputs[k], dtype=np.float32)
    x = f("x")
    meta = f("meta_tokens")
    w_in = f("w_in")[0]
    sizes = [512, 512, 1024, 1024, 4, 4, 512, 512, 1024, 1024]
    offs = np.cumsum(sizes)[:-1]
    ml_q, ml_k, ml_v, ml_o, ml_i, ml_f, rt_q, rt_k, rt_v, rt_g = np.split(w_in, offs, axis=1)

    def swap(cols):
        return cols.reshape(D, 4, 2, 64)[:, :, ::-1, :].reshape(D, 512)

    w_in_r = np.ascontiguousarray(np.concatenate(
        [ml_q, ml_k, ml_i, ml_f, ml_v, ml_o, rt_v, rt_g, rt_q, rt_k], axis=1))
    assert w_in_r.shape == (D, WCOLS)
    convml = f("ml_conv_w")[0]
    convw_ml = np.ascontiguousarray(convml.reshape(4, 8, 128).transpose(2, 1, 0).reshape(128, 32))
    convff = f("ffn_conv_w")[0]
    convw_ffn = np.ascontiguousarray(convff.reshape(3, NJ, 128).transpose(2, 1, 0).reshape(128, NJ * 3))
    b_if = np.ascontiguousarray(np.stack([f("ml_b_i")[0], f("ml_b_f")[0]], axis=1))
    gcat = np.concatenate([f("ml_norm_g")[0], f("rt_norm_g")[0]])
    gcol = np.ascontiguousarray(gcat.reshape(16, 128).T)
    bc = lambda v: np.ascontiguousarray(np.broadcast_to(v[None, :], (128, D)))
    common = dict(w_in_r=w_in_r, w_out=f("w_out")[0], w_up=f("w_up")[0], w_gate=f("w_gate")[0], w_down=f("w_down")[0],
                  convw_ml=convw_ml, convw_ffn=convw_ffn, b_if=b_if, gcol=gcol,
                  g_mix_b=bc(f("norm_mix_g")[0]), g_ffn_b=bc(f("norm_ffn_g")[0]), g_fin_b=bc(f("norm_final_g")))
    common.update(_host_consts())
    tabs = [_rope_tables(2160), _rope_tables(112)]
    in_maps = []
    for core in range(8):
        b, t = core // 2, core % 2
        xs = np.zeros((NPOS, D), np.float32)
        if t == 0:
            xs[2160:2176] = meta
            xs[2176:] = x[b, 0:2048]
        else:
            xs[112:128] = meta
            xs[128:] = x[b]
        m = dict(common)
        m["xs"] = xs
        m["cs_tab"], m["vm_tab"] = tabs[t]
        in_maps.append(m)
    return in_maps


_NC_CACHE = {}


def kernel(**inputs):
    in_maps = _prep_inputs(inputs)
    if "nc" not in _NC_CACHE:
        _NC_CACHE["nc"] = build_program()
    nc = _NC_CACHE["nc"]
    res = run_bass_kernel_spmd(nc, in_maps, core_ids=list(range(8)))
    out = np.zeros((4, 4096, D), np.float32)
    for core in range(8):
        b, t = core // 2, core % 2
        out[b, t * 2048:(t + 1) * 2048] = res.results[core]["out"]
    if DEBUG:
        kernel.debug = [res.results[c]["mixed_d"] for c in range(8)]
    return out
```

```python
import os
from contextlib import ExitStack

import numpy as np
import ml_dtypes

import concourse.bass as bass
import concourse.mybir as mybir
from concourse.bass_utils import run_bass_kernel_spmd

F32 = mybir.dt.float32
BF16 = mybir.dt.bfloat16
ALU = mybir.AluOpType
AF = mybir.ActivationFunctionType

NCH_P = 16
NCH_F = 17
NCH = NCH_P + NCH_F
CH = 128
NPOS = NCH * CH
D = 1024
DFF = 2816
NJ = DFF // 128
WCOLS = 6152
EPS = 1e-6
NEG = -1e30
GATE_CAP = 15.0
SBUF_BASE = 16640
SBUF_END = 229376
S_ML = 128.0 ** -0.5
LOG_GAMMA = [float(np.log1p(-2.0 ** (-(5.0 + h)))) for h in range(4)]
CD = [float(np.exp(lg * CH)) for lg in LOG_GAMMA]

ENGS = ("pe", "act", "dve", "pool", "sp")
DEBUG = bool(int(os.environ.get("MK_DEBUG", "0")))


class _Op:
    __slots__ = ("eng", "fn", "deps", "dma_sem", "dma_cnt", "sig", "idx", "need_sig", "wk")

    def __init__(self, eng, fn):
        self.eng = eng
        self.fn = fn
        self.deps = []
        self.dma_sem = None
        self.dma_cnt = 0
        self.sig = 0
        self.need_sig = False


class Sched:
    def __init__(self):
        self.ops = []
        self.last_w = {}
        self.readers = {}
        self.dma_counts = {}
        self.fence = None

    fake_par = None
    FAKE_KEEP = ("bk", "W", "Cml_f", "Cml_bf", "Crt_f", "Crt_bf", "asb", "mixed_d", "cst", "vmt", "uT")

    def _fk(self, keys):
        if self.fake_par is None:
            return keys
        out = []
        for k in keys:
            base = k[0] if isinstance(k, tuple) else k
            if base == "bk" and os.environ.get("MK_FAKEBK"):
                out.append((k, self.fake_par))
                continue
            if base in self.FAKE_KEEP or base in ("mst", "ident_bf", "mhalf", "i4", "ones4", "zeros4", "convml", "dqk", "cmask",
                                                   "maskneg4", "ident_f", "g_mix_b", "bsc0", "bsc1", "bif"):
                out.append(k)
            else:
                out.append((k, self.fake_par))
        return out

    def add(self, eng, fn, reads=(), writes=(), dma=None):
        reads = self._fk(reads)
        writes = self._fk(writes)
        op = _Op(eng, fn)
        op.wk = list(writes)[:2]
        op.idx = len(self.ops)
        deps = {}

        def dep(o, raw):
            val = self.dma_counts[o.dma_sem] if o.dma_sem is not None else 0
            if o.idx in deps:
                if raw and not deps[o.idx][1]:
                    deps[o.idx] = (o, True, val)
            else:
                deps[o.idx] = (o, raw, val)

        for r in reads:
            w = self.last_w.get(r)
            if w is not None:
                dep(w, True)
        for k in writes:
            w = self.last_w.get(k)
            if w is not None:
                dep(w, False)
            for rd in self.readers.get(k, ()):
                dep(rd, False)
        if self.fence is not None:
            dep(self.fence[eng], False)
        op.deps = list(deps.values())
        for r in reads:
            self.readers.setdefault(r, []).append(op)
        for k in writes:
            self.last_w[k] = op
            self.readers[k] = []
        if dma is not None:
            op.dma_sem = dma
            self.dma_counts[dma] = self.dma_counts.get(dma, 0) + 16
            op.dma_cnt = self.dma_counts[dma]
        self.ops.append(op)
        return op

    def all_keys(self):
        return list(set(self.last_w.keys()) | set(self.readers.keys()))

    def barrier(self):
        keys = self.all_keys()
        fence = {}
        for e in ENGS:
            fence[e] = self.add(e, lambda eng: eng.nop(), writes=keys)
        self.fence = fence

    @staticmethod
    def _skip(d, op, raw):
        if d.dma_sem is not None or op.dma_sem is not None:
            return False
        if d.eng != op.eng:
            return False
        return d.eng == "pe"

    def schedule(self, window=int(os.environ.get("MK_WIN", "48")), lat=float(os.environ.get("MK_LAT", "800"))):
        ops = self.ops
        n = len(ops)
        ndeps = [0] * n
        users = [[] for _ in range(n)]
        for op in ops:
            ndeps[op.idx] = len(op.deps)
            for (d, raw, val) in op.deps:
                users[d.idx].append(op.idx)
        fin = [0.0] * n
        ready_t = [0.0] * n
        pend = {e: [op.idx for op in ops if op.eng == e] for e in ENGS}
        pos = {e: 0 for e in ENGS}
        done = [False] * n
        efree = {e: 0.0 for e in ENGS}
        order = {e: [] for e in ENGS}
        remaining = n
        while remaining:
            best = None
            for e in ENGS:
                lst = pend[e]
                i = pos[e]
                while i < len(lst) and done[lst[i]]:
                    i += 1
                pos[e] = i
                if i >= len(lst):
                    continue
                w = 1 if e == "sp" else window
                seen = 0
                j = i
                cand = None
                dma_blocked = False
                while j < len(lst) and seen < w:
                    k = lst[j]
                    j += 1
                    if done[k]:
                        continue
                    seen += 1
                    isdma = ops[k].dma_sem is not None
                    if isdma and dma_blocked:
                        continue
                    if isdma:
                        dma_blocked = True
                    if ndeps[k] > 0:
                        continue
                    st = max(ready_t[k], efree[e])
                    key = (st, k)
                    if cand is None or key < cand:
                        cand = key
                        if ready_t[k] <= efree[e]:
                            break
                if cand is not None and (best is None or cand < best[0]):
                    best = (cand, e)
            assert best is not None, "scheduler deadlock"
            (st, k), e = best
            op = ops[k]
            c = getattr(op.fn, "cost", 150.0)
            if op.dma_sem is not None:
                efree[e] = st + 60.0
                fin[k] = st + c
            else:
                efree[e] = st + c
                fin[k] = st + c
            done[k] = True
            remaining -= 1
            order[e].append(op)
            for u in users[k]:
                ndeps[u] -= 1
                t = fin[k] + lat
                if t > ready_t[u]:
                    ready_t[u] = t
        self.est_ns = max(fin) if n else 0.0
        if os.environ.get("MK_CRIT"):
            dist = [0.0] * n
            pred = [-1] * n
            for op in ops:
                c = getattr(op.fn, "cost", 150.0)
                best_t, best_p = 0.0, -1
                for (d, raw, val) in op.deps:
                    t = dist[d.idx] + lat
                    if t > best_t:
                        best_t, best_p = t, d.idx
                dist[op.idx] = best_t + c
                pred[op.idx] = best_p
            lim = self.fence["pe"].idx if self.fence else n
            k = max(range(lim), key=lambda i: dist[i])
            print("[crit] dependency-only critical path before fence:", round(dist[k] / 1000), "us")
            path = []
            while k >= 0:
                path.append(k)
                k = pred[k]
            path.reverse()
            import collections
            cnt = collections.Counter(ops[i].eng for i in path)
            print("[crit] path len", len(path), dict(cnt))
            self.crit_path = path
            mid = int(len(path) * float(os.environ.get("MK_CRITPOS", "0.5")))
            for i in path[mid:mid + 70]:
                print("[crit]  ", ops[i].eng, ops[i].wk, round(getattr(ops[i].fn, "cost", 150.0)))
        if os.environ.get("MK_SCHED_DBG"):
            B = 100000.0
            nb = int(self.est_ns // B) + 1
            busy = {e: [0.0] * nb for e in ENGS}
            for op in ops:
                c = getattr(op.fn, "cost", 150.0)
                if op.dma_sem is not None:
                    continue
                b = int((fin[op.idx] - c) // B)
                busy[op.eng][b] += c
            for b in range(nb):
                print(f"[sched] {b*100:6d}us " + " ".join(f"{e}:{busy[e][b]/B*100:5.1f}%" for e in ENGS if e != "sp"))
            if self.fence:
                print("[sched] fence done at", {e: round(fin[o.idx] / 1000) for e, o in self.fence.items()})
        return order

    def emit(self, block, sems, dma_sems, reorder=True):
        for op in self.ops:
            for (d, raw, val) in op.deps:
                if d.dma_sem is None and not self._skip(d, op, raw):
                    d.need_sig = True
        if reorder:
            per_eng = self.schedule()
        else:
            per_eng = {e: [] for e in ENGS}
            for op in self.ops:
                per_eng[op.eng].append(op)
        cnt = {e: 0 for e in ENGS}
        for e in ENGS:
            for op in per_eng[e]:
                if op.dma_sem is None and op.need_sig:
                    cnt[e] += 1
                    op.sig = cnt[e]
        handles = {"pe": "tensor", "act": "scalar", "dve": "vector", "pool": "gpsimd", "sp": "sync"}

        def body(eng_name):
            def _f(eng):
                waited = {}
                for op in per_eng[eng_name]:
                    for (d, raw, val) in op.deps:
                        if d.dma_sem is not None:
                            key = ("dma", d.dma_sem)
                            sem = dma_sems[d.dma_sem]
                        else:
                            if self._skip(d, op, raw):
                                continue
                            key = d.eng
                            val = d.sig
                            sem = sems[d.eng]
                        if waited.get(key, 0) >= val:
                            continue
                        waited[key] = val
                        eng.wait_ge(sem, val)
                    ins = op.fn(eng)
                    if op.dma_sem is not None:
                        ins.then_inc(dma_sems[op.dma_sem], 16)
                    elif op.need_sig:
                        ins.then_inc(sems[eng_name], 1)
            return _f

        for e in ENGS:
            if per_eng[e]:
                getattr(block, handles[e])(body(e))


def _mmcost(lhsT, rhs):
    n = rhs.free_size()
    c = max(lhsT.free_size() / 1.2, n / 2.37, 30.0)
    if rhs.dtype == F32:
        c *= 4.0
    return c


def _fsz(ap):
    return ap.free_size()


def _mm(out, lhsT, rhs, start=True, stop=True):
    f = lambda e: e.matmul(out, lhsT=lhsT, rhs=rhs, start=start, stop=stop)
    f.cost = _mmcost(lhsT, rhs)
    return f


def _mmk(out, pairs):
    def f(e):
        n = len(pairs)
        ins = None
        for i, (l, r) in enumerate(pairs):
            ins = e.matmul(out, lhsT=l, rhs=r, start=(i == 0), stop=(i == n - 1))
        return ins
    f.cost = sum(_mmcost(l, r) for (l, r) in pairs)
    return f


def _trs(items, ident):
    def f(e):
        ins = None
        for (o, i) in items:
            ins = e.transpose(out=o, in_=i, identity=ident)
        return ins
    f.cost = 120.0 * len(items)
    return f


def _act(out, in_, func, bias=None, scale=None, accum=None):
    def f(e):
        kw = {}
        if bias is not None:
            kw["bias"] = bias
        if scale is not None:
            kw["scale"] = scale
        if accum is not None:
            kw["accum_out"] = accum
        return e.activation(out=out, in_=in_, func=func, **kw)
    f.cost = (224.0 + _fsz(out)) / 1.2
    return f


def _ts(out, in0, s1, op0, s2=None, op1=None):
    def f(e):
        if op1 is None:
            return e.tensor_scalar(out=out, in0=in0, scalar1=s1, scalar2=None, op0=op0)
        return e.tensor_scalar(out=out, in0=in0, scalar1=s1, scalar2=s2, op0=op0, op1=op1)
    f.cost = (100.0 + _fsz(out)) / 0.96
    return f


def _tt(out, in0, in1, op):
    f = lambda e: e.tensor_tensor(out=out, in0=in0, in1=in1, op=op)
    f.cost = (100.0 + _fsz(out)) / 0.96
    return f


def _stt(out, in0, scalar, in1, op0, op1):
    f = lambda e: e.scalar_tensor_tensor(out=out, in0=in0, scalar=scalar, in1=in1, op0=op0, op1=op1)
    f.cost = (100.0 + _fsz(out)) / 0.96
    return f


def _cp(out, in_):
    f = lambda e: e.tensor_copy(out=out, in_=in_)
    f.cost = (100.0 + _fsz(out)) / 0.96
    return f


def _dma(out, in_):
    f = lambda e: e.dma_start(out=out, in_=in_)
    f.cost = 2000.0 + 128.0 * _fsz(out) * 4 / 200.0
    return f


def _scan(out, d0, d1, init, op0, op1):
    f = lambda e: e.tensor_tensor_scan(out=out, data0=d0, data1=d1, initial=init, op0=op0, op1=op1)
    f.cost = (100.0 + 2 * _fsz(out)) / 0.96
    return f


class _Arena:
    def __init__(self, nc, base, end):
        self.nc = nc
        self.off = base
        self.end = end

    def alloc(self, name, shape, dt):
        size = 1
        for s in shape[1:]:
            size *= s
        size *= 2 if dt == BF16 else 4
        off = (self.off + 31) // 32 * 32
        assert off + size <= self.end, f"SBUF overflow at {name}: {off + size} > {self.end}"
        t = self.nc.alloc_sbuf_tensor_at(name, list(shape), dt, offset=off)
        self.off = off + size
        return t.ap()


WGROUPS = [(1024, 2056), (512, 1024), (5640, 6152), (3080, 4104), (0, 512), (5128, 5640), (2056, 3080), (4104, 5128)]


def _wgroup_of(col):
    for g, (a, b) in enumerate(WGROUPS):
        if a <= col < b:
            return g
    raise ValueError(col)


def build_program():
    nc = bass.Bass("TRN2", target_bir_lowering=False)

    def din(name, shape, dt=F32):
        return nc.dram_tensor(name, list(shape), dt, kind="ExternalInput").ap()

    xs_d = din("xs", [NPOS, D])
    win_d = din("w_in_r", [D, WCOLS])
    wout_d = din("w_out", [2048, D])
    wup_d = din("w_up", [D, DFF])
    wgate_d = din("w_gate", [D, DFF])
    wdn_d = din("w_down", [DFF, D])
    cs_d = din("cs_tab", [NCH, 128, 1024])
    vm_d = din("vm_tab", [NCH, 4, 2, 128])
    dqk_d = din("dqk", [128, 12])
    cmask_d = din("cmask", [128, 128])
    mneg_d = din("maskneg4", [128, 128])
    identb_d = din("ident_bf", [128, 128], BF16)
    identf_d = din("ident_f", [128, 128])
    i4_d = din("i4", [4, 8])
    convml_d = din("convw_ml", [128, 32])
    convff_d = din("convw_ffn", [128, NJ * 3])
    bif_d = din("b_if", [4, 2])
    gcol_d = din("gcol", [128, 16])
    gmix_d = din("g_mix_b", [128, D])
    gffn_d = din("g_ffn_b", [128, D])
    gfin_d = din("g_fin_b", [128, D])
    out_d = nc.dram_tensor("out", [2048, D], F32, kind="ExternalOutput").ap()
    mixed_d = nc.dram_tensor("mixed_d", [NCH_F * CH, 2048], BF16,
                             kind="ExternalOutput" if DEBUG else "Internal").ap()

    S = Sched()
    banks = [nc.alloc_psum_tensor(f"bank{i}", [128, 512], F32).ap() for i in range(8)]

    per = _Arena(nc, SBUF_BASE, SBUF_END)
    ident_bf = per.alloc("ident_bf", [128, 128], BF16)
    mhalf = per.alloc("mhalf", [128, 4], F32)
    stat = per.alloc("stat", [128, 8], F32)
    PH_BASE = per.off

    S.add("sp", _dma(ident_bf, identb_d), writes=["ident_bf"], dma="cst")
    S.add("pool", lambda e: e.memset(mhalf, -0.5), writes=["mhalf"])

    A1 = _Arena(nc, PH_BASE, SBUF_END)
    W = A1.alloc("W", [128, 8, WCOLS], BF16)
    ident_f = A1.alloc("ident_f", [128, 128], F32)
    cmask = A1.alloc("cmask", [128, 128], F32)
    maskneg4 = A1.alloc("maskneg4", [128, 128], F32)
    dqk = A1.alloc("dqk", [128, 12], F32)
    g_mix_b = A1.alloc("g_mix_b", [128, D], F32)
    convml = A1.alloc("convml", [128, 32], F32)
    i4 = A1.alloc("i4", [4, 8], F32)
    bif = A1.alloc("bif", [4, 2], F32)
    bsc = A1.alloc("bsc", [4, 2], F32)
    ones4 = A1.alloc("ones4", [4, 128], F32)
    zeros4 = A1.alloc("zeros4", [4, 128], F32)
    xin = A1.alloc("xin", [128, D], F32)
    u = A1.alloc("u", [128, D], BF16)
    uT = [A1.alloc(f"uT{i}", [128, 8, 128], BF16) for i in range(2)]
    asb = A1.alloc("asb", [128, 8, 131], F32)
    cacc = [A1.alloc(f"cacc{i}", [128, 128], F32) for i in range(2)]
    qTmls = [A1.alloc(f"qTml{i}", [128, 4, 128], BF16) for i in range(2)]
    kTmls = [A1.alloc(f"kTml{i}", [128, 4, 128], BF16) for i in range(2)]
    rX = [A1.alloc("rX0", [128, 512], F32)] * 2
    rM = [A1.alloc(f"rM{i}", [128, 256], F32) for i in range(4)]
    qtok = A1.alloc("qtok", [128, 4, 128], BF16)
    ktok = A1.alloc("ktok", [128, 4, 128], BF16)
    cst = [A1.alloc("cst0", [128, 1024], F32)] * 2
    vmt = [A1.alloc("vmt0", [4, 2, 128], F32)] * 2
    qTrts = [A1.alloc(f"qTrt{i}", [128, 4, 128], BF16) for i in range(2)]
    kTrts = [A1.alloc(f"kTrt{i}", [128, 4, 128], BF16) for i in range(2)]
    kw = A1.alloc("kw", [128, 4, 128], BF16)
    Vmls = [A1.alloc(f"Vml{i}", [128, 4, 257], BF16) for i in range(2)]
    Vrts = [A1.alloc(f"Vrt{i}", [128, 4, 256], BF16) for i in range(2)]
    ogs = [A1.alloc(f"og{i}", [128, D], F32) for i in range(2)]
    ggs = [A1.alloc(f"gg{i}", [128, D], F32) for i in range(2)]
    mixed = A1.alloc("mixed", [128, 2048], BF16)
    Cml_f = A1.alloc("Cml_f", [128, 4, 257], F32)
    Cml_bfs = [A1.alloc(f"Cml_bf{i}", [128, 4, 257], BF16) for i in range(2)]
    Crt_f = A1.alloc("Crt_f", [128, 4, 256], F32)
    Crt_bfs = [A1.alloc(f"Crt_bf{i}", [128, 4, 256], BF16) for i in range(2)]
    WT = A1.alloc("WT", [128, 4, 128], F32)
    PT = A1.alloc("PT", [128, 4, 128], BF16)
    Wint = A1.alloc("Wint", [128, 512], F32)
    qsT = A1.alloc("qsT", [128, 4, 128], BF16)
    hrt = A1.alloc("hrt", [128, 4, 256], F32)
    hraw = A1.alloc("hraw", [128, 4, 257], F32)
    rows = {n: A1.alloc("row_" + n, [4, 128], F32) for n in
            ("li0", "li1", "li", "ef", "sp", "nbcum", "B", "M", "R2", "R3")}
    rhs_bd = A1.alloc("rhs_bd", [4, 4, 128], F32)
    smr = A1.alloc("smr", [4, 16], F32)
    sm = A1.alloc("sm", [128, 64], F32)
    smps = A1.alloc("smps", [128, 20], F32)
    st6 = A1.alloc("st6", [128, 4, 6], F32)
    mv = A1.alloc("mv", [128, 4, 2], F32)

    EX8 = sm[:, 0:8]
    BT = sm[:, 8:12]
    BTS = sm[:, 12:16]
    WSARG = sm[:, 16:20]
    WSRC = sm[:, 20:24]
    DEC = sm[:, 24:28]
    DD = sm[:, 28:32]
    RDEN = sm[:, 32:36]
    T1 = sm[:, 36:40]
    T2 = sm[:, 40:44]
    RSTD = sm[:, 44:48]
    SC = sm[:, 48:52]
    BI = sm[:, 52:56]

    bT = banks[0]
    bT_bf = bT.bitcast(BF16)

    for nm, dst, src in (("ident_f", ident_f, identf_d), ("cmask", cmask, cmask_d), ("maskneg4", maskneg4, mneg_d),
                         ("dqk", dqk, dqk_d), ("g_mix_b", g_mix_b, gmix_d),
                         ("convml", convml, convml_d), ("i4", i4, i4_d), ("bif", bif, bif_d)):
        S.add("sp", _dma(dst, src), writes=[nm], dma="cst")
    for g, (c0, c1) in enumerate(WGROUPS):
        for k in range(8):
            S.add("pool", _dma(W[:, k, c0:c1], win_d[k * 128:(k + 1) * 128, c0:c1]),
                  writes=[("W", g, k)], dma=f"W{g}")

    def wk(col):
        g = _wgroup_of(col)
        return [("W", g, k) for k in range(8)]

    S.add("pool", lambda e: e.memset(ones4, 1.0), writes=["ones4"])
    S.add("pool", lambda e: e.memset(zeros4, 0.0), writes=["zeros4"])
    S.add("pool", lambda e: e.memset(asb, 0.0), writes=[("asb", t) for t in range(8)])
    S.add("pool", lambda e: e.memset(Vmls[0], 1.0), writes=[("Vml", 0, 0), ("Vml", 0, 1)])
    S.add("pool", lambda e: e.memset(Vmls[1], 1.0), writes=[("Vml", 1, 0), ("Vml", 1, 1)])
    S.add("pool", lambda e: e.memset(Cml_f, 0.0), writes=[("Cml_f", h) for h in range(4)])
    S.add("pool", lambda e: e.memset(Cml_bfs[0], 0.0), writes=[("Cml_bf", 0, h) for h in range(4)])
    S.add("pool", lambda e: e.memset(Crt_f, 0.0), writes=[("Crt_f", h) for h in range(4)])
    S.add("pool", lambda e: e.memset(Crt_bfs[0], 0.0), writes=[("Crt_bf", 0, h) for h in range(4)])
    S.add("pool", lambda e: e.memset(smr, 0.0), writes=["mst", "D1", "diagM", "diagD"])
    S.add("pool", lambda e: e.memset(smr[:, 0:1], NEG), writes=["mst"])
    S.add("dve", _ts(bsc[:, 0:1], bif[:, 0:1], 1.0 / GATE_CAP, ALU.mult), reads=["bif"], writes=["bsc0"])
    S.add("dve", _ts(bsc[:, 1:2], bif[:, 1:2], -1.0, ALU.mult), reads=["bif"], writes=["bsc1"])

    MST = smr[:, 0:1]
    D1 = smr[:, 1:2]
    DIAGM = smr[:, 4:8]
    DIAGD = smr[:, 8:12]

    def loads(c):
        s = c % 2
        S.add("sp", _dma(xin, xs_d[c * 128:(c + 1) * 128, :]), writes=["xin"], dma="xin")

    def loads2(c):
        S.add("sp", _dma(cst[0], cs_d[c]), writes=[("cst", 0)], dma="cst0")
        S.add("sp", _dma(vmt[0], vm_d[c]), writes=[("vmt", 0)], dma="vmt0")

    aslot = [0]

    def next_aslot():
        i = aslot[0] % 4
        aslot[0] += 1
        return i

    tmslot = [0]

    def rmsnorm_T(src, gb, dstT, dst_keys, src_key, gkey):
        S.add("act", _act(u, src, AF.Square, accum=stat[:, 0:1]), reads=[src_key], writes=["u", "ss"])
        S.add("dve", _ts(stat[:, 1:2], stat[:, 0:1], 1.0 / D, ALU.mult, EPS, ALU.add), reads=["ss"], writes=["ms"])
        S.add("pool", _tt(stat[:, 2:3], stat[:, 1:2], mhalf[:, 0:1], ALU.pow), reads=["ms", "mhalf"], writes=["rstd"])
        S.add("dve", _stt(u, src, stat[:, 2:3], gb, ALU.mult, ALU.mult), reads=[src_key, "rstd", gkey], writes=["u"])
        S.add("pe", _trs([(bT_bf[:, k * 128:(k + 1) * 128], u[:, k * 128:(k + 1) * 128]) for k in range(8)], ident_bf),
              reads=["u", "ident_bf"], writes=[("bk", 0)])
        S.add("act", _act(dstT, bT_bf.rearrange("p (k t) -> p k t", k=8), AF.Copy), reads=[("bk", 0)], writes=dst_keys)


    fmb = [0]

    def chunk(c, full):
        rp, wp = c % 2, (c + 1) % 2
        qTml, kTml, qTrt, kTrt = qTmls[rp], kTmls[rp], qTrts[rp], kTrts[rp]
        og, gg = ogs[rp], ggs[rp]
        Vml, Vrt = Vmls[rp], Vrts[rp]
        Cml_bf, Crt_bf = Cml_bfs[rp], Crt_bfs[rp]
        Cml_bfw, Crt_bfw = Cml_bfs[wp], Crt_bfs[wp]
        if os.environ.get("MK_FAKE2"):
            S.fake_par = c % int(os.environ["MK_FAKE2"])
        s = c % 2
        uTs = uT[s]
        rmsnorm_T(xin, g_mix_b, uTs, [("uT", s)], "xin", "g_mix_b")
        if c + 1 < NCH:
            loads(c + 1)

        def fm_group(cols_ms):
            b = 1 + fmb[0] % 2
            fmb[0] += 1
            bk = banks[b]
            for sl, (col, m) in enumerate(cols_ms):
                S.add("pe", _mmk(bk[0:m, sl * 128:(sl + 1) * 128], [(W[:, k, col:col + m], uTs[:, k, :]) for k in range(8)]),
                      reads=wk(col) + [("uT", s)], writes=[("bk", b)])
            return bk, ("bk", b)

        for grp in ((0, 1) if full else (1,)):
            bk, bkey = fm_group([((grp * 4 + t) * 128, 128) for t in range(4)])
            S.add("act", _act(asb[:, grp * 4:grp * 4 + 4, 3:131], bk.rearrange("p (a b) -> p a b", a=4), AF.Copy),
                  reads=[bkey], writes=[("asb", grp * 4 + t) for t in range(4)])
            for t in range(grp * 4, grp * 4 + 4):
                acc = cacc[t % 2]
                ak = ("cacc", t % 2)
                S.add("dve", _ts(acc, asb[:, t, 3:131], convml[:, t * 4 + 3:t * 4 + 4], ALU.mult),
                      reads=[("asb", t), "convml"], writes=[ak])
                for kk in (2, 1, 0):
                    S.add("dve", _stt(acc, asb[:, t, kk:kk + 128], convml[:, t * 4 + kk:t * 4 + kk + 1], acc, ALU.mult, ALU.add),
                          reads=[("asb", t), "convml", ak], writes=[ak])
                S.add("pool", _cp(asb[:, t, 0:3], asb[:, t, 128:131]), reads=[("asb", t)], writes=[("asb", t)])
                if t < 4:
                    S.add("act", _act(qTml[:, t, :], acc, AF.Silu), reads=[ak], writes=[("qTml", rp, t)])
                else:
                    S.add("act", _act(kTml[:, t - 4, :], acc, AF.Silu), reads=[ak], writes=[("kTml", rp, t - 4)])
        bk, gkey = fm_group([(1024, 4), (1028, 4)])
        gi_reg = bk[0:4, 0:128]
        gf_reg = bk[0:4, 128:256]
        R = rows
        vs = vmt[s]
        S.add("act", _act(R["li0"], gi_reg, AF.Tanh, bias=bsc[:, 0:1], scale=1.0 / GATE_CAP), reads=[gkey, "bsc0"], writes=["li0"])
        S.add("act", _act(R["ef"], gf_reg, AF.Exp, bias=bsc[:, 1:2], scale=-1.0), reads=[gkey, "bsc1"], writes=["ef"])
        S.add("dve", _stt(R["li1"], R["li0"], GATE_CAP, vs[:, 0, :], ALU.mult, ALU.mult), reads=["li0", ("vmt", 0)], writes=["li1"])
        S.add("dve", _tt(R["li"], R["li1"], vs[:, 1, :], ALU.add), reads=["li1", ("vmt", 0)], writes=["li"])
        S.add("act", _act(R["sp"], R["ef"], AF.Ln, bias=1.0), reads=["ef"], writes=["sp"])
        S.add("dve", _scan(R["nbcum"], R["sp"], zeros4, 0.0, ALU.add, ALU.add), reads=["sp", "zeros4"], writes=["nbcum"])
        S.add("dve", _tt(R["B"], R["li"], R["nbcum"], ALU.add), reads=["li", "nbcum"], writes=["B"])
        S.add("dve", _scan(R["M"], R["B"], R["B"], MST, ALU.max, ALU.max), reads=["B", "mst"], writes=["M"])
        S.add("dve", _ts(DIAGM, i4[:, 0:4], R["M"][:, 127:128], ALU.mult), reads=["i4", "M"], writes=["diagM"])
        S.add("dve", _tt(D1, MST, R["M"][:, 127:128], ALU.subtract), reads=["mst", "M"], writes=["D1"])
        S.add("dve", _ts(DIAGD, i4[:, 0:4], D1, ALU.mult), reads=["i4", "D1"], writes=["diagD"])
        if full:
            S.add("dve", _tt(R["R2"], R["nbcum"], R["M"], ALU.subtract), reads=["nbcum", "M"], writes=["R2"])
            S.add("dve", _ts(R["R2"], R["R2"], 80.0, ALU.min), reads=["R2"], writes=["R2"])
            S.add("dve", _ts(R["R3"], R["M"], MST, ALU.subtract, -1.0, ALU.mult), reads=["M", "mst"], writes=["R3"])
            for h in range(4):
                S.add("dve", _ts(rhs_bd[:, h, :], R["M"], i4[:, 4 + h:5 + h], ALU.mult), reads=["M", "i4"], writes=[("rhs_bd", h)])
        S.add("dve", _tt(MST, R["M"][:, 127:128], R["nbcum"][:, 127:128], ALU.subtract), reads=["M", "nbcum"], writes=["mst"])

        sb_ = 5
        SMP = banks[sb_][:, 0:32]
        skey = ("bk", sb_)

        def smp_mm(e):
            ins = e.matmul(SMP[:, 0:4], lhsT=R["B"], rhs=i4[:, 0:4], start=True, stop=True)
            if full:
                e.matmul(SMP[:, 4:8], lhsT=R["R2"], rhs=i4[:, 0:4], start=True, stop=True)
                e.matmul(SMP[:, 8:12], lhsT=R["R3"], rhs=i4[:, 0:4], start=True, stop=True)
            e.matmul(SMP[:, 12:16], lhsT=ones4, rhs=DIAGM, start=True, stop=True)
            ins = e.matmul(SMP[:, 16:20], lhsT=ones4, rhs=DIAGD, start=True, stop=True)
            return ins
        S.add("pe", smp_mm, reads=["B", "R2", "R3", "i4", "ones4", "diagM", "diagD"], writes=[skey])
        if full:
            S.add("act", _act(smps, SMP[:, 0:20], AF.Copy), reads=[skey], writes=["smps"])
        else:
            S.add("act", _act(smps[:, 0:4], SMP[:, 0:4], AF.Copy), reads=[skey], writes=["smps"])
            S.add("act", _act(smps[:, 12:20], SMP[:, 12:20], AF.Copy), reads=[skey], writes=["smps"])
        if full:
            S.add("act", _act(EX8, smps[:, 4:12], AF.Exp), reads=["smps"], writes=["ex8"])
        S.add("dve", _tt(WSARG, smps[:, 0:4], smps[:, 12:16], ALU.subtract), reads=["smps"], writes=["wsarg"])
        S.add("act", _act(WSRC, WSARG, AF.Exp, bias=float(np.log(S_ML))), reads=["wsarg"], writes=["wsrc"])
        S.add("act", _act(DEC, smps[:, 16:20], AF.Exp), reads=["smps"], writes=["dec"])
        b5 = banks[5]
        k5 = ("bk", 5)
        if full:
            S.add("dve", _ts(BTS, smps[:, 0:4], float(np.log(S_ML)), ALU.add), reads=["smps"], writes=["BTS"])
            S.add("pe", _mmk(b5, [(ones4, rhs_bd.rearrange("p h t -> p (h t)")), (ident_f, maskneg4.unsqueeze(1).broadcast_to([128, 4, 128]))]),
                  reads=["ones4", "ident_f", "maskneg4"] + [("rhs_bd", h) for h in range(4)], writes=[k5])
            for h in range(4):
                S.add("act", _act(WT[:, h, :], b5[:, h * 128:(h + 1) * 128], AF.Exp, bias=BTS[:, h:h + 1]),
                      reads=[k5, "BTS"], writes=[("WT", h)])
            for h in range(4):
                S.add("dve", _ts(rhs_bd[:, h, :], R["R3"], i4[:, h:h + 1], ALU.mult), reads=["R3", "i4"], writes=[("rhs_bd", h)])
            S.add("pe", _mm(b5, ones4, rhs_bd.rearrange("p h t -> p (h t)")),
                  reads=["ones4"] + [("rhs_bd", h) for h in range(4)], writes=[k5])
            S.add("act", _act(Wint, b5, AF.Exp), reads=[k5], writes=["Wint"])
            S.add("dve", _tt(qsT.rearrange("p h t -> p (h t)"), qTml.rearrange("p h t -> p (h t)"), Wint, ALU.mult),
                  reads=["Wint"] + [("qTml", rp, h) for h in range(4)], writes=["qsT"])

        def tm_tile(col):
            b = 3 + tmslot[0] % 2
            tmslot[0] += 1
            S.add("pe", _mmk(banks[b], [(uTs[:, k, :], W[:, k, col:col + 512]) for k in range(8)]),
                  reads=wk(col) + [("uT", s)], writes=[("bk", b)])
            return banks[b], ("bk", b)

        for hh in range(2):
            reg, rk = tm_tile(1032 + hh * 512)
            S.add("act", _act(Vml[:, 2 * hh:2 * hh + 2, 0:256], reg.rearrange("p (a b) -> p a b", a=2), AF.Copy),
                  reads=[rk], writes=[("Vml", rp, hh)])
        for hh in range(2):
            reg, rk = tm_tile(3080 + hh * 512)
            S.add("dve", _cp(Vrt[:, 2 * hh:2 * hh + 2, :], reg.rearrange("p (a b) -> p a b", a=2)),
                  reads=[rk], writes=[("Vrt", rp, hh)])
        if full:
            for hh in range(2):
                reg, rk = tm_tile(2056 + hh * 512)
                S.add("act", _act(og[:, hh * 512:(hh + 1) * 512], reg, AF.Tanh, scale=0.5), reads=[rk], writes=[("og", rp, hh)])
            for hh in range(2):
                reg, rk = tm_tile(4104 + hh * 512)
                S.add("act", _act(gg[:, hh * 512:(hh + 1) * 512], reg, AF.Silu), reads=[rk], writes=[("gg", rp, hh)])

        def rotary(col, xi, qk, dst, dkey):
            reg, rk = tm_tile(col)
            X = rX[xi]
            xk = ("rX", 0)
            S.add("act", _act(X, reg, AF.Copy), reads=[rk], writes=[xk])
            Xv = X.rearrange("p (h a t) -> p h a t", h=4, a=2)
            Tc = cst[0][:, (qk * 2) * 256:(qk * 2 + 1) * 256].rearrange("p (h t) -> p h t", h=4)
            Ts = cst[0][:, (qk * 2 + 1) * 256:(qk * 2 + 2) * 256].rearrange("p (h t) -> p h t", h=4)
            Mv = [m.rearrange("p (h t) -> p h t", h=4) for m in rM]
            Dv = dst.rearrange("p h (a t) -> p h a t", a=2)
            S.add("dve", _tt(Mv[0], Xv[:, :, 0, :], Tc, ALU.mult), reads=[xk, ("cst", 0)], writes=[("rM", 0)])
            S.add("dve", _tt(Mv[1], Xv[:, :, 1, :], Ts, ALU.mult), reads=[xk, ("cst", 0)], writes=[("rM", 1)])
            S.add("pool", _tt(Mv[2], Xv[:, :, 0, :], Ts, ALU.mult), reads=[xk, ("cst", 0)], writes=[("rM", 2)])
            S.add("pool", _tt(Mv[3], Xv[:, :, 1, :], Tc, ALU.mult), reads=[xk, ("cst", 0)], writes=[("rM", 3)])
            S.add("dve", _tt(Dv[:, :, 0, :], Mv[0], Mv[1], ALU.subtract), reads=[("rM", 0), ("rM", 1)], writes=[(dkey, 0)])
            S.add("pool", _tt(Dv[:, :, 1, :], Mv[2], Mv[3], ALU.add), reads=[("rM", 2), ("rM", 3)], writes=[(dkey, 1)])

        rotary(5640, 0, 1, ktok, "ktok")
        if full and not os.environ.get("MK_X1"):
            rotary(5128, 1, 0, qtok, "qtok")
        if c + 1 < NCH:
            loads2(c + 1)

        ob = [6]

        def next_ob():
            b = 6 + ob[0] % 2
            ob[0] += 1
            return banks[b], ("bk", b)

        kb_, kbk = next_ob()
        b0bf = kb_.bitcast(BF16)
        S.add("pe", _trs([(b0bf[:, t * 128:(t + 1) * 128], kTml[:, t, :]) for t in range(4)], ident_bf),
              reads=[("kTml", rp, h) for h in range(4)] + ["ident_bf"], writes=[kbk])
        for t in range(4):
            S.add("act", _act(kw[:, t, :], b0bf[:, t * 128:(t + 1) * 128], AF.Copy, scale=WSRC[:, t:t + 1]),
                  reads=[kbk, "wsrc"], writes=[("kw", t)])
        if full:
            qb_, qbk = next_ob()
            qbbf = qb_.bitcast(BF16)
            S.add("pe", _trs([(qbbf[:, h * 128:(h + 1) * 128], ktok[:, h, :]) for h in range(4)] +
                             [(qbbf[:, (4 + h) * 128:(5 + h) * 128], qtok[:, h, :]) for h in range(4)], ident_bf),
                  reads=[("ktok", 0), ("ktok", 1), ("qtok", 0), ("qtok", 1), "ident_bf"], writes=[qbk])
            S.add("act", _act(kTrt, qbbf[:, 0:512].rearrange("p (h t) -> p h t", h=4), AF.Copy), reads=[qbk],
                  writes=[("kTrt", rp, h) for h in range(4)])
            S.add("act", _act(qTrt, qbbf[:, 512:1024].rearrange("p (h t) -> p h t", h=4), AF.Copy), reads=[qbk],
                  writes=[("qTrt", rp, h) for h in range(4)])

        for h in range(4):
            bo, ko = next_ob()
            S.add("pe", _mm(bo[:, 0:257], kw[:, h, :], Vml[:, h, :]), reads=[("kw", h), ("Vml", rp, h // 2)], writes=[ko])
            S.add("dve", _stt(Cml_f[:, h, :], Cml_f[:, h, :], DEC[:, h:h + 1], bo[:, 0:257], ALU.mult, ALU.add),
                  reads=[("Cml_f", h), "dec", ko], writes=[("Cml_f", h)])
            S.add("pool", _cp(Cml_bfw[:, h, :], Cml_f[:, h, :]), reads=[("Cml_f", h)], writes=[("Cml_bf", wp, h)])
        for pr in range(2):
            bo, ko = next_ob()
            for hh in range(2):
                h = 2 * pr + hh
                S.add("pe", _mm(bo[:, hh * 256:(hh + 1) * 256], ktok[:, h, :], Vrt[:, h, :]), reads=[("ktok", 0), ("ktok", 1), ("Vrt", rp, pr)], writes=[ko])
            for hh in range(2):
                h = 2 * pr + hh
                S.add("dve", _stt(Crt_f[:, h, :], Crt_f[:, h, :], CD[h], bo[:, hh * 256:(hh + 1) * 256], ALU.mult, ALU.add),
                      reads=[("Crt_f", h), ko], writes=[("Crt_f", h)])
            for hh in range(2):
                h = 2 * pr + hh
                S.add("pool", _ts(Crt_bfw[:, h, :], Crt_f[:, h, :], CD[h], ALU.mult, 0.0, ALU.add),
                      reads=[("Crt_f", h)], writes=[("Crt_bf", wp, h)])
        if full:
            S.add("pe", lambda e: [e.matmul(b5[:, h * 128:(h + 1) * 128], lhsT=kTml[:, h, :], rhs=qTml[:, h, :], start=True, stop=True)
                                   for h in range(4)][-1],
                  reads=[("kTml", rp, h) for h in range(4)] + [("qTml", rp, h) for h in range(4)], writes=[k5])
            S.add("dve", _tt(PT.rearrange("p h t -> p (h t)"), b5, WT.rearrange("p h t -> p (h t)"), ALU.mult),
                  reads=[k5] + [("WT", h) for h in range(4)], writes=["PT"])
            for h in range(4):
                bo, ko = next_ob()
                S.add("pe", _mmk(bo[:, 0:257], [(PT[:, h, :], Vml[:, h, :]), (qsT[:, h, :], Cml_bf[:, h, :])]),
                      reads=["PT", ("Vml", rp, h // 2), "qsT", ("Cml_bf", rp, h)], writes=[ko])
                S.add("act", _act(hraw[:, h, :], bo[:, 0:257], AF.Copy), reads=[ko], writes=[("hraw", h)])
                S.add("dve", lambda e, h=h: e.bn_stats(out=st6[:, h, :], in_=hraw[:, h, 0:256]), reads=[("hraw", h)], writes=[("st6", h)])
                S.add("dve", lambda e, h=h: e.bn_aggr(out=mv[:, h, :], in_=st6[:, h, :]), reads=[("st6", h)], writes=[("mv", h)])
            allh = [("hraw", h) for h in range(4)]
            allmv = [("mv", h) for h in range(4)]
            S.add("dve", _ts(T2, hraw[:, :, 256], -1.0, ALU.mult), reads=allh, writes=["t2"])
            S.add("dve", _tt(DD, T2, hraw[:, :, 256], ALU.max), reads=allh + ["t2"], writes=["dd"])
            S.add("dve", _tt(DD, DD, EX8[:, 0:4], ALU.max), reads=["dd", "ex8"], writes=["dd"])
            S.add("dve", lambda e: e.reciprocal(out=RDEN, in_=DD), reads=["dd"], writes=["rden"])
            S.add("dve", _tt(T1, RDEN, RDEN, ALU.mult), reads=["rden"], writes=["t1"])
            S.add("dve", _tt(T2, T1, mv[:, :, 1], ALU.mult), reads=["t1"] + allmv, writes=["t2"])
            S.add("dve", _ts(T1, T2, EPS, ALU.add), reads=["t2"], writes=["t1"])
            S.add("pool", _tt(RSTD, T1, mhalf, ALU.pow), reads=["t1", "mhalf"], writes=["rstdh"])
            S.add("dve", _tt(SC, RDEN, RSTD, ALU.mult), reads=["rden", "rstdh"], writes=["sc"])
            S.add("dve", _stt(BI, mv[:, :, 0], -1.0, SC, ALU.mult, ALU.mult), reads=allmv + ["sc"], writes=["bi"])
            for h in range(4):
                S.add("act", _act(hraw[:, h, 0:256], hraw[:, h, 0:256], AF.Identity, bias=BI[:, h:h + 1], scale=SC[:, h:h + 1]),
                      reads=[("hraw", h), "sc", "bi"], writes=[("hraw", h)])
            S.add("dve", _stt(mixed[:, 0:1024].rearrange("p (h v) -> p h v", h=4), og.rearrange("p (h v) -> p h v", h=4), 1.0,
                              hraw[:, :, 0:256], ALU.add, ALU.mult),
                  reads=allh + [("og", rp, 0), ("og", rp, 1)], writes=[("mixed", 0)])
            S.add("pe", lambda e: [e.matmul(b5[:, h * 128:(h + 1) * 128], lhsT=kTrt[:, h, :], rhs=qTrt[:, h, :], start=True, stop=True)
                                   for h in range(4)][-1],
                  reads=[("kTrt", rp, h) for h in range(4)] + [("qTrt", rp, h) for h in range(4)], writes=[k5])
            S.add("dve", _tt(PT, b5.rearrange("p (h t) -> p h t", h=4), cmask.unsqueeze(1).broadcast_to([128, 4, 128]), ALU.mult),
                  reads=[k5, "cmask"], writes=["PT"])
            for pr in range(2):
                bo, ko = next_ob()
                for hh in range(2):
                    h = 2 * pr + hh
                    S.add("pe", _mmk(bo[:, hh * 256:(hh + 1) * 256], [(PT[:, h, :], Vrt[:, h, :]), (qTrt[:, h, :], Crt_bf[:, h, :])]),
                          reads=["PT", ("Vrt", rp, pr), ("qTrt", rp, h), ("Crt_bf", rp, h)], writes=[ko])
                S.add("act", _act(hrt[:, 2 * pr:2 * pr + 2, :], bo.rearrange("p (a b) -> p a b", a=2), AF.Copy),
                      reads=[ko], writes=[("hrt", 2 * pr), ("hrt", 2 * pr + 1)])
                for hh in range(2):
                    h = 2 * pr + hh
                    S.add("dve", lambda e, h=h: e.bn_stats(out=st6[:, h, :], in_=hrt[:, h, :]), reads=[("hrt", h)], writes=[("st6", h)])
                    S.add("dve", lambda e, h=h: e.bn_aggr(out=mv[:, h, :], in_=st6[:, h, :]), reads=[("st6", h)], writes=[("mv", h)])
            allr = [("hrt", h) for h in range(4)]
            S.add("dve", _ts(T1, mv[:, :, 1], EPS, ALU.add), reads=allmv, writes=["t1"])
            S.add("pool", _tt(RSTD, T1, mhalf, ALU.pow), reads=["t1", "mhalf"], writes=["rstdh"])
            S.add("dve", _stt(BI, mv[:, :, 0], -1.0, RSTD, ALU.mult, ALU.mult), reads=allmv + ["rstdh"], writes=["bi"])
            for h in range(4):
                S.add("act", _act(hrt[:, h, :], hrt[:, h, :], AF.Identity, bias=BI[:, h:h + 1], scale=RSTD[:, h:h + 1]),
                      reads=[("hrt", h), "rstdh", "bi"], writes=[("hrt", h)])
            S.add("dve", _tt(mixed[:, 1024:2048], hrt.rearrange("p h v -> p (h v)"), gg, ALU.mult),
                  reads=allr + [("gg", rp, 0), ("gg", rp, 1)], writes=[("mixed", 1)])
        if full:
            f = c - NCH_P
            S.add("act", _dma(mixed_d[f * 128:(f + 1) * 128, :], mixed), reads=[("mixed", 0), ("mixed", 1)],
                  writes=[("mixed_d", f)], dma="mxst")


    loads(0)
    loads2(0)
    _stop = int(os.environ.get("MK_STOP_AFTER", str(NCH)))
    for c in range(min(NCH, _stop)):
        chunk(c, c >= NCH_P)
    if _stop < 100 and os.environ.get("MK_STOP_AFTER"):
        S.fake_par = None
        S.add("sp", lambda e: e.nop(), reads=S.all_keys())
        with ExitStack() as es:
            sems = {e: es.enter_context(nc.semaphore("s_" + e)) for e in ENGS}
            dsems = {k: es.enter_context(nc.semaphore("d_" + k)) for k in S.dma_counts}
            block = es.enter_context(nc.Block())
            S.emit(block, sems, dsems)
        return nc

    S.fake_par = None
    S.barrier()
    A3 = _Arena(nc, PH_BASE, SBUF_END)
    NWD = 10
    Wout = A3.alloc("Wout", [128, 16, D], BF16)
    Wup = A3.alloc("Wup", [128, 8, DFF], BF16)
    Wgt = A3.alloc("Wgt", [128, 8, DFF], BF16)
    Wring = A3.alloc("Wring", [128, NWD, D], BF16)
    xh = A3.alloc("xh", [128, 4, D], F32)
    mtoks = [A3.alloc(f"mtok{i}", [128, 2048], BF16) for i in range(2)]
    mixedT = A3.alloc("mixedT", [128, 16, 256], BF16)
    u2Ts = [A3.alloc(f"u2T{i}", [128, 8, 256], BF16) for i in range(2)]
    u3s = [A3.alloc(f"u3_{i}", [128, D], BF16) for i in range(2)]
    NB3 = int(os.environ.get("MK_NB3", "4"))
    asb3s = [A3.alloc(f"asb3_{i}", [128, 258], F32) for i in range(NB3)]
    acc3 = [A3.alloc(f"acc3_{i}", [128, 256], F32) for i in range(NB3)]
    actT = [A3.alloc(f"actT{i}", [128, 256], BF16) for i in range(NB3)]
    halo = A3.alloc("halo", [128, NJ, 2], F32)
    convff = A3.alloc("convff", [128, NJ * 3], F32)
    g_ffn_b = A3.alloc("g_ffn_b", [128, D], F32)
    g_fin_b = A3.alloc("g_fin_b", [128, D], F32)
    gcol = A3.alloc("gcol", [128, 16], F32)
    stat2 = A3.alloc("stat2", [128, 16], F32)

    print("[kernel] sbuf A1 end", A1.off, "A3 end", A3.off, "limit", SBUF_END)
    S.add("sp", _dma(convff, convff_d), writes=["convff"], dma="cst")
    S.add("sp", _dma(g_ffn_b, gffn_d), writes=["g_ffn_b"], dma="cst")
    S.add("sp", _dma(g_fin_b, gfin_d), writes=["g_fin_b"], dma="cst")
    S.add("sp", _dma(gcol, gcol_d), writes=["gcol"], dma="cst")
    S.add("dve", _ts(gcol[:, 0:8], gcol[:, 0:8], 0.5, ALU.mult), reads=["gcol"], writes=["gcol"])
    S.add("pool", lambda e: e.memset(halo, 0.0), writes=[("halo", j) for j in range(NJ)])
    for k in range(16):
        sl = k % 4
        S.add("sp", _dma(xh[:, sl, :], wout_d[k * 128:(k + 1) * 128, :]), writes=[("xh", sl)], dma=f"xh{sl}")
        S.add("act" if k % 2 else "dve",
              (_act(Wout[:, k, :], xh[:, sl, :], AF.Copy, scale=gcol[:, k:k + 1]) if k % 2 else
               _ts(Wout[:, k, :], xh[:, sl, :], gcol[:, k:k + 1], ALU.mult)),
              reads=[("xh", sl), "gcol"], writes=[("Wout", k)])
    for k in range(8):
        S.add("pool", _dma(Wup[:, k, :], wup_d[k * 128:(k + 1) * 128, :]), writes=[("Wup", k)], dma="Wup")
    for k in range(8):
        S.add("pool", _dma(Wgt[:, k, :], wgate_d[k * 128:(k + 1) * 128, :]), writes=[("Wgt", k)], dma="Wgt")

    b_acc = [banks[0], banks[1], banks[2], banks[3]]
    ybk = None
    agrot = [0]
    NAG = int(os.environ.get("MK_NAG", "3"))
    b_y = [banks[4 + NAG], banks[7]] if NAG < 3 else [banks[7], banks[7]]
    ybk = [4 + NAG, 7] if NAG < 3 else [7, 7]
    wdc = [0]
    strot = [0]
    mtc = [0]

    def rms_small(src, skey):
        i = strot[0] % 4
        strot[0] += 1
        return stat2[:, 4 * i:4 * i + 1], stat2[:, 4 * i + 1:4 * i + 2], stat2[:, 4 * i + 2:4 * i + 3], i

    def f3_pre(f0, ntt, is_halo, bp):
        T = ntt * 128
        u2T = u2Ts[bp]
        xsl = [bp * 2 + tt for tt in range(ntt)]
        for tt in range(ntt):
            f = f0 + tt
            p0 = (NCH_P + f) * 128
            S.add("sp", _dma(xh[:, xsl[tt], :], xs_d[p0:p0 + 128, :]), writes=[("xh", xsl[tt])], dma=f"xh{xsl[tt]}")
        for tt in range(ntt):
            f = f0 + tt
            mi = mtc[0] % 2
            mtc[0] += 1
            mtok = mtoks[mi]
            S.add("sp", _dma(mtok, mixed_d[f * 128:(f + 1) * 128, :]), reads=[("mixed_d", f)], writes=[("mtok", mi)], dma=f"mtok{mi}")
            for half in range(2):
                yb = b_y[half].bitcast(BF16)
                S.add("pe", _trs([(yb[:, kk * 128:(kk + 1) * 128], mtok[:, (half * 8 + kk) * 128:(half * 8 + kk + 1) * 128])
                                  for kk in range(8)], ident_bf), reads=[("mtok", mi), "ident_bf"], writes=[("bk", ybk[half])])
                S.add("act" if half else "dve",
                      (_act(mixedT[:, half * 8:half * 8 + 8, tt * 128:(tt + 1) * 128], yb.rearrange("p (k t) -> p k t", k=8), AF.Copy)
                       if half else _cp(mixedT[:, half * 8:half * 8 + 8, tt * 128:(tt + 1) * 128], yb.rearrange("p (k t) -> p k t", k=8))),
                      reads=[("bk", ybk[half])], writes=[("mixedT", tt, half)])
        for tt in range(ntt):
            xv = xh[:, xsl[tt], :]
            for half in range(2):
                S.add("pe", _mmk(b_y[half], [(mixedT[:, k, tt * 128:(tt + 1) * 128], Wout[:, k, half * 512:(half + 1) * 512])
                                             for k in range(16)]),
                      reads=[("mixedT", tt, 0), ("mixedT", tt, 1)] + [("Wout", k) for k in range(16)], writes=[("bk", ybk[half])])
                S.add("dve", _tt(xv[:, half * 512:(half + 1) * 512], b_y[half], xv[:, half * 512:(half + 1) * 512], ALU.add),
                      reads=[("bk", ybk[half]), ("xh", xsl[tt])], writes=[("xh", xsl[tt])])
        for tt in range(ntt):
            xv = xh[:, xsl[tt], :]
            xk = ("xh", xsl[tt])
            ss, ms, rs, si = rms_small(None, None)
            u3 = u3s[tt % 2]
            uk = ("u3", tt % 2)
            S.add("act", _act(u3, xv, AF.Square, accum=ss), reads=[xk], writes=[uk, ("ss", si)])
            S.add("dve", _ts(ms, ss, 1.0 / D, ALU.mult, EPS, ALU.add), reads=[("ss", si)], writes=[("ms", si)])
            S.add("pool", _tt(rs, ms, mhalf[:, 0:1], ALU.pow), reads=[("ms", si), "mhalf"], writes=[("rs", si)])
            S.add("dve", _stt(u3, xv, rs, g_ffn_b, ALU.mult, ALU.mult), reads=[xk, ("rs", si), "g_ffn_b"], writes=[uk])
            yb = b_y[tt % 2].bitcast(BF16)
            S.add("pe", _trs([(yb[:, k * 128:(k + 1) * 128], u3[:, k * 128:(k + 1) * 128]) for k in range(8)], ident_bf),
                  reads=[uk, "ident_bf"], writes=[("bk", ybk[tt % 2])])
            S.add("act", _act(u2T[:, :, tt * 128:(tt + 1) * 128], yb.rearrange("p (k t) -> p k t", k=8), AF.Copy),
                  reads=[("bk", ybk[tt % 2])], writes=[("u2T", bp, tt)])

    def f3_main(f0, ntt, is_halo, bp):
        T = ntt * 128
        u2T = u2Ts[bp]
        xsl = [bp * 2 + tt for tt in range(ntt)]
        u2k = [("u2T", bp, tt) for tt in range(ntt)]
        for j in range(NJ):
            par = agrot[0] % NB3
            agb = 4 + (agrot[0] % NAG)
            agrot[0] += 1
            aT = banks[agb][:, 0:T]
            gT = banks[agb][:, 256:256 + T]
            if not is_halo:
                ws = wdc[0] % NWD
                wdc[0] += 1
                S.add("pool", _dma(Wring[:, ws, :], wdn_d[j * 128:(j + 1) * 128, :]), writes=[("Wring", ws)], dma=f"Wd{ws}")
            S.add("pe", _mmk(aT, [(Wup[:, k, j * 128:(j + 1) * 128], u2T[:, k, 0:T]) for k in range(8)]),
                  reads=u2k + [("Wup", k) for k in range(8)], writes=[("bk", agb)])
            if not is_halo:
                S.add("pe", _mmk(gT, [(Wgt[:, k, j * 128:(j + 1) * 128], u2T[:, k, 0:T]) for k in range(8)]),
                      reads=u2k + [("Wgt", k) for k in range(8)], writes=[("bk", agb)])
            asb3 = asb3s[par]
            S.add("pool", _cp(asb3[:, 0:2], halo[:, j, :]), reads=[("halo", j)], writes=[("asb3h", par)])
            S.add("act", _act(asb3[:, 2:2 + T], aT, AF.Copy), reads=[("bk", agb)], writes=[("asb3", par)])
            S.add("pool", _cp(halo[:, j, :], asb3[:, T:T + 2]), reads=[("asb3", par)], writes=[("halo", j)])
            if is_halo:
                continue
            acc = acc3[par][:, 0:T]
            ak = ("acc3", par)
            S.add("dve", _ts(acc, asb3[:, 2:2 + T], convff[:, j * 3 + 2:j * 3 + 3], ALU.mult), reads=[("asb3", par), "convff"], writes=[ak])
            S.add("dve", _stt(acc, asb3[:, 1:1 + T], convff[:, j * 3 + 1:j * 3 + 2], acc, ALU.mult, ALU.add),
                  reads=[("asb3", par), ("asb3h", par), "convff", ak], writes=[ak])
            S.add("dve", _stt(acc, asb3[:, 0:T], convff[:, j * 3:j * 3 + 1], acc, ALU.mult, ALU.add),
                  reads=[("asb3", par), ("asb3h", par), "convff", ak], writes=[ak])
            S.add("act", _act(acc, acc, AF.Silu), reads=[ak], writes=[ak])
            at = actT[par][:, 0:T]
            S.add("dve", _tt(at, acc, gT, ALU.mult), reads=[ak, ("bk", agb)], writes=[("actT", par)])
            for tt in range(ntt):
                for half in range(2):
                    S.add("pe", _mm(b_acc[tt * 2 + half], at[:, tt * 128:(tt + 1) * 128], Wring[:, ws, half * 512:(half + 1) * 512],
                                    start=(j == 0), stop=(j == NJ - 1)),
                          reads=[("actT", par), ("Wring", ws)], writes=[("bk", tt * 2 + half)])
        if is_halo:
            return
        for tt in range(ntt):
            f = f0 + tt
            xv = xh[:, xsl[tt], :]
            xk = ("xh", xsl[tt])
            for half in range(2):
                S.add("dve", _tt(xv[:, half * 512:(half + 1) * 512], b_acc[tt * 2 + half], xv[:, half * 512:(half + 1) * 512], ALU.add),
                      reads=[("bk", tt * 2 + half), xk], writes=[xk])
            ss, ms, rs, si = rms_small(None, None)
            u3 = u3s[tt % 2]
            uk = ("u3", tt % 2)
            S.add("act", _act(u3, xv, AF.Square, accum=ss), reads=[xk], writes=[uk, ("ss", si)])
            S.add("dve", _ts(ms, ss, 1.0 / D, ALU.mult, EPS, ALU.add), reads=[("ss", si)], writes=[("ms", si)])
            S.add("pool", _tt(rs, ms, mhalf[:, 0:1], ALU.pow), reads=[("ms", si), "mhalf"], writes=[("rs", si)])
            S.add("dve", _stt(xv, xv, rs, g_fin_b, ALU.mult, ALU.mult), reads=[xk, ("rs", si), "g_fin_b"], writes=[xk])
            S.add("act", _dma(out_d[(f - 1) * 128:f * 128, :], xv), reads=[xk], writes=[("out", f)], dma=f"ost{xsl[tt]}")

    blocks = [(0, 1, True, 1)] + [(1 + 2 * b, 2, False, b % 2) for b in range(8)]
    f3_pre(*blocks[0])
    for bi, blk in enumerate(blocks):
        if bi + 1 < len(blocks):
            f3_pre(*blocks[bi + 1])
        f3_main(*blk)

    fin = S.add("sp", lambda e: e.nop(), reads=[("out", f) for f in range(1, NCH_F)])

    with ExitStack() as es:
        sems = {e: es.enter_context(nc.semaphore("s_" + e)) for e in ENGS}
        dsems = {k: es.enter_context(nc.semaphore("d_" + k)) for k in S.dma_counts}
        block = es.enter_context(nc.Block())
        S.emit(block, sems, dsems, reorder=bool(int(os.environ.get("MK_REORDER", "1"))))
        print("[kernel] est_ns", getattr(S, "est_ns", None), "ops", len(S.ops))
    return nc


def _host_consts():
    idx = np.arange(128, dtype=np.float64)
    dqk = np.zeros((128, 12), np.float32)
    for h in range(4):
        dqk[:, h] = np.exp(LOG_GAMMA[h] * (idx + 1.0))
        dqk[:, 4 + h] = S_ML * np.exp(-LOG_GAMMA[h] * (idx + 1.0))
        dqk[:, 8 + h] = S_ML * np.exp(-LOG_GAMMA[h] * (idx + 1.0)) * np.exp(LOG_GAMMA[h] * CH)
    jj, ii = np.meshgrid(np.arange(128), np.arange(128), indexing="ij")
    cm = (jj <= ii).astype(np.float32)
    mneg = np.where(jj <= ii, 0.0, NEG).astype(np.float32)
    i4 = np.concatenate([np.eye(4), -np.eye(4)], axis=1).astype(np.float32)
    return dict(dqk=dqk, cmask=cm, maskneg4=mneg,
                ident_bf=np.eye(128).astype(ml_dtypes.bfloat16), ident_f=np.eye(128, dtype=np.float32), i4=i4)


def _rope_tables(n_null):
    p = np.arange(NPOS, dtype=np.float64)
    pos = np.where(p >= n_null, 48.0 + (p - n_null), 0.0)
    inv = 10000.0 ** (-np.arange(0, 128, 2, dtype=np.float64) / 128.0)
    ang = pos[:, None] * inv[None, :]
    cosr = np.cos(ang).reshape(NCH, 128, 1, 64)
    sinr = np.sin(ang).reshape(NCH, 128, 1, 64)
    idx = np.arange(128, dtype=np.float64)
    dq = np.stack([np.exp(LOG_GAMMA[h] * (idx + 1.0)) for h in range(4)], axis=1)[None, :, :, None]
    dk = np.stack([S_ML * np.exp(-LOG_GAMMA[h] * (idx + 1.0)) for h in range(4)], axis=1)[None, :, :, None]
    tab = np.stack([cosr * dq, sinr * dq, cosr * dk, sinr * dk], axis=2)
    tab = np.ascontiguousarray(tab.reshape(NCH, 128, 1024)).astype(np.float32)
    valid = (p >= n_null).astype(np.float32).reshape(NCH, 128)
    vm = np.stack([valid, (valid - 1.0) * 1e30], axis=1)
    vm = np.ascontiguousarray(np.broadcast_to(vm[:, None], (NCH, 4, 2, 128))).astype(np.float32)
    return tab, vm


def _prep_inputs(inputs):
    f = lambda k: np.asarray(inputs[k], dtype=np.float32)
    x = f("x")
    meta = f("meta_tokens")
    w_in = f("w_in")[0]
    sizes = [512, 512, 1024, 1024, 4, 4, 512, 512, 1024, 1024]
    offs = np.cumsum(sizes)[:-1]
    ml_q, ml_k, ml_v, ml_o, ml_i, ml_f, rt_q, rt_k, rt_v, rt_g = np.split(w_in, offs, axis=1)

    def swap(cols):
        return cols.reshape(D, 4, 2, 64)[:, :, ::-1, :].reshape(D, 512)

    w_in_r = np.ascontiguousarray(np.concatenate(
        [ml_q, ml_k, ml_i, ml_f, ml_v, ml_o, rt_v, rt_g, rt_q, rt_k], axis=1))
    assert w_in_r.shape == (D, WCOLS)
    convml = f("ml_conv_w")[0]
    convw_ml = np.ascontiguousarray(convml.reshape(4, 8, 128).transpose(2, 1, 0).reshape(128, 32))
    convff = f("ffn_conv_w")[0]
    convw_ffn = np.ascontiguousarray(convff.reshape(3, NJ, 128).transpose(2, 1, 0).reshape(128, NJ * 3))
    b_if = np.ascontiguousarray(np.stack([f("ml_b_i")[0], f("ml_b_f")[0]], axis=1))
    gcat = np.concatenate([f("ml_norm_g")[0], f("rt_norm_g")[0]])
    gcol = np.ascontiguousarray(gcat.reshape(16, 128).T)
    bc = lambda v: np.ascontiguousarray(np.broadcast_to(v[None, :], (128, D)))
    common = dict(w_in_r=w_in_r, w_out=f("w_out")[0], w_up=f("w_up")[0], w_gate=f("w_gate")[0], w_down=f("w_down")[0],
                  convw_ml=convw_ml, convw_ffn=convw_ffn, b_if=b_if, gcol=gcol,
                  g_mix_b=bc(f("norm_mix_g")[0]), g_ffn_b=bc(f("norm_ffn_g")[0]), g_fin_b=bc(f("norm_final_g")))
    common.update(_host_consts())
    tabs = [_rope_tables(2160), _rope_tables(112)]
    in_maps = []
    for core in range(8):
        b, t = core // 2, core % 2
        xs = np.zeros((NPOS, D), np.float32)
        if t == 0:
            xs[2160:2176] = meta
            xs[2176:] = x[b, 0:2048]
        else:
            xs[112:128] = meta
            xs[128:] = x[b]
        m = dict(common)
        m["xs"] = xs
        m["cs_tab"], m["vm_tab"] = tabs[t]
        in_maps.append(m)
    return in_maps


_NC_CACHE = {}


def kernel(**inputs):
    in_maps = _prep_inputs(inputs)
    if "nc" not in _NC_CACHE:
        _NC_CACHE["nc"] = build_program()
    nc = _NC_CACHE["nc"]
    res = run_bass_kernel_spmd(nc, in_maps, core_ids=list(range(8)))
    out = np.zeros((4, 4096, D), np.float32)
    for core in range(8):
        b, t = core // 2, core % 2
        out[b, t * 2048:(t + 1) * 2048] = res.results[core]["out"]
    if DEBUG:
        kernel.debug = [res.results[c]["mixed_d"] for c in range(8)]
    return out
```

```python
import os
from contextlib import ExitStack

import numpy as np
import ml_dtypes

import concourse.bass as bass
import concourse.mybir as mybir
from concourse.bass_utils import run_bass_kernel_spmd

F32 = mybir.dt.float32
BF16 = mybir.dt.bfloat16
ALU = mybir.AluOpType
AF = mybir.ActivationFunctionType

NCH_P = 16
NCH_F = 17
NCH = NCH_P + NCH_F
CH = 128
NPOS = NCH * CH
D = 1024
DFF = 2816
NJ = DFF // 128
WCOLS = 6152
EPS = 1e-6
NEG = -1e30
GATE_CAP = 15.0
SBUF_BASE = 16640
SBUF_END = 229376
S_ML = 128.0 ** -0.5
LOG_GAMMA = [float(np.log1p(-2.0 ** (-(5.0 + h)))) for h in range(4)]
CD = [float(np.exp(lg * CH)) for lg in LOG_GAMMA]

ENGS = ("pe", "act", "dve", "pool", "sp")
DEBUG = bool(int(os.environ.get("MK_DEBUG", "0")))


class _Op:
    __slots__ = ("eng", "fn", "deps", "dma_sem", "dma_cnt", "sig", "idx", "need_sig", "wk")

    def __init__(self, eng, fn):
        self.eng = eng
        self.fn = fn
        self.deps = []
        self.dma_sem = None
        self.dma_cnt = 0
        self.sig = 0
        self.need_sig = False


class Sched:
    def __init__(self):
        self.ops = []
        self.last_w = {}
        self.readers = {}
        self.dma_counts = {}
        self.fence = None

    fake_par = None
    FAKE_KEEP = ("bk", "W", "Cml_f", "Cml_bf", "Crt_f", "Crt_bf", "asb", "mixed_d", "cst", "vmt", "uT")

    def _fk(self, keys):
        if self.fake_par is None:
            return keys
        out = []
        for k in keys:
            base = k[0] if isinstance(k, tuple) else k
            if base == "bk" and os.environ.get("MK_FAKEBK"):
                out.append((k, self.fake_par))
                continue
            if base in self.FAKE_KEEP or base in ("mst", "ident_bf", "mhalf", "i4", "ones4", "zeros4", "convml", "dqk", "cmask",
                                                   "maskneg4", "ident_f", "g_mix_b", "bsc0", "bsc1", "bif"):
                out.append(k)
            else:
                out.append((k, self.fake_par))
        return out

    def add(self, eng, fn, reads=(), writes=(), dma=None):
        reads = self._fk(reads)
        writes = self._fk(writes)
        op = _Op(eng, fn)
        op.wk = list(writes)[:2]
        op.idx = len(self.ops)
        deps = {}

        def dep(o, raw):
            val = self.dma_counts[o.dma_sem] if o.dma_sem is not None else 0
            if o.idx in deps:
                if raw and not deps[o.idx][1]:
                    deps[o.idx] = (o, True, val)
            else:
                deps[o.idx] = (o, raw, val)

        for r in reads:
            w = self.last_w.get(r)
            if w is not None:
                dep(w, True)
        for k in writes:
            w = self.last_w.get(k)
            if w is not None:
                dep(w, False)
            for rd in self.readers.get(k, ()):
                dep(rd, False)
        if self.fence is not None:
            dep(self.fence[eng], False)
        op.deps = list(deps.values())
        for r in reads:
            self.readers.setdefault(r, []).append(op)
        for k in writes:
            self.last_w[k] = op
            self.readers[k] = []
        if dma is not None:
            op.dma_sem = dma
            self.dma_counts[dma] = self.dma_counts.get(dma, 0) + 16
            op.dma_cnt = self.dma_counts[dma]
        self.ops.append(op)
        return op

    def all_keys(self):
        return list(set(self.last_w.keys()) | set(self.readers.keys()))

    def barrier(self):
        keys = self.all_keys()
        fence = {}
        for e in ENGS:
            fence[e] = self.add(e, lambda eng: eng.nop(), writes=keys)
        self.fence = fence

    @staticmethod
    def _skip(d, op, raw):
        if d.dma_sem is not None or op.dma_sem is not None:
            return False
        if d.eng != op.eng:
            return False
        return d.eng == "pe"

    def schedule(self, window=int(os.environ.get("MK_WIN", "200")), lat=float(os.environ.get("MK_LAT", "800"))):
        ops = self.ops
        n = len(ops)
        ndeps = [0] * n
        users = [[] for _ in range(n)]
        for op in ops:
            ndeps[op.idx] = len(op.deps)
            for (d, raw, val) in op.deps:
                users[d.idx].append(op.idx)
        fin = [0.0] * n
        ready_t = [0.0] * n
        prio_cp = os.environ.get("MK_PRIO", "cp") == "cp"
        tail = [0.0] * n
        if prio_cp:
            for op in reversed(ops):
                c = getattr(op.fn, "cost", 150.0)
                t = 0.0
                for u in users[op.idx]:
                    if tail[u] + lat > t:
                        t = tail[u] + lat
                tail[op.idx] = t + c
        pend = {e: [op.idx for op in ops if op.eng == e] for e in ENGS}
        pos = {e: 0 for e in ENGS}
        done = [False] * n
        efree = {e: 0.0 for e in ENGS}
        order = {e: [] for e in ENGS}
        remaining = n
        while remaining:
            best = None
            for e in ENGS:
                lst = pend[e]
                i = pos[e]
                while i < len(lst) and done[lst[i]]:
                    i += 1
                pos[e] = i
                if i >= len(lst):
                    continue
                w = 1 if e == "sp" else window
                seen = 0
                j = i
                cand = None
                dma_blocked = False
                while j < len(lst) and seen < w:
                    k = lst[j]
                    j += 1
                    if done[k]:
                        continue
                    seen += 1
                    isdma = ops[k].dma_sem is not None
                    if isdma and dma_blocked:
                        continue
                    if isdma:
                        dma_blocked = True
                    if ndeps[k] > 0:
                        continue
                    st = max(ready_t[k], efree[e])
                    if prio_cp:
                        key = (st, -tail[k], k)
                        if cand is None or key < cand:
                            cand = key
                    else:
                        key = (st, 0.0, k)
                        if cand is None or key < cand:
                            cand = key
                            if ready_t[k] <= efree[e]:
                                break
                if cand is not None and (best is None or cand < best[0]):
                    best = (cand, e)
            assert best is not None, "scheduler deadlock"
            (st, _pr, k), e = best
            op = ops[k]
            c = getattr(op.fn, "cost", 150.0)
            if op.dma_sem is not None:
                efree[e] = st + 60.0
                fin[k] = st + c
            else:
                efree[e] = st + c
                fin[k] = st + c
            done[k] = True
            remaining -= 1
            order[e].append(op)
            for u in users[k]:
                ndeps[u] -= 1
                t = fin[k] + lat
                if t > ready_t[u]:
                    ready_t[u] = t
        self.est_ns = max(fin) if n else 0.0
        if os.environ.get("MK_CRIT"):
            dist = [0.0] * n
            pred = [-1] * n
            for op in ops:
                c = getattr(op.fn, "cost", 150.0)
                best_t, best_p = 0.0, -1
                for (d, raw, val) in op.deps:
                    t = dist[d.idx] + lat
                    if t > best_t:
                        best_t, best_p = t, d.idx
                dist[op.idx] = best_t + c
                pred[op.idx] = best_p
            lim = self.fence["pe"].idx if self.fence else n
            k = max(range(lim), key=lambda i: dist[i])
            print("[crit] dependency-only critical path before fence:", round(dist[k] / 1000), "us")
            path = []
            while k >= 0:
                path.append(k)
                k = pred[k]
            path.reverse()
            import collections
            cnt = collections.Counter(ops[i].eng for i in path)
            print("[crit] path len", len(path), dict(cnt))
            self.crit_path = path
            mid = int(len(path) * float(os.environ.get("MK_CRITPOS", "0.5")))
            for i in path[mid:mid + 70]:
                print("[crit]  ", ops[i].eng, ops[i].wk, round(getattr(ops[i].fn, "cost", 150.0)))
        if os.environ.get("MK_SCHED_DBG"):
            B = 100000.0
            nb = int(self.est_ns // B) + 1
            busy = {e: [0.0] * nb for e in ENGS}
            for op in ops:
                c = getattr(op.fn, "cost", 150.0)
                if op.dma_sem is not None:
                    continue
                b = int((fin[op.idx] - c) // B)
                busy[op.eng][b] += c
            for b in range(nb):
                print(f"[sched] {b*100:6d}us " + " ".join(f"{e}:{busy[e][b]/B*100:5.1f}%" for e in ENGS if e != "sp"))
            if self.fence:
                print("[sched] fence done at", {e: round(fin[o.idx] / 1000) for e, o in self.fence.items()})
        return order

    def eval_order(self, order, lat):
        ops = self.ops
        fin = {}
        pos = {e: 0 for e in ENGS}
        efree = {e: 0.0 for e in ENGS}
        remaining = sum(len(v) for v in order.values())
        while remaining:
            progressed = False
            for e in ENGS:
                while pos[e] < len(order[e]):
                    op = order[e][pos[e]]
                    if any(d.idx not in fin for (d, raw, val) in op.deps):
                        break
                    rt = max([fin[d.idx] + lat for (d, raw, val) in op.deps] + [0.0])
                    st = max(rt, efree[e])
                    c = getattr(op.fn, "cost", 150.0)
                    if op.dma_sem is not None:
                        efree[e] = st + 60.0
                    else:
                        efree[e] = st + c
                    fin[op.idx] = st + c
                    pos[e] += 1
                    remaining -= 1
                    progressed = True
            assert progressed, "order deadlock"
        return max(fin.values())

    def emit(self, block, sems, dma_sems, reorder=True):
        for op in self.ops:
            for (d, raw, val) in op.deps:
                if d.dma_sem is None and not self._skip(d, op, raw):
                    d.need_sig = True
        if reorder:
            per_eng = self.schedule()
            if os.environ.get("MK_EVAL_LAT"):
                print("[kernel] eval fixed order @lat", os.environ["MK_EVAL_LAT"], self.eval_order(per_eng, float(os.environ["MK_EVAL_LAT"])))
        else:
            per_eng = {e: [] for e in ENGS}
            for op in self.ops:
                per_eng[op.eng].append(op)
        cnt = {e: 0 for e in ENGS}
        for e in ENGS:
            for op in per_eng[e]:
                if op.dma_sem is None and op.need_sig:
                    cnt[e] += 1
                    op.sig = cnt[e]
        handles = {"pe": "tensor", "act": "scalar", "dve": "vector", "pool": "gpsimd", "sp": "sync"}

        def body(eng_name):
            def _f(eng):
                waited = {}
                for op in per_eng[eng_name]:
                    for (d, raw, val) in op.deps:
                        if d.dma_sem is not None:
                            key = ("dma", d.dma_sem)
                            sem = dma_sems[d.dma_sem]
                        else:
                            if self._skip(d, op, raw):
                                continue
                            key = d.eng
                            val = d.sig
                            sem = sems[d.eng]
                        if waited.get(key, 0) >= val:
                            continue
                        waited[key] = val
                        eng.wait_ge(sem, val)
                    ins = op.fn(eng)
                    if op.dma_sem is not None:
                        ins.then_inc(dma_sems[op.dma_sem], 16)
                    elif op.need_sig:
                        ins.then_inc(sems[eng_name], 1)
            return _f

        for e in ENGS:
            if per_eng[e]:
                getattr(block, handles[e])(body(e))


def _mmcost(lhsT, rhs):
    n = rhs.free_size()
    c = max(lhsT.free_size() / 1.2, n / 2.37, 30.0)
    if rhs.dtype == F32:
        c *= 4.0
    return c


def _fsz(ap):
    return ap.free_size()


def _mm(out, lhsT, rhs, start=True, stop=True):
    f = lambda e: e.matmul(out, lhsT=lhsT, rhs=rhs, start=start, stop=stop)
    f.cost = _mmcost(lhsT, rhs)
    return f


def _mmk(out, pairs):
    def f(e):
        n = len(pairs)
        ins = None
        for i, (l, r) in enumerate(pairs):
            ins = e.matmul(out, lhsT=l, rhs=r, start=(i == 0), stop=(i == n - 1))
        return ins
    f.cost = sum(_mmcost(l, r) for (l, r) in pairs)
    return f


def _trs(items, ident):
    def f(e):
        ins = None
        for (o, i) in items:
            ins = e.transpose(out=o, in_=i, identity=ident)
        return ins
    f.cost = 120.0 * len(items)
    return f


def _act(out, in_, func, bias=None, scale=None, accum=None):
    def f(e):
        kw = {}
        if bias is not None:
            kw["bias"] = bias
        if scale is not None:
            kw["scale"] = scale
        if accum is not None:
            kw["accum_out"] = accum
        return e.activation(out=out, in_=in_, func=func, **kw)
    f.cost = (224.0 + _fsz(out)) / 1.2
    return f


def _ts(out, in0, s1, op0, s2=None, op1=None):
    def f(e):
        if op1 is None:
            return e.tensor_scalar(out=out, in0=in0, scalar1=s1, scalar2=None, op0=op0)
        return e.tensor_scalar(out=out, in0=in0, scalar1=s1, scalar2=s2, op0=op0, op1=op1)
    f.cost = (100.0 + _fsz(out)) / 0.96
    return f


def _tt(out, in0, in1, op):
    f = lambda e: e.tensor_tensor(out=out, in0=in0, in1=in1, op=op)
    f.cost = (100.0 + _fsz(out)) / 0.96
    return f


def _stt(out, in0, scalar, in1, op0, op1):
    f = lambda e: e.scalar_tensor_tensor(out=out, in0=in0, scalar=scalar, in1=in1, op0=op0, op1=op1)
    f.cost = (100.0 + _fsz(out)) / 0.96
    return f


def _cp(out, in_):
    f = lambda e: e.tensor_copy(out=out, in_=in_)
    f.cost = (100.0 + _fsz(out)) / 0.96
    return f


def _dma(out, in_):
    f = lambda e: e.dma_start(out=out, in_=in_)
    f.cost = 2000.0 + 128.0 * _fsz(out) * 4 / 200.0
    return f


def _scan(out, d0, d1, init, op0, op1):
    f = lambda e: e.tensor_tensor_scan(out=out, data0=d0, data1=d1, initial=init, op0=op0, op1=op1)
    f.cost = (100.0 + 2 * _fsz(out)) / 0.96
    return f


class _Arena:
    def __init__(self, nc, base, end):
        self.nc = nc
        self.off = base
        self.end = end

    def alloc(self, name, shape, dt):
        size = 1
        for s in shape[1:]:
            size *= s
        size *= 2 if dt == BF16 else 4
        off = (self.off + 31) // 32 * 32
        assert off + size <= self.end, f"SBUF overflow at {name}: {off + size} > {self.end}"
        t = self.nc.alloc_sbuf_tensor_at(name, list(shape), dt, offset=off)
        self.off = off + size
        return t.ap()


WGROUPS = [(1024, 2056), (512, 1024), (5640, 6152), (3080, 4104), (0, 512), (5128, 5640), (2056, 3080), (4104, 5128)]


def _wgroup_of(col):
    for g, (a, b) in enumerate(WGROUPS):
        if a <= col < b:
            return g
    raise ValueError(col)


def build_program():
    nc = bass.Bass("TRN2", target_bir_lowering=False)

    def din(name, shape, dt=F32):
        return nc.dram_tensor(name, list(shape), dt, kind="ExternalInput").ap()

    xs_d = din("xs", [NPOS, D])
    win_d = din("w_in_r", [D, WCOLS])
    wout_d = din("w_out", [2048, D])
    wup_d = din("w_up", [D, DFF])
    wgate_d = din("w_gate", [D, DFF])
    wdn_d = din("w_down", [DFF, D])
    cs_d = din("cs_tab", [NCH, 128, 1024])
    vm_d = din("vm_tab", [NCH, 4, 2, 128])
    dqk_d = din("dqk", [128, 12])
    cmask_d = din("cmask", [128, 128])
    mneg_d = din("maskneg4", [128, 128])
    identb_d = din("ident_bf", [128, 128], BF16)
    identf_d = din("ident_f", [128, 128])
    i4_d = din("i4", [4, 8])
    convml_d = din("convw_ml", [128, 32])
    convff_d = din("convw_ffn", [128, NJ * 3])
    bif_d = din("b_if", [4, 2])
    gcol_d = din("gcol", [128, 16])
    gmix_d = din("g_mix_b", [128, D])
    gffn_d = din("g_ffn_b", [128, D])
    gfin_d = din("g_fin_b", [128, D])
    out_d = nc.dram_tensor("out", [2048, D], F32, kind="ExternalOutput").ap()
    mixed_d = nc.dram_tensor("mixed_d", [NCH_F * CH, 2048], BF16,
                             kind="ExternalOutput" if DEBUG else "Internal").ap()

    S = Sched()
    banks = [nc.alloc_psum_tensor(f"bank{i}", [128, 512], F32).ap() for i in range(8)]

    per = _Arena(nc, SBUF_BASE, SBUF_END)
    ident_bf = per.alloc("ident_bf", [128, 128], BF16)
    mhalf = per.alloc("mhalf", [128, 4], F32)
    stat = per.alloc("stat", [128, 8], F32)
    PH_BASE = per.off

    S.add("sp", _dma(ident_bf, identb_d), writes=["ident_bf"], dma="cst")
    S.add("pool", lambda e: e.memset(mhalf, -0.5), writes=["mhalf"])

    A1 = _Arena(nc, PH_BASE, SBUF_END)
    W = A1.alloc("W", [128, 8, WCOLS], BF16)
    ident_f = A1.alloc("ident_f", [128, 128], F32)
    cmask = A1.alloc("cmask", [128, 128], F32)
    maskneg4 = A1.alloc("maskneg4", [128, 128], F32)
    dqk = A1.alloc("dqk", [128, 12], F32)
    g_mix_b = A1.alloc("g_mix_b", [128, D], F32)
    convml = A1.alloc("convml", [128, 32], F32)
    i4 = A1.alloc("i4", [4, 8], F32)
    bif = A1.alloc("bif", [4, 2], F32)
    bsc = A1.alloc("bsc", [4, 2], F32)
    ones4 = A1.alloc("ones4", [4, 128], F32)
    zeros4 = A1.alloc("zeros4", [4, 128], F32)
    xin = A1.alloc("xin", [128, D], F32)
    u = A1.alloc("u", [128, D], BF16)
    uT = [A1.alloc(f"uT{i}", [128, 8, 128], BF16) for i in range(2)]
    asb = A1.alloc("asb", [128, 8, 131], F32)
    cacc = [A1.alloc(f"cacc{i}", [128, 128], F32) for i in range(2)]
    qTmls = [A1.alloc(f"qTml{i}", [128, 4, 128], BF16) for i in range(2)]
    kTmls = [A1.alloc(f"kTml{i}", [128, 4, 128], BF16) for i in range(2)]
    rX = [A1.alloc("rX0", [128, 512], F32)] * 2
    rM = [A1.alloc(f"rM{i}", [128, 256], F32) for i in range(4)]
    qtok = A1.alloc("qtok", [128, 4, 128], BF16)
    ktok = A1.alloc("ktok", [128, 4, 128], BF16)
    cst = [A1.alloc("cst0", [128, 1024], F32)] * 2
    vmt = [A1.alloc("vmt0", [4, 2, 128], F32)] * 2
    qTrts = [A1.alloc(f"qTrt{i}", [128, 4, 128], BF16) for i in range(2)]
    kTrts = [A1.alloc(f"kTrt{i}", [128, 4, 128], BF16) for i in range(2)]
    kw = A1.alloc("kw", [128, 4, 128], BF16)
    Vmls = [A1.alloc(f"Vml{i}", [128, 4, 257], BF16) for i in range(2)]
    Vrts = [A1.alloc(f"Vrt{i}", [128, 4, 256], BF16) for i in range(2)]
    ogs = [A1.alloc(f"og{i}", [128, D], F32) for i in range(2)]
    ggs = [A1.alloc(f"gg{i}", [128, D], F32) for i in range(2)]
    mixed = A1.alloc("mixed", [128, 2048], BF16)
    Cml_f = A1.alloc("Cml_f", [128, 4, 257], F32)
    Cml_bfs = [A1.alloc(f"Cml_bf{i}", [128, 4, 257], BF16) for i in range(2)]
    Crt_f = A1.alloc("Crt_f", [128, 4, 256], F32)
    Crt_bfs = [A1.alloc(f"Crt_bf{i}", [128, 4, 256], BF16) for i in range(2)]
    WT = A1.alloc("WT", [128, 4, 128], F32)
    PT = A1.alloc("PT", [128, 4, 128], BF16)
    Wint = A1.alloc("Wint", [128, 512], F32)
    qsT = A1.alloc("qsT", [128, 4, 128], BF16)
    hrt = A1.alloc("hrt", [128, 4, 256], F32)
    hraw = A1.alloc("hraw", [128, 4, 257], F32)
    rows = {n: A1.alloc("row_" + n, [4, 128], F32) for n in
            ("li0", "li1", "li", "ef", "sp", "nbcum", "B", "M", "R2", "R3")}
    rhs_bd = A1.alloc("rhs_bd", [4, 4, 128], F32)
    smr = A1.alloc("smr", [4, 16], F32)
    sm = A1.alloc("sm", [128, 64], F32)
    smps = A1.alloc("smps", [128, 20], F32)
    st6 = A1.alloc("st6", [128, 4, 6], F32)
    mv = A1.alloc("mv", [128, 4, 2], F32)

    EX8 = sm[:, 0:8]
    BT = sm[:, 8:12]
    BTS = sm[:, 12:16]
    WSARG = sm[:, 16:20]
    WSRC = sm[:, 20:24]
    DEC = sm[:, 24:28]
    DD = sm[:, 28:32]
    RDEN = sm[:, 32:36]
    T1 = sm[:, 36:40]
    T2 = sm[:, 40:44]
    RSTD = sm[:, 44:48]
    SC = sm[:, 48:52]
    BI = sm[:, 52:56]

    bT = banks[0]
    bT_bf = bT.bitcast(BF16)

    for nm, dst, src in (("ident_f", ident_f, identf_d), ("cmask", cmask, cmask_d), ("maskneg4", maskneg4, mneg_d),
                         ("dqk", dqk, dqk_d), ("g_mix_b", g_mix_b, gmix_d),
                         ("convml", convml, convml_d), ("i4", i4, i4_d), ("bif", bif, bif_d)):
        S.add("sp", _dma(dst, src), writes=[nm], dma="cst")
    for g, (c0, c1) in enumerate(WGROUPS):
        for k in range(8):
            S.add("pool", _dma(W[:, k, c0:c1], win_d[k * 128:(k + 1) * 128, c0:c1]),
                  writes=[("W", g, k)], dma=f"W{g}")

    def wk(col):
        g = _wgroup_of(col)
        return [("W", g, k) for k in range(8)]

    S.add("pool", lambda e: e.memset(ones4, 1.0), writes=["ones4"])
    S.add("pool", lambda e: e.memset(zeros4, 0.0), writes=["zeros4"])
    S.add("pool", lambda e: e.memset(asb, 0.0), writes=[("asb", t) for t in range(8)])
    S.add("pool", lambda e: e.memset(Vmls[0], 1.0), writes=[("Vml", 0, 0), ("Vml", 0, 1)])
    S.add("pool", lambda e: e.memset(Vmls[1], 1.0), writes=[("Vml", 1, 0), ("Vml", 1, 1)])
    S.add("pool", lambda e: e.memset(Cml_f, 0.0), writes=[("Cml_f", h) for h in range(4)])
    S.add("pool", lambda e: e.memset(Cml_bfs[0], 0.0), writes=[("Cml_bf", 0, h) for h in range(4)])
    S.add("pool", lambda e: e.memset(Crt_f, 0.0), writes=[("Crt_f", h) for h in range(4)])
    S.add("pool", lambda e: e.memset(Crt_bfs[0], 0.0), writes=[("Crt_bf", 0, h) for h in range(4)])
    S.add("pool", lambda e: e.memset(smr, 0.0), writes=["mst", "D1", "diagM", "diagD"])
    S.add("pool", lambda e: e.memset(smr[:, 0:1], NEG), writes=["mst"])
    S.add("dve", _ts(bsc[:, 0:1], bif[:, 0:1], 1.0 / GATE_CAP, ALU.mult), reads=["bif"], writes=["bsc0"])
    S.add("dve", _ts(bsc[:, 1:2], bif[:, 1:2], -1.0, ALU.mult), reads=["bif"], writes=["bsc1"])

    MST = smr[:, 0:1]
    D1 = smr[:, 1:2]
    DIAGM = smr[:, 4:8]
    DIAGD = smr[:, 8:12]

    def loads(c):
        s = c % 2
        S.add("sp", _dma(xin, xs_d[c * 128:(c + 1) * 128, :]), writes=["xin"], dma="xin")

    def loads2(c):
        S.add("sp", _dma(cst[0], cs_d[c]), writes=[("cst", 0)], dma="cst0")
        S.add("sp", _dma(vmt[0], vm_d[c]), writes=[("vmt", 0)], dma="vmt0")

    aslot = [0]

    def next_aslot():
        i = aslot[0] % 4
        aslot[0] += 1
        return i

    tmslot = [0]

    def rmsnorm_T(src, gb, dstT, dst_keys, src_key, gkey):
        S.add("act", _act(u, src, AF.Square, accum=stat[:, 0:1]), reads=[src_key], writes=["u", "ss"])
        S.add("dve", _ts(stat[:, 1:2], stat[:, 0:1], 1.0 / D, ALU.mult, EPS, ALU.add), reads=["ss"], writes=["ms"])
        S.add("pool", _tt(stat[:, 2:3], stat[:, 1:2], mhalf[:, 0:1], ALU.pow), reads=["ms", "mhalf"], writes=["rstd"])
        S.add("dve", _stt(u, src, stat[:, 2:3], gb, ALU.mult, ALU.mult), reads=[src_key, "rstd", gkey], writes=["u"])
        S.add("pe", _trs([(bT_bf[:, k * 128:(k + 1) * 128], u[:, k * 128:(k + 1) * 128]) for k in range(8)], ident_bf),
              reads=["u", "ident_bf"], writes=[("bk", 0)])
        S.add("act", _act(dstT, bT_bf.rearrange("p (k t) -> p k t", k=8), AF.Copy), reads=[("bk", 0)], writes=dst_keys)


    fmb = [0]

    def chunk(c, full):
        rp, wp = c % 2, (c + 1) % 2
        qTml, kTml, qTrt, kTrt = qTmls[rp], kTmls[rp], qTrts[rp], kTrts[rp]
        og, gg = ogs[rp], ggs[rp]
        Vml, Vrt = Vmls[rp], Vrts[rp]
        Cml_bf, Crt_bf = Cml_bfs[rp], Crt_bfs[rp]
        Cml_bfw, Crt_bfw = Cml_bfs[wp], Crt_bfs[wp]
        if os.environ.get("MK_FAKE2"):
            S.fake_par = c % int(os.environ["MK_FAKE2"])
        s = c % 2
        uTs = uT[s]
        rmsnorm_T(xin, g_mix_b, uTs, [("uT", s)], "xin", "g_mix_b")
        if c + 1 < NCH:
            loads(c + 1)

        def fm_group(cols_ms):
            b = 1 + fmb[0] % 2
            fmb[0] += 1
            bk = banks[b]
            for sl, (col, m) in enumerate(cols_ms):
                S.add("pe", _mmk(bk[0:m, sl * 128:(sl + 1) * 128], [(W[:, k, col:col + m], uTs[:, k, :]) for k in range(8)]),
                      reads=wk(col) + [("uT", s)], writes=[("bk", b)])
            return bk, ("bk", b)

        for grp in ((0, 1) if full else (1,)):
            bk, bkey = fm_group([((grp * 4 + t) * 128, 128) for t in range(4)])
            S.add("act", _act(asb[:, grp * 4:grp * 4 + 4, 3:131], bk.rearrange("p (a b) -> p a b", a=4), AF.Copy),
                  reads=[bkey], writes=[("asb", grp * 4 + t) for t in range(4)])
            for t in range(grp * 4, grp * 4 + 4):
                acc = cacc[t % 2]
                ak = ("cacc", t % 2)
                S.add("dve", _ts(acc, asb[:, t, 3:131], convml[:, t * 4 + 3:t * 4 + 4], ALU.mult),
                      reads=[("asb", t), "convml"], writes=[ak])
                for kk in (2, 1, 0):
                    S.add("dve", _stt(acc, asb[:, t, kk:kk + 128], convml[:, t * 4 + kk:t * 4 + kk + 1], acc, ALU.mult, ALU.add),
                          reads=[("asb", t), "convml", ak], writes=[ak])
                S.add("pool", _cp(asb[:, t, 0:3], asb[:, t, 128:131]), reads=[("asb", t)], writes=[("asb", t)])
                if t < 4:
                    S.add("act", _act(qTml[:, t, :], acc, AF.Silu), reads=[ak], writes=[("qTml", rp, t)])
                else:
                    S.add("act", _act(kTml[:, t - 4, :], acc, AF.Silu), reads=[ak], writes=[("kTml", rp, t - 4)])
        bk, gkey = fm_group([(1024, 4), (1028, 4)])
        gi_reg = bk[0:4, 0:128]
        gf_reg = bk[0:4, 128:256]
        R = rows
        vs = vmt[s]
        S.add("act", _act(R["li0"], gi_reg, AF.Tanh, bias=bsc[:, 0:1], scale=1.0 / GATE_CAP), reads=[gkey, "bsc0"], writes=["li0"])
        S.add("act", _act(R["ef"], gf_reg, AF.Exp, bias=bsc[:, 1:2], scale=-1.0), reads=[gkey, "bsc1"], writes=["ef"])
        S.add("dve", _stt(R["li1"], R["li0"], GATE_CAP, vs[:, 0, :], ALU.mult, ALU.mult), reads=["li0", ("vmt", 0)], writes=["li1"])
        S.add("dve", _tt(R["li"], R["li1"], vs[:, 1, :], ALU.add), reads=["li1", ("vmt", 0)], writes=["li"])
        S.add("act", _act(R["sp"], R["ef"], AF.Ln, bias=1.0), reads=["ef"], writes=["sp"])
        S.add("dve", _scan(R["nbcum"], R["sp"], zeros4, 0.0, ALU.add, ALU.add), reads=["sp", "zeros4"], writes=["nbcum"])
        S.add("dve", _tt(R["B"], R["li"], R["nbcum"], ALU.add), reads=["li", "nbcum"], writes=["B"])
        S.add("dve", _scan(R["M"], R["B"], R["B"], MST, ALU.max, ALU.max), reads=["B", "mst"], writes=["M"])
        S.add("dve", _ts(DIAGM, i4[:, 0:4], R["M"][:, 127:128], ALU.mult), reads=["i4", "M"], writes=["diagM"])
        S.add("dve", _tt(D1, MST, R["M"][:, 127:128], ALU.subtract), reads=["mst", "M"], writes=["D1"])
        S.add("dve", _ts(DIAGD, i4[:, 0:4], D1, ALU.mult), reads=["i4", "D1"], writes=["diagD"])
        if full:
            S.add("dve", _tt(R["R2"], R["nbcum"], R["M"], ALU.subtract), reads=["nbcum", "M"], writes=["R2"])
            S.add("dve", _ts(R["R2"], R["R2"], 80.0, ALU.min), reads=["R2"], writes=["R2"])
            S.add("dve", _ts(R["R3"], R["M"], MST, ALU.subtract, -1.0, ALU.mult), reads=["M", "mst"], writes=["R3"])
            for h in range(4):
                S.add("dve", _ts(rhs_bd[:, h, :], R["M"], i4[:, 4 + h:5 + h], ALU.mult), reads=["M", "i4"], writes=[("rhs_bd", h)])
        S.add("dve", _tt(MST, R["M"][:, 127:128], R["nbcum"][:, 127:128], ALU.subtract), reads=["M", "nbcum"], writes=["mst"])

        sb_ = 5
        SMP = banks[sb_][:, 0:32]
        skey = ("bk", sb_)

        def smp_mm(e):
            ins = e.matmul(SMP[:, 0:4], lhsT=R["B"], rhs=i4[:, 0:4], start=True, stop=True)
            if full:
                e.matmul(SMP[:, 4:8], lhsT=R["R2"], rhs=i4[:, 0:4], start=True, stop=True)
                e.matmul(SMP[:, 8:12], lhsT=R["R3"], rhs=i4[:, 0:4], start=True, stop=True)
            e.matmul(SMP[:, 12:16], lhsT=ones4, rhs=DIAGM, start=True, stop=True)
            ins = e.matmul(SMP[:, 16:20], lhsT=ones4, rhs=DIAGD, start=True, stop=True)
            return ins
        S.add("pe", smp_mm, reads=["B", "R2", "R3", "i4", "ones4", "diagM", "diagD"], writes=[skey])
        if full:
            S.add("act", _act(smps, SMP[:, 0:20], AF.Copy), reads=[skey], writes=["smps"])
        else:
            S.add("act", _act(smps[:, 0:4], SMP[:, 0:4], AF.Copy), reads=[skey], writes=["smps"])
            S.add("act", _act(smps[:, 12:20], SMP[:, 12:20], AF.Copy), reads=[skey], writes=["smps"])
        if full:
            S.add("act", _act(EX8, smps[:, 4:12], AF.Exp), reads=["smps"], writes=["ex8"])
        S.add("dve", _tt(WSARG, smps[:, 0:4], smps[:, 12:16], ALU.subtract), reads=["smps"], writes=["wsarg"])
        S.add("act", _act(WSRC, WSARG, AF.Exp, bias=float(np.log(S_ML))), reads=["wsarg"], writes=["wsrc"])
        S.add("act", _act(DEC, smps[:, 16:20], AF.Exp), reads=["smps"], writes=["dec"])
        b5 = banks[5]
        k5 = ("bk", 5)
        if full:
            S.add("dve", _ts(BTS, smps[:, 0:4], float(np.log(S_ML)), ALU.add), reads=["smps"], writes=["BTS"])
            S.add("pe", _mmk(b5, [(ones4, rhs_bd.rearrange("p h t -> p (h t)")), (ident_f, maskneg4.unsqueeze(1).broadcast_to([128, 4, 128]))]),
                  reads=["ones4", "ident_f", "maskneg4"] + [("rhs_bd", h) for h in range(4)], writes=[k5])
            for h in range(4):
                S.add("act", _act(WT[:, h, :], b5[:, h * 128:(h + 1) * 128], AF.Exp, bias=BTS[:, h:h + 1]),
                      reads=[k5, "BTS"], writes=[("WT", h)])
            for h in range(4):
                S.add("dve", _ts(rhs_bd[:, h, :], R["R3"], i4[:, h:h + 1], ALU.mult), reads=["R3", "i4"], writes=[("rhs_bd", h)])
            S.add("pe", _mm(b5, ones4, rhs_bd.rearrange("p h t -> p (h t)")),
                  reads=["ones4"] + [("rhs_bd", h) for h in range(4)], writes=[k5])
            S.add("act", _act(Wint, b5, AF.Exp), reads=[k5], writes=["Wint"])
            S.add("dve", _tt(qsT.rearrange("p h t -> p (h t)"), qTml.rearrange("p h t -> p (h t)"), Wint, ALU.mult),
                  reads=["Wint"] + [("qTml", rp, h) for h in range(4)], writes=["qsT"])

        def tm_tile(col):
            b = 3 + tmslot[0] % 2
            tmslot[0] += 1
            S.add("pe", _mmk(banks[b], [(uTs[:, k, :], W[:, k, col:col + 512]) for k in range(8)]),
                  reads=wk(col) + [("uT", s)], writes=[("bk", b)])
            return banks[b], ("bk", b)

        for hh in range(2):
            reg, rk = tm_tile(1032 + hh * 512)
            S.add("act", _act(Vml[:, 2 * hh:2 * hh + 2, 0:256], reg.rearrange("p (a b) -> p a b", a=2), AF.Copy),
                  reads=[rk], writes=[("Vml", rp, hh)])
        for hh in range(2):
            reg, rk = tm_tile(3080 + hh * 512)
            S.add("dve", _cp(Vrt[:, 2 * hh:2 * hh + 2, :], reg.rearrange("p (a b) -> p a b", a=2)),
                  reads=[rk], writes=[("Vrt", rp, hh)])
        if full:
            for hh in range(2):
                reg, rk = tm_tile(2056 + hh * 512)
                S.add("act", _act(og[:, hh * 512:(hh + 1) * 512], reg, AF.Tanh, scale=0.5), reads=[rk], writes=[("og", rp, hh)])
            for hh in range(2):
                reg, rk = tm_tile(4104 + hh * 512)
                S.add("act", _act(gg[:, hh * 512:(hh + 1) * 512], reg, AF.Silu), reads=[rk], writes=[("gg", rp, hh)])

        def rotary(col, xi, qk, dst, dkey):
            reg, rk = tm_tile(col)
            X = rX[xi]
            xk = ("rX", 0)
            S.add("act", _act(X, reg, AF.Copy), reads=[rk], writes=[xk])
            Xv = X.rearrange("p (h a t) -> p h a t", h=4, a=2)
            Tc = cst[0][:, (qk * 2) * 256:(qk * 2 + 1) * 256].rearrange("p (h t) -> p h t", h=4)
            Ts = cst[0][:, (qk * 2 + 1) * 256:(qk * 2 + 2) * 256].rearrange("p (h t) -> p h t", h=4)
            Mv = [m.rearrange("p (h t) -> p h t", h=4) for m in rM]
            Dv = dst.rearrange("p h (a t) -> p h a t", a=2)
            S.add("dve", _tt(Mv[0], Xv[:, :, 0, :], Tc, ALU.mult), reads=[xk, ("cst", 0)], writes=[("rM", 0)])
            S.add("dve", _tt(Mv[1], Xv[:, :, 1, :], Ts, ALU.mult), reads=[xk, ("cst", 0)], writes=[("rM", 1)])
            S.add("pool", _tt(Mv[2], Xv[:, :, 0, :], Ts, ALU.mult), reads=[xk, ("cst", 0)], writes=[("rM", 2)])
            S.add("pool", _tt(Mv[3], Xv[:, :, 1, :], Tc, ALU.mult), reads=[xk, ("cst", 0)], writes=[("rM", 3)])
            S.add("dve", _tt(Dv[:, :, 0, :], Mv[0], Mv[1], ALU.subtract), reads=[("rM", 0), ("rM", 1)], writes=[(dkey, 0)])
            S.add("pool", _tt(Dv[:, :, 1, :], Mv[2], Mv[3], ALU.add), reads=[("rM", 2), ("rM", 3)], writes=[(dkey, 1)])

        rotary(5640, 0, 1, ktok, "ktok")
        if full and not os.environ.get("MK_X1"):
            rotary(5128, 1, 0, qtok, "qtok")
        if c + 1 < NCH:
            loads2(c + 1)

        ob = [6]

        def next_ob():
            b = 6 + ob[0] % 2
            ob[0] += 1
            return banks[b], ("bk", b)

        kb_, kbk = next_ob()
        b0bf = kb_.bitcast(BF16)
        S.add("pe", _trs([(b0bf[:, t * 128:(t + 1) * 128], kTml[:, t, :]) for t in range(4)], ident_bf),
              reads=[("kTml", rp, h) for h in range(4)] + ["ident_bf"], writes=[kbk])
        for t in range(4):
            S.add("act", _act(kw[:, t, :], b0bf[:, t * 128:(t + 1) * 128], AF.Copy, scale=WSRC[:, t:t + 1]),
                  reads=[kbk, "wsrc"], writes=[("kw", t)])
        if full:
            qb_, qbk = next_ob()
            qbbf = qb_.bitcast(BF16)
            S.add("pe", _trs([(qbbf[:, h * 128:(h + 1) * 128], ktok[:, h, :]) for h in range(4)] +
                             [(qbbf[:, (4 + h) * 128:(5 + h) * 128], qtok[:, h, :]) for h in range(4)], ident_bf),
                  reads=[("ktok", 0), ("ktok", 1), ("qtok", 0), ("qtok", 1), "ident_bf"], writes=[qbk])
            S.add("act", _act(kTrt, qbbf[:, 0:512].rearrange("p (h t) -> p h t", h=4), AF.Copy), reads=[qbk],
                  writes=[("kTrt", rp, h) for h in range(4)])
            S.add("act", _act(qTrt, qbbf[:, 512:1024].rearrange("p (h t) -> p h t", h=4), AF.Copy), reads=[qbk],
                  writes=[("qTrt", rp, h) for h in range(4)])

        for h in range(4):
            bo, ko = next_ob()
            S.add("pe", _mm(bo[:, 0:257], kw[:, h, :], Vml[:, h, :]), reads=[("kw", h), ("Vml", rp, h // 2)], writes=[ko])
            S.add("dve", _stt(Cml_f[:, h, :], Cml_f[:, h, :], DEC[:, h:h + 1], bo[:, 0:257], ALU.mult, ALU.add),
                  reads=[("Cml_f", h), "dec", ko], writes=[("Cml_f", h)])
            S.add("pool", _cp(Cml_bfw[:, h, :], Cml_f[:, h, :]), reads=[("Cml_f", h)], writes=[("Cml_bf", wp, h)])
        for pr in range(2):
            bo, ko = next_ob()
            for hh in range(2):
                h = 2 * pr + hh
                S.add("pe", _mm(bo[:, hh * 256:(hh + 1) * 256], ktok[:, h, :], Vrt[:, h, :]), reads=[("ktok", 0), ("ktok", 1), ("Vrt", rp, pr)], writes=[ko])
            for hh in range(2):
                h = 2 * pr + hh
                S.add("dve", _stt(Crt_f[:, h, :], Crt_f[:, h, :], CD[h], bo[:, hh * 256:(hh + 1) * 256], ALU.mult, ALU.add),
                      reads=[("Crt_f", h), ko], writes=[("Crt_f", h)])
            for hh in range(2):
                h = 2 * pr + hh
                S.add("pool", _ts(Crt_bfw[:, h, :], Crt_f[:, h, :], CD[h], ALU.mult, 0.0, ALU.add),
                      reads=[("Crt_f", h)], writes=[("Crt_bf", wp, h)])
        if full:
            S.add("pe", lambda e: [e.matmul(b5[:, h * 128:(h + 1) * 128], lhsT=kTml[:, h, :], rhs=qTml[:, h, :], start=True, stop=True)
                                   for h in range(4)][-1],
                  reads=[("kTml", rp, h) for h in range(4)] + [("qTml", rp, h) for h in range(4)], writes=[k5])
            S.add("dve", _tt(PT.rearrange("p h t -> p (h t)"), b5, WT.rearrange("p h t -> p (h t)"), ALU.mult),
                  reads=[k5] + [("WT", h) for h in range(4)], writes=["PT"])
            for h in range(4):
                bo, ko = next_ob()
                S.add("pe", _mmk(bo[:, 0:257], [(PT[:, h, :], Vml[:, h, :]), (qsT[:, h, :], Cml_bf[:, h, :])]),
                      reads=["PT", ("Vml", rp, h // 2), "qsT", ("Cml_bf", rp, h)], writes=[ko])
                S.add("act", _act(hraw[:, h, :], bo[:, 0:257], AF.Copy), reads=[ko], writes=[("hraw", h)])
                S.add("dve", lambda e, h=h: e.bn_stats(out=st6[:, h, :], in_=hraw[:, h, 0:256]), reads=[("hraw", h)], writes=[("st6", h)])
                S.add("dve", lambda e, h=h: e.bn_aggr(out=mv[:, h, :], in_=st6[:, h, :]), reads=[("st6", h)], writes=[("mv", h)])
            allh = [("hraw", h) for h in range(4)]
            allmv = [("mv", h) for h in range(4)]
            S.add("dve", _ts(T2, hraw[:, :, 256], -1.0, ALU.mult), reads=allh, writes=["t2"])
            S.add("dve", _tt(DD, T2, hraw[:, :, 256], ALU.max), reads=allh + ["t2"], writes=["dd"])
            S.add("dve", _tt(DD, DD, EX8[:, 0:4], ALU.max), reads=["dd", "ex8"], writes=["dd"])
            S.add("dve", lambda e: e.reciprocal(out=RDEN, in_=DD), reads=["dd"], writes=["rden"])
            S.add("dve", _tt(T1, RDEN, RDEN, ALU.mult), reads=["rden"], writes=["t1"])
            S.add("dve", _tt(T2, T1, mv[:, :, 1], ALU.mult), reads=["t1"] + allmv, writes=["t2"])
            S.add("dve", _ts(T1, T2, EPS, ALU.add), reads=["t2"], writes=["t1"])
            S.add("pool", _tt(RSTD, T1, mhalf, ALU.pow), reads=["t1", "mhalf"], writes=["rstdh"])
            S.add("dve", _tt(SC, RDEN, RSTD, ALU.mult), reads=["rden", "rstdh"], writes=["sc"])
            S.add("dve", _stt(BI, mv[:, :, 0], -1.0, SC, ALU.mult, ALU.mult), reads=allmv + ["sc"], writes=["bi"])
            for h in range(4):
                S.add("act", _act(hraw[:, h, 0:256], hraw[:, h, 0:256], AF.Identity, bias=BI[:, h:h + 1], scale=SC[:, h:h + 1]),
                      reads=[("hraw", h), "sc", "bi"], writes=[("hraw", h)])
            S.add("dve", _stt(mixed[:, 0:1024].rearrange("p (h v) -> p h v", h=4), og.rearrange("p (h v) -> p h v", h=4), 1.0,
                              hraw[:, :, 0:256], ALU.add, ALU.mult),
                  reads=allh + [("og", rp, 0), ("og", rp, 1)], writes=[("mixed", 0)])
            S.add("pe", lambda e: [e.matmul(b5[:, h * 128:(h + 1) * 128], lhsT=kTrt[:, h, :], rhs=qTrt[:, h, :], start=True, stop=True)
                                   for h in range(4)][-1],
                  reads=[("kTrt", rp, h) for h in range(4)] + [("qTrt", rp, h) for h in range(4)], writes=[k5])
            S.add("dve", _tt(PT, b5.rearrange("p (h t) -> p h t", h=4), cmask.unsqueeze(1).broadcast_to([128, 4, 128]), ALU.mult),
                  reads=[k5, "cmask"], writes=["PT"])
            for pr in range(2):
                bo, ko = next_ob()
                for hh in range(2):
                    h = 2 * pr + hh
                    S.add("pe", _mmk(bo[:, hh * 256:(hh + 1) * 256], [(PT[:, h, :], Vrt[:, h, :]), (qTrt[:, h, :], Crt_bf[:, h, :])]),
                          reads=["PT", ("Vrt", rp, pr), ("qTrt", rp, h), ("Crt_bf", rp, h)], writes=[ko])
                S.add("act", _act(hrt[:, 2 * pr:2 * pr + 2, :], bo.rearrange("p (a b) -> p a b", a=2), AF.Copy),
                      reads=[ko], writes=[("hrt", 2 * pr), ("hrt", 2 * pr + 1)])
                for hh in range(2):
                    h = 2 * pr + hh
                    S.add("dve", lambda e, h=h: e.bn_stats(out=st6[:, h, :], in_=hrt[:, h, :]), reads=[("hrt", h)], writes=[("st6", h)])
                    S.add("dve", lambda e, h=h: e.bn_aggr(out=mv[:, h, :], in_=st6[:, h, :]), reads=[("st6", h)], writes=[("mv", h)])
            allr = [("hrt", h) for h in range(4)]
            S.add("dve", _ts(T1, mv[:, :, 1], EPS, ALU.add), reads=allmv, writes=["t1"])
            S.add("pool", _tt(RSTD, T1, mhalf, ALU.pow), reads=["t1", "mhalf"], writes=["rstdh"])
            S.add("dve", _stt(BI, mv[:, :, 0], -1.0, RSTD, ALU.mult, ALU.mult), reads=allmv + ["rstdh"], writes=["bi"])
            for h in range(4):
                S.add("act", _act(hrt[:, h, :], hrt[:, h, :], AF.Identity, bias=BI[:, h:h + 1], scale=RSTD[:, h:h + 1]),
                      reads=[("hrt", h), "rstdh", "bi"], writes=[("hrt", h)])
            S.add("dve", _tt(mixed[:, 1024:2048], hrt.rearrange("p h v -> p (h v)"), gg, ALU.mult),
                  reads=allr + [("gg", rp, 0), ("gg", rp, 1)], writes=[("mixed", 1)])
        if full:
            f = c - NCH_P
            S.add("act", _dma(mixed_d[f * 128:(f + 1) * 128, :], mixed), reads=[("mixed", 0), ("mixed", 1)],
                  writes=[("mixed_d", f)], dma="mxst")


    loads(0)
    loads2(0)
    _stop = int(os.environ.get("MK_STOP_AFTER", str(NCH)))
    for c in range(min(NCH, _stop)):
        chunk(c, c >= NCH_P)
    if _stop < 100 and os.environ.get("MK_STOP_AFTER"):
        S.fake_par = None
        S.add("sp", lambda e: e.nop(), reads=S.all_keys())
        with ExitStack() as es:
            sems = {e: es.enter_context(nc.semaphore("s_" + e)) for e in ENGS}
            dsems = {k: es.enter_context(nc.semaphore("d_" + k)) for k in S.dma_counts}
            block = es.enter_context(nc.Block())
            S.emit(block, sems, dsems)
        return nc

    S.fake_par = None
    S.barrier()
    A3 = _Arena(nc, PH_BASE, SBUF_END)
    NWD = 10
    Wout = A3.alloc("Wout", [128, 16, D], BF16)
    Wup = A3.alloc("Wup", [128, 8, DFF], BF16)
    Wgt = A3.alloc("Wgt", [128, 8, DFF], BF16)
    Wring = A3.alloc("Wring", [128, NWD, D], BF16)
    xh = A3.alloc("xh", [128, 4, D], F32)
    mtoks = [A3.alloc(f"mtok{i}", [128, 2048], BF16) for i in range(2)]
    mixedT = A3.alloc("mixedT", [128, 16, 256], BF16)
    u2Ts = [A3.alloc(f"u2T{i}", [128, 8, 256], BF16) for i in range(2)]
    u3s = [A3.alloc(f"u3_{i}", [128, D], BF16) for i in range(2)]
    NB3 = int(os.environ.get("MK_NB3", "4"))
    asb3s = [A3.alloc(f"asb3_{i}", [128, 258], F32) for i in range(NB3)]
    acc3 = [A3.alloc(f"acc3_{i}", [128, 256], F32) for i in range(NB3)]
    actT = [A3.alloc(f"actT{i}", [128, 256], BF16) for i in range(NB3)]
    halo = A3.alloc("halo", [128, NJ, 2], F32)
    convff = A3.alloc("convff", [128, NJ * 3], F32)
    g_ffn_b = A3.alloc("g_ffn_b", [128, D], F32)
    g_fin_b = A3.alloc("g_fin_b", [128, D], F32)
    gcol = A3.alloc("gcol", [128, 16], F32)
    stat2 = A3.alloc("stat2", [128, 16], F32)

    print("[kernel] sbuf A1 end", A1.off, "A3 end", A3.off, "limit", SBUF_END)
    S.add("sp", _dma(convff, convff_d), writes=["convff"], dma="cst")
    S.add("sp", _dma(g_ffn_b, gffn_d), writes=["g_ffn_b"], dma="cst")
    S.add("sp", _dma(g_fin_b, gfin_d), writes=["g_fin_b"], dma="cst")
    S.add("sp", _dma(gcol, gcol_d), writes=["gcol"], dma="cst")
    S.add("dve", _ts(gcol[:, 0:8], gcol[:, 0:8], 0.5, ALU.mult), reads=["gcol"], writes=["gcol"])
    S.add("pool", lambda e: e.memset(halo, 0.0), writes=[("halo", j) for j in range(NJ)])
    for k in range(16):
        sl = k % 4
        S.add("sp", _dma(xh[:, sl, :], wout_d[k * 128:(k + 1) * 128, :]), writes=[("xh", sl)], dma=f"xh{sl}")
        S.add("act" if k % 2 else "dve",
              (_act(Wout[:, k, :], xh[:, sl, :], AF.Copy, scale=gcol[:, k:k + 1]) if k % 2 else
               _ts(Wout[:, k, :], xh[:, sl, :], gcol[:, k:k + 1], ALU.mult)),
              reads=[("xh", sl), "gcol"], writes=[("Wout", k)])
    for k in range(8):
        S.add("pool", _dma(Wup[:, k, :], wup_d[k * 128:(k + 1) * 128, :]), writes=[("Wup", k)], dma="Wup")
    for k in range(8):
        S.add("pool", _dma(Wgt[:, k, :], wgate_d[k * 128:(k + 1) * 128, :]), writes=[("Wgt", k)], dma="Wgt")

    b_acc = [banks[0], banks[1], banks[2], banks[3]]
    ybk = None
    agrot = [0]
    NAG = int(os.environ.get("MK_NAG", "3"))
    b_y = [banks[4 + NAG], banks[7]] if NAG < 3 else [banks[7], banks[7]]
    ybk = [4 + NAG, 7] if NAG < 3 else [7, 7]
    wdc = [0]
    strot = [0]
    mtc = [0]

    def rms_small(src, skey):
        i = strot[0] % 4
        strot[0] += 1
        return stat2[:, 4 * i:4 * i + 1], stat2[:, 4 * i + 1:4 * i + 2], stat2[:, 4 * i + 2:4 * i + 3], i

    def f3_pre(f0, ntt, is_halo, bp):
        T = ntt * 128
        u2T = u2Ts[bp]
        xsl = [bp * 2 + tt for tt in range(ntt)]
        for tt in range(ntt):
            f = f0 + tt
            p0 = (NCH_P + f) * 128
            S.add("sp", _dma(xh[:, xsl[tt], :], xs_d[p0:p0 + 128, :]), writes=[("xh", xsl[tt])], dma=f"xh{xsl[tt]}")
        for tt in range(ntt):
            f = f0 + tt
            mi = mtc[0] % 2
            mtc[0] += 1
            mtok = mtoks[mi]
            S.add("sp", _dma(mtok, mixed_d[f * 128:(f + 1) * 128, :]), reads=[("mixed_d", f)], writes=[("mtok", mi)], dma=f"mtok{mi}")
            for half in range(2):
                yb = b_y[half].bitcast(BF16)
                S.add("pe", _trs([(yb[:, kk * 128:(kk + 1) * 128], mtok[:, (half * 8 + kk) * 128:(half * 8 + kk + 1) * 128])
                                  for kk in range(8)], ident_bf), reads=[("mtok", mi), "ident_bf"], writes=[("bk", ybk[half])])
                S.add("act" if half else "dve",
                      (_act(mixedT[:, half * 8:half * 8 + 8, tt * 128:(tt + 1) * 128], yb.rearrange("p (k t) -> p k t", k=8), AF.Copy)
                       if half else _cp(mixedT[:, half * 8:half * 8 + 8, tt * 128:(tt + 1) * 128], yb.rearrange("p (k t) -> p k t", k=8))),
                      reads=[("bk", ybk[half])], writes=[("mixedT", tt, half)])
        for tt in range(ntt):
            xv = xh[:, xsl[tt], :]
            for half in range(2):
                S.add("pe", _mmk(b_y[half], [(mixedT[:, k, tt * 128:(tt + 1) * 128], Wout[:, k, half * 512:(half + 1) * 512])
                                             for k in range(16)]),
                      reads=[("mixedT", tt, 0), ("mixedT", tt, 1)] + [("Wout", k) for k in range(16)], writes=[("bk", ybk[half])])
                S.add("dve", _tt(xv[:, half * 512:(half + 1) * 512], b_y[half], xv[:, half * 512:(half + 1) * 512], ALU.add),
                      reads=[("bk", ybk[half]), ("xh", xsl[tt])], writes=[("xh", xsl[tt])])
        for tt in range(ntt):
            xv = xh[:, xsl[tt], :]
            xk = ("xh", xsl[tt])
            ss, ms, rs, si = rms_small(None, None)
            u3 = u3s[tt % 2]
            uk = ("u3", tt % 2)
            S.add("act", _act(u3, xv, AF.Square, accum=ss), reads=[xk], writes=[uk, ("ss", si)])
            S.add("dve", _ts(ms, ss, 1.0 / D, ALU.mult, EPS, ALU.add), reads=[("ss", si)], writes=[("ms", si)])
            S.add("pool", _tt(rs, ms, mhalf[:, 0:1], ALU.pow), reads=[("ms", si), "mhalf"], writes=[("rs", si)])
            S.add("dve", _stt(u3, xv, rs, g_ffn_b, ALU.mult, ALU.mult), reads=[xk, ("rs", si), "g_ffn_b"], writes=[uk])
            yb = b_y[tt % 2].bitcast(BF16)
            S.add("pe", _trs([(yb[:, k * 128:(k + 1) * 128], u3[:, k * 128:(k + 1) * 128]) for k in range(8)], ident_bf),
                  reads=[uk, "ident_bf"], writes=[("bk", ybk[tt % 2])])
            S.add("act", _act(u2T[:, :, tt * 128:(tt + 1) * 128], yb.rearrange("p (k t) -> p k t", k=8), AF.Copy),
                  reads=[("bk", ybk[tt % 2])], writes=[("u2T", bp, tt)])

    def f3_main(f0, ntt, is_halo, bp):
        T = ntt * 128
        u2T = u2Ts[bp]
        xsl = [bp * 2 + tt for tt in range(ntt)]
        u2k = [("u2T", bp, tt) for tt in range(ntt)]
        for j in range(NJ):
            par = agrot[0] % NB3
            agb = 4 + (agrot[0] % NAG)
            agrot[0] += 1
            aT = banks[agb][:, 0:T]
            gT = banks[agb][:, 256:256 + T]
            if not is_halo:
                ws = wdc[0] % NWD
                wdc[0] += 1
                S.add("pool", _dma(Wring[:, ws, :], wdn_d[j * 128:(j + 1) * 128, :]), writes=[("Wring", ws)], dma=f"Wd{ws}")
            S.add("pe", _mmk(aT, [(Wup[:, k, j * 128:(j + 1) * 128], u2T[:, k, 0:T]) for k in range(8)]),
                  reads=u2k + [("Wup", k) for k in range(8)], writes=[("bk", agb)])
            if not is_halo:
                S.add("pe", _mmk(gT, [(Wgt[:, k, j * 128:(j + 1) * 128], u2T[:, k, 0:T]) for k in range(8)]),
                      reads=u2k + [("Wgt", k) for k in range(8)], writes=[("bk", agb)])
            asb3 = asb3s[par]
            S.add("pool", _cp(asb3[:, 0:2], halo[:, j, :]), reads=[("halo", j)], writes=[("asb3h", par)])
            S.add("act", _act(asb3[:, 2:2 + T], aT, AF.Copy), reads=[("bk", agb)], writes=[("asb3", par)])
            S.add("pool", _cp(halo[:, j, :], asb3[:, T:T + 2]), reads=[("asb3", par)], writes=[("halo", j)])
            if is_halo:
                continue
            acc = acc3[par][:, 0:T]
            ak = ("acc3", par)
            S.add("dve", _ts(acc, asb3[:, 2:2 + T], convff[:, j * 3 + 2:j * 3 + 3], ALU.mult), reads=[("asb3", par), "convff"], writes=[ak])
            S.add("dve", _stt(acc, asb3[:, 1:1 + T], convff[:, j * 3 + 1:j * 3 + 2], acc, ALU.mult, ALU.add),
                  reads=[("asb3", par), ("asb3h", par), "convff", ak], writes=[ak])
            S.add("dve", _stt(acc, asb3[:, 0:T], convff[:, j * 3:j * 3 + 1], acc, ALU.mult, ALU.add),
                  reads=[("asb3", par), ("asb3h", par), "convff", ak], writes=[ak])
            S.add("act", _act(acc, acc, AF.Silu), reads=[ak], writes=[ak])
            at = actT[par][:, 0:T]
            S.add("dve", _tt(at, acc, gT, ALU.mult), reads=[ak, ("bk", agb)], writes=[("actT", par)])
            for tt in range(ntt):
                for half in range(2):
                    S.add("pe", _mm(b_acc[tt * 2 + half], at[:, tt * 128:(tt + 1) * 128], Wring[:, ws, half * 512:(half + 1) * 512],
                                    start=(j == 0), stop=(j == NJ - 1)),
                          reads=[("actT", par), ("Wring", ws)], writes=[("bk", tt * 2 + half)])
        if is_halo:
            return
        for tt in range(ntt):
            f = f0 + tt
            xv = xh[:, xsl[tt], :]
            xk = ("xh", xsl[tt])
            for half in range(2):
                S.add("dve", _tt(xv[:, half * 512:(half + 1) * 512], b_acc[tt * 2 + half], xv[:, half * 512:(half + 1) * 512], ALU.add),
                      reads=[("bk", tt * 2 + half), xk], writes=[xk])
            ss, ms, rs, si = rms_small(None, None)
            u3 = u3s[tt % 2]
            uk = ("u3", tt % 2)
            S.add("act", _act(u3, xv, AF.Square, accum=ss), reads=[xk], writes=[uk, ("ss", si)])
            S.add("dve", _ts(ms, ss, 1.0 / D, ALU.mult, EPS, ALU.add), reads=[("ss", si)], writes=[("ms", si)])
            S.add("pool", _tt(rs, ms, mhalf[:, 0:1], ALU.pow), reads=[("ms", si), "mhalf"], writes=[("rs", si)])
            S.add("dve", _stt(xv, xv, rs, g_fin_b, ALU.mult, ALU.mult), reads=[xk, ("rs", si), "g_fin_b"], writes=[xk])
            S.add("act", _dma(out_d[(f - 1) * 128:f * 128, :], xv), reads=[xk], writes=[("out", f)], dma=f"ost{xsl[tt]}")

    blocks = [(0, 1, True, 1)] + [(1 + 2 * b, 2, False, b % 2) for b in range(8)]
    f3_pre(*blocks[0])
    for bi, blk in enumerate(blocks):
        if bi + 1 < len(blocks):
            f3_pre(*blocks[bi + 1])
        f3_main(*blk)

    fin = S.add("sp", lambda e: e.nop(), reads=[("out", f) for f in range(1, NCH_F)])

    with ExitStack() as es:
        sems = {e: es.enter_context(nc.semaphore("s_" + e)) for e in ENGS}
        dsems = {k: es.enter_context(nc.semaphore("d_" + k)) for k in S.dma_counts}
        block = es.enter_context(nc.Block())
        S.emit(block, sems, dsems, reorder=bool(int(os.environ.get("MK_REORDER", "1"))))
        print("[kernel] est_ns", getattr(S, "est_ns", None), "ops", len(S.ops))
    return nc


def _host_consts():
    idx = np.arange(128, dtype=np.float64)
    dqk = np.zeros((128, 12), np.float32)
    for h in range(4):
        dqk[:, h] = np.exp(LOG_GAMMA[h] * (idx + 1.0))
        dqk[:, 4 + h] = S_ML * np.exp(-LOG_GAMMA[h] * (idx + 1.0))
        dqk[:, 8 + h] = S_ML * np.exp(-LOG_GAMMA[h] * (idx + 1.0)) * np.exp(LOG_GAMMA[h] * CH)
    jj, ii = np.meshgrid(np.arange(128), np.arange(128), indexing="ij")
    cm = (jj <= ii).astype(np.float32)
    mneg = np.where(jj <= ii, 0.0, NEG).astype(np.float32)
    i4 = np.concatenate([np.eye(4), -np.eye(4)], axis=1).astype(np.float32)
    return dict(dqk=dqk, cmask=cm, maskneg4=mneg,
                ident_bf=np.eye(128).astype(ml_dtypes.bfloat16), ident_f=np.eye(128, dtype=np.float32), i4=i4)


def _rope_tables(n_null):
    p = np.arange(NPOS, dtype=np.float64)
    pos = np.where(p >= n_null, 48.0 + (p - n_null), 0.0)
    inv = 10000.0 ** (-np.arange(0, 128, 2, dtype=np.float64) / 128.0)
    ang = pos[:, None] * inv[None, :]
    cosr = np.cos(ang).reshape(NCH, 128, 1, 64)
    sinr = np.sin(ang).reshape(NCH, 128, 1, 64)
    idx = np.arange(128, dtype=np.float64)
    dq = np.stack([np.exp(LOG_GAMMA[h] * (idx + 1.0)) for h in range(4)], axis=1)[None, :, :, None]
    dk = np.stack([S_ML * np.exp(-LOG_GAMMA[h] * (idx + 1.0)) for h in range(4)], axis=1)[None, :, :, None]
    tab = np.stack([cosr * dq, sinr * dq, cosr * dk, sinr * dk], axis=2)
    tab = np.ascontiguousarray(tab.reshape(NCH, 128, 1024)).astype(np.float32)
    valid = (p >= n_null).astype(np.float32).reshape(NCH, 128)
    vm = np.stack([valid, (valid - 1.0) * 1e30], axis=1)
    vm = np.ascontiguousarray(np.broadcast_to(vm[:, None], (NCH, 4, 2, 128))).astype(np.float32)
    return tab, vm


def _prep_inputs(inputs):
    f = lambda k: np.asarray(inputs[k], dtype=np.float32)
    x = f("x")
    meta = f("meta_tokens")
    w_in = f("w_in")[0]
    sizes = [512, 512, 1024, 1024, 4, 4, 512, 512, 1024, 1024]
    offs = np.cumsum(sizes)[:-1]
    ml_q, ml_k, ml_v, ml_o, ml_i, ml_f, rt_q, rt_k, rt_v, rt_g = np.split(w_in, offs, axis=1)

    def swap(cols):
        return cols.reshape(D, 4, 2, 64)[:, :, ::-1, :].reshape(D, 512)

    w_in_r = np.ascontiguousarray(np.concatenate(
        [ml_q, ml_k, ml_i, ml_f, ml_v, ml_o, rt_v, rt_g, rt_q, rt_k], axis=1))
    assert w_in_r.shape == (D, WCOLS)
    convml = f("ml_conv_w")[0]
    convw_ml = np.ascontiguousarray(convml.reshape(4, 8, 128).transpose(2, 1, 0).reshape(128, 32))
    convff = f("ffn_conv_w")[0]
    convw_ffn = np.ascontiguousarray(convff.reshape(3, NJ, 128).transpose(2, 1, 0).reshape(128, NJ * 3))
    b_if = np.ascontiguousarray(np.stack([f("ml_b_i")[0], f("ml_b_f")[0]], axis=1))
    gcat = np.concatenate([f("ml_norm_g")[0], f("rt_norm_g")[0]])
    gcol = np.ascontiguousarray(gcat.reshape(16, 128).T)
    bc = lambda v: np.ascontiguousarray(np.broadcast_to(v[None, :], (128, D)))
    common = dict(w_in_r=w_in_r, w_out=f("w_out")[0], w_up=f("w_up")[0], w_gate=f("w_gate")[0], w_down=f("w_down")[0],
                  convw_ml=convw_ml, convw_ffn=convw_ffn, b_if=b_if, gcol=gcol,
                  g_mix_b=bc(f("norm_mix_g")[0]), g_ffn_b=bc(f("norm_ffn_g")[0]), g_fin_b=bc(f("norm_final_g")))
    common.update(_host_consts())
    tabs = [_rope_tables(2160), _rope_tables(112)]
    in_maps = []
    for core in range(8):
        b, t = core // 2, core % 2
        xs = np.zeros((NPOS, D), np.float32)
        if t == 0:
            xs[2160:2176] = meta
            xs[2176:] = x[b, 0:2048]
        else:
            xs[112:128] = meta
            xs[128:] = x[b]
        m = dict(common)
        m["xs"] = xs
        m["cs_tab"], m["vm_tab"] = tabs[t]
        in_maps.append(m)
    return in_maps


_NC_CACHE = {}


def kernel(**inputs):
    in_maps = _prep_inputs(inputs)
    if "nc" not in _NC_CACHE:
        _NC_CACHE["nc"] = build_program()
    nc = _NC_CACHE["nc"]
    res = run_bass_kernel_spmd(nc, in_maps, core_ids=list(range(8)))
    out = np.zeros((4, 4096, D), np.float32)
    for core in range(8):
        b, t = core // 2, core % 2
        out[b, t * 2048:(t + 1) * 2048] = res.results[core]["out"]
    if DEBUG:
        kernel.debug = [res.results[c]["mixed_d"] for c in range(8)]
    return out
```

```python
import os
from contextlib import ExitStack

import numpy as np
import ml_dtypes

import concourse.bass as bass
import concourse.mybir as mybir
from concourse.bass_utils import run_bass_kernel_spmd

F32 = mybir.dt.float32
BF16 = mybir.dt.bfloat16
ALU = mybir.AluOpType
AF = mybir.ActivationFunctionType

NCH_P = 16
NCH_F = 17
NCH = NCH_P + NCH_F
CH = 128
NPOS = NCH * CH
D = 1024
DFF = 2816
NJ = DFF // 128
WCOLS = 6152
EPS = 1e-6
NEG = -1e30
GATE_CAP = 15.0
SBUF_BASE = 16640
SBUF_END = 229376
S_ML = 128.0 ** -0.5
LOG_GAMMA = [float(np.log1p(-2.0 ** (-(5.0 + h)))) for h in range(4)]
CD = [float(np.exp(lg * CH)) for lg in LOG_GAMMA]

ENGS = ("pe", "act", "dve", "pool", "sp")
DEBUG = bool(int(os.environ.get("MK_DEBUG", "0")))


class _Op:
    __slots__ = ("eng", "fn", "deps", "dma_sem", "dma_cnt", "sig", "idx", "need_sig", "wk")

    def __init__(self, eng, fn):
        self.eng = eng
        self.fn = fn
        self.deps = []
        self.dma_sem = None
        self.dma_cnt = 0
        self.sig = 0
        self.need_sig = False


class Sched:
    def __init__(self):
        self.ops = []
        self.last_w = {}
        self.readers = {}
        self.dma_counts = {}
        self.fence = None

    fake_par = None
    FAKE_KEEP = ("bk", "W", "Cml_f", "Cml_bf", "Crt_f", "Crt_bf", "asb", "mixed_d", "cst", "vmt", "uT")

    def _fk(self, keys):
        if self.fake_par is None:
            return keys
        out = []
        for k in keys:
            base = k[0] if isinstance(k, tuple) else k
            if base == "bk" and os.environ.get("MK_FAKEBK"):
                out.append((k, self.fake_par))
                continue
            if base in self.FAKE_KEEP or base in ("mst", "ident_bf", "mhalf", "i4", "ones4", "zeros4", "convml", "dqk", "cmask",
                                                   "maskneg4", "ident_f", "g_mix_b", "bsc0", "bsc1", "bif"):
                out.append(k)
            else:
                out.append((k, self.fake_par))
        return out

    def add(self, eng, fn, reads=(), writes=(), dma=None):
        reads = self._fk(reads)
        writes = self._fk(writes)
        op = _Op(eng, fn)
        op.wk = list(writes)[:2]
        op.idx = len(self.ops)
        deps = {}

        def dep(o, raw):
            val = self.dma_counts[o.dma_sem] if o.dma_sem is not None else 0
            if o.idx in deps:
                if raw and not deps[o.idx][1]:
                    deps[o.idx] = (o, True, val)
            else:
                deps[o.idx] = (o, raw, val)

        for r in reads:
            w = self.last_w.get(r)
            if w is not None:
                dep(w, True)
        for k in writes:
            w = self.last_w.get(k)
            if w is not None:
                dep(w, False)
            for rd in self.readers.get(k, ()):
                dep(rd, False)
        if self.fence is not None:
            dep(self.fence[eng], False)
        op.deps = list(deps.values())
        for r in reads:
            self.readers.setdefault(r, []).append(op)
        for k in writes:
            self.last_w[k] = op
            self.readers[k] = []
        if dma is not None:
            op.dma_sem = dma
            self.dma_counts[dma] = self.dma_counts.get(dma, 0) + 16
            op.dma_cnt = self.dma_counts[dma]
        self.ops.append(op)
        return op

    def all_keys(self):
        return list(set(self.last_w.keys()) | set(self.readers.keys()))

    def barrier(self, skip=()):
        keys = [k for k in self.all_keys() if not (isinstance(k, tuple) and k[0] in skip)]
        fence = {}
        for e in ENGS:
            fence[e] = self.add(e, lambda eng: eng.nop(), writes=keys)
        self.fence = fence

    @staticmethod
    def _skip(d, op, raw):
        if d.dma_sem is not None or op.dma_sem is not None:
            return False
        if d.eng != op.eng:
            return False
        return d.eng == "pe"

    def schedule(self, window=int(os.environ.get("MK_WIN", "200")), lat=float(os.environ.get("MK_LAT", "800"))):
        ops = self.ops
        n = len(ops)
        ndeps = [0] * n
        users = [[] for _ in range(n)]
        for op in ops:
            ndeps[op.idx] = len(op.deps)
            for (d, raw, val) in op.deps:
                users[d.idx].append(op.idx)
        fin = [0.0] * n
        ready_t = [0.0] * n
        prio_cp = os.environ.get("MK_PRIO", "cp") == "cp"
        tail = [0.0] * n
        if prio_cp:
            for op in reversed(ops):
                c = getattr(op.fn, "cost", 150.0)
                t = 0.0
                for u in users[op.idx]:
                    if tail[u] + lat > t:
                        t = tail[u] + lat
                tail[op.idx] = t + c
        pend = {e: [op.idx for op in ops if op.eng == e] for e in ENGS}
        pos = {e: 0 for e in ENGS}
        done = [False] * n
        efree = {e: 0.0 for e in ENGS}
        dma_free = [0.0]
        order = {e: [] for e in ENGS}
        remaining = n
        while remaining:
            best = None
            for e in ENGS:
                lst = pend[e]
                i = pos[e]
                while i < len(lst) and done[lst[i]]:
                    i += 1
                pos[e] = i
                if i >= len(lst):
                    continue
                w = 1 if e == "sp" else window
                seen = 0
                j = i
                cand = None
                dma_blocked = False
                while j < len(lst) and seen < w:
                    k = lst[j]
                    j += 1
                    if done[k]:
                        continue
                    seen += 1
                    isdma = ops[k].dma_sem is not None
                    if isdma and dma_blocked:
                        continue
                    if isdma:
                        dma_blocked = True
                    if ndeps[k] > 0:
                        continue
                    st = max(ready_t[k], efree[e])
                    if prio_cp:
                        key = (st, -tail[k], k)
                        if cand is None or key < cand:
                            cand = key
                    else:
                        key = (st, 0.0, k)
                        if cand is None or key < cand:
                            cand = key
                            if ready_t[k] <= efree[e]:
                                break
                if cand is not None and (best is None or cand < best[0]):
                    best = (cand, e)
            assert best is not None, "scheduler deadlock"
            (st, _pr, k), e = best
            op = ops[k]
            c = getattr(op.fn, "cost", 150.0)
            if op.dma_sem is not None:
                efree[e] = st + 60.0
                x0 = max(st, dma_free[0])
                dma_free[0] = x0 + getattr(op.fn, "xfer", 0.0)
                fin[k] = dma_free[0] + 2000.0
            else:
                efree[e] = st + c
                fin[k] = st + c
            done[k] = True
            remaining -= 1
            order[e].append(op)
            for u in users[k]:
                ndeps[u] -= 1
                t = fin[k] + lat
                if t > ready_t[u]:
                    ready_t[u] = t
        self.est_ns = max(fin) if n else 0.0
        if os.environ.get("MK_CRIT"):
            dist = [0.0] * n
            pred = [-1] * n
            for op in ops:
                c = getattr(op.fn, "cost", 150.0)
                best_t, best_p = 0.0, -1
                for (d, raw, val) in op.deps:
                    t = dist[d.idx] + lat
                    if t > best_t:
                        best_t, best_p = t, d.idx
                dist[op.idx] = best_t + c
                pred[op.idx] = best_p
            lim = self.fence["pe"].idx if self.fence else n
            k = max(range(lim), key=lambda i: dist[i])
            print("[crit] dependency-only critical path before fence:", round(dist[k] / 1000), "us")
            path = []
            while k >= 0:
                path.append(k)
                k = pred[k]
            path.reverse()
            import collections
            cnt = collections.Counter(ops[i].eng for i in path)
            print("[crit] path len", len(path), dict(cnt))
            self.crit_path = path
            mid = int(len(path) * float(os.environ.get("MK_CRITPOS", "0.5")))
            for i in path[mid:mid + 70]:
                print("[crit]  ", ops[i].eng, ops[i].wk, round(getattr(ops[i].fn, "cost", 150.0)))
        if os.environ.get("MK_SCHED_DBG"):
            B = float(os.environ.get("MK_BUCKET", "100000"))
            nb = int(self.est_ns // B) + 1
            busy = {e: [0.0] * nb for e in ENGS}
            for op in ops:
                c = getattr(op.fn, "cost", 150.0)
                if op.dma_sem is not None:
                    continue
                b = int((fin[op.idx] - c) // B)
                busy[op.eng][b] += c
            for b in range(nb):
                print(f"[sched] {b*B/1000:8.0f}us " + " ".join(f"{e}:{busy[e][b]/B*100:5.1f}%" for e in ENGS if e != "sp"))
            if self.fence:
                print("[sched] fence done at", {e: round(fin[o.idx] / 1000) for e, o in self.fence.items()})
        return order

    def eval_order(self, order, lat):
        ops = self.ops
        fin = {}
        pos = {e: 0 for e in ENGS}
        efree = {e: 0.0 for e in ENGS}
        dma_free = [0.0]
        remaining = sum(len(v) for v in order.values())
        while remaining:
            progressed = False
            for e in ENGS:
                while pos[e] < len(order[e]):
                    op = order[e][pos[e]]
                    if any(d.idx not in fin for (d, raw, val) in op.deps):
                        break
                    rt = max([fin[d.idx] + lat for (d, raw, val) in op.deps] + [0.0])
                    st = max(rt, efree[e])
                    c = getattr(op.fn, "cost", 150.0)
                    if op.dma_sem is not None:
                        efree[e] = st + 60.0
                        x0 = max(st, dma_free[0])
                        dma_free[0] = x0 + getattr(op.fn, "xfer", 0.0)
                        fin[op.idx] = dma_free[0] + 2000.0
                    else:
                        efree[e] = st + c
                        fin[op.idx] = st + c
                    pos[e] += 1
                    remaining -= 1
                    progressed = True
            assert progressed, "order deadlock"
        return max(fin.values())

    def emit(self, block, sems, dma_sems, reorder=True):
        for op in self.ops:
            for (d, raw, val) in op.deps:
                if d.dma_sem is None and not self._skip(d, op, raw):
                    d.need_sig = True
        if reorder:
            per_eng = self.schedule()
            if os.environ.get("MK_EVAL_LAT"):
                print("[kernel] eval fixed order @lat", os.environ["MK_EVAL_LAT"], self.eval_order(per_eng, float(os.environ["MK_EVAL_LAT"])))
        else:
            per_eng = {e: [] for e in ENGS}
            for op in self.ops:
                per_eng[op.eng].append(op)
        cnt = {e: 0 for e in ENGS}
        for e in ENGS:
            for op in per_eng[e]:
                if op.dma_sem is None and op.need_sig:
                    cnt[e] += 1
                    op.sig = cnt[e]
        handles = {"pe": "tensor", "act": "scalar", "dve": "vector", "pool": "gpsimd", "sp": "sync"}

        def body(eng_name):
            def _f(eng):
                waited = {}
                for op in per_eng[eng_name]:
                    for (d, raw, val) in op.deps:
                        if d.dma_sem is not None:
                            key = ("dma", d.dma_sem)
                            sem = dma_sems[d.dma_sem]
                        else:
                            if self._skip(d, op, raw):
                                continue
                            key = d.eng
                            val = d.sig
                            sem = sems[d.eng]
                        if waited.get(key, 0) >= val:
                            continue
                        waited[key] = val
                        eng.wait_ge(sem, val)
                    ins = op.fn(eng)
                    if op.dma_sem is not None:
                        ins.then_inc(dma_sems[op.dma_sem], 16)
                    elif op.need_sig:
                        ins.then_inc(sems[eng_name], 1)
            return _f

        for e in ENGS:
            if per_eng[e]:
                getattr(block, handles[e])(body(e))


def _mmcost(lhsT, rhs):
    n = rhs.free_size()
    c = max(lhsT.free_size() / 1.2, n / 2.37, 30.0)
    if rhs.dtype == F32:
        c *= 4.0
    return c


def _fsz(ap):
    return ap.free_size()


def _mm(out, lhsT, rhs, start=True, stop=True):
    f = lambda e: e.matmul(out, lhsT=lhsT, rhs=rhs, start=start, stop=stop)
    f.cost = _mmcost(lhsT, rhs)
    return f


def _mmk(out, pairs):
    def f(e):
        n = len(pairs)
        ins = None
        for i, (l, r) in enumerate(pairs):
            ins = e.matmul(out, lhsT=l, rhs=r, start=(i == 0), stop=(i == n - 1))
        return ins
    f.cost = sum(_mmcost(l, r) for (l, r) in pairs)
    return f


def _trs(items, ident):
    def f(e):
        ins = None
        for (o, i) in items:
            ins = e.transpose(out=o, in_=i, identity=ident)
        return ins
    f.cost = 120.0 * len(items)
    return f


def _act(out, in_, func, bias=None, scale=None, accum=None):
    def f(e):
        kw = {}
        if bias is not None:
            kw["bias"] = bias
        if scale is not None:
            kw["scale"] = scale
        if accum is not None:
            kw["accum_out"] = accum
        return e.activation(out=out, in_=in_, func=func, **kw)
    f.cost = (224.0 + _fsz(out)) / 1.2
    return f


def _ts(out, in0, s1, op0, s2=None, op1=None):
    def f(e):
        if op1 is None:
            return e.tensor_scalar(out=out, in0=in0, scalar1=s1, scalar2=None, op0=op0)
        return e.tensor_scalar(out=out, in0=in0, scalar1=s1, scalar2=s2, op0=op0, op1=op1)
    f.cost = (100.0 + _fsz(out)) / 0.96
    return f


def _tt(out, in0, in1, op):
    f = lambda e: e.tensor_tensor(out=out, in0=in0, in1=in1, op=op)
    f.cost = (100.0 + _fsz(out)) / 0.96
    return f


def _stt(out, in0, scalar, in1, op0, op1):
    f = lambda e: e.scalar_tensor_tensor(out=out, in0=in0, scalar=scalar, in1=in1, op0=op0, op1=op1)
    f.cost = (100.0 + _fsz(out)) / 0.96
    return f


def _cp(out, in_):
    f = lambda e: e.tensor_copy(out=out, in_=in_)
    f.cost = (100.0 + _fsz(out)) / 0.96
    return f


def _dma(out, in_):
    f = lambda e: e.dma_start(out=out, in_=in_)
    nbytes = out.partition_size() * _fsz(out) * (4 if in_.dtype == F32 else 2)
    f.cost = 2000.0 + nbytes / 330.0
    f.xfer = nbytes / 330.0
    return f


def _scan(out, d0, d1, init, op0, op1):
    f = lambda e: e.tensor_tensor_scan(out=out, data0=d0, data1=d1, initial=init, op0=op0, op1=op1)
    f.cost = (100.0 + 2 * _fsz(out)) / 0.96
    return f


class _Arena:
    def __init__(self, nc, base, end):
        self.nc = nc
        self.off = base
        self.end = end

    def alloc(self, name, shape, dt):
        size = 1
        for s in shape[1:]:
            size *= s
        size *= 2 if dt == BF16 else 4
        off = (self.off + 31) // 32 * 32
        assert off + size <= self.end, f"SBUF overflow at {name}: {off + size} > {self.end}"
        t = self.nc.alloc_sbuf_tensor_at(name, list(shape), dt, offset=off)
        self.off = off + size
        return t.ap()


WGROUPS = [(1024, 2056), (512, 1024), (5640, 6152), (3080, 4104), (0, 512), (5128, 5640), (2056, 3080), (4104, 5128)]


def _wgroup_of(col):
    for g, (a, b) in enumerate(WGROUPS):
        if a <= col < b:
            return g
    raise ValueError(col)


def build_program():
    nc = bass.Bass("TRN2", target_bir_lowering=False)

    def din(name, shape, dt=F32):
        return nc.dram_tensor(name, list(shape), dt, kind="ExternalInput").ap()

    xs_d = din("xs", [NPOS, D])
    win_d = din("w_in_r", [D, WCOLS])
    wout_d = din("w_out", [2048, D])
    wup_d = din("w_up", [D, DFF])
    wgate_d = din("w_gate", [D, DFF])
    wdn_d = din("w_down", [DFF, D])
    cs_d = din("cs_tab", [NCH, 128, 1024])
    vm_d = din("vm_tab", [NCH, 4, 2, 128])
    dqk_d = din("dqk", [128, 12])
    cmask_d = din("cmask", [128, 128])
    mneg_d = din("maskneg4", [128, 128])
    identb_d = din("ident_bf", [128, 128], BF16)
    identf_d = din("ident_f", [128, 128])
    i4_d = din("i4", [4, 8])
    convml_d = din("convw_ml", [128, 32])
    convff_d = din("convw_ffn", [128, NJ * 3])
    bif_d = din("b_if", [4, 2])
    gcol_d = din("gcol", [128, 16])
    gmix_d = din("g_mix_b", [128, D])
    gffn_d = din("g_ffn_b", [128, D])
    gfin_d = din("g_fin_b", [128, D])
    out_d = nc.dram_tensor("out", [2048, D], F32, kind="ExternalOutput").ap()
    mixed_d = nc.dram_tensor("mixed_d", [NCH_F * CH, 2048], BF16,
                             kind="ExternalOutput" if DEBUG else "Internal").ap()

    S = Sched()
    banks = [nc.alloc_psum_tensor(f"bank{i}", [128, 512], F32).ap() for i in range(8)]

    per = _Arena(nc, SBUF_BASE, SBUF_END)
    ident_bf = per.alloc("ident_bf", [128, 128], BF16)
    mhalf = per.alloc("mhalf", [128, 4], F32)
    stat = per.alloc("stat", [128, 8], F32)
    PH_BASE = per.off

    S.add("sp", _dma(ident_bf, identb_d), writes=["ident_bf"], dma="cst")
    S.add("pool", lambda e: e.memset(mhalf, -0.5), writes=["mhalf"])

    A1 = _Arena(nc, PH_BASE, SBUF_END)
    W = A1.alloc("W", [128, 8, WCOLS], BF16)
    ident_f = A1.alloc("ident_f", [128, 128], F32)
    cmask = A1.alloc("cmask", [128, 128], F32)
    maskneg4 = A1.alloc("maskneg4", [128, 128], F32)
    dqk = A1.alloc("dqk", [128, 12], F32)
    g_mix_b = A1.alloc("g_mix_b", [128, D], F32)
    convml = A1.alloc("convml", [128, 32], F32)
    i4 = A1.alloc("i4", [4, 8], F32)
    bif = A1.alloc("bif", [4, 2], F32)
    bsc = A1.alloc("bsc", [4, 2], F32)
    ones4 = A1.alloc("ones4", [4, 128], F32)
    zeros4 = A1.alloc("zeros4", [4, 128], F32)
    xin = A1.alloc("xin", [128, D], F32)
    u = A1.alloc("u", [128, D], BF16)
    uT = [A1.alloc(f"uT{i}", [128, 8, 128], BF16) for i in range(2)]
    asb = A1.alloc("asb", [128, 8, 131], F32)
    cacc = [A1.alloc(f"cacc{i}", [128, 128], F32) for i in range(2)]
    qTmls = [A1.alloc(f"qTml{i}", [128, 4, 128], BF16) for i in range(2)]
    kTmls = [A1.alloc(f"kTml{i}", [128, 4, 128], BF16) for i in range(2)]
    rX = [A1.alloc("rX0", [128, 512], F32)] * 2
    rM = [A1.alloc(f"rM{i}", [128, 256], F32) for i in range(4)]
    qtok = A1.alloc("qtok", [128, 4, 128], BF16)
    ktok = A1.alloc("ktok", [128, 4, 128], BF16)
    cst = [A1.alloc("cst0", [128, 1024], F32)] * 2
    vmt = [A1.alloc("vmt0", [4, 2, 128], F32)] * 2
    qTrts = [A1.alloc(f"qTrt{i}", [128, 4, 128], BF16) for i in range(2)]
    kTrts = [A1.alloc(f"kTrt{i}", [128, 4, 128], BF16) for i in range(2)]
    kw = A1.alloc("kw", [128, 4, 128], BF16)
    Vmls = [A1.alloc(f"Vml{i}", [128, 4, 257], BF16) for i in range(2)]
    Vrts = [A1.alloc(f"Vrt{i}", [128, 4, 256], BF16) for i in range(2)]
    ogs = [A1.alloc(f"og{i}", [128, D], F32) for i in range(2)]
    ggs = [A1.alloc(f"gg{i}", [128, D], F32) for i in range(2)]
    mixed = A1.alloc("mixed", [128, 2048], BF16)
    Cml_f = A1.alloc("Cml_f", [128, 4, 257], F32)
    Cml_bfs = [A1.alloc(f"Cml_bf{i}", [128, 4, 257], BF16) for i in range(2)]
    Crt_f = A1.alloc("Crt_f", [128, 4, 256], F32)
    Crt_bfs = [A1.alloc(f"Crt_bf{i}", [128, 4, 256], BF16) for i in range(2)]
    WT = A1.alloc("WT", [128, 4, 128], F32)
    PT = A1.alloc("PT", [128, 4, 128], BF16)
    Wint = A1.alloc("Wint", [128, 512], F32)
    qsT = A1.alloc("qsT", [128, 4, 128], BF16)
    hrt = A1.alloc("hrt", [128, 4, 256], F32)
    hraw = A1.alloc("hraw", [128, 4, 257], F32)
    rows = {n: A1.alloc("row_" + n, [4, 128], F32) for n in
            ("li0", "li1", "li", "ef", "sp", "nbcum", "B", "M", "R2", "R3")}
    rhs_bd = A1.alloc("rhs_bd", [4, 4, 128], F32)
    smr = A1.alloc("smr", [4, 16], F32)
    sm = A1.alloc("sm", [128, 64], F32)
    smps = A1.alloc("smps", [128, 20], F32)
    st6 = A1.alloc("st6", [128, 4, 6], F32)
    mv = A1.alloc("mv", [128, 4, 2], F32)

    EX8 = sm[:, 0:8]
    BT = sm[:, 8:12]
    BTS = sm[:, 12:16]
    WSARG = sm[:, 16:20]
    WSRC = sm[:, 20:24]
    DEC = sm[:, 24:28]
    DD = sm[:, 28:32]
    RDEN = sm[:, 32:36]
    T1 = sm[:, 36:40]
    T2 = sm[:, 40:44]
    RSTD = sm[:, 44:48]
    SC = sm[:, 48:52]
    BI = sm[:, 52:56]

    bT = banks[0]
    bT_bf = bT.bitcast(BF16)

    for nm, dst, src in (("ident_f", ident_f, identf_d), ("cmask", cmask, cmask_d), ("maskneg4", maskneg4, mneg_d),
                         ("dqk", dqk, dqk_d), ("g_mix_b", g_mix_b, gmix_d),
                         ("convml", convml, convml_d), ("i4", i4, i4_d), ("bif", bif, bif_d)):
        S.add("sp", _dma(dst, src), writes=[nm], dma="cst")
    for g, (c0, c1) in enumerate(WGROUPS):
        for k in range(8):
            S.add("pool", _dma(W[:, k, c0:c1], win_d[k * 128:(k + 1) * 128, c0:c1]),
                  writes=[("W", g, k)], dma=f"W{g}")

    def wk(col):
        g = _wgroup_of(col)
        return [("W", g, k) for k in range(8)]

    S.add("pool", lambda e: e.memset(ones4, 1.0), writes=["ones4"])
    S.add("pool", lambda e: e.memset(zeros4, 0.0), writes=["zeros4"])
    S.add("pool", lambda e: e.memset(asb, 0.0), writes=[("asb", t) for t in range(8)])
    S.add("pool", lambda e: e.memset(Vmls[0], 1.0), writes=[("Vml", 0, 0), ("Vml", 0, 1)])
    S.add("pool", lambda e: e.memset(Vmls[1], 1.0), writes=[("Vml", 1, 0), ("Vml", 1, 1)])
    S.add("pool", lambda e: e.memset(Cml_f, 0.0), writes=[("Cml_f", h) for h in range(4)])
    S.add("pool", lambda e: e.memset(Cml_bfs[0], 0.0), writes=[("Cml_bf", 0, h) for h in range(4)])
    S.add("pool", lambda e: e.memset(Crt_f, 0.0), writes=[("Crt_f", h) for h in range(4)])
    S.add("pool", lambda e: e.memset(Crt_bfs[0], 0.0), writes=[("Crt_bf", 0, h) for h in range(4)])
    S.add("pool", lambda e: e.memset(smr, 0.0), writes=["mst", "D1", "diagM", "diagD"])
    S.add("pool", lambda e: e.memset(smr[:, 0:1], NEG), writes=["mst"])
    S.add("dve", _ts(bsc[:, 0:1], bif[:, 0:1], 1.0 / GATE_CAP, ALU.mult), reads=["bif"], writes=["bsc0"])
    S.add("dve", _ts(bsc[:, 1:2], bif[:, 1:2], -1.0, ALU.mult), reads=["bif"], writes=["bsc1"])

    MST = smr[:, 0:1]
    D1 = smr[:, 1:2]
    DIAGM = smr[:, 4:8]
    DIAGD = smr[:, 8:12]

    def loads(c):
        s = c % 2
        S.add("sp", _dma(xin, xs_d[c * 128:(c + 1) * 128, :]), writes=["xin"], dma="xin")

    def loads2(c):
        S.add("sp", _dma(cst[0], cs_d[c]), writes=[("cst", 0)], dma="cst0")
        S.add("sp", _dma(vmt[0], vm_d[c]), writes=[("vmt", 0)], dma="vmt0")

    aslot = [0]

    def next_aslot():
        i = aslot[0] % 4
        aslot[0] += 1
        return i

    tmslot = [0]

    def rmsnorm_T(src, gb, dstT, dst_keys, src_key, gkey):
        S.add("act", _act(u, src, AF.Square, accum=stat[:, 0:1]), reads=[src_key], writes=["u", "ss"])
        S.add("dve", _ts(stat[:, 1:2], stat[:, 0:1], 1.0 / D, ALU.mult, EPS, ALU.add), reads=["ss"], writes=["ms"])
        S.add("pool", _tt(stat[:, 2:3], stat[:, 1:2], mhalf[:, 0:1], ALU.pow), reads=["ms", "mhalf"], writes=["rstd"])
        S.add("dve", _stt(u, src, stat[:, 2:3], gb, ALU.mult, ALU.mult), reads=[src_key, "rstd", gkey], writes=["u"])
        S.add("pe", _trs([(bT_bf[:, k * 128:(k + 1) * 128], u[:, k * 128:(k + 1) * 128]) for k in range(8)], ident_bf),
              reads=["u", "ident_bf"], writes=[("bk", 0)])
        S.add("act", _act(dstT, bT_bf.rearrange("p (k t) -> p k t", k=8), AF.Copy), reads=[("bk", 0)], writes=dst_keys)


    fmb = [0]

    def chunk(c, full):
        rp, wp = c % 2, (c + 1) % 2
        qTml, kTml, qTrt, kTrt = qTmls[rp], kTmls[rp], qTrts[rp], kTrts[rp]
        og, gg = ogs[rp], ggs[rp]
        Vml, Vrt = Vmls[rp], Vrts[rp]
        Cml_bf, Crt_bf = Cml_bfs[rp], Crt_bfs[rp]
        Cml_bfw, Crt_bfw = Cml_bfs[wp], Crt_bfs[wp]
        if os.environ.get("MK_FAKE2"):
            S.fake_par = c % int(os.environ["MK_FAKE2"])
        s = c % 2
        uTs = uT[s]
        rmsnorm_T(xin, g_mix_b, uTs, [("uT", s)], "xin", "g_mix_b")
        if c + 1 < NCH:
            loads(c + 1)

        def fm_group(cols_ms):
            b = 1 + fmb[0] % 2
            fmb[0] += 1
            bk = banks[b]
            for sl, (col, m) in enumerate(cols_ms):
                S.add("pe", _mmk(bk[0:m, sl * 128:(sl + 1) * 128], [(W[:, k, col:col + m], uTs[:, k, :]) for k in range(8)]),
                      reads=wk(col) + [("uT", s)], writes=[("bk", b)])
            return bk, ("bk", b)

        for grp in ((0, 1) if full else (1,)):
            bk, bkey = fm_group([((grp * 4 + t) * 128, 128) for t in range(4)])
            S.add("act", _act(asb[:, grp * 4:grp * 4 + 4, 3:131], bk.rearrange("p (a b) -> p a b", a=4), AF.Copy),
                  reads=[bkey], writes=[("asb", grp * 4 + t) for t in range(4)])
            for t in range(grp * 4, grp * 4 + 4):
                acc = cacc[t % 2]
                ak = ("cacc", t % 2)
                S.add("dve", _ts(acc, asb[:, t, 3:131], convml[:, t * 4 + 3:t * 4 + 4], ALU.mult),
                      reads=[("asb", t), "convml"], writes=[ak])
                for kk in (2, 1, 0):
                    S.add("dve", _stt(acc, asb[:, t, kk:kk + 128], convml[:, t * 4 + kk:t * 4 + kk + 1], acc, ALU.mult, ALU.add),
                          reads=[("asb", t), "convml", ak], writes=[ak])
                S.add("pool", _cp(asb[:, t, 0:3], asb[:, t, 128:131]), reads=[("asb", t)], writes=[("asb", t)])
                if t < 4:
                    S.add("act", _act(qTml[:, t, :], acc, AF.Silu), reads=[ak], writes=[("qTml", rp, t)])
                else:
                    S.add("act", _act(kTml[:, t - 4, :], acc, AF.Silu), reads=[ak], writes=[("kTml", rp, t - 4)])
        bk, gkey = fm_group([(1024, 4), (1028, 4)])
        gi_reg = bk[0:4, 0:128]
        gf_reg = bk[0:4, 128:256]
        R = rows
        vs = vmt[s]
        S.add("act", _act(R["li0"], gi_reg, AF.Tanh, bias=bsc[:, 0:1], scale=1.0 / GATE_CAP), reads=[gkey, "bsc0"], writes=["li0"])
        S.add("act", _act(R["ef"], gf_reg, AF.Exp, bias=bsc[:, 1:2], scale=-1.0), reads=[gkey, "bsc1"], writes=["ef"])
        S.add("dve", _stt(R["li1"], R["li0"], GATE_CAP, vs[:, 0, :], ALU.mult, ALU.mult), reads=["li0", ("vmt", 0)], writes=["li1"])
        S.add("dve", _tt(R["li"], R["li1"], vs[:, 1, :], ALU.add), reads=["li1", ("vmt", 0)], writes=["li"])
        S.add("act", _act(R["sp"], R["ef"], AF.Ln, bias=1.0), reads=["ef"], writes=["sp"])
        S.add("dve", _scan(R["nbcum"], R["sp"], zeros4, 0.0, ALU.add, ALU.add), reads=["sp", "zeros4"], writes=["nbcum"])
        S.add("dve", _tt(R["B"], R["li"], R["nbcum"], ALU.add), reads=["li", "nbcum"], writes=["B"])
        S.add("dve", _scan(R["M"], R["B"], R["B"], MST, ALU.max, ALU.max), reads=["B", "mst"], writes=["M"])
        S.add("dve", _ts(DIAGM, i4[:, 0:4], R["M"][:, 127:128], ALU.mult), reads=["i4", "M"], writes=["diagM"])
        S.add("dve", _tt(D1, MST, R["M"][:, 127:128], ALU.subtract), reads=["mst", "M"], writes=["D1"])
        S.add("dve", _ts(DIAGD, i4[:, 0:4], D1, ALU.mult), reads=["i4", "D1"], writes=["diagD"])
        if full:
            S.add("dve", _tt(R["R2"], R["nbcum"], R["M"], ALU.subtract), reads=["nbcum", "M"], writes=["R2"])
            S.add("dve", _ts(R["R2"], R["R2"], 80.0, ALU.min), reads=["R2"], writes=["R2"])
            S.add("dve", _ts(R["R3"], R["M"], MST, ALU.subtract, -1.0, ALU.mult), reads=["M", "mst"], writes=["R3"])
            for h in range(4):
                S.add("dve", _ts(rhs_bd[:, h, :], R["M"], i4[:, 4 + h:5 + h], ALU.mult), reads=["M", "i4"], writes=[("rhs_bd", h)])
        S.add("dve", _tt(MST, R["M"][:, 127:128], R["nbcum"][:, 127:128], ALU.subtract), reads=["M", "nbcum"], writes=["mst"])

        sb_ = 5
        SMP = banks[sb_][:, 0:32]
        skey = ("bk", sb_)

        def smp_mm(e):
            ins = e.matmul(SMP[:, 0:4], lhsT=R["B"], rhs=i4[:, 0:4], start=True, stop=True)
            if full:
                e.matmul(SMP[:, 4:8], lhsT=R["R2"], rhs=i4[:, 0:4], start=True, stop=True)
                e.matmul(SMP[:, 8:12], lhsT=R["R3"], rhs=i4[:, 0:4], start=True, stop=True)
            e.matmul(SMP[:, 12:16], lhsT=ones4, rhs=DIAGM, start=True, stop=True)
            ins = e.matmul(SMP[:, 16:20], lhsT=ones4, rhs=DIAGD, start=True, stop=True)
            return ins
        S.add("pe", smp_mm, reads=["B", "R2", "R3", "i4", "ones4", "diagM", "diagD"], writes=[skey])
        if full:
            S.add("act", _act(smps, SMP[:, 0:20], AF.Copy), reads=[skey], writes=["smps"])
        else:
            S.add("act", _act(smps[:, 0:4], SMP[:, 0:4], AF.Copy), reads=[skey], writes=["smps"])
            S.add("act", _act(smps[:, 12:20], SMP[:, 12:20], AF.Copy), reads=[skey], writes=["smps"])
        if full:
            S.add("act", _act(EX8, smps[:, 4:12], AF.Exp), reads=["smps"], writes=["ex8"])
        S.add("dve", _tt(WSARG, smps[:, 0:4], smps[:, 12:16], ALU.subtract), reads=["smps"], writes=["wsarg"])
        S.add("act", _act(WSRC, WSARG, AF.Exp, bias=float(np.log(S_ML))), reads=["wsarg"], writes=["wsrc"])
        S.add("act", _act(DEC, smps[:, 16:20], AF.Exp), reads=["smps"], writes=["dec"])
        b5 = banks[5]
        k5 = ("bk", 5)
        if full:
            S.add("dve", _ts(BTS, smps[:, 0:4], float(np.log(S_ML)), ALU.add), reads=["smps"], writes=["BTS"])
            S.add("pe", _mmk(b5, [(ones4, rhs_bd.rearrange("p h t -> p (h t)")), (ident_f, maskneg4.unsqueeze(1).broadcast_to([128, 4, 128]))]),
                  reads=["ones4", "ident_f", "maskneg4"] + [("rhs_bd", h) for h in range(4)], writes=[k5])
            for h in range(4):
                S.add("act", _act(WT[:, h, :], b5[:, h * 128:(h + 1) * 128], AF.Exp, bias=BTS[:, h:h + 1]),
                      reads=[k5, "BTS"], writes=[("WT", h)])
            for h in range(4):
                S.add("dve", _ts(rhs_bd[:, h, :], R["R3"], i4[:, h:h + 1], ALU.mult), reads=["R3", "i4"], writes=[("rhs_bd", h)])
            S.add("pe", _mm(b5, ones4, rhs_bd.rearrange("p h t -> p (h t)")),
                  reads=["ones4"] + [("rhs_bd", h) for h in range(4)], writes=[k5])
            S.add("act", _act(Wint, b5, AF.Exp), reads=[k5], writes=["Wint"])
            S.add("dve", _tt(qsT.rearrange("p h t -> p (h t)"), qTml.rearrange("p h t -> p (h t)"), Wint, ALU.mult),
                  reads=["Wint"] + [("qTml", rp, h) for h in range(4)], writes=["qsT"])

        def tm_tile(col):
            b = 3 + tmslot[0] % 2
            tmslot[0] += 1
            S.add("pe", _mmk(banks[b], [(uTs[:, k, :], W[:, k, col:col + 512]) for k in range(8)]),
                  reads=wk(col) + [("uT", s)], writes=[("bk", b)])
            return banks[b], ("bk", b)

        for hh in range(2):
            reg, rk = tm_tile(1032 + hh * 512)
            S.add("act", _act(Vml[:, 2 * hh:2 * hh + 2, 0:256], reg.rearrange("p (a b) -> p a b", a=2), AF.Copy),
                  reads=[rk], writes=[("Vml", rp, hh)])
        for hh in range(2):
            reg, rk = tm_tile(3080 + hh * 512)
            S.add("dve", _cp(Vrt[:, 2 * hh:2 * hh + 2, :], reg.rearrange("p (a b) -> p a b", a=2)),
                  reads=[rk], writes=[("Vrt", rp, hh)])
        if full:
            for hh in range(2):
                reg, rk = tm_tile(2056 + hh * 512)
                S.add("act", _act(og[:, hh * 512:(hh + 1) * 512], reg, AF.Tanh, scale=0.5), reads=[rk], writes=[("og", rp, hh)])
            for hh in range(2):
                reg, rk = tm_tile(4104 + hh * 512)
                S.add("act", _act(gg[:, hh * 512:(hh + 1) * 512], reg, AF.Silu), reads=[rk], writes=[("gg", rp, hh)])

        def rotary(col, xi, qk, dst, dkey):
            reg, rk = tm_tile(col)
            X = rX[xi]
            xk = ("rX", 0)
            S.add("act", _act(X, reg, AF.Copy), reads=[rk], writes=[xk])
            Xv = X.rearrange("p (h a t) -> p h a t", h=4, a=2)
            Tc = cst[0][:, (qk * 2) * 256:(qk * 2 + 1) * 256].rearrange("p (h t) -> p h t", h=4)
            Ts = cst[0][:, (qk * 2 + 1) * 256:(qk * 2 + 2) * 256].rearrange("p (h t) -> p h t", h=4)
            Mv = [m.rearrange("p (h t) -> p h t", h=4) for m in rM]
            Dv = dst.rearrange("p h (a t) -> p h a t", a=2)
            S.add("dve", _tt(Mv[0], Xv[:, :, 0, :], Tc, ALU.mult), reads=[xk, ("cst", 0)], writes=[("rM", 0)])
            S.add("dve", _tt(Mv[1], Xv[:, :, 1, :], Ts, ALU.mult), reads=[xk, ("cst", 0)], writes=[("rM", 1)])
            S.add("pool", _tt(Mv[2], Xv[:, :, 0, :], Ts, ALU.mult), reads=[xk, ("cst", 0)], writes=[("rM", 2)])
            S.add("pool", _tt(Mv[3], Xv[:, :, 1, :], Tc, ALU.mult), reads=[xk, ("cst", 0)], writes=[("rM", 3)])
            S.add("dve", _tt(Dv[:, :, 0, :], Mv[0], Mv[1], ALU.subtract), reads=[("rM", 0), ("rM", 1)], writes=[(dkey, 0)])
            S.add("pool", _tt(Dv[:, :, 1, :], Mv[2], Mv[3], ALU.add), reads=[("rM", 2), ("rM", 3)], writes=[(dkey, 1)])

        rotary(5640, 0, 1, ktok, "ktok")
        if full and not os.environ.get("MK_X1"):
            rotary(5128, 1, 0, qtok, "qtok")
        if c + 1 < NCH:
            loads2(c + 1)

        ob = [6]

        def next_ob():
            b = 6 + ob[0] % 2
            ob[0] += 1
            return banks[b], ("bk", b)

        kb_, kbk = next_ob()
        b0bf = kb_.bitcast(BF16)
        S.add("pe", _trs([(b0bf[:, t * 128:(t + 1) * 128], kTml[:, t, :]) for t in range(4)], ident_bf),
              reads=[("kTml", rp, h) for h in range(4)] + ["ident_bf"], writes=[kbk])
        for t in range(4):
            S.add("act", _act(kw[:, t, :], b0bf[:, t * 128:(t + 1) * 128], AF.Copy, scale=WSRC[:, t:t + 1]),
                  reads=[kbk, "wsrc"], writes=[("kw", t)])
        if full:
            qb_, qbk = next_ob()
            qbbf = qb_.bitcast(BF16)
            S.add("pe", _trs([(qbbf[:, h * 128:(h + 1) * 128], ktok[:, h, :]) for h in range(4)] +
                             [(qbbf[:, (4 + h) * 128:(5 + h) * 128], qtok[:, h, :]) for h in range(4)], ident_bf),
                  reads=[("ktok", 0), ("ktok", 1), ("qtok", 0), ("qtok", 1), "ident_bf"], writes=[qbk])
            S.add("act", _act(kTrt, qbbf[:, 0:512].rearrange("p (h t) -> p h t", h=4), AF.Copy), reads=[qbk],
                  writes=[("kTrt", rp, h) for h in range(4)])
            S.add("act", _act(qTrt, qbbf[:, 512:1024].rearrange("p (h t) -> p h t", h=4), AF.Copy), reads=[qbk],
                  writes=[("qTrt", rp, h) for h in range(4)])

        for h in range(4):
            bo, ko = next_ob()
            S.add("pe", _mm(bo[:, 0:257], kw[:, h, :], Vml[:, h, :]), reads=[("kw", h), ("Vml", rp, h // 2)], writes=[ko])
            S.add("dve", _stt(Cml_f[:, h, :], Cml_f[:, h, :], DEC[:, h:h + 1], bo[:, 0:257], ALU.mult, ALU.add),
                  reads=[("Cml_f", h), "dec", ko], writes=[("Cml_f", h)])
            S.add("pool", _cp(Cml_bfw[:, h, :], Cml_f[:, h, :]), reads=[("Cml_f", h)], writes=[("Cml_bf", wp, h)])
        for pr in range(2):
            bo, ko = next_ob()
            for hh in range(2):
                h = 2 * pr + hh
                S.add("pe", _mm(bo[:, hh * 256:(hh + 1) * 256], ktok[:, h, :], Vrt[:, h, :]), reads=[("ktok", 0), ("ktok", 1), ("Vrt", rp, pr)], writes=[ko])
            for hh in range(2):
                h = 2 * pr + hh
                S.add("dve", _stt(Crt_f[:, h, :], Crt_f[:, h, :], CD[h], bo[:, hh * 256:(hh + 1) * 256], ALU.mult, ALU.add),
                      reads=[("Crt_f", h), ko], writes=[("Crt_f", h)])
            for hh in range(2):
                h = 2 * pr + hh
                S.add("pool", _ts(Crt_bfw[:, h, :], Crt_f[:, h, :], CD[h], ALU.mult, 0.0, ALU.add),
                      reads=[("Crt_f", h)], writes=[("Crt_bf", wp, h)])
        if full:
            S.add("pe", lambda e: [e.matmul(b5[:, h * 128:(h + 1) * 128], lhsT=kTml[:, h, :], rhs=qTml[:, h, :], start=True, stop=True)
                                   for h in range(4)][-1],
                  reads=[("kTml", rp, h) for h in range(4)] + [("qTml", rp, h) for h in range(4)], writes=[k5])
            S.add("dve", _tt(PT.rearrange("p h t -> p (h t)"), b5, WT.rearrange("p h t -> p (h t)"), ALU.mult),
                  reads=[k5] + [("WT", h) for h in range(4)], writes=["PT"])
            for h in range(4):
                bo, ko = next_ob()
                S.add("pe", _mmk(bo[:, 0:257], [(PT[:, h, :], Vml[:, h, :]), (qsT[:, h, :], Cml_bf[:, h, :])]),
                      reads=["PT", ("Vml", rp, h // 2), "qsT", ("Cml_bf", rp, h)], writes=[ko])
                S.add("act", _act(hraw[:, h, :], bo[:, 0:257], AF.Copy), reads=[ko], writes=[("hraw", h)])
                S.add("dve", lambda e, h=h: e.bn_stats(out=st6[:, h, :], in_=hraw[:, h, 0:256]), reads=[("hraw", h)], writes=[("st6", h)])
                S.add("dve", lambda e, h=h: e.bn_aggr(out=mv[:, h, :], in_=st6[:, h, :]), reads=[("st6", h)], writes=[("mv", h)])
            allh = [("hraw", h) for h in range(4)]
            allmv = [("mv", h) for h in range(4)]
            S.add("dve", _ts(T2, hraw[:, :, 256], -1.0, ALU.mult), reads=allh, writes=["t2"])
            S.add("dve", _tt(DD, T2, hraw[:, :, 256], ALU.max), reads=allh + ["t2"], writes=["dd"])
            S.add("dve", _tt(DD, DD, EX8[:, 0:4], ALU.max), reads=["dd", "ex8"], writes=["dd"])
            S.add("dve", lambda e: e.reciprocal(out=RDEN, in_=DD), reads=["dd"], writes=["rden"])
            S.add("dve", _tt(T1, RDEN, RDEN, ALU.mult), reads=["rden"], writes=["t1"])
            S.add("dve", _tt(T2, T1, mv[:, :, 1], ALU.mult), reads=["t1"] + allmv, writes=["t2"])
            S.add("dve", _ts(T1, T2, EPS, ALU.add), reads=["t2"], writes=["t1"])
            S.add("pool", _tt(RSTD, T1, mhalf, ALU.pow), reads=["t1", "mhalf"], writes=["rstdh"])
            S.add("dve", _tt(SC, RDEN, RSTD, ALU.mult), reads=["rden", "rstdh"], writes=["sc"])
            S.add("dve", _stt(BI, mv[:, :, 0], -1.0, SC, ALU.mult, ALU.mult), reads=allmv + ["sc"], writes=["bi"])
            for h in range(4):
                S.add("act", _act(hraw[:, h, 0:256], hraw[:, h, 0:256], AF.Identity, bias=BI[:, h:h + 1], scale=SC[:, h:h + 1]),
                      reads=[("hraw", h), "sc", "bi"], writes=[("hraw", h)])
            S.add("dve", _stt(mixed[:, 0:1024].rearrange("p (h v) -> p h v", h=4), og.rearrange("p (h v) -> p h v", h=4), 1.0,
                              hraw[:, :, 0:256], ALU.add, ALU.mult),
                  reads=allh + [("og", rp, 0), ("og", rp, 1)], writes=[("mixed", 0)])
            S.add("pe", lambda e: [e.matmul(b5[:, h * 128:(h + 1) * 128], lhsT=kTrt[:, h, :], rhs=qTrt[:, h, :], start=True, stop=True)
                                   for h in range(4)][-1],
                  reads=[("kTrt", rp, h) for h in range(4)] + [("qTrt", rp, h) for h in range(4)], writes=[k5])
            S.add("dve", _tt(PT, b5.rearrange("p (h t) -> p h t", h=4), cmask.unsqueeze(1).broadcast_to([128, 4, 128]), ALU.mult),
                  reads=[k5, "cmask"], writes=["PT"])
            for pr in range(2):
                bo, ko = next_ob()
                for hh in range(2):
                    h = 2 * pr + hh
                    S.add("pe", _mmk(bo[:, hh * 256:(hh + 1) * 256], [(PT[:, h, :], Vrt[:, h, :]), (qTrt[:, h, :], Crt_bf[:, h, :])]),
                          reads=["PT", ("Vrt", rp, pr), ("qTrt", rp, h), ("Crt_bf", rp, h)], writes=[ko])
                S.add("act", _act(hrt[:, 2 * pr:2 * pr + 2, :], bo.rearrange("p (a b) -> p a b", a=2), AF.Copy),
                      reads=[ko], writes=[("hrt", 2 * pr), ("hrt", 2 * pr + 1)])
                for hh in range(2):
                    h = 2 * pr + hh
                    S.add("dve", lambda e, h=h: e.bn_stats(out=st6[:, h, :], in_=hrt[:, h, :]), reads=[("hrt", h)], writes=[("st6", h)])
                    S.add("dve", lambda e, h=h: e.bn_aggr(out=mv[:, h, :], in_=st6[:, h, :]), reads=[("st6", h)], writes=[("mv", h)])
            allr = [("hrt", h) for h in range(4)]
            S.add("dve", _ts(T1, mv[:, :, 1], EPS, ALU.add), reads=allmv, writes=["t1"])
            S.add("pool", _tt(RSTD, T1, mhalf, ALU.pow), reads=["t1", "mhalf"], writes=["rstdh"])
            S.add("dve", _stt(BI, mv[:, :, 0], -1.0, RSTD, ALU.mult, ALU.mult), reads=allmv + ["rstdh"], writes=["bi"])
            for h in range(4):
                S.add("act", _act(hrt[:, h, :], hrt[:, h, :], AF.Identity, bias=BI[:, h:h + 1], scale=RSTD[:, h:h + 1]),
                      reads=[("hrt", h), "rstdh", "bi"], writes=[("hrt", h)])
            S.add("dve", _tt(mixed[:, 1024:2048], hrt.rearrange("p h v -> p (h v)"), gg, ALU.mult),
                  reads=allr + [("gg", rp, 0), ("gg", rp, 1)], writes=[("mixed", 1)])
        if full:
            f = c - NCH_P
            S.add("act", _dma(mixed_d[f * 128:(f + 1) * 128, :], mixed), reads=[("mixed", 0), ("mixed", 1)],
                  writes=[("mixed_d", f)], dma="mxst")


    loads(0)
    loads2(0)
    _stop = int(os.environ.get("MK_STOP_AFTER", str(NCH)))
    for c in range(min(NCH, _stop)):
        chunk(c, c >= NCH_P)
    if _stop < 100 and os.environ.get("MK_STOP_AFTER"):
        S.fake_par = None
        S.add("sp", lambda e: e.nop(), reads=S.all_keys())
        with ExitStack() as es:
            sems = {e: es.enter_context(nc.semaphore("s_" + e)) for e in ENGS}
            dsems = {k: es.enter_context(nc.semaphore("d_" + k)) for k in S.dma_counts}
            block = es.enter_context(nc.Block())
            S.emit(block, sems, dsems)
        return nc

    S.fake_par = None
    A3 = _Arena(nc, PH_BASE, SBUF_END)
    Wup = A3.alloc("Wup", [128, 8, DFF], BF16)
    Wgt = A3.alloc("Wgt", [128, 8, DFF], BF16)
    assert A3.off <= PH_BASE + 8 * WCOLS * 2, "Wup/Wgt must alias W only"
    allW = [("W", g, k) for g in range(len(WGROUPS)) for k in range(8)]
    for k in range(8):
        S.add("pool", _dma(Wup[:, k, :], wup_d[k * 128:(k + 1) * 128, :]), writes=[("Wup", k)] + (allW if k == 0 else []), dma="Wup")
    for k in range(8):
        S.add("pool", _dma(Wgt[:, k, :], wgate_d[k * 128:(k + 1) * 128, :]), writes=[("Wgt", k)], reads=[("Wup", 0)] if False else [], dma="Wgt")
    S.barrier(skip=("Wup", "Wgt"))
    NWD = 10
    Wout = A3.alloc("Wout", [128, 16, D], BF16)
    Wring = A3.alloc("Wring", [128, NWD, D], BF16)
    xh = A3.alloc("xh", [128, 4, D], F32)
    mtoks = [A3.alloc(f"mtok{i}", [128, 2048], BF16) for i in range(2)]
    mixedT = A3.alloc("mixedT", [128, 16, 256], BF16)
    u2Ts = [A3.alloc(f"u2T{i}", [128, 8, 256], BF16) for i in range(2)]
    u3s = [A3.alloc(f"u3_{i}", [128, D], BF16) for i in range(2)]
    NB3 = int(os.environ.get("MK_NB3", "4"))
    asb3s = [A3.alloc(f"asb3_{i}", [128, 258], F32) for i in range(NB3)]
    acc3 = [A3.alloc(f"acc3_{i}", [128, 256], F32) for i in range(NB3)]
    actT = [A3.alloc(f"actT{i}", [128, 256], BF16) for i in range(NB3)]
    halo = A3.alloc("halo", [128, NJ, 2], F32)
    convff = A3.alloc("convff", [128, NJ * 3], F32)
    g_ffn_b = A3.alloc("g_ffn_b", [128, D], F32)
    g_fin_b = A3.alloc("g_fin_b", [128, D], F32)
    gcol = A3.alloc("gcol", [128, 16], F32)
    stat2 = A3.alloc("stat2", [128, 16], F32)

    print("[kernel] sbuf A1 end", A1.off, "A3 end", A3.off, "limit", SBUF_END)
    S.add("sp", _dma(convff, convff_d), writes=["convff"], dma="cst")
    S.add("sp", _dma(g_ffn_b, gffn_d), writes=["g_ffn_b"], dma="cst")
    S.add("sp", _dma(g_fin_b, gfin_d), writes=["g_fin_b"], dma="cst")
    S.add("sp", _dma(gcol, gcol_d), writes=["gcol"], dma="cst")
    S.add("dve", _ts(gcol[:, 0:8], gcol[:, 0:8], 0.5, ALU.mult), reads=["gcol"], writes=["gcol"])
    S.add("pool", lambda e: e.memset(halo, 0.0), writes=[("halo", j) for j in range(NJ)])
    for k in range(16):
        sl = k % 4
        S.add("sp", _dma(xh[:, sl, :], wout_d[k * 128:(k + 1) * 128, :]), writes=[("xh", sl)], dma=f"xh{sl}")
        S.add("act" if k % 2 else "dve",
              (_act(Wout[:, k, :], xh[:, sl, :], AF.Copy, scale=gcol[:, k:k + 1]) if k % 2 else
               _ts(Wout[:, k, :], xh[:, sl, :], gcol[:, k:k + 1], ALU.mult)),
              reads=[("xh", sl), "gcol"], writes=[("Wout", k)])

    b_acc = [banks[0], banks[1], banks[2], banks[3]]
    ybk = None
    agrot = [0]
    NAG = int(os.environ.get("MK_NAG", "3"))
    b_y = [banks[4 + NAG], banks[7]] if NAG < 3 else [banks[7], banks[7]]
    ybk = [4 + NAG, 7] if NAG < 3 else [7, 7]
    wdc = [0]
    strot = [0]
    mtc = [0]

    def rms_small(src, skey):
        i = strot[0] % 4
        strot[0] += 1
        return stat2[:, 4 * i:4 * i + 1], stat2[:, 4 * i + 1:4 * i + 2], stat2[:, 4 * i + 2:4 * i + 3], i

    def f3_pre(f0, ntt, is_halo, bp):
        T = ntt * 128
        u2T = u2Ts[bp]
        xsl = [bp * 2 + tt for tt in range(ntt)]
        for tt in range(ntt):
            f = f0 + tt
            p0 = (NCH_P + f) * 128
            S.add("sp", _dma(xh[:, xsl[tt], :], xs_d[p0:p0 + 128, :]), writes=[("xh", xsl[tt])], dma=f"xh{xsl[tt]}")
        for tt in range(ntt):
            f = f0 + tt
            mi = mtc[0] % 2
            mtc[0] += 1
            mtok = mtoks[mi]
            S.add("sp", _dma(mtok, mixed_d[f * 128:(f + 1) * 128, :]), reads=[("mixed_d", f)], writes=[("mtok", mi)], dma=f"mtok{mi}")
            for half in range(2):
                yb = b_y[half].bitcast(BF16)
                S.add("pe", _trs([(yb[:, kk * 128:(kk + 1) * 128], mtok[:, (half * 8 + kk) * 128:(half * 8 + kk + 1) * 128])
                                  for kk in range(8)], ident_bf), reads=[("mtok", mi), "ident_bf"], writes=[("bk", ybk[half])])
                S.add("act" if half else "dve",
                      (_act(mixedT[:, half * 8:half * 8 + 8, tt * 128:(tt + 1) * 128], yb.rearrange("p (k t) -> p k t", k=8), AF.Copy)
                       if half else _cp(mixedT[:, half * 8:half * 8 + 8, tt * 128:(tt + 1) * 128], yb.rearrange("p (k t) -> p k t", k=8))),
                      reads=[("bk", ybk[half])], writes=[("mixedT", tt, half)])
        for tt in range(ntt):
            xv = xh[:, xsl[tt], :]
            for half in range(2):
                S.add("pe", _mmk(b_y[half], [(mixedT[:, k, tt * 128:(tt + 1) * 128], Wout[:, k, half * 512:(half + 1) * 512])
                                             for k in range(16)]),
                      reads=[("mixedT", tt, 0), ("mixedT", tt, 1)] + [("Wout", k) for k in range(16)], writes=[("bk", ybk[half])])
                S.add("dve", _tt(xv[:, half * 512:(half + 1) * 512], b_y[half], xv[:, half * 512:(half + 1) * 512], ALU.add),
                      reads=[("bk", ybk[half]), ("xh", xsl[tt])], writes=[("xh", xsl[tt])])
        for tt in range(ntt):
            xv = xh[:, xsl[tt], :]
            xk = ("xh", xsl[tt])
            ss, ms, rs, si = rms_small(None, None)
            u3 = u3s[tt % 2]
            uk = ("u3", tt % 2)
            S.add("act", _act(u3, xv, AF.Square, accum=ss), reads=[xk], writes=[uk, ("ss", si)])
            S.add("dve", _ts(ms, ss, 1.0 / D, ALU.mult, EPS, ALU.add), reads=[("ss", si)], writes=[("ms", si)])
            S.add("pool", _tt(rs, ms, mhalf[:, 0:1], ALU.pow), reads=[("ms", si), "mhalf"], writes=[("rs", si)])
            S.add("dve", _stt(u3, xv, rs, g_ffn_b, ALU.mult, ALU.mult), reads=[xk, ("rs", si), "g_ffn_b"], writes=[uk])
            yb = b_y[tt % 2].bitcast(BF16)
            S.add("pe", _trs([(yb[:, k * 128:(k + 1) * 128], u3[:, k * 128:(k + 1) * 128]) for k in range(8)], ident_bf),
                  reads=[uk, "ident_bf"], writes=[("bk", ybk[tt % 2])])
            S.add("act", _act(u2T[:, :, tt * 128:(tt + 1) * 128], yb.rearrange("p (k t) -> p k t", k=8), AF.Copy),
                  reads=[("bk", ybk[tt % 2])], writes=[("u2T", bp, tt)])

    def f3_main(f0, ntt, is_halo, bp):
        T = ntt * 128
        u2T = u2Ts[bp]
        xsl = [bp * 2 + tt for tt in range(ntt)]
        u2k = [("u2T", bp, tt) for tt in range(ntt)]
        for j in range(NJ):
            par = agrot[0] % NB3
            agb = 4 + (agrot[0] % NAG)
            agrot[0] += 1
            aT = banks[agb][:, 0:T]
            gT = banks[agb][:, 256:256 + T]
            if not is_halo:
                ws = wdc[0] % NWD
                wdc[0] += 1
                S.add("pool", _dma(Wring[:, ws, :], wdn_d[j * 128:(j + 1) * 128, :]), writes=[("Wring", ws)], dma=f"Wd{ws}")
            S.add("pe", _mmk(aT, [(Wup[:, k, j * 128:(j + 1) * 128], u2T[:, k, 0:T]) for k in range(8)]),
                  reads=u2k + [("Wup", k) for k in range(8)], writes=[("bk", agb)])
            if not is_halo:
                S.add("pe", _mmk(gT, [(Wgt[:, k, j * 128:(j + 1) * 128], u2T[:, k, 0:T]) for k in range(8)]),
                      reads=u2k + [("Wgt", k) for k in range(8)], writes=[("bk", agb)])
            asb3 = asb3s[par]
            S.add("pool", _cp(asb3[:, 0:2], halo[:, j, :]), reads=[("halo", j)], writes=[("asb3h", par)])
            S.add("act", _act(asb3[:, 2:2 + T], aT, AF.Copy), reads=[("bk", agb)], writes=[("asb3", par)])
            S.add("pool", _cp(halo[:, j, :], asb3[:, T:T + 2]), reads=[("asb3", par)], writes=[("halo", j)])
            if is_halo:
                continue
            acc = acc3[par][:, 0:T]
            ak = ("acc3", par)
            S.add("dve", _ts(acc, asb3[:, 2:2 + T], convff[:, j * 3 + 2:j * 3 + 3], ALU.mult), reads=[("asb3", par), "convff"], writes=[ak])
            S.add("dve", _stt(acc, asb3[:, 1:1 + T], convff[:, j * 3 + 1:j * 3 + 2], acc, ALU.mult, ALU.add),
                  reads=[("asb3", par), ("asb3h", par), "convff", ak], writes=[ak])
            S.add("dve", _stt(acc, asb3[:, 0:T], convff[:, j * 3:j * 3 + 1], acc, ALU.mult, ALU.add),
                  reads=[("asb3", par), ("asb3h", par), "convff", ak], writes=[ak])
            S.add("act", _act(acc, acc, AF.Silu), reads=[ak], writes=[ak])
            at = actT[par][:, 0:T]
            S.add("dve", _tt(at, acc, gT, ALU.mult), reads=[ak, ("bk", agb)], writes=[("actT", par)])
            for tt in range(ntt):
                for half in range(2):
                    S.add("pe", _mm(b_acc[tt * 2 + half], at[:, tt * 128:(tt + 1) * 128], Wring[:, ws, half * 512:(half + 1) * 512],
                                    start=(j == 0), stop=(j == NJ - 1)),
                          reads=[("actT", par), ("Wring", ws)], writes=[("bk", tt * 2 + half)])
        if is_halo:
            return
        for tt in range(ntt):
            f = f0 + tt
            xv = xh[:, xsl[tt], :]
            xk = ("xh", xsl[tt])
            for half in range(2):
                S.add("dve", _tt(xv[:, half * 512:(half + 1) * 512], b_acc[tt * 2 + half], xv[:, half * 512:(half + 1) * 512], ALU.add),
                      reads=[("bk", tt * 2 + half), xk], writes=[xk])
            ss, ms, rs, si = rms_small(None, None)
            u3 = u3s[tt % 2]
            uk = ("u3", tt % 2)
            S.add("act", _act(u3, xv, AF.Square, accum=ss), reads=[xk], writes=[uk, ("ss", si)])
            S.add("dve", _ts(ms, ss, 1.0 / D, ALU.mult, EPS, ALU.add), reads=[("ss", si)], writes=[("ms", si)])
            S.add("pool", _tt(rs, ms, mhalf[:, 0:1], ALU.pow), reads=[("ms", si), "mhalf"], writes=[("rs", si)])
            S.add("dve", _stt(xv, xv, rs, g_fin_b, ALU.mult, ALU.mult), reads=[xk, ("rs", si), "g_fin_b"], writes=[xk])
            S.add("act", _dma(out_d[(f - 1) * 128:f * 128, :], xv), reads=[xk], writes=[("out", f)], dma=f"ost{xsl[tt]}")

    blocks = [(0, 1, True, 1)] + [(1 + 2 * b, 2, False, b % 2) for b in range(8)]
    f3_pre(*blocks[0])
    for bi, blk in enumerate(blocks):
        if bi + 1 < len(blocks):
            f3_pre(*blocks[bi + 1])
        f3_main(*blk)

    fin = S.add("sp", lambda e: e.nop(), reads=[("out", f) for f in range(1, NCH_F)])

    with ExitStack() as es:
        sems = {e: es.enter_context(nc.semaphore("s_" + e)) for e in ENGS}
        dsems = {k: es.enter_context(nc.semaphore("d_" + k)) for k in S.dma_counts}
        block = es.enter_context(nc.Block())
        S.emit(block, sems, dsems, reorder=bool(int(os.environ.get("MK_REORDER", "1"))))
        print("[kernel] est_ns", getattr(S, "est_ns", None), "ops", len(S.ops))
    return nc


def _host_consts():
    idx = np.arange(128, dtype=np.float64)
    dqk = np.zeros((128, 12), np.float32)
    for h in range(4):
        dqk[:, h] = np.exp(LOG_GAMMA[h] * (idx + 1.0))
        dqk[:, 4 + h] = S_ML * np.exp(-LOG_GAMMA[h] * (idx + 1.0))
        dqk[:, 8 + h] = S_ML * np.exp(-LOG_GAMMA[h] * (idx + 1.0)) * np.exp(LOG_GAMMA[h] * CH)
    jj, ii = np.meshgrid(np.arange(128), np.arange(128), indexing="ij")
    cm = (jj <= ii).astype(np.float32)
    mneg = np.where(jj <= ii, 0.0, NEG).astype(np.float32)
    i4 = np.concatenate([np.eye(4), -np.eye(4)], axis=1).astype(np.float32)
    return dict(dqk=dqk, cmask=cm, maskneg4=mneg,
                ident_bf=np.eye(128).astype(ml_dtypes.bfloat16), ident_f=np.eye(128, dtype=np.float32), i4=i4)


def _rope_tables(n_null):
    p = np.arange(NPOS, dtype=np.float64)
    pos = np.where(p >= n_null, 48.0 + (p - n_null), 0.0)
    inv = 10000.0 ** (-np.arange(0, 128, 2, dtype=np.float64) / 128.0)
    ang = pos[:, None] * inv[None, :]
    cosr = np.cos(ang).reshape(NCH, 128, 1, 64)
    sinr = np.sin(ang).reshape(NCH, 128, 1, 64)
    idx = np.arange(128, dtype=np.float64)
    dq = np.stack([np.exp(LOG_GAMMA[h] * (idx + 1.0)) for h in range(4)], axis=1)[None, :, :, None]
    dk = np.stack([S_ML * np.exp(-LOG_GAMMA[h] * (idx + 1.0)) for h in range(4)], axis=1)[None, :, :, None]
    tab = np.stack([cosr * dq, sinr * dq, cosr * dk, sinr * dk], axis=2)
    tab = np.ascontiguousarray(tab.reshape(NCH, 128, 1024)).astype(np.float32)
    valid = (p >= n_null).astype(np.float32).reshape(NCH, 128)
    vm = np.stack([valid, (valid - 1.0) * 1e30], axis=1)
    vm = np.ascontiguousarray(np.broadcast_to(vm[:, None], (NCH, 4, 2, 128))).astype(np.float32)
    return tab, vm


def _prep_inputs(inputs):
    f = lambda k: np.asarray(inputs[k], dtype=np.float32)
    x = f("x")
    meta = f("meta_tokens")
    w_in = f("w_in")[0]
    sizes = [512, 512, 1024, 1024, 4, 4, 512, 512, 1024, 1024]
    offs = np.cumsum(sizes)[:-1]
    ml_q, ml_k, ml_v, ml_o, ml_i, ml_f, rt_q, rt_k, rt_v, rt_g = np.split(w_in, offs, axis=1)

    def swap(cols):
        return cols.reshape(D, 4, 2, 64)[:, :, ::-1, :].reshape(D, 512)

    w_in_r = np.ascontiguousarray(np.concatenate(
        [ml_q, ml_k, ml_i, ml_f, ml_v, ml_o, rt_v, rt_g, rt_q, rt_k], axis=1))
    assert w_in_r.shape == (D, WCOLS)
    convml = f("ml_conv_w")[0]
    convw_ml = np.ascontiguousarray(convml.reshape(4, 8, 128).transpose(2, 1, 0).reshape(128, 32))
    convff = f("ffn_conv_w")[0]
    convw_ffn = np.ascontiguousarray(convff.reshape(3, NJ, 128).transpose(2, 1, 0).reshape(128, NJ * 3))
    b_if = np.ascontiguousarray(np.stack([f("ml_b_i")[0], f("ml_b_f")[0]], axis=1))
    gcat = np.concatenate([f("ml_norm_g")[0], f("rt_norm_g")[0]])
    gcol = np.ascontiguousarray(gcat.reshape(16, 128).T)
    bc = lambda v: np.ascontiguousarray(np.broadcast_to(v[None, :], (128, D)))
    common = dict(w_in_r=w_in_r, w_out=f("w_out")[0], w_up=f("w_up")[0], w_gate=f("w_gate")[0], w_down=f("w_down")[0],
                  convw_ml=convw_ml, convw_ffn=convw_ffn, b_if=b_if, gcol=gcol,
                  g_mix_b=bc(f("norm_mix_g")[0]), g_ffn_b=bc(f("norm_ffn_g")[0]), g_fin_b=bc(f("norm_final_g")))
    common.update(_host_consts())
    tabs = [_rope_tables(2160), _rope_tables(112)]
    in_maps = []
    for core in range(8):
        b, t = core // 2, core % 2
        xs = np.zeros((NPOS, D), np.float32)
        if t == 0:
            xs[2160:2176] = meta
            xs[2176:] = x[b, 0:2048]
        else:
            xs[112:128] = meta
            xs[128:] = x[b]
        m = dict(common)
        m["xs"] = xs
        m["cs_tab"], m["vm_tab"] = tabs[t]
        in_maps.append(m)
    return in_maps


_NC_CACHE = {}


def kernel(**inputs):
    in_maps = _prep_inputs(inputs)
    if "nc" not in _NC_CACHE:
        _NC_CACHE["nc"] = build_program()
    nc = _NC_CACHE["nc"]
    res = run_bass_kernel_spmd(nc, in_maps, core_ids=list(range(8)))
    out = np.zeros((4, 4096, D), np.float32)
    for core in range(8):
        b, t = core // 2, core % 2
        out[b, t * 2048:(t + 1) * 2048] = res.results[core]["out"]
    if DEBUG:
        kernel.debug = [res.results[c]["mixed_d"] for c in range(8)]
    return out
```

```python
import os
from contextlib import ExitStack

import numpy as np
import ml_dtypes

import concourse.bass as bass
import concourse.mybir as mybir
from concourse.bass_utils import run_bass_kernel_spmd

F32 = mybir.dt.float32
BF16 = mybir.dt.bfloat16
ALU = mybir.AluOpType
AF = mybir.ActivationFunctionType

NCH_P = 16
NCH_F = 17
NCH = NCH_P + NCH_F
CH = 128
NPOS = NCH * CH
D = 1024
DFF = 2816
NJ = DFF // 128
WCOLS = 6152
EPS = 1e-6
NEG = -1e30
GATE_CAP = 15.0
SBUF_BASE = 16640
SBUF_END = 229376
S_ML = 128.0 ** -0.5
LOG_GAMMA = [float(np.log1p(-2.0 ** (-(5.0 + h)))) for h in range(4)]
CD = [float(np.exp(lg * CH)) for lg in LOG_GAMMA]

ENGS = ("pe", "act", "dve", "pool", "sp")
DEBUG = bool(int(os.environ.get("MK_DEBUG", "0")))


class _Op:
    __slots__ = ("eng", "fn", "deps", "dma_sem", "dma_cnt", "sig", "idx", "need_sig", "wk")

    def __init__(self, eng, fn):
        self.eng = eng
        self.fn = fn
        self.deps = []
        self.dma_sem = None
        self.dma_cnt = 0
        self.sig = 0
        self.need_sig = False


class Sched:
    def __init__(self):
        self.ops = []
        self.last_w = {}
        self.readers = {}
        self.dma_counts = {}
        self.fence = None

    fake_par = None
    FAKE_KEEP = ("bk", "W", "Cml_f", "Cml_bf", "Crt_f", "Crt_bf", "asb", "mixed_d", "cst", "vmt", "uT")

    def _fk(self, keys):
        if self.fake_par is None:
            return keys
        out = []
        for k in keys:
            base = k[0] if isinstance(k, tuple) else k
            if base == "bk" and os.environ.get("MK_FAKEBK"):
                out.append((k, self.fake_par))
                continue
            if base in self.FAKE_KEEP or base in ("mst", "ident_bf", "mhalf", "i4", "ones4", "zeros4", "convml", "dqk", "cmask",
                                                   "maskneg4", "ident_f", "g_mix_b", "bsc0", "bsc1", "bif"):
                out.append(k)
            else:
                out.append((k, self.fake_par))
        return out

    def add(self, eng, fn, reads=(), writes=(), dma=None):
        reads = self._fk(reads)
        writes = self._fk(writes)
        op = _Op(eng, fn)
        op.wk = list(writes)[:2]
        op.idx = len(self.ops)
        deps = {}

        def dep(o, raw):
            val = self.dma_counts[o.dma_sem] if o.dma_sem is not None else 0
            if o.idx in deps:
                if raw and not deps[o.idx][1]:
                    deps[o.idx] = (o, True, val)
            else:
                deps[o.idx] = (o, raw, val)

        for r in reads:
            w = self.last_w.get(r)
            if w is not None:
                dep(w, True)
        for k in writes:
            w = self.last_w.get(k)
            if w is not None:
                dep(w, False)
            for rd in self.readers.get(k, ()):
                dep(rd, False)
        if self.fence is not None:
            dep(self.fence[eng], False)
        op.deps = list(deps.values())
        for r in reads:
            self.readers.setdefault(r, []).append(op)
        for k in writes:
            self.last_w[k] = op
            self.readers[k] = []
        if dma is not None:
            op.dma_sem = dma
            self.dma_counts[dma] = self.dma_counts.get(dma, 0) + 16
            op.dma_cnt = self.dma_counts[dma]
        self.ops.append(op)
        return op

    def all_keys(self):
        return list(set(self.last_w.keys()) | set(self.readers.keys()))

    def barrier(self, skip=()):
        keys = [k for k in self.all_keys() if not (isinstance(k, tuple) and k[0] in skip)]
        fence = {}
        for e in ENGS:
            fence[e] = self.add(e, lambda eng: eng.nop(), writes=keys)
        self.fence = fence

    @staticmethod
    def _skip(d, op, raw):
        if d.dma_sem is not None or op.dma_sem is not None:
            return False
        if d.eng != op.eng:
            return False
        return d.eng == "pe"

    def schedule(self, window=int(os.environ.get("MK_WIN", "200")), lat=float(os.environ.get("MK_LAT", "800"))):
        ops = self.ops
        n = len(ops)
        ndeps = [0] * n
        users = [[] for _ in range(n)]
        for op in ops:
            ndeps[op.idx] = len(op.deps)
            for (d, raw, val) in op.deps:
                users[d.idx].append(op.idx)
        fin = [0.0] * n
        ready_t = [0.0] * n
        prio_cp = os.environ.get("MK_PRIO", "cp") == "cp"
        tail = [0.0] * n
        if prio_cp:
            for op in reversed(ops):
                c = getattr(op.fn, "cost", 150.0)
                t = 0.0
                tl = lat * float(os.environ.get("MK_TAILF", "1.0"))
                for u in users[op.idx]:
                    if tail[u] + tl > t:
                        t = tail[u] + tl
                tail[op.idx] = t + c * float(os.environ.get("MK_COSTF", "1.0"))
        pend = {e: [op.idx for op in ops if op.eng == e] for e in ENGS}
        pos = {e: 0 for e in ENGS}
        done = [False] * n
        efree = {e: 0.0 for e in ENGS}
        order = {e: [] for e in ENGS}
        remaining = n
        while remaining:
            best = None
            for e in ENGS:
                lst = pend[e]
                i = pos[e]
                while i < len(lst) and done[lst[i]]:
                    i += 1
                pos[e] = i
                if i >= len(lst):
                    continue
                w = 1 if e == "sp" else window
                seen = 0
                j = i
                cand = None
                dma_blocked = False
                while j < len(lst) and seen < w:
                    k = lst[j]
                    j += 1
                    if done[k]:
                        continue
                    seen += 1
                    isdma = ops[k].dma_sem is not None
                    if isdma and dma_blocked:
                        continue
                    if isdma:
                        dma_blocked = True
                    if ndeps[k] > 0:
                        continue
                    st = max(ready_t[k], efree[e])
                    if prio_cp:
                        key = (st, -tail[k], k)
                        if cand is None or key < cand:
                            cand = key
                    else:
                        key = (st, 0.0, k)
                        if cand is None or key < cand:
                            cand = key
                            if ready_t[k] <= efree[e]:
                                break
                if cand is not None and (best is None or cand < best[0]):
                    best = (cand, e)
            assert best is not None, "scheduler deadlock"
            (st, _pr, k), e = best
            op = ops[k]
            c = getattr(op.fn, "cost", 150.0)
            if op.dma_sem is not None:
                efree[e] = st + 60.0
                fin[k] = st + c
            else:
                efree[e] = st + c
                fin[k] = st + c
            done[k] = True
            remaining -= 1
            order[e].append(op)
            for u in users[k]:
                ndeps[u] -= 1
                t = fin[k] + lat
                if t > ready_t[u]:
                    ready_t[u] = t
        self.est_ns = max(fin) if n else 0.0
        if os.environ.get("MK_CRIT"):
            dist = [0.0] * n
            pred = [-1] * n
            for op in ops:
                c = getattr(op.fn, "cost", 150.0)
                best_t, best_p = 0.0, -1
                for (d, raw, val) in op.deps:
                    t = dist[d.idx] + lat
                    if t > best_t:
                        best_t, best_p = t, d.idx
                dist[op.idx] = best_t + c
                pred[op.idx] = best_p
            lim = self.fence["pe"].idx if self.fence else n
            k = max(range(lim), key=lambda i: dist[i])
            print("[crit] dependency-only critical path before fence:", round(dist[k] / 1000), "us")
            path = []
            while k >= 0:
                path.append(k)
                k = pred[k]
            path.reverse()
            import collections
            cnt = collections.Counter(ops[i].eng for i in path)
            print("[crit] path len", len(path), dict(cnt))
            self.crit_path = path
            mid = int(len(path) * float(os.environ.get("MK_CRITPOS", "0.5")))
            for i in path[mid:mid + 70]:
                print("[crit]  ", ops[i].eng, ops[i].wk, round(getattr(ops[i].fn, "cost", 150.0)))
        if os.environ.get("MK_SCHED_DBG"):
            B = 100000.0
            nb = int(self.est_ns // B) + 1
            busy = {e: [0.0] * nb for e in ENGS}
            for op in ops:
                c = getattr(op.fn, "cost", 150.0)
                if op.dma_sem is not None:
                    continue
                b = int((fin[op.idx] - c) // B)
                busy[op.eng][b] += c
            for b in range(nb):
                print(f"[sched] {b*100:6d}us " + " ".join(f"{e}:{busy[e][b]/B*100:5.1f}%" for e in ENGS if e != "sp"))
            if self.fence:
                print("[sched] fence done at", {e: round(fin[o.idx] / 1000) for e, o in self.fence.items()})
        return order

    def eval_order(self, order, lat):
        ops = self.ops
        fin = {}
        pos = {e: 0 for e in ENGS}
        efree = {e: 0.0 for e in ENGS}
        remaining = sum(len(v) for v in order.values())
        while remaining:
            progressed = False
            for e in ENGS:
                while pos[e] < len(order[e]):
                    op = order[e][pos[e]]
                    if any(d.idx not in fin for (d, raw, val) in op.deps):
                        break
                    rt = max([fin[d.idx] + lat for (d, raw, val) in op.deps] + [0.0])
                    st = max(rt, efree[e])
                    c = getattr(op.fn, "cost", 150.0)
                    if op.dma_sem is not None:
                        efree[e] = st + 60.0
                    else:
                        efree[e] = st + c
                    fin[op.idx] = st + c
                    pos[e] += 1
                    remaining -= 1
                    progressed = True
            assert progressed, "order deadlock"
        return max(fin.values())

    def emit(self, block, sems, dma_sems, reorder=True):
        for op in self.ops:
            for (d, raw, val) in op.deps:
                if d.dma_sem is None and not self._skip(d, op, raw):
                    d.need_sig = True
        if reorder:
            per_eng = self.schedule()
            if os.environ.get("MK_EVAL_LAT"):
                print("[kernel] eval fixed order @lat", os.environ["MK_EVAL_LAT"], self.eval_order(per_eng, float(os.environ["MK_EVAL_LAT"])))
        else:
            per_eng = {e: [] for e in ENGS}
            for op in self.ops:
                per_eng[op.eng].append(op)
        cnt = {e: 0 for e in ENGS}
        for e in ENGS:
            for op in per_eng[e]:
                if op.dma_sem is None and op.need_sig:
                    cnt[e] += 1
                    op.sig = cnt[e]
        handles = {"pe": "tensor", "act": "scalar", "dve": "vector", "pool": "gpsimd", "sp": "sync"}

        def body(eng_name):
            def _f(eng):
                waited = {}
                for op in per_eng[eng_name]:
                    for (d, raw, val) in op.deps:
                        if d.dma_sem is not None:
                            key = ("dma", d.dma_sem)
                            sem = dma_sems[d.dma_sem]
                        else:
                            if self._skip(d, op, raw):
                                continue
                            key = d.eng
                            val = d.sig
                            sem = sems[d.eng]
                        if waited.get(key, 0) >= val:
                            continue
                        waited[key] = val
                        eng.wait_ge(sem, val)
                    ins = op.fn(eng)
                    if op.dma_sem is not None:
                        ins.then_inc(dma_sems[op.dma_sem], 16)
                    elif op.need_sig:
                        ins.then_inc(sems[eng_name], 1)
            return _f

        for e in ENGS:
            if per_eng[e]:
                getattr(block, handles[e])(body(e))


def _mmcost(lhsT, rhs):
    n = rhs.free_size()
    c = max(lhsT.free_size() / 1.2, n / 2.37, 30.0)
    if rhs.dtype == F32:
        c *= 4.0
    return c


def _fsz(ap):
    return ap.free_size()


def _mm(out, lhsT, rhs, start=True, stop=True):
    f = lambda e: e.matmul(out, lhsT=lhsT, rhs=rhs, start=start, stop=stop)
    f.cost = _mmcost(lhsT, rhs)
    return f


def _mmk(out, pairs):
    def f(e):
        n = len(pairs)
        ins = None
        for i, (l, r) in enumerate(pairs):
            ins = e.matmul(out, lhsT=l, rhs=r, start=(i == 0), stop=(i == n - 1))
        return ins
    f.cost = sum(_mmcost(l, r) for (l, r) in pairs)
    return f


def _trs(items, ident):
    def f(e):
        ins = None
        for (o, i) in items:
            ins = e.transpose(out=o, in_=i, identity=ident)
        return ins
    f.cost = 120.0 * len(items)
    return f


def _act(out, in_, func, bias=None, scale=None, accum=None):
    def f(e):
        kw = {}
        if bias is not None:
            kw["bias"] = bias
        if scale is not None:
            kw["scale"] = scale
        if accum is not None:
            kw["accum_out"] = accum
        return e.activation(out=out, in_=in_, func=func, **kw)
    f.cost = (224.0 + _fsz(out)) / 1.2
    return f


def _ts(out, in0, s1, op0, s2=None, op1=None):
    def f(e):
        if op1 is None:
            return e.tensor_scalar(out=out, in0=in0, scalar1=s1, scalar2=None, op0=op0)
        return e.tensor_scalar(out=out, in0=in0, scalar1=s1, scalar2=s2, op0=op0, op1=op1)
    f.cost = (100.0 + _fsz(out)) / 0.96
    return f


def _tt(out, in0, in1, op):
    f = lambda e: e.tensor_tensor(out=out, in0=in0, in1=in1, op=op)
    f.cost = (100.0 + _fsz(out)) / 0.96
    return f


def _stt(out, in0, scalar, in1, op0, op1):
    f = lambda e: e.scalar_tensor_tensor(out=out, in0=in0, scalar=scalar, in1=in1, op0=op0, op1=op1)
    f.cost = (100.0 + _fsz(out)) / 0.96
    return f


def _cp(out, in_):
    f = lambda e: e.tensor_copy(out=out, in_=in_)
    f.cost = (100.0 + _fsz(out)) / 0.96
    return f


def _dma(out, in_):
    f = lambda e: e.dma_start(out=out, in_=in_)
    f.cost = 2000.0 + 128.0 * _fsz(out) * 4 / 200.0
    return f


def _scan(out, d0, d1, init, op0, op1):
    f = lambda e: e.tensor_tensor_scan(out=out, data0=d0, data1=d1, initial=init, op0=op0, op1=op1)
    f.cost = (100.0 + 2 * _fsz(out)) / 0.96
    return f


class _Arena:
    def __init__(self, nc, base, end):
        self.nc = nc
        self.off = base
        self.end = end

    def alloc(self, name, shape, dt):
        size = 1
        for s in shape[1:]:
            size *= s
        size *= 2 if dt == BF16 else 4
        off = (self.off + 31) // 32 * 32
        assert off + size <= self.end, f"SBUF overflow at {name}: {off + size} > {self.end}"
        t = self.nc.alloc_sbuf_tensor_at(name, list(shape), dt, offset=off)
        self.off = off + size
        return t.ap()


WGROUPS = [(1024, 2056), (512, 1024), (5640, 6152), (3080, 4104), (0, 512), (5128, 5640), (2056, 3080), (4104, 5128)]


def _wgroup_of(col):
    for g, (a, b) in enumerate(WGROUPS):
        if a <= col < b:
            return g
    raise ValueError(col)


def build_program():
    nc = bass.Bass("TRN2", target_bir_lowering=False)

    def din(name, shape, dt=F32):
        return nc.dram_tensor(name, list(shape), dt, kind="ExternalInput").ap()

    xs_d = din("xs", [NPOS, D])
    win_d = din("w_in_r", [D, WCOLS])
    wout_d = din("w_out", [2048, D])
    wup_d = din("w_up", [D, DFF])
    wgate_d = din("w_gate", [D, DFF])
    wdn_d = din("w_down", [DFF, D])
    cs_d = din("cs_tab", [NCH, 128, 1024])
    vm_d = din("vm_tab", [NCH, 4, 2, 128])
    dqk_d = din("dqk", [128, 12])
    cmask_d = din("cmask", [128, 128])
    mneg_d = din("maskneg4", [128, 128])
    identb_d = din("ident_bf", [128, 128], BF16)
    identf_d = din("ident_f", [128, 128])
    i4_d = din("i4", [4, 8])
    convml_d = din("convw_ml", [128, 32])
    convff_d = din("convw_ffn", [128, NJ * 3])
    bif_d = din("b_if", [4, 2])
    gcol_d = din("gcol", [128, 16])
    gmix_d = din("g_mix_b", [128, D])
    gffn_d = din("g_ffn_b", [128, D])
    gfin_d = din("g_fin_b", [128, D])
    out_d = nc.dram_tensor("out", [2048, D], F32, kind="ExternalOutput").ap()
    mixed_d = nc.dram_tensor("mixed_d", [NCH_F * CH, 2048], BF16,
                             kind="ExternalOutput" if DEBUG else "Internal").ap()

    S = Sched()
    banks = [nc.alloc_psum_tensor(f"bank{i}", [128, 512], F32).ap() for i in range(8)]

    per = _Arena(nc, SBUF_BASE, SBUF_END)
    ident_bf = per.alloc("ident_bf", [128, 128], BF16)
    mhalf = per.alloc("mhalf", [128, 4], F32)
    stat = per.alloc("stat", [128, 8], F32)
    PH_BASE = per.off

    S.add("sp", _dma(ident_bf, identb_d), writes=["ident_bf"], dma="cst")
    S.add("pool", lambda e: e.memset(mhalf, -0.5), writes=["mhalf"])

    A1 = _Arena(nc, PH_BASE, SBUF_END)
    W = A1.alloc("W", [128, 8, WCOLS], BF16)
    ident_f = A1.alloc("ident_f", [128, 128], F32)
    cmask = A1.alloc("cmask", [128, 128], F32)
    maskneg4 = A1.alloc("maskneg4", [128, 128], F32)
    dqk = A1.alloc("dqk", [128, 12], F32)
    g_mix_b = A1.alloc("g_mix_b", [128, D], F32)
    convml = A1.alloc("convml", [128, 32], F32)
    i4 = A1.alloc("i4", [4, 8], F32)
    bif = A1.alloc("bif", [4, 2], F32)
    bsc = A1.alloc("bsc", [4, 2], F32)
    ones4 = A1.alloc("ones4", [4, 128], F32)
    zeros4 = A1.alloc("zeros4", [4, 128], F32)
    xin = A1.alloc("xin", [128, D], F32)
    u = A1.alloc("u", [128, D], BF16)
    uT = [A1.alloc(f"uT{i}", [128, 8, 128], BF16) for i in range(2)]
    asb = A1.alloc("asb", [128, 8, 131], F32)
    cacc = [A1.alloc(f"cacc{i}", [128, 128], F32) for i in range(2)]
    qTmls = [A1.alloc(f"qTml{i}", [128, 4, 128], BF16) for i in range(2)]
    kTmls = [A1.alloc(f"kTml{i}", [128, 4, 128], BF16) for i in range(2)]
    rX = [A1.alloc("rX0", [128, 512], F32)] * 2
    rM = [A1.alloc(f"rM{i}", [128, 256], F32) for i in range(4)]
    qtok = A1.alloc("qtok", [128, 4, 128], BF16)
    ktok = A1.alloc("ktok", [128, 4, 128], BF16)
    cst = [A1.alloc("cst0", [128, 1024], F32)] * 2
    vmt = [A1.alloc("vmt0", [4, 2, 128], F32)] * 2
    qTrts = [A1.alloc(f"qTrt{i}", [128, 4, 128], BF16) for i in range(2)]
    kTrts = [A1.alloc(f"kTrt{i}", [128, 4, 128], BF16) for i in range(2)]
    kw = A1.alloc("kw", [128, 4, 128], BF16)
    Vmls = [A1.alloc(f"Vml{i}", [128, 4, 257], BF16) for i in range(2)]
    Vrts = [A1.alloc(f"Vrt{i}", [128, 4, 256], BF16) for i in range(2)]
    ogs = [A1.alloc(f"og{i}", [128, D], F32) for i in range(2)]
    ggs = [A1.alloc(f"gg{i}", [128, D], F32) for i in range(2)]
    mixed = A1.alloc("mixed", [128, 2048], BF16)
    Cml_f = A1.alloc("Cml_f", [128, 4, 257], F32)
    Cml_bfs = [A1.alloc(f"Cml_bf{i}", [128, 4, 257], BF16) for i in range(2)]
    Crt_f = A1.alloc("Crt_f", [128, 4, 256], F32)
    Crt_bfs = [A1.alloc(f"Crt_bf{i}", [128, 4, 256], BF16) for i in range(2)]
    WT = A1.alloc("WT", [128, 4, 128], F32)
    PT = A1.alloc("PT", [128, 4, 128], BF16)
    Wint = A1.alloc("Wint", [128, 512], F32)
    qsT = A1.alloc("qsT", [128, 4, 128], BF16)
    hrt = A1.alloc("hrt", [128, 4, 256], F32)
    hraw = A1.alloc("hraw", [128, 4, 257], F32)
    rows = {n: A1.alloc("row_" + n, [4, 128], F32) for n in
            ("li0", "li1", "li", "ef", "sp", "nbcum", "B", "M", "R2", "R3")}
    rhs_bd = A1.alloc("rhs_bd", [4, 4, 128], F32)
    smr = A1.alloc("smr", [4, 16], F32)
    sm = A1.alloc("sm", [128, 64], F32)
    smps = A1.alloc("smps", [128, 20], F32)
    st6 = A1.alloc("st6", [128, 4, 6], F32)
    mv = A1.alloc("mv", [128, 4, 2], F32)

    EX8 = sm[:, 0:8]
    BT = sm[:, 8:12]
    BTS = sm[:, 12:16]
    WSARG = sm[:, 16:20]
    WSRC = sm[:, 20:24]
    DEC = sm[:, 24:28]
    DD = sm[:, 28:32]
    RDEN = sm[:, 32:36]
    T1 = sm[:, 36:40]
    T2 = sm[:, 40:44]
    RSTD = sm[:, 44:48]
    SC = sm[:, 48:52]
    BI = sm[:, 52:56]

    bT = banks[0]
    bT_bf = bT.bitcast(BF16)

    for nm, dst, src in (("ident_f", ident_f, identf_d), ("cmask", cmask, cmask_d), ("maskneg4", maskneg4, mneg_d),
                         ("dqk", dqk, dqk_d), ("g_mix_b", g_mix_b, gmix_d),
                         ("convml", convml, convml_d), ("i4", i4, i4_d), ("bif", bif, bif_d)):
        S.add("sp", _dma(dst, src), writes=[nm], dma="cst")
    for g, (c0, c1) in enumerate(WGROUPS):
        for k in range(8):
            S.add("pool", _dma(W[:, k, c0:c1], win_d[k * 128:(k + 1) * 128, c0:c1]),
                  writes=[("W", g, k)], dma=f"W{g}")

    def wk(col):
        g = _wgroup_of(col)
        return [("W", g, k) for k in range(8)]

    S.add("pool", lambda e: e.memset(ones4, 1.0), writes=["ones4"])
    S.add("pool", lambda e: e.memset(zeros4, 0.0), writes=["zeros4"])
    S.add("pool", lambda e: e.memset(asb, 0.0), writes=[("asb", t) for t in range(8)])
    S.add("pool", lambda e: e.memset(Vmls[0], 1.0), writes=[("Vml", 0, 0), ("Vml", 0, 1)])
    S.add("pool", lambda e: e.memset(Vmls[1], 1.0), writes=[("Vml", 1, 0), ("Vml", 1, 1)])
    S.add("pool", lambda e: e.memset(Cml_f, 0.0), writes=[("Cml_f", h) for h in range(4)])
    S.add("pool", lambda e: e.memset(Cml_bfs[0], 0.0), writes=[("Cml_bf", 0, h) for h in range(4)])
    S.add("pool", lambda e: e.memset(Crt_f, 0.0), writes=[("Crt_f", h) for h in range(4)])
    S.add("pool", lambda e: e.memset(Crt_bfs[0], 0.0), writes=[("Crt_bf", 0, h) for h in range(4)])
    S.add("pool", lambda e: e.memset(smr, 0.0), writes=["mst", "D1", "diagM", "diagD"])
    S.add("pool", lambda e: e.memset(smr[:, 0:1], NEG), writes=["mst"])
    S.add("dve", _ts(bsc[:, 0:1], bif[:, 0:1], 1.0 / GATE_CAP, ALU.mult), reads=["bif"], writes=["bsc0"])
    S.add("dve", _ts(bsc[:, 1:2], bif[:, 1:2], -1.0, ALU.mult), reads=["bif"], writes=["bsc1"])

    MST = smr[:, 0:1]
    D1 = smr[:, 1:2]
    DIAGM = smr[:, 4:8]
    DIAGD = smr[:, 8:12]

    def loads(c):
        s = c % 2
        S.add("sp", _dma(xin, xs_d[c * 128:(c + 1) * 128, :]), writes=["xin"], dma="xin")

    def loads2(c):
        S.add("sp", _dma(cst[0], cs_d[c]), writes=[("cst", 0)], dma="cst0")
        S.add("sp", _dma(vmt[0], vm_d[c]), writes=[("vmt", 0)], dma="vmt0")

    aslot = [0]

    def next_aslot():
        i = aslot[0] % 4
        aslot[0] += 1
        return i

    tmslot = [0]

    def rmsnorm_T(src, gb, dstT, dst_keys, src_key, gkey):
        S.add("act", _act(u, src, AF.Square, accum=stat[:, 0:1]), reads=[src_key], writes=["u", "ss"])
        S.add("dve", _ts(stat[:, 1:2], stat[:, 0:1], 1.0 / D, ALU.mult, EPS, ALU.add), reads=["ss"], writes=["ms"])
        S.add("pool", _tt(stat[:, 2:3], stat[:, 1:2], mhalf[:, 0:1], ALU.pow), reads=["ms", "mhalf"], writes=["rstd"])
        S.add("dve", _stt(u, src, stat[:, 2:3], gb, ALU.mult, ALU.mult), reads=[src_key, "rstd", gkey], writes=["u"])
        S.add("pe", _trs([(bT_bf[:, k * 128:(k + 1) * 128], u[:, k * 128:(k + 1) * 128]) for k in range(8)], ident_bf),
              reads=["u", "ident_bf"], writes=[("bk", 0)])
        S.add("act", _act(dstT, bT_bf.rearrange("p (k t) -> p k t", k=8), AF.Copy), reads=[("bk", 0)], writes=dst_keys)


    fmb = [0]

    def chunk(c, full):
        rp, wp = c % 2, (c + 1) % 2
        qTml, kTml, qTrt, kTrt = qTmls[rp], kTmls[rp], qTrts[rp], kTrts[rp]
        og, gg = ogs[rp], ggs[rp]
        Vml, Vrt = Vmls[rp], Vrts[rp]
        Cml_bf, Crt_bf = Cml_bfs[rp], Crt_bfs[rp]
        Cml_bfw, Crt_bfw = Cml_bfs[wp], Crt_bfs[wp]
        if os.environ.get("MK_FAKE2"):
            S.fake_par = c % int(os.environ["MK_FAKE2"])
        s = c % 2
        uTs = uT[s]
        rmsnorm_T(xin, g_mix_b, uTs, [("uT", s)], "xin", "g_mix_b")
        if c + 1 < NCH:
            loads(c + 1)

        def fm_group(cols_ms):
            b = 1 + fmb[0] % 2
            fmb[0] += 1
            bk = banks[b]
            for sl, (col, m) in enumerate(cols_ms):
                S.add("pe", _mmk(bk[0:m, sl * 128:(sl + 1) * 128], [(W[:, k, col:col + m], uTs[:, k, :]) for k in range(8)]),
                      reads=wk(col) + [("uT", s)], writes=[("bk", b)])
            return bk, ("bk", b)

        for grp in ((0, 1) if full else (1,)):
            bk, bkey = fm_group([((grp * 4 + t) * 128, 128) for t in range(4)])
            S.add("act", _act(asb[:, grp * 4:grp * 4 + 4, 3:131], bk.rearrange("p (a b) -> p a b", a=4), AF.Copy),
                  reads=[bkey], writes=[("asb", grp * 4 + t) for t in range(4)])
            for t in range(grp * 4, grp * 4 + 4):
                acc = cacc[t % 2]
                ak = ("cacc", t % 2)
                S.add("dve", _ts(acc, asb[:, t, 3:131], convml[:, t * 4 + 3:t * 4 + 4], ALU.mult),
                      reads=[("asb", t), "convml"], writes=[ak])
                for kk in (2, 1, 0):
                    S.add("dve", _stt(acc, asb[:, t, kk:kk + 128], convml[:, t * 4 + kk:t * 4 + kk + 1], acc, ALU.mult, ALU.add),
                          reads=[("asb", t), "convml", ak], writes=[ak])
                S.add("pool", _cp(asb[:, t, 0:3], asb[:, t, 128:131]), reads=[("asb", t)], writes=[("asb", t)])
                if t < 4:
                    S.add("act", _act(qTml[:, t, :], acc, AF.Silu), reads=[ak], writes=[("qTml", rp, t)])
                else:
                    S.add("act", _act(kTml[:, t - 4, :], acc, AF.Silu), reads=[ak], writes=[("kTml", rp, t - 4)])
        bk, gkey = fm_group([(1024, 4), (1028, 4)])
        gi_reg = bk[0:4, 0:128]
        gf_reg = bk[0:4, 128:256]
        R = rows
        vs = vmt[s]
        S.add("act", _act(R["li0"], gi_reg, AF.Tanh, bias=bsc[:, 0:1], scale=1.0 / GATE_CAP), reads=[gkey, "bsc0"], writes=["li0"])
        S.add("act", _act(R["ef"], gf_reg, AF.Exp, bias=bsc[:, 1:2], scale=-1.0), reads=[gkey, "bsc1"], writes=["ef"])
        S.add("dve", _stt(R["li1"], R["li0"], GATE_CAP, vs[:, 0, :], ALU.mult, ALU.mult), reads=["li0", ("vmt", 0)], writes=["li1"])
        S.add("dve", _tt(R["li"], R["li1"], vs[:, 1, :], ALU.add), reads=["li1", ("vmt", 0)], writes=["li"])
        S.add("act", _act(R["sp"], R["ef"], AF.Ln, bias=1.0), reads=["ef"], writes=["sp"])
        S.add("dve", _scan(R["nbcum"], R["sp"], zeros4, 0.0, ALU.add, ALU.add), reads=["sp", "zeros4"], writes=["nbcum"])
        S.add("dve", _tt(R["B"], R["li"], R["nbcum"], ALU.add), reads=["li", "nbcum"], writes=["B"])
        S.add("dve", _scan(R["M"], R["B"], R["B"], MST, ALU.max, ALU.max), reads=["B", "mst"], writes=["M"])
        S.add("dve", _ts(DIAGM, i4[:, 0:4], R["M"][:, 127:128], ALU.mult), reads=["i4", "M"], writes=["diagM"])
        S.add("dve", _tt(D1, MST, R["M"][:, 127:128], ALU.subtract), reads=["mst", "M"], writes=["D1"])
        S.add("dve", _ts(DIAGD, i4[:, 0:4], D1, ALU.mult), reads=["i4", "D1"], writes=["diagD"])
        if full:
            S.add("dve", _tt(R["R2"], R["nbcum"], R["M"], ALU.subtract), reads=["nbcum", "M"], writes=["R2"])
            S.add("dve", _ts(R["R2"], R["R2"], 80.0, ALU.min), reads=["R2"], writes=["R2"])
            S.add("dve", _ts(R["R3"], R["M"], MST, ALU.subtract, -1.0, ALU.mult), reads=["M", "mst"], writes=["R3"])
            for h in range(4):
                S.add("dve", _ts(rhs_bd[:, h, :], R["M"], i4[:, 4 + h:5 + h], ALU.mult), reads=["M", "i4"], writes=[("rhs_bd", h)])
        S.add("dve", _tt(MST, R["M"][:, 127:128], R["nbcum"][:, 127:128], ALU.subtract), reads=["M", "nbcum"], writes=["mst"])

        sb_ = 5
        SMP = banks[sb_][:, 0:32]
        skey = ("bk", sb_)

        def smp_mm(e):
            ins = e.matmul(SMP[:, 0:4], lhsT=R["B"], rhs=i4[:, 0:4], start=True, stop=True)
            if full:
                e.matmul(SMP[:, 4:8], lhsT=R["R2"], rhs=i4[:, 0:4], start=True, stop=True)
                e.matmul(SMP[:, 8:12], lhsT=R["R3"], rhs=i4[:, 0:4], start=True, stop=True)
            e.matmul(SMP[:, 12:16], lhsT=ones4, rhs=DIAGM, start=True, stop=True)
            ins = e.matmul(SMP[:, 16:20], lhsT=ones4, rhs=DIAGD, start=True, stop=True)
            return ins
        S.add("pe", smp_mm, reads=["B", "R2", "R3", "i4", "ones4", "diagM", "diagD"], writes=[skey])
        if full:
            S.add("act", _act(smps, SMP[:, 0:20], AF.Copy), reads=[skey], writes=["smps"])
        else:
            S.add("act", _act(smps[:, 0:4], SMP[:, 0:4], AF.Copy), reads=[skey], writes=["smps"])
            S.add("act", _act(smps[:, 12:20], SMP[:, 12:20], AF.Copy), reads=[skey], writes=["smps"])
        if full:
            S.add("act", _act(EX8, smps[:, 4:12], AF.Exp), reads=["smps"], writes=["ex8"])
        S.add("dve", _tt(WSARG, smps[:, 0:4], smps[:, 12:16], ALU.subtract), reads=["smps"], writes=["wsarg"])
        S.add("act", _act(WSRC, WSARG, AF.Exp, bias=float(np.log(S_ML))), reads=["wsarg"], writes=["wsrc"])
        S.add("act", _act(DEC, smps[:, 16:20], AF.Exp), reads=["smps"], writes=["dec"])
        b5 = banks[5]
        k5 = ("bk", 5)
        if full:
            S.add("dve", _ts(BTS, smps[:, 0:4], float(np.log(S_ML)), ALU.add), reads=["smps"], writes=["BTS"])
            S.add("pe", _mmk(b5, [(ones4, rhs_bd.rearrange("p h t -> p (h t)")), (ident_f, maskneg4.unsqueeze(1).broadcast_to([128, 4, 128]))]),
                  reads=["ones4", "ident_f", "maskneg4"] + [("rhs_bd", h) for h in range(4)], writes=[k5])
            for h in range(4):
                S.add("act", _act(WT[:, h, :], b5[:, h * 128:(h + 1) * 128], AF.Exp, bias=BTS[:, h:h + 1]),
                      reads=[k5, "BTS"], writes=[("WT", h)])
            for h in range(4):
                S.add("dve", _ts(rhs_bd[:, h, :], R["R3"], i4[:, h:h + 1], ALU.mult), reads=["R3", "i4"], writes=[("rhs_bd", h)])
            S.add("pe", _mm(b5, ones4, rhs_bd.rearrange("p h t -> p (h t)")),
                  reads=["ones4"] + [("rhs_bd", h) for h in range(4)], writes=[k5])
            S.add("act", _act(Wint, b5, AF.Exp), reads=[k5], writes=["Wint"])
            S.add("dve", _tt(qsT.rearrange("p h t -> p (h t)"), qTml.rearrange("p h t -> p (h t)"), Wint, ALU.mult),
                  reads=["Wint"] + [("qTml", rp, h) for h in range(4)], writes=["qsT"])

        def tm_tile(col):
            b = 3 + tmslot[0] % 2
            tmslot[0] += 1
            S.add("pe", _mmk(banks[b], [(uTs[:, k, :], W[:, k, col:col + 512]) for k in range(8)]),
                  reads=wk(col) + [("uT", s)], writes=[("bk", b)])
            return banks[b], ("bk", b)

        for hh in range(2):
            reg, rk = tm_tile(1032 + hh * 512)
            S.add("act", _act(Vml[:, 2 * hh:2 * hh + 2, 0:256], reg.rearrange("p (a b) -> p a b", a=2), AF.Copy),
                  reads=[rk], writes=[("Vml", rp, hh)])
        for hh in range(2):
            reg, rk = tm_tile(3080 + hh * 512)
            S.add("dve", _cp(Vrt[:, 2 * hh:2 * hh + 2, :], reg.rearrange("p (a b) -> p a b", a=2)),
                  reads=[rk], writes=[("Vrt", rp, hh)])
        if full:
            for hh in range(2):
                reg, rk = tm_tile(2056 + hh * 512)
                S.add("act", _act(og[:, hh * 512:(hh + 1) * 512], reg, AF.Tanh, scale=0.5), reads=[rk], writes=[("og", rp, hh)])
            for hh in range(2):
                reg, rk = tm_tile(4104 + hh * 512)
                S.add("act", _act(gg[:, hh * 512:(hh + 1) * 512], reg, AF.Silu), reads=[rk], writes=[("gg", rp, hh)])

        def rotary(col, xi, qk, dst, dkey):
            reg, rk = tm_tile(col)
            X = rX[xi]
            xk = ("rX", 0)
            S.add("act", _act(X, reg, AF.Copy), reads=[rk], writes=[xk])
            Xv = X.rearrange("p (h a t) -> p h a t", h=4, a=2)
            Tc = cst[0][:, (qk * 2) * 256:(qk * 2 + 1) * 256].rearrange("p (h t) -> p h t", h=4)
            Ts = cst[0][:, (qk * 2 + 1) * 256:(qk * 2 + 2) * 256].rearrange("p (h t) -> p h t", h=4)
            Mv = [m.rearrange("p (h t) -> p h t", h=4) for m in rM]
            Dv = dst.rearrange("p h (a t) -> p h a t", a=2)
            S.add("dve", _tt(Mv[0], Xv[:, :, 0, :], Tc, ALU.mult), reads=[xk, ("cst", 0)], writes=[("rM", 0)])
            S.add("dve", _tt(Mv[1], Xv[:, :, 1, :], Ts, ALU.mult), reads=[xk, ("cst", 0)], writes=[("rM", 1)])
            S.add("pool", _tt(Mv[2], Xv[:, :, 0, :], Ts, ALU.mult), reads=[xk, ("cst", 0)], writes=[("rM", 2)])
            S.add("pool", _tt(Mv[3], Xv[:, :, 1, :], Tc, ALU.mult), reads=[xk, ("cst", 0)], writes=[("rM", 3)])
            S.add("dve", _tt(Dv[:, :, 0, :], Mv[0], Mv[1], ALU.subtract), reads=[("rM", 0), ("rM", 1)], writes=[(dkey, 0)])
            S.add("pool", _tt(Dv[:, :, 1, :], Mv[2], Mv[3], ALU.add), reads=[("rM", 2), ("rM", 3)], writes=[(dkey, 1)])

        rotary(5640, 0, 1, ktok, "ktok")
        if full and not os.environ.get("MK_X1"):
            rotary(5128, 1, 0, qtok, "qtok")
        if c + 1 < NCH:
            loads2(c + 1)

        ob = [6]

        def next_ob():
            b = 6 + ob[0] % 2
            ob[0] += 1
            return banks[b], ("bk", b)

        kb_, kbk = next_ob()
        b0bf = kb_.bitcast(BF16)
        S.add("pe", _trs([(b0bf[:, t * 128:(t + 1) * 128], kTml[:, t, :]) for t in range(4)], ident_bf),
              reads=[("kTml", rp, h) for h in range(4)] + ["ident_bf"], writes=[kbk])
        for t in range(4):
            S.add("act", _act(kw[:, t, :], b0bf[:, t * 128:(t + 1) * 128], AF.Copy, scale=WSRC[:, t:t + 1]),
                  reads=[kbk, "wsrc"], writes=[("kw", t)])
        if full:
            qb_, qbk = next_ob()
            qbbf = qb_.bitcast(BF16)
            S.add("pe", _trs([(qbbf[:, h * 128:(h + 1) * 128], ktok[:, h, :]) for h in range(4)] +
                             [(qbbf[:, (4 + h) * 128:(5 + h) * 128], qtok[:, h, :]) for h in range(4)], ident_bf),
                  reads=[("ktok", 0), ("ktok", 1), ("qtok", 0), ("qtok", 1), "ident_bf"], writes=[qbk])
            S.add("act", _act(kTrt, qbbf[:, 0:512].rearrange("p (h t) -> p h t", h=4), AF.Copy), reads=[qbk],
                  writes=[("kTrt", rp, h) for h in range(4)])
            S.add("act", _act(qTrt, qbbf[:, 512:1024].rearrange("p (h t) -> p h t", h=4), AF.Copy), reads=[qbk],
                  writes=[("qTrt", rp, h) for h in range(4)])

        for h in range(4):
            bo, ko = next_ob()
            S.add("pe", _mm(bo[:, 0:257], kw[:, h, :], Vml[:, h, :]), reads=[("kw", h), ("Vml", rp, h // 2)], writes=[ko])
            S.add("dve", _stt(Cml_f[:, h, :], Cml_f[:, h, :], DEC[:, h:h + 1], bo[:, 0:257], ALU.mult, ALU.add),
                  reads=[("Cml_f", h), "dec", ko], writes=[("Cml_f", h)])
            S.add("pool", _cp(Cml_bfw[:, h, :], Cml_f[:, h, :]), reads=[("Cml_f", h)], writes=[("Cml_bf", wp, h)])
        for pr in range(2):
            bo, ko = next_ob()
            for hh in range(2):
                h = 2 * pr + hh
                S.add("pe", _mm(bo[:, hh * 256:(hh + 1) * 256], ktok[:, h, :], Vrt[:, h, :]), reads=[("ktok", 0), ("ktok", 1), ("Vrt", rp, pr)], writes=[ko])
            for hh in range(2):
                h = 2 * pr + hh
                S.add("dve", _stt(Crt_f[:, h, :], Crt_f[:, h, :], CD[h], bo[:, hh * 256:(hh + 1) * 256], ALU.mult, ALU.add),
                      reads=[("Crt_f", h), ko], writes=[("Crt_f", h)])
            for hh in range(2):
                h = 2 * pr + hh
                S.add("pool", _ts(Crt_bfw[:, h, :], Crt_f[:, h, :], CD[h], ALU.mult, 0.0, ALU.add),
                      reads=[("Crt_f", h)], writes=[("Crt_bf", wp, h)])
        if full:
            S.add("pe", lambda e: [e.matmul(b5[:, h * 128:(h + 1) * 128], lhsT=kTml[:, h, :], rhs=qTml[:, h, :], start=True, stop=True)
                                   for h in range(4)][-1],
                  reads=[("kTml", rp, h) for h in range(4)] + [("qTml", rp, h) for h in range(4)], writes=[k5])
            S.add("dve", _tt(PT.rearrange("p h t -> p (h t)"), b5, WT.rearrange("p h t -> p (h t)"), ALU.mult),
                  reads=[k5] + [("WT", h) for h in range(4)], writes=["PT"])
            for h in range(4):
                bo, ko = next_ob()
                S.add("pe", _mmk(bo[:, 0:257], [(PT[:, h, :], Vml[:, h, :]), (qsT[:, h, :], Cml_bf[:, h, :])]),
                      reads=["PT", ("Vml", rp, h // 2), "qsT", ("Cml_bf", rp, h)], writes=[ko])
                S.add("act", _act(hraw[:, h, :], bo[:, 0:257], AF.Copy), reads=[ko], writes=[("hraw", h)])
                S.add("dve", lambda e, h=h: e.bn_stats(out=st6[:, h, :], in_=hraw[:, h, 0:256]), reads=[("hraw", h)], writes=[("st6", h)])
                S.add("dve", lambda e, h=h: e.bn_aggr(out=mv[:, h, :], in_=st6[:, h, :]), reads=[("st6", h)], writes=[("mv", h)])
            allh = [("hraw", h) for h in range(4)]
            allmv = [("mv", h) for h in range(4)]
            S.add("dve", _ts(T2, hraw[:, :, 256], -1.0, ALU.mult), reads=allh, writes=["t2"])
            S.add("dve", _tt(DD, T2, hraw[:, :, 256], ALU.max), reads=allh + ["t2"], writes=["dd"])
            S.add("dve", _tt(DD, DD, EX8[:, 0:4], ALU.max), reads=["dd", "ex8"], writes=["dd"])
            S.add("dve", lambda e: e.reciprocal(out=RDEN, in_=DD), reads=["dd"], writes=["rden"])
            S.add("dve", _tt(T1, RDEN, RDEN, ALU.mult), reads=["rden"], writes=["t1"])
            S.add("dve", _tt(T2, T1, mv[:, :, 1], ALU.mult), reads=["t1"] + allmv, writes=["t2"])
            S.add("dve", _ts(T1, T2, EPS, ALU.add), reads=["t2"], writes=["t1"])
            S.add("pool", _tt(RSTD, T1, mhalf, ALU.pow), reads=["t1", "mhalf"], writes=["rstdh"])
            S.add("dve", _tt(SC, RDEN, RSTD, ALU.mult), reads=["rden", "rstdh"], writes=["sc"])
            S.add("dve", _stt(BI, mv[:, :, 0], -1.0, SC, ALU.mult, ALU.mult), reads=allmv + ["sc"], writes=["bi"])
            for h in range(4):
                S.add("act", _act(hraw[:, h, 0:256], hraw[:, h, 0:256], AF.Identity, bias=BI[:, h:h + 1], scale=SC[:, h:h + 1]),
                      reads=[("hraw", h), "sc", "bi"], writes=[("hraw", h)])
            S.add("dve", _stt(mixed[:, 0:1024].rearrange("p (h v) -> p h v", h=4), og.rearrange("p (h v) -> p h v", h=4), 1.0,
                              hraw[:, :, 0:256], ALU.add, ALU.mult),
                  reads=allh + [("og", rp, 0), ("og", rp, 1)], writes=[("mixed", 0)])
            S.add("pe", lambda e: [e.matmul(b5[:, h * 128:(h + 1) * 128], lhsT=kTrt[:, h, :], rhs=qTrt[:, h, :], start=True, stop=True)
                                   for h in range(4)][-1],
                  reads=[("kTrt", rp, h) for h in range(4)] + [("qTrt", rp, h) for h in range(4)], writes=[k5])
            S.add("dve", _tt(PT, b5.rearrange("p (h t) -> p h t", h=4), cmask.unsqueeze(1).broadcast_to([128, 4, 128]), ALU.mult),
                  reads=[k5, "cmask"], writes=["PT"])
            for pr in range(2):
                bo, ko = next_ob()
                for hh in range(2):
                    h = 2 * pr + hh
                    S.add("pe", _mmk(bo[:, hh * 256:(hh + 1) * 256], [(PT[:, h, :], Vrt[:, h, :]), (qTrt[:, h, :], Crt_bf[:, h, :])]),
                          reads=["PT", ("Vrt", rp, pr), ("qTrt", rp, h), ("Crt_bf", rp, h)], writes=[ko])
                S.add("act", _act(hrt[:, 2 * pr:2 * pr + 2, :], bo.rearrange("p (a b) -> p a b", a=2), AF.Copy),
                      reads=[ko], writes=[("hrt", 2 * pr), ("hrt", 2 * pr + 1)])
                for hh in range(2):
                    h = 2 * pr + hh
                    S.add("dve", lambda e, h=h: e.bn_stats(out=st6[:, h, :], in_=hrt[:, h, :]), reads=[("hrt", h)], writes=[("st6", h)])
                    S.add("dve", lambda e, h=h: e.bn_aggr(out=mv[:, h, :], in_=st6[:, h, :]), reads=[("st6", h)], writes=[("mv", h)])
            allr = [("hrt", h) for h in range(4)]
            S.add("dve", _ts(T1, mv[:, :, 1], EPS, ALU.add), reads=allmv, writes=["t1"])
            S.add("pool", _tt(RSTD, T1, mhalf, ALU.pow), reads=["t1", "mhalf"], writes=["rstdh"])
            S.add("dve", _stt(BI, mv[:, :, 0], -1.0, RSTD, ALU.mult, ALU.mult), reads=allmv + ["rstdh"], writes=["bi"])
            for h in range(4):
                S.add("act", _act(hrt[:, h, :], hrt[:, h, :], AF.Identity, bias=BI[:, h:h + 1], scale=RSTD[:, h:h + 1]),
                      reads=[("hrt", h), "rstdh", "bi"], writes=[("hrt", h)])
            S.add("dve", _tt(mixed[:, 1024:2048], hrt.rearrange("p h v -> p (h v)"), gg, ALU.mult),
                  reads=allr + [("gg", rp, 0), ("gg", rp, 1)], writes=[("mixed", 1)])
        if full:
            f = c - NCH_P
            S.add("act", _dma(mixed_d[f * 128:(f + 1) * 128, :], mixed), reads=[("mixed", 0), ("mixed", 1)],
                  writes=[("mixed_d", f)], dma="mxst")


    loads(0)
    loads2(0)
    _stop = int(os.environ.get("MK_STOP_AFTER", str(NCH)))
    for c in range(min(NCH, _stop)):
        chunk(c, c >= NCH_P)
    if _stop < 100 and os.environ.get("MK_STOP_AFTER"):
        S.fake_par = None
        S.add("sp", lambda e: e.nop(), reads=S.all_keys())
        with ExitStack() as es:
            sems = {e: es.enter_context(nc.semaphore("s_" + e)) for e in ENGS}
            dsems = {k: es.enter_context(nc.semaphore("d_" + k)) for k in S.dma_counts}
            block = es.enter_context(nc.Block())
            S.emit(block, sems, dsems)
        return nc

    S.fake_par = None
    A3 = _Arena(nc, PH_BASE, SBUF_END)
    Wup = A3.alloc("Wup", [128, 8, DFF], BF16)
    Wout = A3.alloc("Wout", [128, 16, D], BF16)
    xh = A3.alloc("xh", [128, 4, D], F32)
    gcol = A3.alloc("gcol", [128, 16], F32)
    assert A3.off <= PH_BASE + 8 * WCOLS * 2, "early F3 weights must alias W only"
    allW = [("W", g, k) for g in range(len(WGROUPS)) for k in range(8)]
    S.add("pool", _dma(Wup[:, 0, :], wup_d[0:128, :]), writes=[("Wup", 0)] + allW, dma="Wup")
    for k in range(1, 8):
        S.add("pool", _dma(Wup[:, k, :], wup_d[k * 128:(k + 1) * 128, :]), reads=[("Wup", 0)], writes=[("Wup", k)], dma="Wup")
    S.add("sp", _dma(gcol, gcol_d), reads=[("Wup", 0)], writes=["gcol"], dma="gcol")
    S.add("dve", _ts(gcol[:, 0:8], gcol[:, 0:8], 0.5, ALU.mult), reads=["gcol"], writes=["gcol"])
    for k in range(16):
        sl = k % 4
        S.add("sp", _dma(xh[:, sl, :], wout_d[k * 128:(k + 1) * 128, :]), reads=[("Wup", 0)], writes=[("xh", sl)], dma=f"xh{sl}")
        S.add("act" if k % 2 else "dve",
              (_act(Wout[:, k, :], xh[:, sl, :], AF.Copy, scale=gcol[:, k:k + 1]) if k % 2 else
               _ts(Wout[:, k, :], xh[:, sl, :], gcol[:, k:k + 1], ALU.mult)),
              reads=[("xh", sl), "gcol", ("Wup", 0)], writes=[("Wout", k)])
    S.barrier(skip=("Wup", "Wout", "W"))
    NWD = 10
    Wgt = A3.alloc("Wgt", [128, 8, DFF], BF16)
    Wring = A3.alloc("Wring", [128, NWD, D], BF16)
    mtoks = [A3.alloc(f"mtok{i}", [128, 2048], BF16) for i in range(2)]
    mixedT = A3.alloc("mixedT", [128, 16, 256], BF16)
    u2Ts = [A3.alloc(f"u2T{i}", [128, 8, 256], BF16) for i in range(2)]
    u3s = [A3.alloc(f"u3_{i}", [128, D], BF16) for i in range(2)]
    NB3 = int(os.environ.get("MK_NB3", "4"))
    asb3s = [A3.alloc(f"asb3_{i}", [128, 258], F32) for i in range(NB3)]
    acc3 = [A3.alloc(f"acc3_{i}", [128, 256], F32) for i in range(NB3)]
    actT = [A3.alloc(f"actT{i}", [128, 256], BF16) for i in range(NB3)]
    halo = A3.alloc("halo", [128, NJ, 2], F32)
    convff = A3.alloc("convff", [128, NJ * 3], F32)
    g_ffn_b = A3.alloc("g_ffn_b", [128, D], F32)
    g_fin_b = A3.alloc("g_fin_b", [128, D], F32)
    stat2 = A3.alloc("stat2", [128, 16], F32)

    print("[kernel] sbuf A1 end", A1.off, "A3 end", A3.off, "limit", SBUF_END)
    S.add("sp", _dma(convff, convff_d), writes=["convff"], dma="cst")
    S.add("sp", _dma(g_ffn_b, gffn_d), writes=["g_ffn_b"], dma="cst")
    S.add("sp", _dma(g_fin_b, gfin_d), writes=["g_fin_b"], dma="cst")
    S.add("pool", lambda e: e.memset(halo, 0.0), writes=[("halo", j) for j in range(NJ)])
    for k in range(8):
        S.add("pool", _dma(Wgt[:, k, :], wgate_d[k * 128:(k + 1) * 128, :]), writes=[("Wgt", k)], dma="Wgt")

    b_acc = [banks[0], banks[1], banks[2], banks[3]]
    ybk = None
    agrot = [0]
    NAG = int(os.environ.get("MK_NAG", "3"))
    b_y = [banks[4 + NAG], banks[7]] if NAG < 3 else [banks[7], banks[7]]
    ybk = [4 + NAG, 7] if NAG < 3 else [7, 7]
    wdc = [0]
    strot = [0]
    mtc = [0]

    def rms_small(src, skey):
        i = strot[0] % 4
        strot[0] += 1
        return stat2[:, 4 * i:4 * i + 1], stat2[:, 4 * i + 1:4 * i + 2], stat2[:, 4 * i + 2:4 * i + 3], i

    def f3_pre(f0, ntt, is_halo, bp):
        T = ntt * 128
        u2T = u2Ts[bp]
        xsl = [bp * 2 + tt for tt in range(ntt)]
        for tt in range(ntt):
            f = f0 + tt
            p0 = (NCH_P + f) * 128
            S.add("sp", _dma(xh[:, xsl[tt], :], xs_d[p0:p0 + 128, :]), writes=[("xh", xsl[tt])], dma=f"xh{xsl[tt]}")
        for tt in range(ntt):
            f = f0 + tt
            mi = mtc[0] % 2
            mtc[0] += 1
            mtok = mtoks[mi]
            S.add("sp", _dma(mtok, mixed_d[f * 128:(f + 1) * 128, :]), reads=[("mixed_d", f)], writes=[("mtok", mi)], dma=f"mtok{mi}")
            for half in range(2):
                yb = b_y[half].bitcast(BF16)
                S.add("pe", _trs([(yb[:, kk * 128:(kk + 1) * 128], mtok[:, (half * 8 + kk) * 128:(half * 8 + kk + 1) * 128])
                                  for kk in range(8)], ident_bf), reads=[("mtok", mi), "ident_bf"], writes=[("bk", ybk[half])])
                S.add("act" if half else "dve",
                      (_act(mixedT[:, half * 8:half * 8 + 8, tt * 128:(tt + 1) * 128], yb.rearrange("p (k t) -> p k t", k=8), AF.Copy)
                       if half else _cp(mixedT[:, half * 8:half * 8 + 8, tt * 128:(tt + 1) * 128], yb.rearrange("p (k t) -> p k t", k=8))),
                      reads=[("bk", ybk[half])], writes=[("mixedT", tt, half)])
        for tt in range(ntt):
            xv = xh[:, xsl[tt], :]
            for half in range(2):
                S.add("pe", _mmk(b_y[half], [(mixedT[:, k, tt * 128:(tt + 1) * 128], Wout[:, k, half * 512:(half + 1) * 512])
                                             for k in range(16)]),
                      reads=[("mixedT", tt, 0), ("mixedT", tt, 1)] + [("Wout", k) for k in range(16)], writes=[("bk", ybk[half])])
                S.add("dve", _tt(xv[:, half * 512:(half + 1) * 512], b_y[half], xv[:, half * 512:(half + 1) * 512], ALU.add),
                      reads=[("bk", ybk[half]), ("xh", xsl[tt])], writes=[("xh", xsl[tt])])
        for tt in range(ntt):
            xv = xh[:, xsl[tt], :]
            xk = ("xh", xsl[tt])
            ss, ms, rs, si = rms_small(None, None)
            u3 = u3s[tt % 2]
            uk = ("u3", tt % 2)
            S.add("act", _act(u3, xv, AF.Square, accum=ss), reads=[xk], writes=[uk, ("ss", si)])
            S.add("dve", _ts(ms, ss, 1.0 / D, ALU.mult, EPS, ALU.add), reads=[("ss", si)], writes=[("ms", si)])
            S.add("pool", _tt(rs, ms, mhalf[:, 0:1], ALU.pow), reads=[("ms", si), "mhalf"], writes=[("rs", si)])
            S.add("dve", _stt(u3, xv, rs, g_ffn_b, ALU.mult, ALU.mult), reads=[xk, ("rs", si), "g_ffn_b"], writes=[uk])
            yb = b_y[tt % 2].bitcast(BF16)
            S.add("pe", _trs([(yb[:, k * 128:(k + 1) * 128], u3[:, k * 128:(k + 1) * 128]) for k in range(8)], ident_bf),
                  reads=[uk, "ident_bf"], writes=[("bk", ybk[tt % 2])])
            S.add("act", _act(u2T[:, :, tt * 128:(tt + 1) * 128], yb.rearrange("p (k t) -> p k t", k=8), AF.Copy),
                  reads=[("bk", ybk[tt % 2])], writes=[("u2T", bp, tt)])

    def f3_main(f0, ntt, is_halo, bp):
        T = ntt * 128
        u2T = u2Ts[bp]
        xsl = [bp * 2 + tt for tt in range(ntt)]
        u2k = [("u2T", bp, tt) for tt in range(ntt)]
        for j in range(NJ):
            par = agrot[0] % NB3
            agb = 4 + (agrot[0] % NAG)
            agrot[0] += 1
            aT = banks[agb][:, 0:T]
            gT = banks[agb][:, 256:256 + T]
            if not is_halo:
                ws = wdc[0] % NWD
                wdc[0] += 1
                S.add("pool", _dma(Wring[:, ws, :], wdn_d[j * 128:(j + 1) * 128, :]), writes=[("Wring", ws)], dma=f"Wd{ws}")
            S.add("pe", _mmk(aT, [(Wup[:, k, j * 128:(j + 1) * 128], u2T[:, k, 0:T]) for k in range(8)]),
                  reads=u2k + [("Wup", k) for k in range(8)], writes=[("bk", agb)])
            if not is_halo:
                S.add("pe", _mmk(gT, [(Wgt[:, k, j * 128:(j + 1) * 128], u2T[:, k, 0:T]) for k in range(8)]),
                      reads=u2k + [("Wgt", k) for k in range(8)], writes=[("bk", agb)])
            asb3 = asb3s[par]
            S.add("pool", _cp(asb3[:, 0:2], halo[:, j, :]), reads=[("halo", j)], writes=[("asb3h", par)])
            S.add("act", _act(asb3[:, 2:2 + T], aT, AF.Copy), reads=[("bk", agb)], writes=[("asb3", par)])
            S.add("pool", _cp(halo[:, j, :], asb3[:, T:T + 2]), reads=[("asb3", par)], writes=[("halo", j)])
            if is_halo:
                continue
            acc = acc3[par][:, 0:T]
            ak = ("acc3", par)
            S.add("dve", _ts(acc, asb3[:, 2:2 + T], convff[:, j * 3 + 2:j * 3 + 3], ALU.mult), reads=[("asb3", par), "convff"], writes=[ak])
            S.add("dve", _stt(acc, asb3[:, 1:1 + T], convff[:, j * 3 + 1:j * 3 + 2], acc, ALU.mult, ALU.add),
                  reads=[("asb3", par), ("asb3h", par), "convff", ak], writes=[ak])
            S.add("dve", _stt(acc, asb3[:, 0:T], convff[:, j * 3:j * 3 + 1], acc, ALU.mult, ALU.add),
                  reads=[("asb3", par), ("asb3h", par), "convff", ak], writes=[ak])
            S.add("act", _act(acc, acc, AF.Silu), reads=[ak], writes=[ak])
            at = actT[par][:, 0:T]
            S.add("dve", _tt(at, acc, gT, ALU.mult), reads=[ak, ("bk", agb)], writes=[("actT", par)])
            for tt in range(ntt):
                for half in range(2):
                    S.add("pe", _mm(b_acc[tt * 2 + half], at[:, tt * 128:(tt + 1) * 128], Wring[:, ws, half * 512:(half + 1) * 512],
                                    start=(j == 0), stop=(j == NJ - 1)),
                          reads=[("actT", par), ("Wring", ws)], writes=[("bk", tt * 2 + half)])
        if is_halo:
            return
        for tt in range(ntt):
            f = f0 + tt
            xv = xh[:, xsl[tt], :]
            xk = ("xh", xsl[tt])
            for half in range(2):
                S.add("dve", _tt(xv[:, half * 512:(half + 1) * 512], b_acc[tt * 2 + half], xv[:, half * 512:(half + 1) * 512], ALU.add),
                      reads=[("bk", tt * 2 + half), xk], writes=[xk])
            ss, ms, rs, si = rms_small(None, None)
            u3 = u3s[tt % 2]
            uk = ("u3", tt % 2)
            S.add("act", _act(u3, xv, AF.Square, accum=ss), reads=[xk], writes=[uk, ("ss", si)])
            S.add("dve", _ts(ms, ss, 1.0 / D, ALU.mult, EPS, ALU.add), reads=[("ss", si)], writes=[("ms", si)])
            S.add("pool", _tt(rs, ms, mhalf[:, 0:1], ALU.pow), reads=[("ms", si), "mhalf"], writes=[("rs", si)])
            S.add("dve", _stt(xv, xv, rs, g_fin_b, ALU.mult, ALU.mult), reads=[xk, ("rs", si), "g_fin_b"], writes=[xk])
            S.add("act", _dma(out_d[(f - 1) * 128:f * 128, :], xv), reads=[xk], writes=[("out", f)], dma=f"ost{xsl[tt]}")

    blocks = [(0, 1, True, 1)] + [(1 + 2 * b, 2, False, b % 2) for b in range(8)]
    f3_pre(*blocks[0])
    for bi, blk in enumerate(blocks):
        if bi + 1 < len(blocks):
            f3_pre(*blocks[bi + 1])
        f3_main(*blk)

    fin = S.add("sp", lambda e: e.nop(), reads=[("out", f) for f in range(1, NCH_F)])

    with ExitStack() as es:
        sems = {e: es.enter_context(nc.semaphore("s_" + e)) for e in ENGS}
        dsems = {k: es.enter_context(nc.semaphore("d_" + k)) for k in S.dma_counts}
        block = es.enter_context(nc.Block())
        S.emit(block, sems, dsems, reorder=bool(int(os.environ.get("MK_REORDER", "1"))))
        print("[kernel] est_ns", getattr(S, "est_ns", None), "ops", len(S.ops))
    return nc


def _host_consts():
    idx = np.arange(128, dtype=np.float64)
    dqk = np.zeros((128, 12), np.float32)
    for h in range(4):
        dqk[:, h] = np.exp(LOG_GAMMA[h] * (idx + 1.0))
        dqk[:, 4 + h] = S_ML * np.exp(-LOG_GAMMA[h] * (idx + 1.0))
        dqk[:, 8 + h] = S_ML * np.exp(-LOG_GAMMA[h] * (idx + 1.0)) * np.exp(LOG_GAMMA[h] * CH)
    jj, ii = np.meshgrid(np.arange(128), np.arange(128), indexing="ij")
    cm = (jj <= ii).astype(np.float32)
    mneg = np.where(jj <= ii, 0.0, NEG).astype(np.float32)
    i4 = np.concatenate([np.eye(4), -np.eye(4)], axis=1).astype(np.float32)
    return dict(dqk=dqk, cmask=cm, maskneg4=mneg,
                ident_bf=np.eye(128).astype(ml_dtypes.bfloat16), ident_f=np.eye(128, dtype=np.float32), i4=i4)


def _rope_tables(n_null):
    p = np.arange(NPOS, dtype=np.float64)
    pos = np.where(p >= n_null, 48.0 + (p - n_null), 0.0)
    inv = 10000.0 ** (-np.arange(0, 128, 2, dtype=np.float64) / 128.0)
    ang = pos[:, None] * inv[None, :]
    cosr = np.cos(ang).reshape(NCH, 128, 1, 64)
    sinr = np.sin(ang).reshape(NCH, 128, 1, 64)
    idx = np.arange(128, dtype=np.float64)
    dq = np.stack([np.exp(LOG_GAMMA[h] * (idx + 1.0)) for h in range(4)], axis=1)[None, :, :, None]
    dk = np.stack([S_ML * np.exp(-LOG_GAMMA[h] * (idx + 1.0)) for h in range(4)], axis=1)[None, :, :, None]
    tab = np.stack([cosr * dq, sinr * dq, cosr * dk, sinr * dk], axis=2)
    tab = np.ascontiguousarray(tab.reshape(NCH, 128, 1024)).astype(np.float32)
    valid = (p >= n_null).astype(np.float32).reshape(NCH, 128)
    vm = np.stack([valid, (valid - 1.0) * 1e30], axis=1)
    vm = np.ascontiguousarray(np.broadcast_to(vm[:, None], (NCH, 4, 2, 128))).astype(np.float32)
    return tab, vm


def _prep_inputs(inputs):
    f = lambda k: np.asarray(inputs[k], dtype=np.float32)
    x = f("x")
    meta = f("meta_tokens")
    w_in = f("w_in")[0]
    sizes = [512, 512, 1024, 1024, 4, 4, 512, 512, 1024, 1024]
    offs = np.cumsum(sizes)[:-1]
    ml_q, ml_k, ml_v, ml_o, ml_i, ml_f, rt_q, rt_k, rt_v, rt_g = np.split(w_in, offs, axis=1)

    def swap(cols):
        return cols.reshape(D, 4, 2, 64)[:, :, ::-1, :].reshape(D, 512)

    w_in_r = np.ascontiguousarray(np.concatenate(
        [ml_q, ml_k, ml_i, ml_f, ml_v, ml_o, rt_v, rt_g, rt_q, rt_k], axis=1))
    assert w_in_r.shape == (D, WCOLS)
    convml = f("ml_conv_w")[0]
    convw_ml = np.ascontiguousarray(convml.reshape(4, 8, 128).transpose(2, 1, 0).reshape(128, 32))
    convff = f("ffn_conv_w")[0]
    convw_ffn = np.ascontiguousarray(convff.reshape(3, NJ, 128).transpose(2, 1, 0).reshape(128, NJ * 3))
    b_if = np.ascontiguousarray(np.stack([f("ml_b_i")[0], f("ml_b_f")[0]], axis=1))
    gcat = np.concatenate([f("ml_norm_g")[0], f("rt_norm_g")[0]])
    gcol = np.ascontiguousarray(gcat.reshape(16, 128).T)
    bc = lambda v: np.ascontiguousarray(np.broadcast_to(v[None, :], (128, D)))
    common = dict(w_in_r=w_in_r, w_out=f("w_out")[0], w_up=f("w_up")[0], w_gate=f("w_gate")[0], w_down=f("w_down")[0],
                  convw_ml=convw_ml, convw_ffn=convw_ffn, b_if=b_if, gcol=gcol,
                  g_mix_b=bc(f("norm_mix_g")[0]), g_ffn_b=bc(f("norm_ffn_g")[0]), g_fin_b=bc(f("norm_final_g")))
    common.update(_host_consts())
    tabs = [_rope_tables(2160), _rope_tables(112)]
    in_maps = []
    for core in range(8):
        b, t = core // 2, core % 2
        xs = np.zeros((NPOS, D), np.float32)
        if t == 0:
            xs[2160:2176] = meta
            xs[2176:] = x[b, 0:2048]
        else:
            xs[112:128] = meta
            xs[128:] = x[b]
        m = dict(common)
        m["xs"] = xs
        m["cs_tab"], m["vm_tab"] = tabs[t]
        in_maps.append(m)
    return in_maps


_NC_CACHE = {}


def kernel(**inputs):
    in_maps = _prep_inputs(inputs)
    if "nc" not in _NC_CACHE:
        _NC_CACHE["nc"] = build_program()
    nc = _NC_CACHE["nc"]
    res = run_bass_kernel_spmd(nc, in_maps, core_ids=list(range(8)))
    out = np.zeros((4, 4096, D), np.float32)
    for core in range(8):
        b, t = core // 2, core % 2
        out[b, t * 2048:(t + 1) * 2048] = res.results[core]["out"]
    if DEBUG:
        kernel.debug = [res.results[c]["mixed_d"] for c in range(8)]
    return out
```

```python
import os
from contextlib import ExitStack

import numpy as np
import ml_dtypes

import concourse.bass as bass
import concourse.mybir as mybir
from concourse.bass_utils import run_bass_kernel_spmd

F32 = mybir.dt.float32
BF16 = mybir.dt.bfloat16
ALU = mybir.AluOpType
AF = mybir.ActivationFunctionType

NCH_P = 16
NCH_F = 17
NCH = NCH_P + NCH_F
CH = 128
NPOS = NCH * CH
D = 1024
DFF = 2816
NJ = DFF // 128
WCOLS = 6152
EPS = 1e-6
NEG = -1e30
GATE_CAP = 15.0
SBUF_BASE = 16640
SBUF_END = 229376
S_ML = 128.0 ** -0.5
LOG_GAMMA = [float(np.log1p(-2.0 ** (-(5.0 + h)))) for h in range(4)]
CD = [float(np.exp(lg * CH)) for lg in LOG_GAMMA]

ENGS = ("pe", "act", "dve", "pool", "sp")
DEBUG = bool(int(os.environ.get("MK_DEBUG", "0")))


class _Op:
    __slots__ = ("eng", "fn", "deps", "dma_sem", "dma_cnt", "sig", "idx", "need_sig", "wk")

    def __init__(self, eng, fn):
        self.eng = eng
        self.fn = fn
        self.deps = []
        self.dma_sem = None
        self.dma_cnt = 0
        self.sig = 0
        self.need_sig = False


class Sched:
    def __init__(self):
        self.ops = []
        self.last_w = {}
        self.readers = {}
        self.dma_counts = {}
        self.fence = None

    fake_par = None
    FAKE_KEEP = ("bk", "W", "Cml_f", "Cml_bf", "Crt_f", "Crt_bf", "asb", "mixed_d", "cst", "vmt", "uT")

    def _fk(self, keys):
        if self.fake_par is None:
            return keys
        out = []
        for k in keys:
            base = k[0] if isinstance(k, tuple) else k
            if base == "bk" and os.environ.get("MK_FAKEBK"):
                out.append((k, self.fake_par))
                continue
            if base in self.FAKE_KEEP or base in ("mst", "ident_bf", "mhalf", "i4", "ones4", "zeros4", "convml", "dqk", "cmask",
                                                   "maskneg4", "ident_f", "g_mix_b", "bsc0", "bsc1", "bif"):
                out.append(k)
            else:
                out.append((k, self.fake_par))
        return out

    def add(self, eng, fn, reads=(), writes=(), dma=None):
        reads = self._fk(reads)
        writes = self._fk(writes)
        op = _Op(eng, fn)
        op.wk = list(writes)[:2]
        op.idx = len(self.ops)
        deps = {}

        def dep(o, raw):
            val = self.dma_counts[o.dma_sem] if o.dma_sem is not None else 0
            if o.idx in deps:
                if raw and not deps[o.idx][1]:
                    deps[o.idx] = (o, True, val)
            else:
                deps[o.idx] = (o, raw, val)

        for r in reads:
            w = self.last_w.get(r)
            if w is not None:
                dep(w, True)
        for k in writes:
            w = self.last_w.get(k)
            if w is not None:
                dep(w, False)
            for rd in self.readers.get(k, ()):
                dep(rd, False)
        if self.fence is not None:
            dep(self.fence[eng], False)
        op.deps = list(deps.values())
        for r in reads:
            self.readers.setdefault(r, []).append(op)
        for k in writes:
            self.last_w[k] = op
            self.readers[k] = []
        if dma is not None:
            op.dma_sem = dma
            self.dma_counts[dma] = self.dma_counts.get(dma, 0) + 16
            op.dma_cnt = self.dma_counts[dma]
        self.ops.append(op)
        return op

    def all_keys(self):
        return list(set(self.last_w.keys()) | set(self.readers.keys()))

    def barrier(self, skip=()):
        keys = [k for k in self.all_keys() if not (isinstance(k, tuple) and k[0] in skip)]
        fence = {}
        for e in ENGS:
            fence[e] = self.add(e, lambda eng: eng.nop(), writes=keys)
        self.fence = fence

    @staticmethod
    def _skip(d, op, raw):
        if d.dma_sem is not None or op.dma_sem is not None:
            return False
        if d.eng != op.eng:
            return False
        return d.eng == "pe"

    def schedule(self, window=int(os.environ.get("MK_WIN", "200")), lat=float(os.environ.get("MK_LAT", "800"))):
        ops = self.ops
        n = len(ops)
        ndeps = [0] * n
        users = [[] for _ in range(n)]
        for op in ops:
            ndeps[op.idx] = len(op.deps)
            for (d, raw, val) in op.deps:
                users[d.idx].append(op.idx)
        fin = [0.0] * n
        ready_t = [0.0] * n
        prio_cp = os.environ.get("MK_PRIO", "cp") == "cp"
        tail = [0.0] * n
        if prio_cp:
            for op in reversed(ops):
                c = getattr(op.fn, "cost", 150.0)
                t = 0.0
                tl = lat * float(os.environ.get("MK_TAILF", "1.0"))
                for u in users[op.idx]:
                    if tail[u] + tl > t:
                        t = tail[u] + tl
                tail[op.idx] = t + c * float(os.environ.get("MK_COSTF", "1.0"))
        pend = {e: [op.idx for op in ops if op.eng == e] for e in ENGS}
        pos = {e: 0 for e in ENGS}
        done = [False] * n
        efree = {e: 0.0 for e in ENGS}
        order = {e: [] for e in ENGS}
        cur_tab = [None]
        nsw = [0]
        TABSW = float(os.environ.get("MK_TABSW", "1000"))
        remaining = n
        while remaining:
            best = None
            for e in ENGS:
                lst = pend[e]
                i = pos[e]
                while i < len(lst) and done[lst[i]]:
                    i += 1
                pos[e] = i
                if i >= len(lst):
                    continue
                w = 1 if e == "sp" else window
                seen = 0
                j = i
                cand = None
                dma_blocked = False
                while j < len(lst) and seen < w:
                    k = lst[j]
                    j += 1
                    if done[k]:
                        continue
                    seen += 1
                    isdma = ops[k].dma_sem is not None
                    if isdma and dma_blocked:
                        continue
                    if isdma:
                        dma_blocked = True
                    if ndeps[k] > 0:
                        continue
                    st = max(ready_t[k], efree[e])
                    if e == "act" and TABSW > 0.0:
                        tb = getattr(ops[k].fn, "tab", None)
                        if tb is not None and tb != cur_tab[0]:
                            st = st + TABSW
                    if prio_cp:
                        key = (st, -tail[k], k)
                        if cand is None or key < cand:
                            cand = key
                    else:
                        key = (st, 0.0, k)
                        if cand is None or key < cand:
                            cand = key
                            if ready_t[k] <= efree[e]:
                                break
                if cand is not None and (best is None or cand < best[0]):
                    best = (cand, e)
            assert best is not None, "scheduler deadlock"
            (st, _pr, k), e = best
            op = ops[k]
            c = getattr(op.fn, "cost", 150.0)
            if e == "act":
                tb = getattr(op.fn, "tab", None)
                if tb is not None and tb != cur_tab[0]:
                    cur_tab[0] = tb
                    nsw[0] += 1
            if op.dma_sem is not None:
                efree[e] = st + 60.0
                fin[k] = st + c
            else:
                efree[e] = st + c
                fin[k] = st + c
            done[k] = True
            remaining -= 1
            order[e].append(op)
            for u in users[k]:
                ndeps[u] -= 1
                t = fin[k] + lat
                if t > ready_t[u]:
                    ready_t[u] = t
        self.est_ns = max(fin) if n else 0.0
        self.n_tabsw = nsw[0]
        if os.environ.get("MK_CRIT"):
            dist = [0.0] * n
            pred = [-1] * n
            for op in ops:
                c = getattr(op.fn, "cost", 150.0)
                best_t, best_p = 0.0, -1
                for (d, raw, val) in op.deps:
                    t = dist[d.idx] + lat
                    if t > best_t:
                        best_t, best_p = t, d.idx
                dist[op.idx] = best_t + c
                pred[op.idx] = best_p
            lim = self.fence["pe"].idx if self.fence else n
            k = max(range(lim), key=lambda i: dist[i])
            print("[crit] dependency-only critical path before fence:", round(dist[k] / 1000), "us")
            path = []
            while k >= 0:
                path.append(k)
                k = pred[k]
            path.reverse()
            import collections
            cnt = collections.Counter(ops[i].eng for i in path)
            print("[crit] path len", len(path), dict(cnt))
            self.crit_path = path
            mid = int(len(path) * float(os.environ.get("MK_CRITPOS", "0.5")))
            for i in path[mid:mid + 70]:
                print("[crit]  ", ops[i].eng, ops[i].wk, round(getattr(ops[i].fn, "cost", 150.0)))
        if os.environ.get("MK_SCHED_DBG"):
            B = 100000.0
            nb = int(self.est_ns // B) + 1
            busy = {e: [0.0] * nb for e in ENGS}
            for op in ops:
                c = getattr(op.fn, "cost", 150.0)
                if op.dma_sem is not None:
                    continue
                b = int((fin[op.idx] - c) // B)
                busy[op.eng][b] += c
            for b in range(nb):
                print(f"[sched] {b*100:6d}us " + " ".join(f"{e}:{busy[e][b]/B*100:5.1f}%" for e in ENGS if e != "sp"))
            if self.fence:
                print("[sched] fence done at", {e: round(fin[o.idx] / 1000) for e, o in self.fence.items()})
        return order

    def eval_order(self, order, lat):
        ops = self.ops
        fin = {}
        pos = {e: 0 for e in ENGS}
        efree = {e: 0.0 for e in ENGS}
        ecur = [None]
        esw = [0]
        self._esw = esw
        remaining = sum(len(v) for v in order.values())
        while remaining:
            progressed = False
            for e in ENGS:
                while pos[e] < len(order[e]):
                    op = order[e][pos[e]]
                    if any(d.idx not in fin for (d, raw, val) in op.deps):
                        break
                    rt = max([fin[d.idx] + lat for (d, raw, val) in op.deps] + [0.0])
                    st = max(rt, efree[e])
                    c = getattr(op.fn, "cost", 150.0)
                    if e == "act":
                        tb = getattr(op.fn, "tab", None)
                        if tb is not None and tb != ecur[0]:
                            ecur[0] = tb
                            st += 1283.0
                            esw[0] += 1
                    if op.dma_sem is not None:
                        efree[e] = st + 60.0
                    else:
                        efree[e] = st + c
                    fin[op.idx] = st + c
                    pos[e] += 1
                    remaining -= 1
                    progressed = True
            assert progressed, "order deadlock"
        return max(fin.values())

    def emit(self, block, sems, dma_sems, reorder=True):
        for op in self.ops:
            for (d, raw, val) in op.deps:
                if d.dma_sem is None and not self._skip(d, op, raw):
                    d.need_sig = True
        if reorder:
            per_eng = self.schedule()
            if os.environ.get("MK_EVAL_LAT"):
                print("[kernel] eval fixed order @lat", os.environ["MK_EVAL_LAT"], self.eval_order(per_eng, float(os.environ["MK_EVAL_LAT"])), "tab switches", self._esw[0])
        else:
            per_eng = {e: [] for e in ENGS}
            for op in self.ops:
                per_eng[op.eng].append(op)
        cnt = {e: 0 for e in ENGS}
        for e in ENGS:
            for op in per_eng[e]:
                if op.dma_sem is None and op.need_sig:
                    cnt[e] += 1
                    op.sig = cnt[e]
        handles = {"pe": "tensor", "act": "scalar", "dve": "vector", "pool": "gpsimd", "sp": "sync"}

        def body(eng_name):
            def _f(eng):
                waited = {}
                for op in per_eng[eng_name]:
                    for (d, raw, val) in op.deps:
                        if d.dma_sem is not None:
                            key = ("dma", d.dma_sem)
                            sem = dma_sems[d.dma_sem]
                        else:
                            if self._skip(d, op, raw):
                                continue
                            key = d.eng
                            val = d.sig
                            sem = sems[d.eng]
                        if waited.get(key, 0) >= val:
                            continue
                        waited[key] = val
                        eng.wait_ge(sem, val)
                    ins = op.fn(eng)
                    if op.dma_sem is not None:
                        ins.then_inc(dma_sems[op.dma_sem], 16)
                    elif op.need_sig:
                        ins.then_inc(sems[eng_name], 1)
            return _f

        for e in ENGS:
            if per_eng[e]:
                getattr(block, handles[e])(body(e))


def _mmcost(lhsT, rhs):
    n = rhs.free_size()
    c = max(lhsT.free_size() / 1.2, n / 2.37, 30.0)
    if rhs.dtype == F32:
        c *= 4.0
    return c


def _fsz(ap):
    return ap.free_size()


def _mm(out, lhsT, rhs, start=True, stop=True):
    f = lambda e: e.matmul(out, lhsT=lhsT, rhs=rhs, start=start, stop=stop)
    f.cost = _mmcost(lhsT, rhs)
    return f


def _mmk(out, pairs):
    def f(e):
        n = len(pairs)
        ins = None
        for i, (l, r) in enumerate(pairs):
            ins = e.matmul(out, lhsT=l, rhs=r, start=(i == 0), stop=(i == n - 1))
        return ins
    f.cost = sum(_mmcost(l, r) for (l, r) in pairs)
    return f


def _trs(items, ident):
    def f(e):
        ins = None
        for (o, i) in items:
            ins = e.transpose(out=o, in_=i, identity=ident)
        return ins
    f.cost = 120.0 * len(items)
    return f


def _act(out, in_, func, bias=None, scale=None, accum=None):
    def f(e):
        kw = {}
        if bias is not None:
            kw["bias"] = bias
        if scale is not None:
            kw["scale"] = scale
        if accum is not None:
            kw["accum_out"] = accum
        return e.activation(out=out, in_=in_, func=func, **kw)
    f.cost = (224.0 + _fsz(out)) / 1.2
    f.tab = {AF.Silu: "S", AF.Tanh: "S", AF.Exp: "E", AF.Ln: "E"}.get(func)
    return f


def _ts(out, in0, s1, op0, s2=None, op1=None):
    def f(e):
        if op1 is None:
            return e.tensor_scalar(out=out, in0=in0, scalar1=s1, scalar2=None, op0=op0)
        return e.tensor_scalar(out=out, in0=in0, scalar1=s1, scalar2=s2, op0=op0, op1=op1)
    f.cost = (100.0 + _fsz(out)) / 0.96
    return f


def _tt(out, in0, in1, op):
    f = lambda e: e.tensor_tensor(out=out, in0=in0, in1=in1, op=op)
    f.cost = (100.0 + _fsz(out)) / 0.96
    return f


def _stt(out, in0, scalar, in1, op0, op1):
    f = lambda e: e.scalar_tensor_tensor(out=out, in0=in0, scalar=scalar, in1=in1, op0=op0, op1=op1)
    f.cost = (100.0 + _fsz(out)) / 0.96
    return f


def _cp(out, in_):
    f = lambda e: e.tensor_copy(out=out, in_=in_)
    f.cost = (100.0 + _fsz(out)) / 0.96
    return f


def _dma(out, in_):
    f = lambda e: e.dma_start(out=out, in_=in_)
    f.cost = 2000.0 + 128.0 * _fsz(out) * 4 / 200.0
    return f


def _scan(out, d0, d1, init, op0, op1):
    f = lambda e: e.tensor_tensor_scan(out=out, data0=d0, data1=d1, initial=init, op0=op0, op1=op1)
    f.cost = (100.0 + 2 * _fsz(out)) / 0.96
    return f


class _Arena:
    def __init__(self, nc, base, end):
        self.nc = nc
        self.off = base
        self.end = end

    def alloc(self, name, shape, dt):
        size = 1
        for s in shape[1:]:
            size *= s
        size *= 2 if dt == BF16 else 4
        off = (self.off + 31) // 32 * 32
        assert off + size <= self.end, f"SBUF overflow at {name}: {off + size} > {self.end}"
        t = self.nc.alloc_sbuf_tensor_at(name, list(shape), dt, offset=off)
        self.off = off + size
        return t.ap()


WGROUPS = [(1024, 2056), (512, 1024), (5640, 6152), (3080, 4104), (0, 512), (5128, 5640), (2056, 3080), (4104, 5128)]


def _wgroup_of(col):
    for g, (a, b) in enumerate(WGROUPS):
        if a <= col < b:
            return g
    raise ValueError(col)


def build_program():
    nc = bass.Bass("TRN2", target_bir_lowering=False)

    def din(name, shape, dt=F32):
        return nc.dram_tensor(name, list(shape), dt, kind="ExternalInput").ap()

    xs_d = din("xs", [NPOS, D])
    win_d = din("w_in_r", [D, WCOLS])
    wout_d = din("w_out", [2048, D])
    wup_d = din("w_up", [D, DFF])
    wgate_d = din("w_gate", [D, DFF])
    wdn_d = din("w_down", [DFF, D])
    cs_d = din("cs_tab", [NCH, 128, 1024])
    vm_d = din("vm_tab", [NCH, 4, 2, 128])
    dqk_d = din("dqk", [128, 12])
    cmask_d = din("cmask", [128, 128])
    mneg_d = din("maskneg4", [128, 128])
    identb_d = din("ident_bf", [128, 128], BF16)
    identf_d = din("ident_f", [128, 128])
    i4_d = din("i4", [4, 8])
    convml_d = din("convw_ml", [128, 32])
    convff_d = din("convw_ffn", [128, NJ * 3])
    bif_d = din("b_if", [4, 2])
    gcol_d = din("gcol", [128, 16])
    gmix_d = din("g_mix_b", [128, D])
    gffn_d = din("g_ffn_b", [128, D])
    gfin_d = din("g_fin_b", [128, D])
    out_d = nc.dram_tensor("out", [2048, D], F32, kind="ExternalOutput").ap()
    mixed_d = nc.dram_tensor("mixed_d", [NCH_F * CH, 2048], BF16,
                             kind="ExternalOutput" if DEBUG else "Internal").ap()

    S = Sched()
    banks = [nc.alloc_psum_tensor(f"bank{i}", [128, 512], F32).ap() for i in range(8)]

    per = _Arena(nc, SBUF_BASE, SBUF_END)
    ident_bf = per.alloc("ident_bf", [128, 128], BF16)
    mhalf = per.alloc("mhalf", [128, 4], F32)
    stat = per.alloc("stat", [128, 8], F32)
    PH_BASE = per.off

    S.add("sp", _dma(ident_bf, identb_d), writes=["ident_bf"], dma="cst")
    S.add("pool", lambda e: e.memset(mhalf, -0.5), writes=["mhalf"])

    A1 = _Arena(nc, PH_BASE, SBUF_END)
    W = A1.alloc("W", [128, 8, WCOLS], BF16)
    ident_f = A1.alloc("ident_f", [128, 128], F32)
    cmask = A1.alloc("cmask", [128, 128], F32)
    maskneg4 = A1.alloc("maskneg4", [128, 128], F32)
    dqk = A1.alloc("dqk", [128, 12], F32)
    g_mix_b = A1.alloc("g_mix_b", [128, D], F32)
    convml = A1.alloc("convml", [128, 32], F32)
    i4 = A1.alloc("i4", [4, 8], F32)
    bif = A1.alloc("bif", [4, 2], F32)
    bsc = A1.alloc("bsc", [4, 2], F32)
    ones4 = A1.alloc("ones4", [4, 128], F32)
    zeros4 = A1.alloc("zeros4", [4, 128], F32)
    xin = A1.alloc("xin", [128, D], F32)
    u = A1.alloc("u", [128, D], BF16)
    uT = [A1.alloc(f"uT{i}", [128, 8, 128], BF16) for i in range(2)]
    asb = A1.alloc("asb", [128, 8, 131], F32)
    cacc = [A1.alloc(f"cacc{i}", [128, 128], F32) for i in range(2)]
    qTmls = [A1.alloc(f"qTml{i}", [128, 4, 128], BF16) for i in range(2)]
    kTmls = [A1.alloc(f"kTml{i}", [128, 4, 128], BF16) for i in range(2)]
    rX = [A1.alloc("rX0", [128, 512], F32)] * 2
    rM = [A1.alloc(f"rM{i}", [128, 256], F32) for i in range(4)]
    qtok = A1.alloc("qtok", [128, 4, 128], BF16)
    ktok = A1.alloc("ktok", [128, 4, 128], BF16)
    cst = [A1.alloc("cst0", [128, 1024], F32)] * 2
    vmt = [A1.alloc("vmt0", [4, 2, 128], F32)] * 2
    qTrts = [A1.alloc(f"qTrt{i}", [128, 4, 128], BF16) for i in range(2)]
    kTrts = [A1.alloc(f"kTrt{i}", [128, 4, 128], BF16) for i in range(2)]
    kw = A1.alloc("kw", [128, 4, 128], BF16)
    Vmls = [A1.alloc(f"Vml{i}", [128, 4, 257], BF16) for i in range(2)]
    Vrts = [A1.alloc(f"Vrt{i}", [128, 4, 256], BF16) for i in range(2)]
    ogs = [A1.alloc(f"og{i}", [128, D], F32) for i in range(2)]
    ggs = [A1.alloc(f"gg{i}", [128, D], F32) for i in range(2)]
    mixed = A1.alloc("mixed", [128, 2048], BF16)
    Cml_f = A1.alloc("Cml_f", [128, 4, 257], F32)
    Cml_bfs = [A1.alloc(f"Cml_bf{i}", [128, 4, 257], BF16) for i in range(2)]
    Crt_f = A1.alloc("Crt_f", [128, 4, 256], F32)
    Crt_bfs = [A1.alloc(f"Crt_bf{i}", [128, 4, 256], BF16) for i in range(2)]
    WT = A1.alloc("WT", [128, 4, 128], F32)
    PT = A1.alloc("PT", [128, 4, 128], BF16)
    Wint = A1.alloc("Wint", [128, 512], F32)
    qsT = A1.alloc("qsT", [128, 4, 128], BF16)
    hrt = A1.alloc("hrt", [128, 4, 256], F32)
    hraw = A1.alloc("hraw", [128, 4, 257], F32)
    rows = {n: A1.alloc("row_" + n, [4, 128], F32) for n in
            ("li0", "li1", "li", "ef", "sp", "nbcum", "B", "M", "R2", "R3")}
    rhs_bd = A1.alloc("rhs_bd", [4, 4, 128], F32)
    smr = A1.alloc("smr", [4, 16], F32)
    sm = A1.alloc("sm", [128, 64], F32)
    smps = A1.alloc("smps", [128, 20], F32)
    st6 = A1.alloc("st6", [128, 4, 6], F32)
    mv = A1.alloc("mv", [128, 4, 2], F32)

    EX8 = sm[:, 0:8]
    BT = sm[:, 8:12]
    BTS = sm[:, 12:16]
    WSARG = sm[:, 16:20]
    WSRC = sm[:, 20:24]
    DEC = sm[:, 24:28]
    DD = sm[:, 28:32]
    RDEN = sm[:, 32:36]
    T1 = sm[:, 36:40]
    T2 = sm[:, 40:44]
    RSTD = sm[:, 44:48]
    SC = sm[:, 48:52]
    BI = sm[:, 52:56]

    bT = banks[0]
    bT_bf = bT.bitcast(BF16)

    for nm, dst, src in (("ident_f", ident_f, identf_d), ("cmask", cmask, cmask_d), ("maskneg4", maskneg4, mneg_d),
                         ("dqk", dqk, dqk_d), ("g_mix_b", g_mix_b, gmix_d),
                         ("convml", convml, convml_d), ("i4", i4, i4_d), ("bif", bif, bif_d)):
        S.add("sp", _dma(dst, src), writes=[nm], dma="cst")
    for g, (c0, c1) in enumerate(WGROUPS):
        for k in range(8):
            S.add("pool", _dma(W[:, k, c0:c1], win_d[k * 128:(k + 1) * 128, c0:c1]),
                  writes=[("W", g, k)], dma=f"W{g}")

    def wk(col):
        g = _wgroup_of(col)
        return [("W", g, k) for k in range(8)]

    S.add("pool", lambda e: e.memset(ones4, 1.0), writes=["ones4"])
    S.add("pool", lambda e: e.memset(zeros4, 0.0), writes=["zeros4"])
    S.add("pool", lambda e: e.memset(asb, 0.0), writes=[("asb", t) for t in range(8)])
    S.add("pool", lambda e: e.memset(Vmls[0], 1.0), writes=[("Vml", 0, 0), ("Vml", 0, 1)])
    S.add("pool", lambda e: e.memset(Vmls[1], 1.0), writes=[("Vml", 1, 0), ("Vml", 1, 1)])
    S.add("pool", lambda e: e.memset(Cml_f, 0.0), writes=[("Cml_f", h) for h in range(4)])
    S.add("pool", lambda e: e.memset(Cml_bfs[0], 0.0), writes=[("Cml_bf", 0, h) for h in range(4)])
    S.add("pool", lambda e: e.memset(Crt_f, 0.0), writes=[("Crt_f", h) for h in range(4)])
    S.add("pool", lambda e: e.memset(Crt_bfs[0], 0.0), writes=[("Crt_bf", 0, h) for h in range(4)])
    S.add("pool", lambda e: e.memset(smr, 0.0), writes=["mst", "D1", "diagM", "diagD"])
    S.add("pool", lambda e: e.memset(smr[:, 0:1], NEG), writes=["mst"])
    S.add("dve", _ts(bsc[:, 0:1], bif[:, 0:1], 1.0 / GATE_CAP, ALU.mult), reads=["bif"], writes=["bsc0"])
    S.add("dve", _ts(bsc[:, 1:2], bif[:, 1:2], -1.0, ALU.mult), reads=["bif"], writes=["bsc1"])

    MST = smr[:, 0:1]
    D1 = smr[:, 1:2]
    DIAGM = smr[:, 4:8]
    DIAGD = smr[:, 8:12]

    def loads(c):
        s = c % 2
        S.add("sp", _dma(xin, xs_d[c * 128:(c + 1) * 128, :]), writes=["xin"], dma="xin")

    def loads2(c):
        S.add("sp", _dma(cst[0], cs_d[c]), writes=[("cst", 0)], dma="cst0")
        S.add("sp", _dma(vmt[0], vm_d[c]), writes=[("vmt", 0)], dma="vmt0")

    aslot = [0]

    def next_aslot():
        i = aslot[0] % 4
        aslot[0] += 1
        return i

    tmslot = [0]

    def rmsnorm_T(src, gb, dstT, dst_keys, src_key, gkey):
        S.add("act", _act(u, src, AF.Square, accum=stat[:, 0:1]), reads=[src_key], writes=["u", "ss"])
        S.add("dve", _ts(stat[:, 1:2], stat[:, 0:1], 1.0 / D, ALU.mult, EPS, ALU.add), reads=["ss"], writes=["ms"])
        S.add("pool", _tt(stat[:, 2:3], stat[:, 1:2], mhalf[:, 0:1], ALU.pow), reads=["ms", "mhalf"], writes=["rstd"])
        S.add("dve", _stt(u, src, stat[:, 2:3], gb, ALU.mult, ALU.mult), reads=[src_key, "rstd", gkey], writes=["u"])
        S.add("pe", _trs([(bT_bf[:, k * 128:(k + 1) * 128], u[:, k * 128:(k + 1) * 128]) for k in range(8)], ident_bf),
              reads=["u", "ident_bf"], writes=[("bk", 0)])
        S.add("act", _act(dstT, bT_bf.rearrange("p (k t) -> p k t", k=8), AF.Copy), reads=[("bk", 0)], writes=dst_keys)


    fmb = [0]

    def chunk(c, full):
        rp, wp = c % 2, (c + 1) % 2
        qTml, kTml, qTrt, kTrt = qTmls[rp], kTmls[rp], qTrts[rp], kTrts[rp]
        og, gg = ogs[rp], ggs[rp]
        Vml, Vrt = Vmls[rp], Vrts[rp]
        Cml_bf, Crt_bf = Cml_bfs[rp], Crt_bfs[rp]
        Cml_bfw, Crt_bfw = Cml_bfs[wp], Crt_bfs[wp]
        if os.environ.get("MK_FAKE2"):
            S.fake_par = c % int(os.environ["MK_FAKE2"])
        s = c % 2
        uTs = uT[s]
        rmsnorm_T(xin, g_mix_b, uTs, [("uT", s)], "xin", "g_mix_b")
        if c + 1 < NCH:
            loads(c + 1)

        def fm_group(cols_ms):
            b = 1 + fmb[0] % 2
            fmb[0] += 1
            bk = banks[b]
            for sl, (col, m) in enumerate(cols_ms):
                S.add("pe", _mmk(bk[0:m, sl * 128:(sl + 1) * 128], [(W[:, k, col:col + m], uTs[:, k, :]) for k in range(8)]),
                      reads=wk(col) + [("uT", s)], writes=[("bk", b)])
            return bk, ("bk", b)

        for grp in ((0, 1) if full else (1,)):
            bk, bkey = fm_group([((grp * 4 + t) * 128, 128) for t in range(4)])
            S.add("act", _act(asb[:, grp * 4:grp * 4 + 4, 3:131], bk.rearrange("p (a b) -> p a b", a=4), AF.Copy),
                  reads=[bkey], writes=[("asb", grp * 4 + t) for t in range(4)])
            for t in range(grp * 4, grp * 4 + 4):
                acc = cacc[t % 2]
                ak = ("cacc", t % 2)
                S.add("dve", _ts(acc, asb[:, t, 3:131], convml[:, t * 4 + 3:t * 4 + 4], ALU.mult),
                      reads=[("asb", t), "convml"], writes=[ak])
                for kk in (2, 1, 0):
                    S.add("dve", _stt(acc, asb[:, t, kk:kk + 128], convml[:, t * 4 + kk:t * 4 + kk + 1], acc, ALU.mult, ALU.add),
                          reads=[("asb", t), "convml", ak], writes=[ak])
                S.add("pool", _cp(asb[:, t, 0:3], asb[:, t, 128:131]), reads=[("asb", t)], writes=[("asb", t)])
                if t < 4:
                    S.add("act", _act(qTml[:, t, :], acc, AF.Silu), reads=[ak], writes=[("qTml", rp, t)])
                else:
                    S.add("act", _act(kTml[:, t - 4, :], acc, AF.Silu), reads=[ak], writes=[("kTml", rp, t - 4)])
        bk, gkey = fm_group([(1024, 4), (1028, 4)])
        gi_reg = bk[0:4, 0:128]
        gf_reg = bk[0:4, 128:256]
        R = rows
        vs = vmt[s]
        S.add("act", _act(R["li0"], gi_reg, AF.Tanh, bias=bsc[:, 0:1], scale=1.0 / GATE_CAP), reads=[gkey, "bsc0"], writes=["li0"])
        S.add("act", _act(R["ef"], gf_reg, AF.Exp, bias=bsc[:, 1:2], scale=-1.0), reads=[gkey, "bsc1"], writes=["ef"])
        S.add("dve", _stt(R["li1"], R["li0"], GATE_CAP, vs[:, 0, :], ALU.mult, ALU.mult), reads=["li0", ("vmt", 0)], writes=["li1"])
        S.add("dve", _tt(R["li"], R["li1"], vs[:, 1, :], ALU.add), reads=["li1", ("vmt", 0)], writes=["li"])
        S.add("act", _act(R["sp"], R["ef"], AF.Ln, bias=1.0), reads=["ef"], writes=["sp"])
        S.add("dve", _scan(R["nbcum"], R["sp"], zeros4, 0.0, ALU.add, ALU.add), reads=["sp", "zeros4"], writes=["nbcum"])
        S.add("dve", _tt(R["B"], R["li"], R["nbcum"], ALU.add), reads=["li", "nbcum"], writes=["B"])
        S.add("dve", _scan(R["M"], R["B"], R["B"], MST, ALU.max, ALU.max), reads=["B", "mst"], writes=["M"])
        S.add("dve", _ts(DIAGM, i4[:, 0:4], R["M"][:, 127:128], ALU.mult), reads=["i4", "M"], writes=["diagM"])
        S.add("dve", _tt(D1, MST, R["M"][:, 127:128], ALU.subtract), reads=["mst", "M"], writes=["D1"])
        S.add("dve", _ts(DIAGD, i4[:, 0:4], D1, ALU.mult), reads=["i4", "D1"], writes=["diagD"])
        if full:
            S.add("dve", _tt(R["R2"], R["nbcum"], R["M"], ALU.subtract), reads=["nbcum", "M"], writes=["R2"])
            S.add("dve", _ts(R["R2"], R["R2"], 80.0, ALU.min), reads=["R2"], writes=["R2"])
            S.add("dve", _ts(R["R3"], R["M"], MST, ALU.subtract, -1.0, ALU.mult), reads=["M", "mst"], writes=["R3"])
            for h in range(4):
                S.add("dve", _ts(rhs_bd[:, h, :], R["M"], i4[:, 4 + h:5 + h], ALU.mult), reads=["M", "i4"], writes=[("rhs_bd", h)])
        S.add("dve", _tt(MST, R["M"][:, 127:128], R["nbcum"][:, 127:128], ALU.subtract), reads=["M", "nbcum"], writes=["mst"])

        sb_ = 5
        SMP = banks[sb_][:, 0:32]
        skey = ("bk", sb_)

        def smp_mm(e):
            ins = e.matmul(SMP[:, 0:4], lhsT=R["B"], rhs=i4[:, 0:4], start=True, stop=True)
            if full:
                e.matmul(SMP[:, 4:8], lhsT=R["R2"], rhs=i4[:, 0:4], start=True, stop=True)
                e.matmul(SMP[:, 8:12], lhsT=R["R3"], rhs=i4[:, 0:4], start=True, stop=True)
            e.matmul(SMP[:, 12:16], lhsT=ones4, rhs=DIAGM, start=True, stop=True)
            ins = e.matmul(SMP[:, 16:20], lhsT=ones4, rhs=DIAGD, start=True, stop=True)
            return ins
        S.add("pe", smp_mm, reads=["B", "R2", "R3", "i4", "ones4", "diagM", "diagD"], writes=[skey])
        if full:
            S.add("act", _act(smps, SMP[:, 0:20], AF.Copy), reads=[skey], writes=["smps"])
        else:
            S.add("act", _act(smps[:, 0:4], SMP[:, 0:4], AF.Copy), reads=[skey], writes=["smps"])
            S.add("act", _act(smps[:, 12:20], SMP[:, 12:20], AF.Copy), reads=[skey], writes=["smps"])
        if full:
            S.add("act", _act(EX8, smps[:, 4:12], AF.Exp), reads=["smps"], writes=["ex8"])
        S.add("dve", _tt(WSARG, smps[:, 0:4], smps[:, 12:16], ALU.subtract), reads=["smps"], writes=["wsarg"])
        S.add("act", _act(WSRC, WSARG, AF.Exp, bias=float(np.log(S_ML))), reads=["wsarg"], writes=["wsrc"])
        S.add("act", _act(DEC, smps[:, 16:20], AF.Exp), reads=["smps"], writes=["dec"])
        b5 = banks[5]
        k5 = ("bk", 5)
        if full:
            S.add("dve", _ts(BTS, smps[:, 0:4], float(np.log(S_ML)), ALU.add), reads=["smps"], writes=["BTS"])
            S.add("pe", _mmk(b5, [(ones4, rhs_bd.rearrange("p h t -> p (h t)")), (ident_f, maskneg4.unsqueeze(1).broadcast_to([128, 4, 128]))]),
                  reads=["ones4", "ident_f", "maskneg4"] + [("rhs_bd", h) for h in range(4)], writes=[k5])
            for h in range(4):
                S.add("act", _act(WT[:, h, :], b5[:, h * 128:(h + 1) * 128], AF.Exp, bias=BTS[:, h:h + 1]),
                      reads=[k5, "BTS"], writes=[("WT", h)])
            for h in range(4):
                S.add("dve", _ts(rhs_bd[:, h, :], R["R3"], i4[:, h:h + 1], ALU.mult), reads=["R3", "i4"], writes=[("rhs_bd", h)])
            S.add("pe", _mm(b5, ones4, rhs_bd.rearrange("p h t -> p (h t)")),
                  reads=["ones4"] + [("rhs_bd", h) for h in range(4)], writes=[k5])
            S.add("act", _act(Wint, b5, AF.Exp), reads=[k5], writes=["Wint"])
            S.add("dve", _tt(qsT.rearrange("p h t -> p (h t)"), qTml.rearrange("p h t -> p (h t)"), Wint, ALU.mult),
                  reads=["Wint"] + [("qTml", rp, h) for h in range(4)], writes=["qsT"])

        def tm_tile(col):
            b = 3 + tmslot[0] % 2
            tmslot[0] += 1
            S.add("pe", _mmk(banks[b], [(uTs[:, k, :], W[:, k, col:col + 512]) for k in range(8)]),
                  reads=wk(col) + [("uT", s)], writes=[("bk", b)])
            return banks[b], ("bk", b)

        for hh in range(2):
            reg, rk = tm_tile(1032 + hh * 512)
            S.add("act", _act(Vml[:, 2 * hh:2 * hh + 2, 0:256], reg.rearrange("p (a b) -> p a b", a=2), AF.Copy),
                  reads=[rk], writes=[("Vml", rp, hh)])
        for hh in range(2):
            reg, rk = tm_tile(3080 + hh * 512)
            S.add("dve", _cp(Vrt[:, 2 * hh:2 * hh + 2, :], reg.rearrange("p (a b) -> p a b", a=2)),
                  reads=[rk], writes=[("Vrt", rp, hh)])
        if full:
            for hh in range(2):
                reg, rk = tm_tile(2056 + hh * 512)
                S.add("act", _act(og[:, hh * 512:(hh + 1) * 512], reg, AF.Tanh, scale=0.5), reads=[rk], writes=[("og", rp, hh)])
            for hh in range(2):
                reg, rk = tm_tile(4104 + hh * 512)
                S.add("act", _act(gg[:, hh * 512:(hh + 1) * 512], reg, AF.Silu), reads=[rk], writes=[("gg", rp, hh)])

        def rotary(col, xi, qk, dst, dkey):
            reg, rk = tm_tile(col)
            X = rX[xi]
            xk = ("rX", 0)
            S.add("act", _act(X, reg, AF.Copy), reads=[rk], writes=[xk])
            Xv = X.rearrange("p (h a t) -> p h a t", h=4, a=2)
            Tc = cst[0][:, (qk * 2) * 256:(qk * 2 + 1) * 256].rearrange("p (h t) -> p h t", h=4)
            Ts = cst[0][:, (qk * 2 + 1) * 256:(qk * 2 + 2) * 256].rearrange("p (h t) -> p h t", h=4)
            Mv = [m.rearrange("p (h t) -> p h t", h=4) for m in rM]
            Dv = dst.rearrange("p h (a t) -> p h a t", a=2)
            S.add("dve", _tt(Mv[0], Xv[:, :, 0, :], Tc, ALU.mult), reads=[xk, ("cst", 0)], writes=[("rM", 0)])
            S.add("dve", _tt(Mv[1], Xv[:, :, 1, :], Ts, ALU.mult), reads=[xk, ("cst", 0)], writes=[("rM", 1)])
            S.add("pool", _tt(Mv[2], Xv[:, :, 0, :], Ts, ALU.mult), reads=[xk, ("cst", 0)], writes=[("rM", 2)])
            S.add("pool", _tt(Mv[3], Xv[:, :, 1, :], Tc, ALU.mult), reads=[xk, ("cst", 0)], writes=[("rM", 3)])
            S.add("dve", _tt(Dv[:, :, 0, :], Mv[0], Mv[1], ALU.subtract), reads=[("rM", 0), ("rM", 1)], writes=[(dkey, 0)])
            S.add("pool", _tt(Dv[:, :, 1, :], Mv[2], Mv[3], ALU.add), reads=[("rM", 2), ("rM", 3)], writes=[(dkey, 1)])

        rotary(5640, 0, 1, ktok, "ktok")
        if full and not os.environ.get("MK_X1"):
            rotary(5128, 1, 0, qtok, "qtok")
        if c + 1 < NCH:
            loads2(c + 1)

        ob = [6]

        def next_ob():
            b = 6 + ob[0] % 2
            ob[0] += 1
            return banks[b], ("bk", b)

        kb_, kbk = next_ob()
        b0bf = kb_.bitcast(BF16)
        S.add("pe", _trs([(b0bf[:, t * 128:(t + 1) * 128], kTml[:, t, :]) for t in range(4)], ident_bf),
              reads=[("kTml", rp, h) for h in range(4)] + ["ident_bf"], writes=[kbk])
        for t in range(4):
            S.add("act", _act(kw[:, t, :], b0bf[:, t * 128:(t + 1) * 128], AF.Copy, scale=WSRC[:, t:t + 1]),
                  reads=[kbk, "wsrc"], writes=[("kw", t)])
        if full:
            qb_, qbk = next_ob()
            qbbf = qb_.bitcast(BF16)
            S.add("pe", _trs([(qbbf[:, h * 128:(h + 1) * 128], ktok[:, h, :]) for h in range(4)] +
                             [(qbbf[:, (4 + h) * 128:(5 + h) * 128], qtok[:, h, :]) for h in range(4)], ident_bf),
                  reads=[("ktok", 0), ("ktok", 1), ("qtok", 0), ("qtok", 1), "ident_bf"], writes=[qbk])
            S.add("act", _act(kTrt, qbbf[:, 0:512].rearrange("p (h t) -> p h t", h=4), AF.Copy), reads=[qbk],
                  writes=[("kTrt", rp, h) for h in range(4)])
            S.add("act", _act(qTrt, qbbf[:, 512:1024].rearrange("p (h t) -> p h t", h=4), AF.Copy), reads=[qbk],
                  writes=[("qTrt", rp, h) for h in range(4)])

        for h in range(4):
            bo, ko = next_ob()
            S.add("pe", _mm(bo[:, 0:257], kw[:, h, :], Vml[:, h, :]), reads=[("kw", h), ("Vml", rp, h // 2)], writes=[ko])
            S.add("dve", _stt(Cml_f[:, h, :], Cml_f[:, h, :], DEC[:, h:h + 1], bo[:, 0:257], ALU.mult, ALU.add),
                  reads=[("Cml_f", h), "dec", ko], writes=[("Cml_f", h)])
            S.add("pool", _cp(Cml_bfw[:, h, :], Cml_f[:, h, :]), reads=[("Cml_f", h)], writes=[("Cml_bf", wp, h)])
        for pr in range(2):
            bo, ko = next_ob()
            for hh in range(2):
                h = 2 * pr + hh
                S.add("pe", _mm(bo[:, hh * 256:(hh + 1) * 256], ktok[:, h, :], Vrt[:, h, :]), reads=[("ktok", 0), ("ktok", 1), ("Vrt", rp, pr)], writes=[ko])
            for hh in range(2):
                h = 2 * pr + hh
                S.add("dve", _stt(Crt_f[:, h, :], Crt_f[:, h, :], CD[h], bo[:, hh * 256:(hh + 1) * 256], ALU.mult, ALU.add),
                      reads=[("Crt_f", h), ko], writes=[("Crt_f", h)])
            for hh in range(2):
                h = 2 * pr + hh
                S.add("pool", _ts(Crt_bfw[:, h, :], Crt_f[:, h, :], CD[h], ALU.mult, 0.0, ALU.add),
                      reads=[("Crt_f", h)], writes=[("Crt_bf", wp, h)])
        if full:
            S.add("pe", lambda e: [e.matmul(b5[:, h * 128:(h + 1) * 128], lhsT=kTml[:, h, :], rhs=qTml[:, h, :], start=True, stop=True)
                                   for h in range(4)][-1],
                  reads=[("kTml", rp, h) for h in range(4)] + [("qTml", rp, h) for h in range(4)], writes=[k5])
            S.add("dve", _tt(PT.rearrange("p h t -> p (h t)"), b5, WT.rearrange("p h t -> p (h t)"), ALU.mult),
                  reads=[k5] + [("WT", h) for h in range(4)], writes=["PT"])
            for h in range(4):
                bo, ko = next_ob()
                S.add("pe", _mmk(bo[:, 0:257], [(PT[:, h, :], Vml[:, h, :]), (qsT[:, h, :], Cml_bf[:, h, :])]),
                      reads=["PT", ("Vml", rp, h // 2), "qsT", ("Cml_bf", rp, h)], writes=[ko])
                S.add("act", _act(hraw[:, h, :], bo[:, 0:257], AF.Copy), reads=[ko], writes=[("hraw", h)])
                S.add("dve", lambda e, h=h: e.bn_stats(out=st6[:, h, :], in_=hraw[:, h, 0:256]), reads=[("hraw", h)], writes=[("st6", h)])
                S.add("dve", lambda e, h=h: e.bn_aggr(out=mv[:, h, :], in_=st6[:, h, :]), reads=[("st6", h)], writes=[("mv", h)])
            allh = [("hraw", h) for h in range(4)]
            allmv = [("mv", h) for h in range(4)]
            S.add("dve", _ts(T2, hraw[:, :, 256], -1.0, ALU.mult), reads=allh, writes=["t2"])
            S.add("dve", _tt(DD, T2, hraw[:, :, 256], ALU.max), reads=allh + ["t2"], writes=["dd"])
            S.add("dve", _tt(DD, DD, EX8[:, 0:4], ALU.max), reads=["dd", "ex8"], writes=["dd"])
            S.add("dve", lambda e: e.reciprocal(out=RDEN, in_=DD), reads=["dd"], writes=["rden"])
            S.add("dve", _tt(T1, RDEN, RDEN, ALU.mult), reads=["rden"], writes=["t1"])
            S.add("dve", _tt(T2, T1, mv[:, :, 1], ALU.mult), reads=["t1"] + allmv, writes=["t2"])
            S.add("dve", _ts(T1, T2, EPS, ALU.add), reads=["t2"], writes=["t1"])
            S.add("pool", _tt(RSTD, T1, mhalf, ALU.pow), reads=["t1", "mhalf"], writes=["rstdh"])
            S.add("dve", _tt(SC, RDEN, RSTD, ALU.mult), reads=["rden", "rstdh"], writes=["sc"])
            S.add("dve", _stt(BI, mv[:, :, 0], -1.0, SC, ALU.mult, ALU.mult), reads=allmv + ["sc"], writes=["bi"])
            for h in range(4):
                S.add("act", _act(hraw[:, h, 0:256], hraw[:, h, 0:256], AF.Identity, bias=BI[:, h:h + 1], scale=SC[:, h:h + 1]),
                      reads=[("hraw", h), "sc", "bi"], writes=[("hraw", h)])
            S.add("dve", _stt(mixed[:, 0:1024].rearrange("p (h v) -> p h v", h=4), og.rearrange("p (h v) -> p h v", h=4), 1.0,
                              hraw[:, :, 0:256], ALU.add, ALU.mult),
                  reads=allh + [("og", rp, 0), ("og", rp, 1)], writes=[("mixed", 0)])
            S.add("pe", lambda e: [e.matmul(b5[:, h * 128:(h + 1) * 128], lhsT=kTrt[:, h, :], rhs=qTrt[:, h, :], start=True, stop=True)
                                   for h in range(4)][-1],
                  reads=[("kTrt", rp, h) for h in range(4)] + [("qTrt", rp, h) for h in range(4)], writes=[k5])
            S.add("dve", _tt(PT, b5.rearrange("p (h t) -> p h t", h=4), cmask.unsqueeze(1).broadcast_to([128, 4, 128]), ALU.mult),
                  reads=[k5, "cmask"], writes=["PT"])
            for pr in range(2):
                bo, ko = next_ob()
                for hh in range(2):
                    h = 2 * pr + hh
                    S.add("pe", _mmk(bo[:, hh * 256:(hh + 1) * 256], [(PT[:, h, :], Vrt[:, h, :]), (qTrt[:, h, :], Crt_bf[:, h, :])]),
                          reads=["PT", ("Vrt", rp, pr), ("qTrt", rp, h), ("Crt_bf", rp, h)], writes=[ko])
                S.add("act", _act(hrt[:, 2 * pr:2 * pr + 2, :], bo.rearrange("p (a b) -> p a b", a=2), AF.Copy),
                      reads=[ko], writes=[("hrt", 2 * pr), ("hrt", 2 * pr + 1)])
                for hh in range(2):
                    h = 2 * pr + hh
                    S.add("dve", lambda e, h=h: e.bn_stats(out=st6[:, h, :], in_=hrt[:, h, :]), reads=[("hrt", h)], writes=[("st6", h)])
                    S.add("dve", lambda e, h=h: e.bn_aggr(out=mv[:, h, :], in_=st6[:, h, :]), reads=[("st6", h)], writes=[("mv", h)])
            allr = [("hrt", h) for h in range(4)]
            S.add("dve", _ts(T1, mv[:, :, 1], EPS, ALU.add), reads=allmv, writes=["t1"])
            S.add("pool", _tt(RSTD, T1, mhalf, ALU.pow), reads=["t1", "mhalf"], writes=["rstdh"])
            S.add("dve", _stt(BI, mv[:, :, 0], -1.0, RSTD, ALU.mult, ALU.mult), reads=allmv + ["rstdh"], writes=["bi"])
            for h in range(4):
                S.add("act", _act(hrt[:, h, :], hrt[:, h, :], AF.Identity, bias=BI[:, h:h + 1], scale=RSTD[:, h:h + 1]),
                      reads=[("hrt", h), "rstdh", "bi"], writes=[("hrt", h)])
            S.add("dve", _tt(mixed[:, 1024:2048], hrt.rearrange("p h v -> p (h v)"), gg, ALU.mult),
                  reads=allr + [("gg", rp, 0), ("gg", rp, 1)], writes=[("mixed", 1)])
        if full:
            f = c - NCH_P
            S.add("act", _dma(mixed_d[f * 128:(f + 1) * 128, :], mixed), reads=[("mixed", 0), ("mixed", 1)],
                  writes=[("mixed_d", f)], dma="mxst")


    loads(0)
    loads2(0)
    _stop = int(os.environ.get("MK_STOP_AFTER", str(NCH)))
    for c in range(min(NCH, _stop)):
        chunk(c, c >= NCH_P)
    if _stop < 100 and os.environ.get("MK_STOP_AFTER"):
        S.fake_par = None
        S.add("sp", lambda e: e.nop(), reads=S.all_keys())
        with ExitStack() as es:
            sems = {e: es.enter_context(nc.semaphore("s_" + e)) for e in ENGS}
            dsems = {k: es.enter_context(nc.semaphore("d_" + k)) for k in S.dma_counts}
            block = es.enter_context(nc.Block())
            S.emit(block, sems, dsems)
        return nc

    S.fake_par = None
    A3 = _Arena(nc, PH_BASE, SBUF_END)
    Wup = A3.alloc("Wup", [128, 8, DFF], BF16)
    Wout = A3.alloc("Wout", [128, 16, D], BF16)
    xh = A3.alloc("xh", [128, 4, D], F32)
    gcol = A3.alloc("gcol", [128, 16], F32)
    assert A3.off <= PH_BASE + 8 * WCOLS * 2, "early F3 weights must alias W only"
    allW = [("W", g, k) for g in range(len(WGROUPS)) for k in range(8)]
    S.add("pool", _dma(Wup[:, 0, :], wup_d[0:128, :]), writes=[("Wup", 0)] + allW, dma="Wup")
    for k in range(1, 8):
        S.add("pool", _dma(Wup[:, k, :], wup_d[k * 128:(k + 1) * 128, :]), reads=[("Wup", 0)], writes=[("Wup", k)], dma="Wup")
    S.add("sp", _dma(gcol, gcol_d), reads=[("Wup", 0)], writes=["gcol"], dma="gcol")
    S.add("dve", _ts(gcol[:, 0:8], gcol[:, 0:8], 0.5, ALU.mult), reads=["gcol"], writes=["gcol"])
    for k in range(16):
        sl = k % 4
        S.add("sp", _dma(xh[:, sl, :], wout_d[k * 128:(k + 1) * 128, :]), reads=[("Wup", 0)], writes=[("xh", sl)], dma=f"xh{sl}")
        S.add("act" if k % 2 else "dve",
              (_act(Wout[:, k, :], xh[:, sl, :], AF.Copy, scale=gcol[:, k:k + 1]) if k % 2 else
               _ts(Wout[:, k, :], xh[:, sl, :], gcol[:, k:k + 1], ALU.mult)),
              reads=[("xh", sl), "gcol", ("Wup", 0)], writes=[("Wout", k)])
    S.barrier(skip=("Wup", "Wout", "W"))
    NWD = 10
    Wgt = A3.alloc("Wgt", [128, 8, DFF], BF16)
    Wring = A3.alloc("Wring", [128, NWD, D], BF16)
    mtoks = [A3.alloc(f"mtok{i}", [128, 2048], BF16) for i in range(2)]
    mixedT = A3.alloc("mixedT", [128, 16, 256], BF16)
    u2Ts = [A3.alloc(f"u2T{i}", [128, 8, 256], BF16) for i in range(2)]
    u3s = [A3.alloc(f"u3_{i}", [128, D], BF16) for i in range(2)]
    NB3 = int(os.environ.get("MK_NB3", "4"))
    asb3s = [A3.alloc(f"asb3_{i}", [128, 258], F32) for i in range(NB3)]
    acc3 = [A3.alloc(f"acc3_{i}", [128, 256], F32) for i in range(NB3)]
    actT = [A3.alloc(f"actT{i}", [128, 256], BF16) for i in range(NB3)]
    halo = A3.alloc("halo", [128, NJ, 2], F32)
    convff = A3.alloc("convff", [128, NJ * 3], F32)
    g_ffn_b = A3.alloc("g_ffn_b", [128, D], F32)
    g_fin_b = A3.alloc("g_fin_b", [128, D], F32)
    stat2 = A3.alloc("stat2", [128, 16], F32)

    print("[kernel] sbuf A1 end", A1.off, "A3 end", A3.off, "limit", SBUF_END)
    S.add("sp", _dma(convff, convff_d), writes=["convff"], dma="cst")
    S.add("sp", _dma(g_ffn_b, gffn_d), writes=["g_ffn_b"], dma="cst")
    S.add("sp", _dma(g_fin_b, gfin_d), writes=["g_fin_b"], dma="cst")
    S.add("pool", lambda e: e.memset(halo, 0.0), writes=[("halo", j) for j in range(NJ)])
    for k in range(8):
        S.add("pool", _dma(Wgt[:, k, :], wgate_d[k * 128:(k + 1) * 128, :]), writes=[("Wgt", k)], dma="Wgt")

    b_acc = [banks[0], banks[1], banks[2], banks[3]]
    ybk = None
    agrot = [0]
    NAG = int(os.environ.get("MK_NAG", "3"))
    b_y = [banks[4 + NAG], banks[7]] if NAG < 3 else [banks[7], banks[7]]
    ybk = [4 + NAG, 7] if NAG < 3 else [7, 7]
    wdc = [0]
    strot = [0]
    mtc = [0]

    def rms_small(src, skey):
        i = strot[0] % 4
        strot[0] += 1
        return stat2[:, 4 * i:4 * i + 1], stat2[:, 4 * i + 1:4 * i + 2], stat2[:, 4 * i + 2:4 * i + 3], i

    def f3_pre(f0, ntt, is_halo, bp):
        T = ntt * 128
        u2T = u2Ts[bp]
        xsl = [bp * 2 + tt for tt in range(ntt)]
        for tt in range(ntt):
            f = f0 + tt
            p0 = (NCH_P + f) * 128
            S.add("sp", _dma(xh[:, xsl[tt], :], xs_d[p0:p0 + 128, :]), writes=[("xh", xsl[tt])], dma=f"xh{xsl[tt]}")
        for tt in range(ntt):
            f = f0 + tt
            mi = mtc[0] % 2
            mtc[0] += 1
            mtok = mtoks[mi]
            S.add("sp", _dma(mtok, mixed_d[f * 128:(f + 1) * 128, :]), reads=[("mixed_d", f)], writes=[("mtok", mi)], dma=f"mtok{mi}")
            for half in range(2):
                yb = b_y[half].bitcast(BF16)
                S.add("pe", _trs([(yb[:, kk * 128:(kk + 1) * 128], mtok[:, (half * 8 + kk) * 128:(half * 8 + kk + 1) * 128])
                                  for kk in range(8)], ident_bf), reads=[("mtok", mi), "ident_bf"], writes=[("bk", ybk[half])])
                S.add("act" if half else "dve",
                      (_act(mixedT[:, half * 8:half * 8 + 8, tt * 128:(tt + 1) * 128], yb.rearrange("p (k t) -> p k t", k=8), AF.Copy)
                       if half else _cp(mixedT[:, half * 8:half * 8 + 8, tt * 128:(tt + 1) * 128], yb.rearrange("p (k t) -> p k t", k=8))),
                      reads=[("bk", ybk[half])], writes=[("mixedT", tt, half)])
        for tt in range(ntt):
            xv = xh[:, xsl[tt], :]
            for half in range(2):
                S.add("pe", _mmk(b_y[half], [(mixedT[:, k, tt * 128:(tt + 1) * 128], Wout[:, k, half * 512:(half + 1) * 512])
                                             for k in range(16)]),
                      reads=[("mixedT", tt, 0), ("mixedT", tt, 1)] + [("Wout", k) for k in range(16)], writes=[("bk", ybk[half])])
                S.add("dve", _tt(xv[:, half * 512:(half + 1) * 512], b_y[half], xv[:, half * 512:(half + 1) * 512], ALU.add),
                      reads=[("bk", ybk[half]), ("xh", xsl[tt])], writes=[("xh", xsl[tt])])
        for tt in range(ntt):
            xv = xh[:, xsl[tt], :]
            xk = ("xh", xsl[tt])
            ss, ms, rs, si = rms_small(None, None)
            u3 = u3s[tt % 2]
            uk = ("u3", tt % 2)
            S.add("act", _act(u3, xv, AF.Square, accum=ss), reads=[xk], writes=[uk, ("ss", si)])
            S.add("dve", _ts(ms, ss, 1.0 / D, ALU.mult, EPS, ALU.add), reads=[("ss", si)], writes=[("ms", si)])
            S.add("pool", _tt(rs, ms, mhalf[:, 0:1], ALU.pow), reads=[("ms", si), "mhalf"], writes=[("rs", si)])
            S.add("dve", _stt(u3, xv, rs, g_ffn_b, ALU.mult, ALU.mult), reads=[xk, ("rs", si), "g_ffn_b"], writes=[uk])
            yb = b_y[tt % 2].bitcast(BF16)
            S.add("pe", _trs([(yb[:, k * 128:(k + 1) * 128], u3[:, k * 128:(k + 1) * 128]) for k in range(8)], ident_bf),
                  reads=[uk, "ident_bf"], writes=[("bk", ybk[tt % 2])])
            S.add("act", _act(u2T[:, :, tt * 128:(tt + 1) * 128], yb.rearrange("p (k t) -> p k t", k=8), AF.Copy),
                  reads=[("bk", ybk[tt % 2])], writes=[("u2T", bp, tt)])

    def f3_main(f0, ntt, is_halo, bp):
        T = ntt * 128
        u2T = u2Ts[bp]
        xsl = [bp * 2 + tt for tt in range(ntt)]
        u2k = [("u2T", bp, tt) for tt in range(ntt)]
        for j in range(NJ):
            par = agrot[0] % NB3
            agb = 4 + (agrot[0] % NAG)
            agrot[0] += 1
            aT = banks[agb][:, 0:T]
            gT = banks[agb][:, 256:256 + T]
            if not is_halo:
                ws = wdc[0] % NWD
                wdc[0] += 1
                S.add("pool", _dma(Wring[:, ws, :], wdn_d[j * 128:(j + 1) * 128, :]), writes=[("Wring", ws)], dma=f"Wd{ws}")
            S.add("pe", _mmk(aT, [(Wup[:, k, j * 128:(j + 1) * 128], u2T[:, k, 0:T]) for k in range(8)]),
                  reads=u2k + [("Wup", k) for k in range(8)], writes=[("bk", agb)])
            if not is_halo:
                S.add("pe", _mmk(gT, [(Wgt[:, k, j * 128:(j + 1) * 128], u2T[:, k, 0:T]) for k in range(8)]),
                      reads=u2k + [("Wgt", k) for k in range(8)], writes=[("bk", agb)])
            asb3 = asb3s[par]
            S.add("pool", _cp(asb3[:, 0:2], halo[:, j, :]), reads=[("halo", j)], writes=[("asb3h", par)])
            S.add("act", _act(asb3[:, 2:2 + T], aT, AF.Copy), reads=[("bk", agb)], writes=[("asb3", par)])
            S.add("pool", _cp(halo[:, j, :], asb3[:, T:T + 2]), reads=[("asb3", par)], writes=[("halo", j)])
            if is_halo:
                continue
            acc = acc3[par][:, 0:T]
            ak = ("acc3", par)
            S.add("dve", _ts(acc, asb3[:, 2:2 + T], convff[:, j * 3 + 2:j * 3 + 3], ALU.mult), reads=[("asb3", par), "convff"], writes=[ak])
            S.add("dve", _stt(acc, asb3[:, 1:1 + T], convff[:, j * 3 + 1:j * 3 + 2], acc, ALU.mult, ALU.add),
                  reads=[("asb3", par), ("asb3h", par), "convff", ak], writes=[ak])
            S.add("dve", _stt(acc, asb3[:, 0:T], convff[:, j * 3:j * 3 + 1], acc, ALU.mult, ALU.add),
                  reads=[("asb3", par), ("asb3h", par), "convff", ak], writes=[ak])
            S.add("act", _act(acc, acc, AF.Silu), reads=[ak], writes=[ak])
            at = actT[par][:, 0:T]
            S.add("dve", _tt(at, acc, gT, ALU.mult), reads=[ak, ("bk", agb)], writes=[("actT", par)])
            for tt in range(ntt):
                for half in range(2):
                    S.add("pe", _mm(b_acc[tt * 2 + half], at[:, tt * 128:(tt + 1) * 128], Wring[:, ws, half * 512:(half + 1) * 512],
                                    start=(j == 0), stop=(j == NJ - 1)),
                          reads=[("actT", par), ("Wring", ws)], writes=[("bk", tt * 2 + half)])
        if is_halo:
            return
        for tt in range(ntt):
            f = f0 + tt
            xv = xh[:, xsl[tt], :]
            xk = ("xh", xsl[tt])
            for half in range(2):
                S.add("dve", _tt(xv[:, half * 512:(half + 1) * 512], b_acc[tt * 2 + half], xv[:, half * 512:(half + 1) * 512], ALU.add),
                      reads=[("bk", tt * 2 + half), xk], writes=[xk])
            ss, ms, rs, si = rms_small(None, None)
            u3 = u3s[tt % 2]
            uk = ("u3", tt % 2)
            S.add("act", _act(u3, xv, AF.Square, accum=ss), reads=[xk], writes=[uk, ("ss", si)])
            S.add("dve", _ts(ms, ss, 1.0 / D, ALU.mult, EPS, ALU.add), reads=[("ss", si)], writes=[("ms", si)])
            S.add("pool", _tt(rs, ms, mhalf[:, 0:1], ALU.pow), reads=[("ms", si), "mhalf"], writes=[("rs", si)])
            S.add("dve", _stt(xv, xv, rs, g_fin_b, ALU.mult, ALU.mult), reads=[xk, ("rs", si), "g_fin_b"], writes=[xk])
            S.add("act", _dma(out_d[(f - 1) * 128:f * 128, :], xv), reads=[xk], writes=[("out", f)], dma=f"ost{xsl[tt]}")

    blocks = [(0, 1, True, 1)] + [(1 + 2 * b, 2, False, b % 2) for b in range(8)]
    f3_pre(*blocks[0])
    for bi, blk in enumerate(blocks):
        if bi + 1 < len(blocks):
            f3_pre(*blocks[bi + 1])
        f3_main(*blk)

    fin = S.add("sp", lambda e: e.nop(), reads=[("out", f) for f in range(1, NCH_F)])

    with ExitStack() as es:
        sems = {e: es.enter_context(nc.semaphore("s_" + e)) for e in ENGS}
        dsems = {k: es.enter_context(nc.semaphore("d_" + k)) for k in S.dma_counts}
        block = es.enter_context(nc.Block())
        S.emit(block, sems, dsems, reorder=bool(int(os.environ.get("MK_REORDER", "1"))))
        print("[kernel] est_ns", getattr(S, "est_ns", None), "ops", len(S.ops))
    return nc


def _host_consts():
    idx = np.arange(128, dtype=np.float64)
    dqk = np.zeros((128, 12), np.float32)
    for h in range(4):
        dqk[:, h] = np.exp(LOG_GAMMA[h] * (idx + 1.0))
        dqk[:, 4 + h] = S_ML * np.exp(-LOG_GAMMA[h] * (idx + 1.0))
        dqk[:, 8 + h] = S_ML * np.exp(-LOG_GAMMA[h] * (idx + 1.0)) * np.exp(LOG_GAMMA[h] * CH)
    jj, ii = np.meshgrid(np.arange(128), np.arange(128), indexing="ij")
    cm = (jj <= ii).astype(np.float32)
    mneg = np.where(jj <= ii, 0.0, NEG).astype(np.float32)
    i4 = np.concatenate([np.eye(4), -np.eye(4)], axis=1).astype(np.float32)
    return dict(dqk=dqk, cmask=cm, maskneg4=mneg,
                ident_bf=np.eye(128).astype(ml_dtypes.bfloat16), ident_f=np.eye(128, dtype=np.float32), i4=i4)


def _rope_tables(n_null):
    p = np.arange(NPOS, dtype=np.float64)
    pos = np.where(p >= n_null, 48.0 + (p - n_null), 0.0)
    inv = 10000.0 ** (-np.arange(0, 128, 2, dtype=np.float64) / 128.0)
    ang = pos[:, None] * inv[None, :]
    cosr = np.cos(ang).reshape(NCH, 128, 1, 64)
    sinr = np.sin(ang).reshape(NCH, 128, 1, 64)
    idx = np.arange(128, dtype=np.float64)
    dq = np.stack([np.exp(LOG_GAMMA[h] * (idx + 1.0)) for h in range(4)], axis=1)[None, :, :, None]
    dk = np.stack([S_ML * np.exp(-LOG_GAMMA[h] * (idx + 1.0)) for h in range(4)], axis=1)[None, :, :, None]
    tab = np.stack([cosr * dq, sinr * dq, cosr * dk, sinr * dk], axis=2)
    tab = np.ascontiguousarray(tab.reshape(NCH, 128, 1024)).astype(np.float32)
    valid = (p >= n_null).astype(np.float32).reshape(NCH, 128)
    vm = np.stack([valid, (valid - 1.0) * 1e30], axis=1)
    vm = np.ascontiguousarray(np.broadcast_to(vm[:, None], (NCH, 4, 2, 128))).astype(np.float32)
    return tab, vm


def _prep_inputs(inputs):
    f = lambda k: np.asarray(inputs[k], dtype=np.float32)
    x = f("x")
    meta = f("meta_tokens")
    w_in = f("w_in")[0]
    sizes = [512, 512, 1024, 1024, 4, 4, 512, 512, 1024, 1024]
    offs = np.cumsum(sizes)[:-1]
    ml_q, ml_k, ml_v, ml_o, ml_i, ml_f, rt_q, rt_k, rt_v, rt_g = np.split(w_in, offs, axis=1)

    def swap(cols):
        return cols.reshape(D, 4, 2, 64)[:, :, ::-1, :].reshape(D, 512)

    w_in_r = np.ascontiguousarray(np.concatenate(
        [ml_q, ml_k, ml_i, ml_f, ml_v, ml_o, rt_v, rt_g, rt_q, rt_k], axis=1))
    assert w_in_r.shape == (D, WCOLS)
    convml = f("ml_conv_w")[0]
    convw_ml = np.ascontiguousarray(convml.reshape(4, 8, 128).transpose(2, 1, 0).reshape(128, 32))
    convff = f("ffn_conv_w")[0]
    convw_ffn = np.ascontiguousarray(convff.reshape(3, NJ, 128).transpose(2, 1, 0).reshape(128, NJ * 3))
    b_if = np.ascontiguousarray(np.stack([f("ml_b_i")[0], f("ml_b_f")[0]], axis=1))
    gcat = np.concatenate([f("ml_norm_g")[0], f("rt_norm_g")[0]])
    gcol = np.ascontiguousarray(gcat.reshape(16, 128).T)
    bc = lambda v: np.ascontiguousarray(np.broadcast_to(v[None, :], (128, D)))
    common = dict(w_in_r=w_in_r, w_out=f("w_out")[0], w_up=f("w_up")[0], w_gate=f("w_gate")[0], w_down=f("w_down")[0],
                  convw_ml=convw_ml, convw_ffn=convw_ffn, b_if=b_if, gcol=gcol,
                  g_mix_b=bc(f("norm_mix_g")[0]), g_ffn_b=bc(f("norm_ffn_g")[0]), g_fin_b=bc(f("norm_final_g")))
    common.update(_host_consts())
    tabs = [_rope_tables(2160), _rope_tables(112)]
    in_maps = []
    for core in range(8):
        b, t = core // 2, core % 2
        xs = np.zeros((NPOS, D), np.float32)
        if t == 0:
            xs[2160:2176] = meta
            xs[2176:] = x[b, 0:2048]
        else:
            xs[112:128] = meta
            xs[128:] = x[b]
        m = dict(common)
        m["xs"] = xs
        m["cs_tab"], m["vm_tab"] = tabs[t]
        in_maps.append(m)
    return in_maps


_NC_CACHE = {}


def kernel(**inputs):
    in_maps = _prep_inputs(inputs)
    if "nc" not in _NC_CACHE:
        _NC_CACHE["nc"] = build_program()
    nc = _NC_CACHE["nc"]
    res = run_bass_kernel_spmd(nc, in_maps, core_ids=list(range(8)))
    out = np.zeros((4, 4096, D), np.float32)
    for core in range(8):
        b, t = core // 2, core % 2
        out[b, t * 2048:(t + 1) * 2048] = res.results[core]["out"]
    if DEBUG:
        kernel.debug = [res.results[c]["mixed_d"] for c in range(8)]
    return out
```

```python
import os
from contextlib import ExitStack

import numpy as np
import ml_dtypes

import concourse.bass as bass
import concourse.mybir as mybir
from concourse.bass_utils import run_bass_kernel_spmd

F32 = mybir.dt.float32
BF16 = mybir.dt.bfloat16
ALU = mybir.AluOpType
AF = mybir.ActivationFunctionType

NCH_P = 16
NCH_F = 17
NCH = NCH_P + NCH_F
CH = 128
NPOS = NCH * CH
D = 1024
DFF = 2816
NJ = DFF // 128
WCOLS = 6152
EPS = 1e-6
NEG = -1e30
GATE_CAP = 15.0
SBUF_BASE = 16640
SBUF_END = 229376
S_ML = 128.0 ** -0.5
LOG_GAMMA = [float(np.log1p(-2.0 ** (-(5.0 + h)))) for h in range(4)]
CD = [float(np.exp(lg * CH)) for lg in LOG_GAMMA]

ENGS = ("pe", "act", "dve", "pool", "sp")
DEBUG = bool(int(os.environ.get("MK_DEBUG", "0")))


class _Op:
    __slots__ = ("eng", "fn", "deps", "dma_sem", "dma_cnt", "sig", "idx", "need_sig", "wk")

    def __init__(self, eng, fn):
        self.eng = eng
        self.fn = fn
        self.deps = []
        self.dma_sem = None
        self.dma_cnt = 0
        self.sig = 0
        self.need_sig = False


class Sched:
    def __init__(self):
        self.ops = []
        self.last_w = {}
        self.readers = {}
        self.dma_counts = {}
        self.fence = None

    fake_par = None
    FAKE_KEEP = ("bk", "W", "Cml_f", "Cml_bf", "Crt_f", "Crt_bf", "asb", "mixed_d", "cst", "vmt", "uT")

    def _fk(self, keys):
        if self.fake_par is None:
            return keys
        out = []
        for k in keys:
            base = k[0] if isinstance(k, tuple) else k
            if base == "bk" and os.environ.get("MK_FAKEBK"):
                out.append((k, self.fake_par))
                continue
            if base in self.FAKE_KEEP or base in ("mst", "ident_bf", "mhalf", "i4", "ones4", "zeros4", "convml", "dqk", "cmask",
                                                   "maskneg4", "ident_f", "g_mix_b", "bsc0", "bsc1", "bif"):
                out.append(k)
            else:
                out.append((k, self.fake_par))
        return out

    def add(self, eng, fn, reads=(), writes=(), dma=None):
        reads = self._fk(reads)
        writes = self._fk(writes)
        op = _Op(eng, fn)
        op.wk = list(writes)[:2]
        op.idx = len(self.ops)
        deps = {}

        def dep(o, raw):
            val = self.dma_counts[o.dma_sem] if o.dma_sem is not None else 0
            if o.idx in deps:
                if raw and not deps[o.idx][1]:
                    deps[o.idx] = (o, True, val)
            else:
                deps[o.idx] = (o, raw, val)

        for r in reads:
            w = self.last_w.get(r)
            if w is not None:
                dep(w, True)
        for k in writes:
            w = self.last_w.get(k)
            if w is not None:
                dep(w, False)
            for rd in self.readers.get(k, ()):
                dep(rd, False)
        if self.fence is not None:
            dep(self.fence[eng], False)
        op.deps = list(deps.values())
        for r in reads:
            self.readers.setdefault(r, []).append(op)
        for k in writes:
            self.last_w[k] = op
            self.readers[k] = []
        if dma is not None:
            op.dma_sem = dma
            self.dma_counts[dma] = self.dma_counts.get(dma, 0) + 16
            op.dma_cnt = self.dma_counts[dma]
        self.ops.append(op)
        return op

    def all_keys(self):
        return list(set(self.last_w.keys()) | set(self.readers.keys()))

    def barrier(self, skip=()):
        keys = [k for k in self.all_keys() if not (isinstance(k, tuple) and k[0] in skip)]
        fence = {}
        for e in ENGS:
            fence[e] = self.add(e, lambda eng: eng.nop(), writes=keys)
        self.fence = fence

    @staticmethod
    def _skip(d, op, raw):
        if d.dma_sem is not None or op.dma_sem is not None:
            return False
        if d.eng != op.eng:
            return False
        return d.eng == "pe"

    CALIB = bool(int(os.environ.get("MK_CALIB", "1")))

    def _cost(self, op):
        c = getattr(op.fn, "cost", 150.0)
        if self.CALIB and op.dma_sem is None:
            if op.eng == "pool":
                c = 100.0 + 2.6 * (c * 0.96 - 100.0)
            elif op.eng == "dve":
                c = c + 100.0
            elif op.eng == "act":
                c = max(120.0, c - 85.0)
        return c

    def schedule(self, window=int(os.environ.get("MK_WIN", "200")), lat=float(os.environ.get("MK_LAT", "800"))):
        ops = self.ops
        n = len(ops)
        ndeps = [0] * n
        users = [[] for _ in range(n)]
        for op in ops:
            ndeps[op.idx] = len(op.deps)
            for (d, raw, val) in op.deps:
                users[d.idx].append(op.idx)
        fin = [0.0] * n
        ready_t = [0.0] * n
        prio_cp = os.environ.get("MK_PRIO", "cp") == "cp"
        tail = [0.0] * n
        if prio_cp:
            for op in reversed(ops):
                c = self._cost(op)
                t = 0.0
                tl = lat * float(os.environ.get("MK_TAILF", "1.0"))
                for u in users[op.idx]:
                    if tail[u] + tl > t:
                        t = tail[u] + tl
                tail[op.idx] = t + c * float(os.environ.get("MK_COSTF", "1.0"))
        pend = {e: [op.idx for op in ops if op.eng == e] for e in ENGS}
        pos = {e: 0 for e in ENGS}
        done = [False] * n
        efree = {e: 0.0 for e in ENGS}
        order = {e: [] for e in ENGS}
        cur_tab = [None]
        nsw = [0]
        TABSW = float(os.environ.get("MK_TABSW", "1000"))
        remaining = n
        while remaining:
            best = None
            for e in ENGS:
                lst = pend[e]
                i = pos[e]
                while i < len(lst) and done[lst[i]]:
                    i += 1
                pos[e] = i
                if i >= len(lst):
                    continue
                w = 1 if e == "sp" else window
                seen = 0
                j = i
                cand = None
                dma_blocked = False
                while j < len(lst) and seen < w:
                    k = lst[j]
                    j += 1
                    if done[k]:
                        continue
                    seen += 1
                    isdma = ops[k].dma_sem is not None
                    if isdma and dma_blocked:
                        continue
                    if isdma:
                        dma_blocked = True
                    if ndeps[k] > 0:
                        continue
                    st = max(ready_t[k], efree[e])
                    if e == "act" and TABSW > 0.0:
                        tb = getattr(ops[k].fn, "tab", None)
                        if tb is not None and tb != cur_tab[0]:
                            st = st + TABSW
                    if prio_cp:
                        key = (st, -tail[k], k)
                        if cand is None or key < cand:
                            cand = key
                    else:
                        key = (st, 0.0, k)
                        if cand is None or key < cand:
                            cand = key
                            if ready_t[k] <= efree[e]:
                                break
                if cand is not None and (best is None or cand < best[0]):
                    best = (cand, e)
            assert best is not None, "scheduler deadlock"
            (st, _pr, k), e = best
            op = ops[k]
            c = self._cost(op)
            if e == "act":
                tb = getattr(op.fn, "tab", None)
                if tb is not None and tb != cur_tab[0]:
                    cur_tab[0] = tb
                    nsw[0] += 1
            if op.dma_sem is not None:
                efree[e] = st + 60.0
                fin[k] = st + c
            else:
                efree[e] = st + c
                fin[k] = st + c
            done[k] = True
            remaining -= 1
            order[e].append(op)
            for u in users[k]:
                ndeps[u] -= 1
                t = fin[k] + lat
                if t > ready_t[u]:
                    ready_t[u] = t
        self.est_ns = max(fin) if n else 0.0
        self.n_tabsw = nsw[0]
        if os.environ.get("MK_CRIT"):
            dist = [0.0] * n
            pred = [-1] * n
            for op in ops:
                c = getattr(op.fn, "cost", 150.0)
                best_t, best_p = 0.0, -1
                for (d, raw, val) in op.deps:
                    t = dist[d.idx] + lat
                    if t > best_t:
                        best_t, best_p = t, d.idx
                dist[op.idx] = best_t + c
                pred[op.idx] = best_p
            lim = self.fence["pe"].idx if self.fence else n
            k = max(range(lim), key=lambda i: dist[i])
            print("[crit] dependency-only critical path before fence:", round(dist[k] / 1000), "us")
            path = []
            while k >= 0:
                path.append(k)
                k = pred[k]
            path.reverse()
            import collections
            cnt = collections.Counter(ops[i].eng for i in path)
            print("[crit] path len", len(path), dict(cnt))
            self.crit_path = path
            mid = int(len(path) * float(os.environ.get("MK_CRITPOS", "0.5")))
            for i in path[mid:mid + 70]:
                print("[crit]  ", ops[i].eng, ops[i].wk, round(getattr(ops[i].fn, "cost", 150.0)))
        if os.environ.get("MK_SCHED_DBG"):
            B = 100000.0
            nb = int(self.est_ns // B) + 1
            busy = {e: [0.0] * nb for e in ENGS}
            for op in ops:
                c = getattr(op.fn, "cost", 150.0)
                if op.dma_sem is not None:
                    continue
                b = int((fin[op.idx] - c) // B)
                busy[op.eng][b] += c
            for b in range(nb):
                print(f"[sched] {b*100:6d}us " + " ".join(f"{e}:{busy[e][b]/B*100:5.1f}%" for e in ENGS if e != "sp"))
            if self.fence:
                print("[sched] fence done at", {e: round(fin[o.idx] / 1000) for e, o in self.fence.items()})
        return order

    def eval_order(self, order, lat):
        ops = self.ops
        fin = {}
        pos = {e: 0 for e in ENGS}
        efree = {e: 0.0 for e in ENGS}
        ecur = [None]
        esw = [0]
        self._esw = esw
        remaining = sum(len(v) for v in order.values())
        while remaining:
            progressed = False
            for e in ENGS:
                while pos[e] < len(order[e]):
                    op = order[e][pos[e]]
                    if any(d.idx not in fin for (d, raw, val) in op.deps):
                        break
                    rt = max([fin[d.idx] + lat for (d, raw, val) in op.deps] + [0.0])
                    st = max(rt, efree[e])
                    c = self._cost(op)
                    if e == "act":
                        tb = getattr(op.fn, "tab", None)
                        if tb is not None and tb != ecur[0]:
                            ecur[0] = tb
                            st += 1283.0
                            esw[0] += 1
                    if op.dma_sem is not None:
                        efree[e] = st + 60.0
                    else:
                        efree[e] = st + c
                    fin[op.idx] = st + c
                    pos[e] += 1
                    remaining -= 1
                    progressed = True
            assert progressed, "order deadlock"
        return max(fin.values())

    def emit(self, block, sems, dma_sems, reorder=True):
        for op in self.ops:
            for (d, raw, val) in op.deps:
                if d.dma_sem is None and not self._skip(d, op, raw):
                    d.need_sig = True
        if reorder:
            per_eng = self.schedule()
            if os.environ.get("MK_EVAL_LAT"):
                print("[kernel] eval fixed order @lat", os.environ["MK_EVAL_LAT"], self.eval_order(per_eng, float(os.environ["MK_EVAL_LAT"])), "tab switches", self._esw[0])
        else:
            per_eng = {e: [] for e in ENGS}
            for op in self.ops:
                per_eng[op.eng].append(op)
        cnt = {e: 0 for e in ENGS}
        for e in ENGS:
            for op in per_eng[e]:
                if op.dma_sem is None and op.need_sig:
                    cnt[e] += 1
                    op.sig = cnt[e]
        handles = {"pe": "tensor", "act": "scalar", "dve": "vector", "pool": "gpsimd", "sp": "sync"}

        def body(eng_name):
            def _f(eng):
                waited = {}
                for op in per_eng[eng_name]:
                    for (d, raw, val) in op.deps:
                        if d.dma_sem is not None:
                            key = ("dma", d.dma_sem)
                            sem = dma_sems[d.dma_sem]
                        else:
                            if self._skip(d, op, raw):
                                continue
                            key = d.eng
                            val = d.sig
                            sem = sems[d.eng]
                        if waited.get(key, 0) >= val:
                            continue
                        waited[key] = val
                        eng.wait_ge(sem, val)
                    ins = op.fn(eng)
                    if op.dma_sem is not None:
                        ins.then_inc(dma_sems[op.dma_sem], 16)
                    elif op.need_sig:
                        ins.then_inc(sems[eng_name], 1)
            return _f

        for e in ENGS:
            if per_eng[e]:
                getattr(block, handles[e])(body(e))


def _mmcost(lhsT, rhs):
    n = rhs.free_size()
    c = max(lhsT.free_size() / 1.2, n / 2.37, 30.0)
    if rhs.dtype == F32:
        c *= 4.0
    return c


def _fsz(ap):
    return ap.free_size()


def _mm(out, lhsT, rhs, start=True, stop=True):
    f = lambda e: e.matmul(out, lhsT=lhsT, rhs=rhs, start=start, stop=stop)
    f.cost = _mmcost(lhsT, rhs)
    return f


def _mmk(out, pairs):
    def f(e):
        n = len(pairs)
        ins = None
        for i, (l, r) in enumerate(pairs):
            ins = e.matmul(out, lhsT=l, rhs=r, start=(i == 0), stop=(i == n - 1))
        return ins
    f.cost = sum(_mmcost(l, r) for (l, r) in pairs)
    return f


def _trs(items, ident):
    def f(e):
        ins = None
        for (o, i) in items:
            ins = e.transpose(out=o, in_=i, identity=ident)
        return ins
    f.cost = 120.0 * len(items)
    return f


def _act(out, in_, func, bias=None, scale=None, accum=None):
    def f(e):
        kw = {}
        if bias is not None:
            kw["bias"] = bias
        if scale is not None:
            kw["scale"] = scale
        if accum is not None:
            kw["accum_out"] = accum
        return e.activation(out=out, in_=in_, func=func, **kw)
    f.cost = (224.0 + _fsz(out)) / 1.2
    f.tab = {AF.Silu: "S", AF.Tanh: "S", AF.Exp: "E", AF.Ln: "E"}.get(func)
    return f


def _ts(out, in0, s1, op0, s2=None, op1=None):
    def f(e):
        if op1 is None:
            return e.tensor_scalar(out=out, in0=in0, scalar1=s1, scalar2=None, op0=op0)
        return e.tensor_scalar(out=out, in0=in0, scalar1=s1, scalar2=s2, op0=op0, op1=op1)
    f.cost = (100.0 + _fsz(out)) / 0.96
    return f


def _tt(out, in0, in1, op):
    f = lambda e: e.tensor_tensor(out=out, in0=in0, in1=in1, op=op)
    f.cost = (100.0 + _fsz(out)) / 0.96
    return f


def _stt(out, in0, scalar, in1, op0, op1):
    f = lambda e: e.scalar_tensor_tensor(out=out, in0=in0, scalar=scalar, in1=in1, op0=op0, op1=op1)
    f.cost = (100.0 + _fsz(out)) / 0.96
    return f


def _cp(out, in_):
    f = lambda e: e.tensor_copy(out=out, in_=in_)
    f.cost = (100.0 + _fsz(out)) / 0.96
    return f


def _dma(out, in_):
    f = lambda e: e.dma_start(out=out, in_=in_)
    f.cost = 2000.0 + 128.0 * _fsz(out) * 4 / 200.0
    return f


def _scan(out, d0, d1, init, op0, op1):
    f = lambda e: e.tensor_tensor_scan(out=out, data0=d0, data1=d1, initial=init, op0=op0, op1=op1)
    f.cost = (100.0 + 2 * _fsz(out)) / 0.96
    return f


class _Arena:
    def __init__(self, nc, base, end):
        self.nc = nc
        self.off = base
        self.end = end

    def alloc(self, name, shape, dt):
        size = 1
        for s in shape[1:]:
            size *= s
        size *= 2 if dt == BF16 else 4
        off = (self.off + 31) // 32 * 32
        assert off + size <= self.end, f"SBUF overflow at {name}: {off + size} > {self.end}"
        t = self.nc.alloc_sbuf_tensor_at(name, list(shape), dt, offset=off)
        self.off = off + size
        return t.ap()


WGROUPS = [(1024, 2056), (512, 1024), (5640, 6152), (3080, 4104), (0, 512), (5128, 5640), (2056, 3080), (4104, 5128)]


def _wgroup_of(col):
    for g, (a, b) in enumerate(WGROUPS):
        if a <= col < b:
            return g
    raise ValueError(col)


def build_program():
    nc = bass.Bass("TRN2", target_bir_lowering=False)

    def din(name, shape, dt=F32):
        return nc.dram_tensor(name, list(shape), dt, kind="ExternalInput").ap()

    xs_d = din("xs", [NPOS, D])
    win_d = din("w_in_r", [D, WCOLS])
    wout_d = din("w_out", [2048, D])
    wup_d = din("w_up", [D, DFF])
    wgate_d = din("w_gate", [D, DFF])
    wdn_d = din("w_down", [DFF, D])
    cs_d = din("cs_tab", [NCH, 128, 1024])
    vm_d = din("vm_tab", [NCH, 4, 2, 128])
    dqk_d = din("dqk", [128, 12])
    cmask_d = din("cmask", [128, 128])
    mneg_d = din("maskneg4", [128, 128])
    identb_d = din("ident_bf", [128, 128], BF16)
    identf_d = din("ident_f", [128, 128])
    i4_d = din("i4", [4, 8])
    convml_d = din("convw_ml", [128, 32])
    convff_d = din("convw_ffn", [128, NJ * 3])
    bif_d = din("b_if", [4, 2])
    gcol_d = din("gcol", [128, 16])
    gmix_d = din("g_mix_b", [128, D])
    gffn_d = din("g_ffn_b", [128, D])
    gfin_d = din("g_fin_b", [128, D])
    out_d = nc.dram_tensor("out", [2048, D], F32, kind="ExternalOutput").ap()
    mixed_d = nc.dram_tensor("mixed_d", [NCH_F * CH, 2048], BF16,
                             kind="ExternalOutput" if DEBUG else "Internal").ap()

    S = Sched()
    banks = [nc.alloc_psum_tensor(f"bank{i}", [128, 512], F32).ap() for i in range(8)]

    per = _Arena(nc, SBUF_BASE, SBUF_END)
    ident_bf = per.alloc("ident_bf", [128, 128], BF16)
    mhalf = per.alloc("mhalf", [128, 4], F32)
    stat = per.alloc("stat", [128, 8], F32)
    PH_BASE = per.off

    S.add("sp", _dma(ident_bf, identb_d), writes=["ident_bf"], dma="cst")
    S.add("pool", lambda e: e.memset(mhalf, -0.5), writes=["mhalf"])

    A1 = _Arena(nc, PH_BASE, SBUF_END)
    W = A1.alloc("W", [128, 8, WCOLS], BF16)
    ident_f = A1.alloc("ident_f", [128, 128], F32)
    cmask = A1.alloc("cmask", [128, 128], F32)
    maskneg4 = A1.alloc("maskneg4", [128, 128], F32)
    dqk = A1.alloc("dqk", [128, 12], F32)
    g_mix_b = A1.alloc("g_mix_b", [128, D], F32)
    convml = A1.alloc("convml", [128, 32], F32)
    i4 = A1.alloc("i4", [4, 8], F32)
    bif = A1.alloc("bif", [4, 2], F32)
    bsc = A1.alloc("bsc", [4, 2], F32)
    ones4 = A1.alloc("ones4", [4, 128], F32)
    zeros4 = A1.alloc("zeros4", [4, 128], F32)
    xin = A1.alloc("xin", [128, D], F32)
    u = A1.alloc("u", [128, D], BF16)
    uT = [A1.alloc(f"uT{i}", [128, 8, 128], BF16) for i in range(2)]
    asb = A1.alloc("asb", [128, 8, 131], F32)
    cacc = [A1.alloc(f"cacc{i}", [128, 128], F32) for i in range(2)]
    qTmls = [A1.alloc(f"qTml{i}", [128, 4, 128], BF16) for i in range(2)]
    kTmls = [A1.alloc(f"kTml{i}", [128, 4, 128], BF16) for i in range(2)]
    rX = [A1.alloc("rX0", [128, 512], F32)] * 2
    rM = [A1.alloc(f"rM{i}", [128, 256], F32) for i in range(4)]
    qtok = A1.alloc("qtok", [128, 4, 128], BF16)
    ktok = A1.alloc("ktok", [128, 4, 128], BF16)
    cst = [A1.alloc("cst0", [128, 1024], F32)] * 2
    vmt = [A1.alloc("vmt0", [4, 2, 128], F32)] * 2
    qTrts = [A1.alloc(f"qTrt{i}", [128, 4, 128], BF16) for i in range(2)]
    kTrts = [A1.alloc(f"kTrt{i}", [128, 4, 128], BF16) for i in range(2)]
    kw = A1.alloc("kw", [128, 4, 128], BF16)
    Vmls = [A1.alloc(f"Vml{i}", [128, 4, 257], BF16) for i in range(2)]
    Vrts = [A1.alloc(f"Vrt{i}", [128, 4, 256], BF16) for i in range(2)]
    ogs = [A1.alloc(f"og{i}", [128, D], F32) for i in range(2)]
    ggs = [A1.alloc(f"gg{i}", [128, D], F32) for i in range(2)]
    mixed = A1.alloc("mixed", [128, 2048], BF16)
    Cml_f = A1.alloc("Cml_f", [128, 4, 257], F32)
    Cml_bfs = [A1.alloc(f"Cml_bf{i}", [128, 4, 257], BF16) for i in range(2)]
    Crt_f = A1.alloc("Crt_f", [128, 4, 256], F32)
    Crt_bfs = [A1.alloc(f"Crt_bf{i}", [128, 4, 256], BF16) for i in range(2)]
    WT = A1.alloc("WT", [128, 4, 128], F32)
    PT = A1.alloc("PT", [128, 4, 128], BF16)
    Wint = A1.alloc("Wint", [128, 512], F32)
    qsT = A1.alloc("qsT", [128, 4, 128], BF16)
    hrt = A1.alloc("hrt", [128, 4, 256], F32)
    hraw = A1.alloc("hraw", [128, 4, 257], F32)
    rows = {n: A1.alloc("row_" + n, [4, 128], F32) for n in
            ("li0", "li1", "li", "ef", "sp", "nbcum", "B", "M", "R2", "R3")}
    rhs_bd = A1.alloc("rhs_bd", [4, 4, 128], F32)
    smr = A1.alloc("smr", [4, 16], F32)
    sm = A1.alloc("sm", [128, 64], F32)
    smps = A1.alloc("smps", [128, 20], F32)
    st6 = A1.alloc("st6", [128, 4, 6], F32)
    mv = A1.alloc("mv", [128, 4, 2], F32)

    EX8 = sm[:, 0:8]
    BT = sm[:, 8:12]
    BTS = sm[:, 12:16]
    WSARG = sm[:, 16:20]
    WSRC = sm[:, 20:24]
    DEC = sm[:, 24:28]
    DD = sm[:, 28:32]
    RDEN = sm[:, 32:36]
    T1 = sm[:, 36:40]
    T2 = sm[:, 40:44]
    RSTD = sm[:, 44:48]
    SC = sm[:, 48:52]
    BI = sm[:, 52:56]

    bT = banks[0]
    bT_bf = bT.bitcast(BF16)

    for nm, dst, src in (("ident_f", ident_f, identf_d), ("cmask", cmask, cmask_d), ("maskneg4", maskneg4, mneg_d),
                         ("dqk", dqk, dqk_d), ("g_mix_b", g_mix_b, gmix_d),
                         ("convml", convml, convml_d), ("i4", i4, i4_d), ("bif", bif, bif_d)):
        S.add("sp", _dma(dst, src), writes=[nm], dma="cst")
    for g, (c0, c1) in enumerate(WGROUPS):
        for k in range(8):
            S.add("pool", _dma(W[:, k, c0:c1], win_d[k * 128:(k + 1) * 128, c0:c1]),
                  writes=[("W", g, k)], dma=f"W{g}")

    def wk(col):
        g = _wgroup_of(col)
        return [("W", g, k) for k in range(8)]

    S.add("pool", lambda e: e.memset(ones4, 1.0), writes=["ones4"])
    S.add("pool", lambda e: e.memset(zeros4, 0.0), writes=["zeros4"])
    S.add("pool", lambda e: e.memset(asb, 0.0), writes=[("asb", t) for t in range(8)])
    S.add("pool", lambda e: e.memset(Vmls[0], 1.0), writes=[("Vml", 0, 0), ("Vml", 0, 1)])
    S.add("pool", lambda e: e.memset(Vmls[1], 1.0), writes=[("Vml", 1, 0), ("Vml", 1, 1)])
    S.add("pool", lambda e: e.memset(Cml_f, 0.0), writes=[("Cml_f", h) for h in range(4)])
    S.add("pool", lambda e: e.memset(Cml_bfs[0], 0.0), writes=[("Cml_bf", 0, h) for h in range(4)])
    S.add("pool", lambda e: e.memset(Crt_f, 0.0), writes=[("Crt_f", h) for h in range(4)])
    S.add("pool", lambda e: e.memset(Crt_bfs[0], 0.0), writes=[("Crt_bf", 0, h) for h in range(4)])
    S.add("pool", lambda e: e.memset(smr, 0.0), writes=["mst", "D1", "diagM", "diagD"])
    S.add("pool", lambda e: e.memset(smr[:, 0:1], NEG), writes=["mst"])
    S.add("dve", _ts(bsc[:, 0:1], bif[:, 0:1], 1.0 / GATE_CAP, ALU.mult), reads=["bif"], writes=["bsc0"])
    S.add("dve", _ts(bsc[:, 1:2], bif[:, 1:2], -1.0, ALU.mult), reads=["bif"], writes=["bsc1"])

    MST = smr[:, 0:1]
    D1 = smr[:, 1:2]
    DIAGM = smr[:, 4:8]
    DIAGD = smr[:, 8:12]

    def loads(c):
        s = c % 2
        S.add("sp", _dma(xin, xs_d[c * 128:(c + 1) * 128, :]), writes=["xin"], dma="xin")

    def loads2(c):
        S.add("sp", _dma(cst[0], cs_d[c]), writes=[("cst", 0)], dma="cst0")
        S.add("sp", _dma(vmt[0], vm_d[c]), writes=[("vmt", 0)], dma="vmt0")

    aslot = [0]

    def next_aslot():
        i = aslot[0] % 4
        aslot[0] += 1
        return i

    tmslot = [0]

    def rmsnorm_T(src, gb, dstT, dst_keys, src_key, gkey):
        S.add("act", _act(u, src, AF.Square, accum=stat[:, 0:1]), reads=[src_key], writes=["u", "ss"])
        S.add("dve", _ts(stat[:, 1:2], stat[:, 0:1], 1.0 / D, ALU.mult, EPS, ALU.add), reads=["ss"], writes=["ms"])
        S.add("pool", _tt(stat[:, 2:3], stat[:, 1:2], mhalf[:, 0:1], ALU.pow), reads=["ms", "mhalf"], writes=["rstd"])
        S.add("dve", _stt(u, src, stat[:, 2:3], gb, ALU.mult, ALU.mult), reads=[src_key, "rstd", gkey], writes=["u"])
        S.add("pe", _trs([(bT_bf[:, k * 128:(k + 1) * 128], u[:, k * 128:(k + 1) * 128]) for k in range(8)], ident_bf),
              reads=["u", "ident_bf"], writes=[("bk", 0)])
        S.add("act", _act(dstT, bT_bf.rearrange("p (k t) -> p k t", k=8), AF.Copy), reads=[("bk", 0)], writes=dst_keys)


    fmb = [0]

    def chunk(c, full):
        rp, wp = c % 2, (c + 1) % 2
        qTml, kTml, qTrt, kTrt = qTmls[rp], kTmls[rp], qTrts[rp], kTrts[rp]
        og, gg = ogs[rp], ggs[rp]
        Vml, Vrt = Vmls[rp], Vrts[rp]
        Cml_bf, Crt_bf = Cml_bfs[rp], Crt_bfs[rp]
        Cml_bfw, Crt_bfw = Cml_bfs[wp], Crt_bfs[wp]
        if os.environ.get("MK_FAKE2"):
            S.fake_par = c % int(os.environ["MK_FAKE2"])
        s = c % 2
        uTs = uT[s]
        rmsnorm_T(xin, g_mix_b, uTs, [("uT", s)], "xin", "g_mix_b")
        if c + 1 < NCH:
            loads(c + 1)

        def fm_group(cols_ms):
            b = 1 + fmb[0] % 2
            fmb[0] += 1
            bk = banks[b]
            for sl, (col, m) in enumerate(cols_ms):
                S.add("pe", _mmk(bk[0:m, sl * 128:(sl + 1) * 128], [(W[:, k, col:col + m], uTs[:, k, :]) for k in range(8)]),
                      reads=wk(col) + [("uT", s)], writes=[("bk", b)])
            return bk, ("bk", b)

        for grp in ((0, 1) if full else (1,)):
            bk, bkey = fm_group([((grp * 4 + t) * 128, 128) for t in range(4)])
            S.add("act", _act(asb[:, grp * 4:grp * 4 + 4, 3:131], bk.rearrange("p (a b) -> p a b", a=4), AF.Copy),
                  reads=[bkey], writes=[("asb", grp * 4 + t) for t in range(4)])
            for t in range(grp * 4, grp * 4 + 4):
                acc = cacc[t % 2]
                ak = ("cacc", t % 2)
                S.add("dve", _ts(acc, asb[:, t, 3:131], convml[:, t * 4 + 3:t * 4 + 4], ALU.mult),
                      reads=[("asb", t), "convml"], writes=[ak])
                for kk in (2, 1, 0):
                    S.add("dve", _stt(acc, asb[:, t, kk:kk + 128], convml[:, t * 4 + kk:t * 4 + kk + 1], acc, ALU.mult, ALU.add),
                          reads=[("asb", t), "convml", ak], writes=[ak])
                S.add("pool", _cp(asb[:, t, 0:3], asb[:, t, 128:131]), reads=[("asb", t)], writes=[("asb", t)])
                if t < 4:
                    S.add("act", _act(qTml[:, t, :], acc, AF.Silu), reads=[ak], writes=[("qTml", rp, t)])
                else:
                    S.add("act", _act(kTml[:, t - 4, :], acc, AF.Silu), reads=[ak], writes=[("kTml", rp, t - 4)])
        bk, gkey = fm_group([(1024, 4), (1028, 4)])
        gi_reg = bk[0:4, 0:128]
        gf_reg = bk[0:4, 128:256]
        R = rows
        vs = vmt[s]
        S.add("act", _act(R["li0"], gi_reg, AF.Tanh, bias=bsc[:, 0:1], scale=1.0 / GATE_CAP), reads=[gkey, "bsc0"], writes=["li0"])
        S.add("act", _act(R["ef"], gf_reg, AF.Exp, bias=bsc[:, 1:2], scale=-1.0), reads=[gkey, "bsc1"], writes=["ef"])
        S.add("dve", _stt(R["li1"], R["li0"], GATE_CAP, vs[:, 0, :], ALU.mult, ALU.mult), reads=["li0", ("vmt", 0)], writes=["li1"])
        S.add("dve", _tt(R["li"], R["li1"], vs[:, 1, :], ALU.add), reads=["li1", ("vmt", 0)], writes=["li"])
        S.add("act", _act(R["sp"], R["ef"], AF.Ln, bias=1.0), reads=["ef"], writes=["sp"])
        S.add("dve", _scan(R["nbcum"], R["sp"], zeros4, 0.0, ALU.add, ALU.add), reads=["sp", "zeros4"], writes=["nbcum"])
        S.add("dve", _tt(R["B"], R["li"], R["nbcum"], ALU.add), reads=["li", "nbcum"], writes=["B"])
        S.add("dve", _scan(R["M"], R["B"], R["B"], MST, ALU.max, ALU.max), reads=["B", "mst"], writes=["M"])
        S.add("dve", _ts(DIAGM, i4[:, 0:4], R["M"][:, 127:128], ALU.mult), reads=["i4", "M"], writes=["diagM"])
        S.add("dve", _tt(D1, MST, R["M"][:, 127:128], ALU.subtract), reads=["mst", "M"], writes=["D1"])
        S.add("dve", _ts(DIAGD, i4[:, 0:4], D1, ALU.mult), reads=["i4", "D1"], writes=["diagD"])
        if full:
            S.add("dve", _tt(R["R2"], R["nbcum"], R["M"], ALU.subtract), reads=["nbcum", "M"], writes=["R2"])
            S.add("dve", _ts(R["R2"], R["R2"], 80.0, ALU.min), reads=["R2"], writes=["R2"])
            S.add("dve", _ts(R["R3"], R["M"], MST, ALU.subtract, -1.0, ALU.mult), reads=["M", "mst"], writes=["R3"])
            for h in range(4):
                S.add("dve", _ts(rhs_bd[:, h, :], R["M"], i4[:, 4 + h:5 + h], ALU.mult), reads=["M", "i4"], writes=[("rhs_bd", h)])
        S.add("dve", _tt(MST, R["M"][:, 127:128], R["nbcum"][:, 127:128], ALU.subtract), reads=["M", "nbcum"], writes=["mst"])

        sb_ = 5
        SMP = banks[sb_][:, 0:32]
        skey = ("bk", sb_)

        def smp_mm(e):
            ins = e.matmul(SMP[:, 0:4], lhsT=R["B"], rhs=i4[:, 0:4], start=True, stop=True)
            if full:
                e.matmul(SMP[:, 4:8], lhsT=R["R2"], rhs=i4[:, 0:4], start=True, stop=True)
                e.matmul(SMP[:, 8:12], lhsT=R["R3"], rhs=i4[:, 0:4], start=True, stop=True)
            e.matmul(SMP[:, 12:16], lhsT=ones4, rhs=DIAGM, start=True, stop=True)
            ins = e.matmul(SMP[:, 16:20], lhsT=ones4, rhs=DIAGD, start=True, stop=True)
            return ins
        S.add("pe", smp_mm, reads=["B", "R2", "R3", "i4", "ones4", "diagM", "diagD"], writes=[skey])
        if full:
            S.add("act", _act(smps, SMP[:, 0:20], AF.Copy), reads=[skey], writes=["smps"])
        else:
            S.add("act", _act(smps[:, 0:4], SMP[:, 0:4], AF.Copy), reads=[skey], writes=["smps"])
            S.add("act", _act(smps[:, 12:20], SMP[:, 12:20], AF.Copy), reads=[skey], writes=["smps"])
        if full:
            S.add("act", _act(EX8, smps[:, 4:12], AF.Exp), reads=["smps"], writes=["ex8"])
        S.add("dve", _tt(WSARG, smps[:, 0:4], smps[:, 12:16], ALU.subtract), reads=["smps"], writes=["wsarg"])
        S.add("act", _act(WSRC, WSARG, AF.Exp, bias=float(np.log(S_ML))), reads=["wsarg"], writes=["wsrc"])
        S.add("act", _act(DEC, smps[:, 16:20], AF.Exp), reads=["smps"], writes=["dec"])
        b5 = banks[5]
        k5 = ("bk", 5)
        if full:
            S.add("dve", _ts(BTS, smps[:, 0:4], float(np.log(S_ML)), ALU.add), reads=["smps"], writes=["BTS"])
            S.add("pe", _mmk(b5, [(ones4, rhs_bd.rearrange("p h t -> p (h t)")), (ident_f, maskneg4.unsqueeze(1).broadcast_to([128, 4, 128]))]),
                  reads=["ones4", "ident_f", "maskneg4"] + [("rhs_bd", h) for h in range(4)], writes=[k5])
            for h in range(4):
                S.add("act", _act(WT[:, h, :], b5[:, h * 128:(h + 1) * 128], AF.Exp, bias=BTS[:, h:h + 1]),
                      reads=[k5, "BTS"], writes=[("WT", h)])
            for h in range(4):
                S.add("dve", _ts(rhs_bd[:, h, :], R["R3"], i4[:, h:h + 1], ALU.mult), reads=["R3", "i4"], writes=[("rhs_bd", h)])
            S.add("pe", _mm(b5, ones4, rhs_bd.rearrange("p h t -> p (h t)")),
                  reads=["ones4"] + [("rhs_bd", h) for h in range(4)], writes=[k5])
            S.add("act", _act(Wint, b5, AF.Exp), reads=[k5], writes=["Wint"])
            S.add("dve", _tt(qsT.rearrange("p h t -> p (h t)"), qTml.rearrange("p h t -> p (h t)"), Wint, ALU.mult),
                  reads=["Wint"] + [("qTml", rp, h) for h in range(4)], writes=["qsT"])

        def tm_tile(col):
            b = 3 + tmslot[0] % 2
            tmslot[0] += 1
            S.add("pe", _mmk(banks[b], [(uTs[:, k, :], W[:, k, col:col + 512]) for k in range(8)]),
                  reads=wk(col) + [("uT", s)], writes=[("bk", b)])
            return banks[b], ("bk", b)

        for hh in range(2):
            reg, rk = tm_tile(1032 + hh * 512)
            S.add("act", _act(Vml[:, 2 * hh:2 * hh + 2, 0:256], reg.rearrange("p (a b) -> p a b", a=2), AF.Copy),
                  reads=[rk], writes=[("Vml", rp, hh)])
        for hh in range(2):
            reg, rk = tm_tile(3080 + hh * 512)
            S.add("dve", _cp(Vrt[:, 2 * hh:2 * hh + 2, :], reg.rearrange("p (a b) -> p a b", a=2)),
                  reads=[rk], writes=[("Vrt", rp, hh)])
        if full:
            for hh in range(2):
                reg, rk = tm_tile(2056 + hh * 512)
                S.add("act", _act(og[:, hh * 512:(hh + 1) * 512], reg, AF.Tanh, scale=0.5), reads=[rk], writes=[("og", rp, hh)])
            for hh in range(2):
                reg, rk = tm_tile(4104 + hh * 512)
                S.add("act", _act(gg[:, hh * 512:(hh + 1) * 512], reg, AF.Silu), reads=[rk], writes=[("gg", rp, hh)])

        def rotary(col, xi, qk, dst, dkey):
            reg, rk = tm_tile(col)
            X = rX[xi]
            xk = ("rX", 0)
            S.add("act", _act(X, reg, AF.Copy), reads=[rk], writes=[xk])
            Xv = X.rearrange("p (h a t) -> p h a t", h=4, a=2)
            Tc = cst[0][:, (qk * 2) * 256:(qk * 2 + 1) * 256].rearrange("p (h t) -> p h t", h=4)
            Ts = cst[0][:, (qk * 2 + 1) * 256:(qk * 2 + 2) * 256].rearrange("p (h t) -> p h t", h=4)
            Mv = [m.rearrange("p (h t) -> p h t", h=4) for m in rM]
            Dv = dst.rearrange("p h (a t) -> p h a t", a=2)
            S.add("dve", _tt(Mv[0], Xv[:, :, 0, :], Tc, ALU.mult), reads=[xk, ("cst", 0)], writes=[("rM", 0)])
            S.add("dve", _tt(Mv[1], Xv[:, :, 1, :], Ts, ALU.mult), reads=[xk, ("cst", 0)], writes=[("rM", 1)])
            S.add("pool", _tt(Mv[2], Xv[:, :, 0, :], Ts, ALU.mult), reads=[xk, ("cst", 0)], writes=[("rM", 2)])
            S.add("pool", _tt(Mv[3], Xv[:, :, 1, :], Tc, ALU.mult), reads=[xk, ("cst", 0)], writes=[("rM", 3)])
            S.add("dve", _tt(Dv[:, :, 0, :], Mv[0], Mv[1], ALU.subtract), reads=[("rM", 0), ("rM", 1)], writes=[(dkey, 0)])
            S.add("pool", _tt(Dv[:, :, 1, :], Mv[2], Mv[3], ALU.add), reads=[("rM", 2), ("rM", 3)], writes=[(dkey, 1)])

        rotary(5640, 0, 1, ktok, "ktok")
        if full and not os.environ.get("MK_X1"):
            rotary(5128, 1, 0, qtok, "qtok")
        if c + 1 < NCH:
            loads2(c + 1)

        ob = [6]

        def next_ob():
            b = 6 + ob[0] % 2
            ob[0] += 1
            return banks[b], ("bk", b)

        kb_, kbk = next_ob()
        b0bf = kb_.bitcast(BF16)
        S.add("pe", _trs([(b0bf[:, t * 128:(t + 1) * 128], kTml[:, t, :]) for t in range(4)], ident_bf),
              reads=[("kTml", rp, h) for h in range(4)] + ["ident_bf"], writes=[kbk])
        for t in range(4):
            S.add("act", _act(kw[:, t, :], b0bf[:, t * 128:(t + 1) * 128], AF.Copy, scale=WSRC[:, t:t + 1]),
                  reads=[kbk, "wsrc"], writes=[("kw", t)])
        if full:
            qb_, qbk = next_ob()
            qbbf = qb_.bitcast(BF16)
            S.add("pe", _trs([(qbbf[:, h * 128:(h + 1) * 128], ktok[:, h, :]) for h in range(4)] +
                             [(qbbf[:, (4 + h) * 128:(5 + h) * 128], qtok[:, h, :]) for h in range(4)], ident_bf),
                  reads=[("ktok", 0), ("ktok", 1), ("qtok", 0), ("qtok", 1), "ident_bf"], writes=[qbk])
            S.add("act", _act(kTrt, qbbf[:, 0:512].rearrange("p (h t) -> p h t", h=4), AF.Copy), reads=[qbk],
                  writes=[("kTrt", rp, h) for h in range(4)])
            S.add("act", _act(qTrt, qbbf[:, 512:1024].rearrange("p (h t) -> p h t", h=4), AF.Copy), reads=[qbk],
                  writes=[("qTrt", rp, h) for h in range(4)])

        for h in range(4):
            bo, ko = next_ob()
            S.add("pe", _mm(bo[:, 0:257], kw[:, h, :], Vml[:, h, :]), reads=[("kw", h), ("Vml", rp, h // 2)], writes=[ko])
            S.add("dve", _stt(Cml_f[:, h, :], Cml_f[:, h, :], DEC[:, h:h + 1], bo[:, 0:257], ALU.mult, ALU.add),
                  reads=[("Cml_f", h), "dec", ko], writes=[("Cml_f", h)])
            S.add("pool", _cp(Cml_bfw[:, h, :], Cml_f[:, h, :]), reads=[("Cml_f", h)], writes=[("Cml_bf", wp, h)])
        for pr in range(2):
            bo, ko = next_ob()
            for hh in range(2):
                h = 2 * pr + hh
                S.add("pe", _mm(bo[:, hh * 256:(hh + 1) * 256], ktok[:, h, :], Vrt[:, h, :]), reads=[("ktok", 0), ("ktok", 1), ("Vrt", rp, pr)], writes=[ko])
            for hh in range(2):
                h = 2 * pr + hh
                S.add("dve", _stt(Crt_f[:, h, :], Crt_f[:, h, :], CD[h], bo[:, hh * 256:(hh + 1) * 256], ALU.mult, ALU.add),
                      reads=[("Crt_f", h), ko], writes=[("Crt_f", h)])
            for hh in range(2):
                h = 2 * pr + hh
                S.add("pool", _ts(Crt_bfw[:, h, :], Crt_f[:, h, :], CD[h], ALU.mult, 0.0, ALU.add),
                      reads=[("Crt_f", h)], writes=[("Crt_bf", wp, h)])
        if full:
            S.add("pe", lambda e: [e.matmul(b5[:, h * 128:(h + 1) * 128], lhsT=kTml[:, h, :], rhs=qTml[:, h, :], start=True, stop=True)
                                   for h in range(4)][-1],
                  reads=[("kTml", rp, h) for h in range(4)] + [("qTml", rp, h) for h in range(4)], writes=[k5])
            S.add("dve", _tt(PT.rearrange("p h t -> p (h t)"), b5, WT.rearrange("p h t -> p (h t)"), ALU.mult),
                  reads=[k5] + [("WT", h) for h in range(4)], writes=["PT"])
            for h in range(4):
                bo, ko = next_ob()
                S.add("pe", _mmk(bo[:, 0:257], [(PT[:, h, :], Vml[:, h, :]), (qsT[:, h, :], Cml_bf[:, h, :])]),
                      reads=["PT", ("Vml", rp, h // 2), "qsT", ("Cml_bf", rp, h)], writes=[ko])
                S.add("act", _act(hraw[:, h, :], bo[:, 0:257], AF.Copy), reads=[ko], writes=[("hraw", h)])
                S.add("dve", lambda e, h=h: e.bn_stats(out=st6[:, h, :], in_=hraw[:, h, 0:256]), reads=[("hraw", h)], writes=[("st6", h)])
                S.add("dve", lambda e, h=h: e.bn_aggr(out=mv[:, h, :], in_=st6[:, h, :]), reads=[("st6", h)], writes=[("mv", h)])
            allh = [("hraw", h) for h in range(4)]
            allmv = [("mv", h) for h in range(4)]
            S.add("dve", _ts(T2, hraw[:, :, 256], -1.0, ALU.mult), reads=allh, writes=["t2"])
            S.add("dve", _tt(DD, T2, hraw[:, :, 256], ALU.max), reads=allh + ["t2"], writes=["dd"])
            S.add("dve", _tt(DD, DD, EX8[:, 0:4], ALU.max), reads=["dd", "ex8"], writes=["dd"])
            S.add("dve", lambda e: e.reciprocal(out=RDEN, in_=DD), reads=["dd"], writes=["rden"])
            S.add("dve", _tt(T1, RDEN, RDEN, ALU.mult), reads=["rden"], writes=["t1"])
            S.add("dve", _tt(T2, T1, mv[:, :, 1], ALU.mult), reads=["t1"] + allmv, writes=["t2"])
            S.add("dve", _ts(T1, T2, EPS, ALU.add), reads=["t2"], writes=["t1"])
            S.add("pool", _tt(RSTD, T1, mhalf, ALU.pow), reads=["t1", "mhalf"], writes=["rstdh"])
            S.add("dve", _tt(SC, RDEN, RSTD, ALU.mult), reads=["rden", "rstdh"], writes=["sc"])
            S.add("dve", _stt(BI, mv[:, :, 0], -1.0, SC, ALU.mult, ALU.mult), reads=allmv + ["sc"], writes=["bi"])
            for h in range(4):
                S.add("act", _act(hraw[:, h, 0:256], hraw[:, h, 0:256], AF.Identity, bias=BI[:, h:h + 1], scale=SC[:, h:h + 1]),
                      reads=[("hraw", h), "sc", "bi"], writes=[("hraw", h)])
            S.add("dve", _stt(mixed[:, 0:1024].rearrange("p (h v) -> p h v", h=4), og.rearrange("p (h v) -> p h v", h=4), 1.0,
                              hraw[:, :, 0:256], ALU.add, ALU.mult),
                  reads=allh + [("og", rp, 0), ("og", rp, 1)], writes=[("mixed", 0)])
            S.add("pe", lambda e: [e.matmul(b5[:, h * 128:(h + 1) * 128], lhsT=kTrt[:, h, :], rhs=qTrt[:, h, :], start=True, stop=True)
                                   for h in range(4)][-1],
                  reads=[("kTrt", rp, h) for h in range(4)] + [("qTrt", rp, h) for h in range(4)], writes=[k5])
            S.add("dve", _tt(PT, b5.rearrange("p (h t) -> p h t", h=4), cmask.unsqueeze(1).broadcast_to([128, 4, 128]), ALU.mult),
                  reads=[k5, "cmask"], writes=["PT"])
            for pr in range(2):
                bo, ko = next_ob()
                for hh in range(2):
                    h = 2 * pr + hh
                    S.add("pe", _mmk(bo[:, hh * 256:(hh + 1) * 256], [(PT[:, h, :], Vrt[:, h, :]), (qTrt[:, h, :], Crt_bf[:, h, :])]),
                          reads=["PT", ("Vrt", rp, pr), ("qTrt", rp, h), ("Crt_bf", rp, h)], writes=[ko])
                S.add("act", _act(hrt[:, 2 * pr:2 * pr + 2, :], bo.rearrange("p (a b) -> p a b", a=2), AF.Copy),
                      reads=[ko], writes=[("hrt", 2 * pr), ("hrt", 2 * pr + 1)])
                for hh in range(2):
                    h = 2 * pr + hh
                    S.add("dve", lambda e, h=h: e.bn_stats(out=st6[:, h, :], in_=hrt[:, h, :]), reads=[("hrt", h)], writes=[("st6", h)])
                    S.add("dve", lambda e, h=h: e.bn_aggr(out=mv[:, h, :], in_=st6[:, h, :]), reads=[("st6", h)], writes=[("mv", h)])
            allr = [("hrt", h) for h in range(4)]
            S.add("dve", _ts(T1, mv[:, :, 1], EPS, ALU.add), reads=allmv, writes=["t1"])
            S.add("pool", _tt(RSTD, T1, mhalf, ALU.pow), reads=["t1", "mhalf"], writes=["rstdh"])
            S.add("dve", _stt(BI, mv[:, :, 0], -1.0, RSTD, ALU.mult, ALU.mult), reads=allmv + ["rstdh"], writes=["bi"])
            for h in range(4):
                S.add("act", _act(hrt[:, h, :], hrt[:, h, :], AF.Identity, bias=BI[:, h:h + 1], scale=RSTD[:, h:h + 1]),
                      reads=[("hrt", h), "rstdh", "bi"], writes=[("hrt", h)])
            S.add("dve", _tt(mixed[:, 1024:2048], hrt.rearrange("p h v -> p (h v)"), gg, ALU.mult),
                  reads=allr + [("gg", rp, 0), ("gg", rp, 1)], writes=[("mixed", 1)])
        if full:
            f = c - NCH_P
            S.add("act", _dma(mixed_d[f * 128:(f + 1) * 128, :], mixed), reads=[("mixed", 0), ("mixed", 1)],
                  writes=[("mixed_d", f)], dma="mxst")


    loads(0)
    loads2(0)
    _stop = int(os.environ.get("MK_STOP_AFTER", str(NCH)))
    for c in range(min(NCH, _stop)):
        chunk(c, c >= NCH_P)
    if _stop < 100 and os.environ.get("MK_STOP_AFTER"):
        S.fake_par = None
        S.add("sp", lambda e: e.nop(), reads=S.all_keys())
        with ExitStack() as es:
            sems = {e: es.enter_context(nc.semaphore("s_" + e)) for e in ENGS}
            dsems = {k: es.enter_context(nc.semaphore("d_" + k)) for k in S.dma_counts}
            block = es.enter_context(nc.Block())
            S.emit(block, sems, dsems)
        return nc

    S.fake_par = None
    A3 = _Arena(nc, PH_BASE, SBUF_END)
    Wup = A3.alloc("Wup", [128, 8, DFF], BF16)
    Wout = A3.alloc("Wout", [128, 16, D], BF16)
    xh = A3.alloc("xh", [128, 4, D], F32)
    gcol = A3.alloc("gcol", [128, 16], F32)
    assert A3.off <= PH_BASE + 8 * WCOLS * 2, "early F3 weights must alias W only"
    allW = [("W", g, k) for g in range(len(WGROUPS)) for k in range(8)]
    S.add("pool", _dma(Wup[:, 0, :], wup_d[0:128, :]), writes=[("Wup", 0)] + allW, dma="Wup")
    for k in range(1, 8):
        S.add("pool", _dma(Wup[:, k, :], wup_d[k * 128:(k + 1) * 128, :]), reads=[("Wup", 0)], writes=[("Wup", k)], dma="Wup")
    S.add("sp", _dma(gcol, gcol_d), reads=[("Wup", 0)], writes=["gcol"], dma="gcol")
    S.add("dve", _ts(gcol[:, 0:8], gcol[:, 0:8], 0.5, ALU.mult), reads=["gcol"], writes=["gcol"])
    for k in range(16):
        sl = k % 4
        S.add("sp", _dma(xh[:, sl, :], wout_d[k * 128:(k + 1) * 128, :]), reads=[("Wup", 0)], writes=[("xh", sl)], dma=f"xh{sl}")
        S.add("act" if k % 2 else "dve",
              (_act(Wout[:, k, :], xh[:, sl, :], AF.Copy, scale=gcol[:, k:k + 1]) if k % 2 else
               _ts(Wout[:, k, :], xh[:, sl, :], gcol[:, k:k + 1], ALU.mult)),
              reads=[("xh", sl), "gcol", ("Wup", 0)], writes=[("Wout", k)])
    S.barrier(skip=("Wup", "Wout", "W"))
    NWD = 10
    Wgt = A3.alloc("Wgt", [128, 8, DFF], BF16)
    Wring = A3.alloc("Wring", [128, NWD, D], BF16)
    mtoks = [A3.alloc(f"mtok{i}", [128, 2048], BF16) for i in range(2)]
    mixedT = A3.alloc("mixedT", [128, 16, 256], BF16)
    u2Ts = [A3.alloc(f"u2T{i}", [128, 8, 256], BF16) for i in range(2)]
    u3s = [A3.alloc(f"u3_{i}", [128, D], BF16) for i in range(2)]
    NB3 = int(os.environ.get("MK_NB3", "4"))
    asb3s = [A3.alloc(f"asb3_{i}", [128, 258], F32) for i in range(NB3)]
    acc3 = [A3.alloc(f"acc3_{i}", [128, 256], F32) for i in range(NB3)]
    actT = [A3.alloc(f"actT{i}", [128, 256], BF16) for i in range(NB3)]
    halo = A3.alloc("halo", [128, NJ, 2], F32)
    convff = A3.alloc("convff", [128, NJ * 3], F32)
    g_ffn_b = A3.alloc("g_ffn_b", [128, D], F32)
    g_fin_b = A3.alloc("g_fin_b", [128, D], F32)
    stat2 = A3.alloc("stat2", [128, 16], F32)

    print("[kernel] sbuf A1 end", A1.off, "A3 end", A3.off, "limit", SBUF_END)
    S.add("sp", _dma(convff, convff_d), writes=["convff"], dma="cst")
    S.add("sp", _dma(g_ffn_b, gffn_d), writes=["g_ffn_b"], dma="cst")
    S.add("sp", _dma(g_fin_b, gfin_d), writes=["g_fin_b"], dma="cst")
    S.add("pool", lambda e: e.memset(halo, 0.0), writes=[("halo", j) for j in range(NJ)])
    for k in range(8):
        S.add("pool", _dma(Wgt[:, k, :], wgate_d[k * 128:(k + 1) * 128, :]), writes=[("Wgt", k)], dma="Wgt")

    b_acc = [banks[0], banks[1], banks[2], banks[3]]
    ybk = None
    agrot = [0]
    NAG = int(os.environ.get("MK_NAG", "3"))
    b_y = [banks[4 + NAG], banks[7]] if NAG < 3 else [banks[7], banks[7]]
    ybk = [4 + NAG, 7] if NAG < 3 else [7, 7]
    wdc = [0]
    strot = [0]
    mtc = [0]

    def rms_small(src, skey):
        i = strot[0] % 4
        strot[0] += 1
        return stat2[:, 4 * i:4 * i + 1], stat2[:, 4 * i + 1:4 * i + 2], stat2[:, 4 * i + 2:4 * i + 3], i

    def f3_pre(f0, ntt, is_halo, bp):
        T = ntt * 128
        u2T = u2Ts[bp]
        xsl = [bp * 2 + tt for tt in range(ntt)]
        for tt in range(ntt):
            f = f0 + tt
            p0 = (NCH_P + f) * 128
            S.add("sp", _dma(xh[:, xsl[tt], :], xs_d[p0:p0 + 128, :]), writes=[("xh", xsl[tt])], dma=f"xh{xsl[tt]}")
        for tt in range(ntt):
            f = f0 + tt
            mi = mtc[0] % 2
            mtc[0] += 1
            mtok = mtoks[mi]
            S.add("sp", _dma(mtok, mixed_d[f * 128:(f + 1) * 128, :]), reads=[("mixed_d", f)], writes=[("mtok", mi)], dma=f"mtok{mi}")
            for half in range(2):
                yb = b_y[half].bitcast(BF16)
                S.add("pe", _trs([(yb[:, kk * 128:(kk + 1) * 128], mtok[:, (half * 8 + kk) * 128:(half * 8 + kk + 1) * 128])
                                  for kk in range(8)], ident_bf), reads=[("mtok", mi), "ident_bf"], writes=[("bk", ybk[half])])
                S.add("act" if half else "dve",
                      (_act(mixedT[:, half * 8:half * 8 + 8, tt * 128:(tt + 1) * 128], yb.rearrange("p (k t) -> p k t", k=8), AF.Copy)
                       if half else _cp(mixedT[:, half * 8:half * 8 + 8, tt * 128:(tt + 1) * 128], yb.rearrange("p (k t) -> p k t", k=8))),
                      reads=[("bk", ybk[half])], writes=[("mixedT", tt, half)])
        for tt in range(ntt):
            xv = xh[:, xsl[tt], :]
            for half in range(2):
                S.add("pe", _mmk(b_y[half], [(mixedT[:, k, tt * 128:(tt + 1) * 128], Wout[:, k, half * 512:(half + 1) * 512])
                                             for k in range(16)]),
                      reads=[("mixedT", tt, 0), ("mixedT", tt, 1)] + [("Wout", k) for k in range(16)], writes=[("bk", ybk[half])])
                S.add("dve", _tt(xv[:, half * 512:(half + 1) * 512], b_y[half], xv[:, half * 512:(half + 1) * 512], ALU.add),
                      reads=[("bk", ybk[half]), ("xh", xsl[tt])], writes=[("xh", xsl[tt])])
        for tt in range(ntt):
            xv = xh[:, xsl[tt], :]
            xk = ("xh", xsl[tt])
            ss, ms, rs, si = rms_small(None, None)
            u3 = u3s[tt % 2]
            uk = ("u3", tt % 2)
            S.add("act", _act(u3, xv, AF.Square, accum=ss), reads=[xk], writes=[uk, ("ss", si)])
            S.add("dve", _ts(ms, ss, 1.0 / D, ALU.mult, EPS, ALU.add), reads=[("ss", si)], writes=[("ms", si)])
            S.add("pool", _tt(rs, ms, mhalf[:, 0:1], ALU.pow), reads=[("ms", si), "mhalf"], writes=[("rs", si)])
            S.add("dve", _stt(u3, xv, rs, g_ffn_b, ALU.mult, ALU.mult), reads=[xk, ("rs", si), "g_ffn_b"], writes=[uk])
            yb = b_y[tt % 2].bitcast(BF16)
            S.add("pe", _trs([(yb[:, k * 128:(k + 1) * 128], u3[:, k * 128:(k + 1) * 128]) for k in range(8)], ident_bf),
                  reads=[uk, "ident_bf"], writes=[("bk", ybk[tt % 2])])
            S.add("act", _act(u2T[:, :, tt * 128:(tt + 1) * 128], yb.rearrange("p (k t) -> p k t", k=8), AF.Copy),
                  reads=[("bk", ybk[tt % 2])], writes=[("u2T", bp, tt)])

    def f3_main(f0, ntt, is_halo, bp):
        T = ntt * 128
        u2T = u2Ts[bp]
        xsl = [bp * 2 + tt for tt in range(ntt)]
        u2k = [("u2T", bp, tt) for tt in range(ntt)]
        for j in range(NJ):
            par = agrot[0] % NB3
            agb = 4 + (agrot[0] % NAG)
            agrot[0] += 1
            aT = banks[agb][:, 0:T]
            gT = banks[agb][:, 256:256 + T]
            if not is_halo:
                ws = wdc[0] % NWD
                wdc[0] += 1
                S.add("pool", _dma(Wring[:, ws, :], wdn_d[j * 128:(j + 1) * 128, :]), writes=[("Wring", ws)], dma=f"Wd{ws}")
            S.add("pe", _mmk(aT, [(Wup[:, k, j * 128:(j + 1) * 128], u2T[:, k, 0:T]) for k in range(8)]),
                  reads=u2k + [("Wup", k) for k in range(8)], writes=[("bk", agb)])
            if not is_halo:
                S.add("pe", _mmk(gT, [(Wgt[:, k, j * 128:(j + 1) * 128], u2T[:, k, 0:T]) for k in range(8)]),
                      reads=u2k + [("Wgt", k) for k in range(8)], writes=[("bk", agb)])
            asb3 = asb3s[par]
            S.add("pool", _cp(asb3[:, 0:2], halo[:, j, :]), reads=[("halo", j)], writes=[("asb3h", par)])
            S.add("act", _act(asb3[:, 2:2 + T], aT, AF.Copy), reads=[("bk", agb)], writes=[("asb3", par)])
            S.add("pool", _cp(halo[:, j, :], asb3[:, T:T + 2]), reads=[("asb3", par)], writes=[("halo", j)])
            if is_halo:
                continue
            acc = acc3[par][:, 0:T]
            ak = ("acc3", par)
            S.add("dve", _ts(acc, asb3[:, 2:2 + T], convff[:, j * 3 + 2:j * 3 + 3], ALU.mult), reads=[("asb3", par), "convff"], writes=[ak])
            S.add("dve", _stt(acc, asb3[:, 1:1 + T], convff[:, j * 3 + 1:j * 3 + 2], acc, ALU.mult, ALU.add),
                  reads=[("asb3", par), ("asb3h", par), "convff", ak], writes=[ak])
            S.add("dve", _stt(acc, asb3[:, 0:T], convff[:, j * 3:j * 3 + 1], acc, ALU.mult, ALU.add),
                  reads=[("asb3", par), ("asb3h", par), "convff", ak], writes=[ak])
            S.add("act", _act(acc, acc, AF.Silu), reads=[ak], writes=[ak])
            at = actT[par][:, 0:T]
            S.add("dve", _tt(at, acc, gT, ALU.mult), reads=[ak, ("bk", agb)], writes=[("actT", par)])
            for tt in range(ntt):
                for half in range(2):
                    S.add("pe", _mm(b_acc[tt * 2 + half], at[:, tt * 128:(tt + 1) * 128], Wring[:, ws, half * 512:(half + 1) * 512],
                                    start=(j == 0), stop=(j == NJ - 1)),
                          reads=[("actT", par), ("Wring", ws)], writes=[("bk", tt * 2 + half)])
        if is_halo:
            return
        for tt in range(ntt):
            f = f0 + tt
            xv = xh[:, xsl[tt], :]
            xk = ("xh", xsl[tt])
            for half in range(2):
                S.add("dve", _tt(xv[:, half * 512:(half + 1) * 512], b_acc[tt * 2 + half], xv[:, half * 512:(half + 1) * 512], ALU.add),
                      reads=[("bk", tt * 2 + half), xk], writes=[xk])
            ss, ms, rs, si = rms_small(None, None)
            u3 = u3s[tt % 2]
            uk = ("u3", tt % 2)
            S.add("act", _act(u3, xv, AF.Square, accum=ss), reads=[xk], writes=[uk, ("ss", si)])
            S.add("dve", _ts(ms, ss, 1.0 / D, ALU.mult, EPS, ALU.add), reads=[("ss", si)], writes=[("ms", si)])
            S.add("pool", _tt(rs, ms, mhalf[:, 0:1], ALU.pow), reads=[("ms", si), "mhalf"], writes=[("rs", si)])
            S.add("dve", _stt(xv, xv, rs, g_fin_b, ALU.mult, ALU.mult), reads=[xk, ("rs", si), "g_fin_b"], writes=[xk])
            S.add("act", _dma(out_d[(f - 1) * 128:f * 128, :], xv), reads=[xk], writes=[("out", f)], dma=f"ost{xsl[tt]}")

    blocks = [(0, 1, True, 1)] + [(1 + 2 * b, 2, False, b % 2) for b in range(8)]
    f3_pre(*blocks[0])
    for bi, blk in enumerate(blocks):
        if bi + 1 < len(blocks):
            f3_pre(*blocks[bi + 1])
        f3_main(*blk)

    fin = S.add("sp", lambda e: e.nop(), reads=[("out", f) for f in range(1, NCH_F)])

    with ExitStack() as es:
        sems = {e: es.enter_context(nc.semaphore("s_" + e)) for e in ENGS}
        dsems = {k: es.enter_context(nc.semaphore("d_" + k)) for k in S.dma_counts}
        block = es.enter_context(nc.Block())
        S.emit(block, sems, dsems, reorder=bool(int(os.environ.get("MK_REORDER", "1"))))
        print("[kernel] est_ns", getattr(S, "est_ns", None), "ops", len(S.ops))
    return nc


def _host_consts():
    idx = np.arange(128, dtype=np.float64)
    dqk = np.zeros((128, 12), np.float32)
    for h in range(4):
        dqk[:, h] = np.exp(LOG_GAMMA[h] * (idx + 1.0))
        dqk[:, 4 + h] = S_ML * np.exp(-LOG_GAMMA[h] * (idx + 1.0))
        dqk[:, 8 + h] = S_ML * np.exp(-LOG_GAMMA[h] * (idx + 1.0)) * np.exp(LOG_GAMMA[h] * CH)
    jj, ii = np.meshgrid(np.arange(128), np.arange(128), indexing="ij")
    cm = (jj <= ii).astype(np.float32)
    mneg = np.where(jj <= ii, 0.0, NEG).astype(np.float32)
    i4 = np.concatenate([np.eye(4), -np.eye(4)], axis=1).astype(np.float32)
    return dict(dqk=dqk, cmask=cm, maskneg4=mneg,
                ident_bf=np.eye(128).astype(ml_dtypes.bfloat16), ident_f=np.eye(128, dtype=np.float32), i4=i4)


def _rope_tables(n_null):
    p = np.arange(NPOS, dtype=np.float64)
    pos = np.where(p >= n_null, 48.0 + (p - n_null), 0.0)
    inv = 10000.0 ** (-np.arange(0, 128, 2, dtype=np.float64) / 128.0)
    ang = pos[:, None] * inv[None, :]
    cosr = np.cos(ang).reshape(NCH, 128, 1, 64)
    sinr = np.sin(ang).reshape(NCH, 128, 1, 64)
    idx = np.arange(128, dtype=np.float64)
    dq = np.stack([np.exp(LOG_GAMMA[h] * (idx + 1.0)) for h in range(4)], axis=1)[None, :, :, None]
    dk = np.stack([S_ML * np.exp(-LOG_GAMMA[h] * (idx + 1.0)) for h in range(4)], axis=1)[None, :, :, None]
    tab = np.stack([cosr * dq, sinr * dq, cosr * dk, sinr * dk], axis=2)
    tab = np.ascontiguousarray(tab.reshape(NCH, 128, 1024)).astype(np.float32)
    valid = (p >= n_null).astype(np.float32).reshape(NCH, 128)
    vm = np.stack([valid, (valid - 1.0) * 1e30], axis=1)
    vm = np.ascontiguousarray(np.broadcast_to(vm[:, None], (NCH, 4, 2, 128))).astype(np.float32)
    return tab, vm


def _prep_inputs(inputs):
    f = lambda k: np.asarray(inputs[k], dtype=np.float32)
    x = f("x")
    meta = f("meta_tokens")
    w_in = f("w_in")[0]
    sizes = [512, 512, 1024, 1024, 4, 4, 512, 512, 1024, 1024]
    offs = np.cumsum(sizes)[:-1]
    ml_q, ml_k, ml_v, ml_o, ml_i, ml_f, rt_q, rt_k, rt_v, rt_g = np.split(w_in, offs, axis=1)

    def swap(cols):
        return cols.reshape(D, 4, 2, 64)[:, :, ::-1, :].reshape(D, 512)

    w_in_r = np.ascontiguousarray(np.concatenate(
        [ml_q, ml_k, ml_i, ml_f, ml_v, ml_o, rt_v, rt_g, rt_q, rt_k], axis=1))
    assert w_in_r.shape == (D, WCOLS)
    convml = f("ml_conv_w")[0]
    convw_ml = np.ascontiguousarray(convml.reshape(4, 8, 128).transpose(2, 1, 0).reshape(128, 32))
    convff = f("ffn_conv_w")[0]
    convw_ffn = np.ascontiguousarray(convff.reshape(3, NJ, 128).transpose(2, 1, 0).reshape(128, NJ * 3))
    b_if = np.ascontiguousarray(np.stack([f("ml_b_i")[0], f("ml_b_f")[0]], axis=1))
    gcat = np.concatenate([f("ml_norm_g")[0], f("rt_norm_g")[0]])
    gcol = np.ascontiguousarray(gcat.reshape(16, 128).T)
    bc = lambda v: np.ascontiguousarray(np.broadcast_to(v[None, :], (128, D)))
    common = dict(w_in_r=w_in_r, w_out=f("w_out")[0], w_up=f("w_up")[0], w_gate=f("w_gate")[0], w_down=f("w_down")[0],
                  convw_ml=convw_ml, convw_ffn=convw_ffn, b_if=b_if, gcol=gcol,
                  g_mix_b=bc(f("norm_mix_g")[0]), g_ffn_b=bc(f("norm_ffn_g")[0]), g_fin_b=bc(f("norm_final_g")))
    common.update(_host_consts())
    tabs = [_rope_tables(2160), _rope_tables(112)]
    in_maps = []
    for core in range(8):
        b, t = core // 2, core % 2
        xs = np.zeros((NPOS, D), np.float32)
        if t == 0:
            xs[2160:2176] = meta
            xs[2176:] = x[b, 0:2048]
        else:
            xs[112:128] = meta
            xs[128:] = x[b]
        m = dict(common)
        m["xs"] = xs
        m["cs_tab"], m["vm_tab"] = tabs[t]
        in_maps.append(m)
    return in_maps


_NC_CACHE = {}


def kernel(**inputs):
    in_maps = _prep_inputs(inputs)
    if "nc" not in _NC_CACHE:
        _NC_CACHE["nc"] = build_program()
    nc = _NC_CACHE["nc"]
    res = run_bass_kernel_spmd(nc, in_maps, core_ids=list(range(8)))
    out = np.zeros((4, 4096, D), np.float32)
    for core in range(8):
        b, t = core // 2, core % 2
        out[b, t * 2048:(t + 1) * 2048] = res.results[core]["out"]
    if DEBUG:
        kernel.debug = [res.results[c]["mixed_d"] for c in range(8)]
    return out
```

```python
import os
from contextlib import ExitStack

import numpy as np
import ml_dtypes

import concourse.bass as bass
import concourse.mybir as mybir
from concourse.bass_utils import run_bass_kernel_spmd

F32 = mybir.dt.float32
BF16 = mybir.dt.bfloat16
ALU = mybir.AluOpType
AF = mybir.ActivationFunctionType

NCH_P = 16
NCH_F = 17
NCH = NCH_P + NCH_F
CH = 128
NPOS = NCH * CH
D = 1024
DFF = 2816
NJ = DFF // 128
WCOLS = 6152
EPS = 1e-6
NEG = -1e30
GATE_CAP = 15.0
SBUF_BASE = 16640
SBUF_END = 229376
S_ML = 128.0 ** -0.5
LOG_GAMMA = [float(np.log1p(-2.0 ** (-(5.0 + h)))) for h in range(4)]
CD = [float(np.exp(lg * CH)) for lg in LOG_GAMMA]

ENGS = ("pe", "act", "dve", "pool", "sp")
DEBUG = bool(int(os.environ.get("MK_DEBUG", "0")))


class _Op:
    __slots__ = ("eng", "fn", "deps", "dma_sem", "dma_cnt", "sig", "idx", "need_sig", "wk")

    def __init__(self, eng, fn):
        self.eng = eng
        self.fn = fn
        self.deps = []
        self.dma_sem = None
        self.dma_cnt = 0
        self.sig = 0
        self.need_sig = False


class Sched:
    def __init__(self):
        self.ops = []
        self.last_w = {}
        self.readers = {}
        self.dma_counts = {}
        self.fence = None

    fake_par = None
    FAKE_KEEP = ("bk", "W", "Cml_f", "Cml_bf", "Crt_f", "Crt_bf", "asb", "mixed_d", "cst", "vmt", "uT")

    def _fk(self, keys):
        if self.fake_par is None:
            return keys
        out = []
        for k in keys:
            base = k[0] if isinstance(k, tuple) else k
            if base == "bk" and os.environ.get("MK_FAKEBK"):
                out.append((k, self.fake_par))
                continue
            if base in self.FAKE_KEEP or base in ("mst", "ident_bf", "mhalf", "i4", "ones4", "zeros4", "convml", "dqk", "cmask",
                                                   "maskneg4", "ident_f", "g_mix_b", "bsc0", "bsc1", "bif"):
                out.append(k)
            else:
                out.append((k, self.fake_par))
        return out

    def add(self, eng, fn, reads=(), writes=(), dma=None):
        reads = self._fk(reads)
        writes = self._fk(writes)
        op = _Op(eng, fn)
        op.wk = list(writes)[:2]
        op.idx = len(self.ops)
        deps = {}

        def dep(o, raw):
            val = self.dma_counts[o.dma_sem] if o.dma_sem is not None else 0
            if o.idx in deps:
                if raw and not deps[o.idx][1]:
                    deps[o.idx] = (o, True, val)
            else:
                deps[o.idx] = (o, raw, val)

        for r in reads:
            w = self.last_w.get(r)
            if w is not None:
                dep(w, True)
        for k in writes:
            w = self.last_w.get(k)
            if w is not None:
                dep(w, False)
            for rd in self.readers.get(k, ()):
                dep(rd, False)
        if self.fence is not None:
            dep(self.fence[eng], False)
        op.deps = list(deps.values())
        for r in reads:
            self.readers.setdefault(r, []).append(op)
        for k in writes:
            self.last_w[k] = op
            self.readers[k] = []
        if dma is not None:
            op.dma_sem = dma
            self.dma_counts[dma] = self.dma_counts.get(dma, 0) + 16
            op.dma_cnt = self.dma_counts[dma]
        self.ops.append(op)
        return op

    def all_keys(self):
        return list(set(self.last_w.keys()) | set(self.readers.keys()))

    def barrier(self, skip=()):
        keys = [k for k in self.all_keys() if not (isinstance(k, tuple) and k[0] in skip)]
        fence = {}
        for e in ENGS:
            fence[e] = self.add(e, lambda eng: eng.nop(), writes=keys)
        self.fence = fence

    @staticmethod
    def _skip(d, op, raw):
        if d.dma_sem is not None or op.dma_sem is not None:
            return False
        if d.eng != op.eng:
            return False
        return d.eng == "pe"

    CALIB = bool(int(os.environ.get("MK_CALIB", "1")))

    def _cost(self, op):
        c = getattr(op.fn, "cost", 150.0)
        if self.CALIB and op.dma_sem is None:
            if op.eng == "pool":
                c = 100.0 + 2.6 * (c * 0.96 - 100.0)
        return c

    def schedule(self, window=int(os.environ.get("MK_WIN", "200")), lat=float(os.environ.get("MK_LAT", "800"))):
        ops = self.ops
        n = len(ops)
        ndeps = [0] * n
        users = [[] for _ in range(n)]
        for op in ops:
            ndeps[op.idx] = len(op.deps)
            for (d, raw, val) in op.deps:
                users[d.idx].append(op.idx)
        fin = [0.0] * n
        ready_t = [0.0] * n
        prio_cp = os.environ.get("MK_PRIO", "cp") == "cp"
        tail = [0.0] * n
        if prio_cp:
            for op in reversed(ops):
                c = self._cost(op)
                t = 0.0
                tl = lat * float(os.environ.get("MK_TAILF", "1.0"))
                for u in users[op.idx]:
                    if tail[u] + tl > t:
                        t = tail[u] + tl
                tail[op.idx] = t + c * float(os.environ.get("MK_COSTF", "1.0"))
        pend = {e: [op.idx for op in ops if op.eng == e] for e in ENGS}
        pos = {e: 0 for e in ENGS}
        done = [False] * n
        efree = {e: 0.0 for e in ENGS}
        order = {e: [] for e in ENGS}
        cur_tab = [None]
        nsw = [0]
        TABSW = float(os.environ.get("MK_TABSW", "1000"))
        remaining = n
        while remaining:
            best = None
            for e in ENGS:
                lst = pend[e]
                i = pos[e]
                while i < len(lst) and done[lst[i]]:
                    i += 1
                pos[e] = i
                if i >= len(lst):
                    continue
                w = 1 if e == "sp" else window
                seen = 0
                j = i
                cand = None
                dma_blocked = False
                while j < len(lst) and seen < w:
                    k = lst[j]
                    j += 1
                    if done[k]:
                        continue
                    seen += 1
                    isdma = ops[k].dma_sem is not None
                    if isdma and dma_blocked:
                        continue
                    if isdma:
                        dma_blocked = True
                    if ndeps[k] > 0:
                        continue
                    st = max(ready_t[k], efree[e])
                    if e == "act" and TABSW > 0.0:
                        tb = getattr(ops[k].fn, "tab", None)
                        if tb is not None and tb != cur_tab[0]:
                            st = st + TABSW
                    if prio_cp:
                        key = (st, -tail[k], k)
                        if cand is None or key < cand:
                            cand = key
                    else:
                        key = (st, 0.0, k)
                        if cand is None or key < cand:
                            cand = key
                            if ready_t[k] <= efree[e]:
                                break
                if cand is not None and (best is None or cand < best[0]):
                    best = (cand, e)
            assert best is not None, "scheduler deadlock"
            (st, _pr, k), e = best
            op = ops[k]
            c = self._cost(op)
            if e == "act":
                tb = getattr(op.fn, "tab", None)
                if tb is not None and tb != cur_tab[0]:
                    cur_tab[0] = tb
                    nsw[0] += 1
            if op.dma_sem is not None:
                efree[e] = st + 60.0
                fin[k] = st + c
            else:
                efree[e] = st + c
                fin[k] = st + c
            done[k] = True
            remaining -= 1
            order[e].append(op)
            for u in users[k]:
                ndeps[u] -= 1
                t = fin[k] + lat
                if t > ready_t[u]:
                    ready_t[u] = t
        self.est_ns = max(fin) if n else 0.0
        self.n_tabsw = nsw[0]
        if os.environ.get("MK_CRIT"):
            dist = [0.0] * n
            pred = [-1] * n
            for op in ops:
                c = getattr(op.fn, "cost", 150.0)
                best_t, best_p = 0.0, -1
                for (d, raw, val) in op.deps:
                    t = dist[d.idx] + lat
                    if t > best_t:
                        best_t, best_p = t, d.idx
                dist[op.idx] = best_t + c
                pred[op.idx] = best_p
            lim = self.fence["pe"].idx if self.fence else n
            k = max(range(lim), key=lambda i: dist[i])
            print("[crit] dependency-only critical path before fence:", round(dist[k] / 1000), "us")
            path = []
            while k >= 0:
                path.append(k)
                k = pred[k]
            path.reverse()
            import collections
            cnt = collections.Counter(ops[i].eng for i in path)
            print("[crit] path len", len(path), dict(cnt))
            self.crit_path = path
            mid = int(len(path) * float(os.environ.get("MK_CRITPOS", "0.5")))
            for i in path[mid:mid + 70]:
                print("[crit]  ", ops[i].eng, ops[i].wk, round(getattr(ops[i].fn, "cost", 150.0)))
        if os.environ.get("MK_SCHED_DBG"):
            B = 100000.0
            nb = int(self.est_ns // B) + 1
            busy = {e: [0.0] * nb for e in ENGS}
            for op in ops:
                c = getattr(op.fn, "cost", 150.0)
                if op.dma_sem is not None:
                    continue
                b = int((fin[op.idx] - c) // B)
                busy[op.eng][b] += c
            for b in range(nb):
                print(f"[sched] {b*100:6d}us " + " ".join(f"{e}:{busy[e][b]/B*100:5.1f}%" for e in ENGS if e != "sp"))
            if self.fence:
                print("[sched] fence done at", {e: round(fin[o.idx] / 1000) for e, o in self.fence.items()})
        return order

    def eval_order(self, order, lat):
        ops = self.ops
        fin = {}
        pos = {e: 0 for e in ENGS}
        efree = {e: 0.0 for e in ENGS}
        ecur = [None]
        esw = [0]
        self._esw = esw
        remaining = sum(len(v) for v in order.values())
        while remaining:
            progressed = False
            for e in ENGS:
                while pos[e] < len(order[e]):
                    op = order[e][pos[e]]
                    if any(d.idx not in fin for (d, raw, val) in op.deps):
                        break
                    rt = max([fin[d.idx] + lat for (d, raw, val) in op.deps] + [0.0])
                    st = max(rt, efree[e])
                    c = self._cost(op)
                    if e == "act":
                        tb = getattr(op.fn, "tab", None)
                        if tb is not None and tb != ecur[0]:
                            ecur[0] = tb
                            st += 1283.0
                            esw[0] += 1
                    if op.dma_sem is not None:
                        efree[e] = st + 60.0
                    else:
                        efree[e] = st + c
                    fin[op.idx] = st + c
                    pos[e] += 1
                    remaining -= 1
                    progressed = True
            assert progressed, "order deadlock"
        return max(fin.values())

    def emit(self, block, sems, dma_sems, reorder=True):
        for op in self.ops:
            for (d, raw, val) in op.deps:
                if d.dma_sem is None and not self._skip(d, op, raw):
                    d.need_sig = True
        if reorder:
            per_eng = self.schedule()
            if os.environ.get("MK_EVAL_LAT"):
                print("[kernel] eval fixed order @lat", os.environ["MK_EVAL_LAT"], self.eval_order(per_eng, float(os.environ["MK_EVAL_LAT"])), "tab switches", self._esw[0])
        else:
            per_eng = {e: [] for e in ENGS}
            for op in self.ops:
                per_eng[op.eng].append(op)
        cnt = {e: 0 for e in ENGS}
        for e in ENGS:
            for op in per_eng[e]:
                if op.dma_sem is None and op.need_sig:
                    cnt[e] += 1
                    op.sig = cnt[e]
        handles = {"pe": "tensor", "act": "scalar", "dve": "vector", "pool": "gpsimd", "sp": "sync"}

        def body(eng_name):
            def _f(eng):
                waited = {}
                for op in per_eng[eng_name]:
                    for (d, raw, val) in op.deps:
                        if d.dma_sem is not None:
                            key = ("dma", d.dma_sem)
                            sem = dma_sems[d.dma_sem]
                        else:
                            if self._skip(d, op, raw):
                                continue
                            key = d.eng
                            val = d.sig
                            sem = sems[d.eng]
                        if waited.get(key, 0) >= val:
                            continue
                        waited[key] = val
                        eng.wait_ge(sem, val)
                    ins = op.fn(eng)
                    if op.dma_sem is not None:
                        ins.then_inc(dma_sems[op.dma_sem], 16)
                    elif op.need_sig:
                        ins.then_inc(sems[eng_name], 1)
            return _f

        for e in ENGS:
            if per_eng[e]:
                getattr(block, handles[e])(body(e))


def _mmcost(lhsT, rhs):
    n = rhs.free_size()
    c = max(lhsT.free_size() / 1.2, n / 2.37, 30.0)
    if rhs.dtype == F32:
        c *= 4.0
    return c


def _fsz(ap):
    return ap.free_size()


def _mm(out, lhsT, rhs, start=True, stop=True):
    f = lambda e: e.matmul(out, lhsT=lhsT, rhs=rhs, start=start, stop=stop)
    f.cost = _mmcost(lhsT, rhs)
    return f


def _mmk(out, pairs):
    def f(e):
        n = len(pairs)
        ins = None
        for i, (l, r) in enumerate(pairs):
            ins = e.matmul(out, lhsT=l, rhs=r, start=(i == 0), stop=(i == n - 1))
        return ins
    f.cost = sum(_mmcost(l, r) for (l, r) in pairs)
    return f


def _trs(items, ident):
    def f(e):
        ins = None
        for (o, i) in items:
            ins = e.transpose(out=o, in_=i, identity=ident)
        return ins
    f.cost = 120.0 * len(items)
    return f


def _act(out, in_, func, bias=None, scale=None, accum=None):
    def f(e):
        kw = {}
        if bias is not None:
            kw["bias"] = bias
        if scale is not None:
            kw["scale"] = scale
        if accum is not None:
            kw["accum_out"] = accum
        return e.activation(out=out, in_=in_, func=func, **kw)
    f.cost = (224.0 + _fsz(out)) / 1.2
    f.tab = {AF.Silu: "S", AF.Tanh: "S", AF.Exp: "E", AF.Ln: "E"}.get(func)
    return f


def _ts(out, in0, s1, op0, s2=None, op1=None):
    def f(e):
        if op1 is None:
            return e.tensor_scalar(out=out, in0=in0, scalar1=s1, scalar2=None, op0=op0)
        return e.tensor_scalar(out=out, in0=in0, scalar1=s1, scalar2=s2, op0=op0, op1=op1)
    f.cost = (100.0 + _fsz(out)) / 0.96
    return f


def _tt(out, in0, in1, op):
    f = lambda e: e.tensor_tensor(out=out, in0=in0, in1=in1, op=op)
    f.cost = (100.0 + _fsz(out)) / 0.96
    return f


def _stt(out, in0, scalar, in1, op0, op1):
    f = lambda e: e.scalar_tensor_tensor(out=out, in0=in0, scalar=scalar, in1=in1, op0=op0, op1=op1)
    f.cost = (100.0 + _fsz(out)) / 0.96
    return f


def _cp(out, in_):
    f = lambda e: e.tensor_copy(out=out, in_=in_)
    f.cost = (100.0 + _fsz(out)) / 0.96
    return f


def _dma(out, in_):
    f = lambda e: e.dma_start(out=out, in_=in_)
    f.cost = 2000.0 + 128.0 * _fsz(out) * 4 / 200.0
    return f


def _scan(out, d0, d1, init, op0, op1):
    f = lambda e: e.tensor_tensor_scan(out=out, data0=d0, data1=d1, initial=init, op0=op0, op1=op1)
    f.cost = (100.0 + 2 * _fsz(out)) / 0.96
    return f


class _Arena:
    def __init__(self, nc, base, end):
        self.nc = nc
        self.off = base
        self.end = end

    def alloc(self, name, shape, dt):
        size = 1
        for s in shape[1:]:
            size *= s
        size *= 2 if dt == BF16 else 4
        off = (self.off + 31) // 32 * 32
        assert off + size <= self.end, f"SBUF overflow at {name}: {off + size} > {self.end}"
        t = self.nc.alloc_sbuf_tensor_at(name, list(shape), dt, offset=off)
        self.off = off + size
        return t.ap()


WGROUPS = [(1024, 2056), (512, 1024), (5640, 6152), (3080, 4104), (0, 512), (5128, 5640), (2056, 3080), (4104, 5128)]


def _wgroup_of(col):
    for g, (a, b) in enumerate(WGROUPS):
        if a <= col < b:
            return g
    raise ValueError(col)


def build_program():
    nc = bass.Bass("TRN2", target_bir_lowering=False)

    def din(name, shape, dt=F32):
        return nc.dram_tensor(name, list(shape), dt, kind="ExternalInput").ap()

    xs_d = din("xs", [NPOS, D])
    win_d = din("w_in_r", [D, WCOLS])
    wout_d = din("w_out", [2048, D])
    wup_d = din("w_up", [D, DFF])
    wgate_d = din("w_gate", [D, DFF])
    wdn_d = din("w_down", [DFF, D])
    cs_d = din("cs_tab", [NCH, 128, 1024])
    vm_d = din("vm_tab", [NCH, 4, 2, 128])
    dqk_d = din("dqk", [128, 12])
    cmask_d = din("cmask", [128, 128])
    mneg_d = din("maskneg4", [128, 128])
    identb_d = din("ident_bf", [128, 128], BF16)
    identf_d = din("ident_f", [128, 128])
    i4_d = din("i4", [4, 8])
    convml_d = din("convw_ml", [128, 32])
    convff_d = din("convw_ffn", [128, NJ * 3])
    bif_d = din("b_if", [4, 2])
    gcol_d = din("gcol", [128, 16])
    gmix_d = din("g_mix_b", [128, D])
    gffn_d = din("g_ffn_b", [128, D])
    gfin_d = din("g_fin_b", [128, D])
    out_d = nc.dram_tensor("out", [2048, D], F32, kind="ExternalOutput").ap()
    mixed_d = nc.dram_tensor("mixed_d", [NCH_F * CH, 2048], BF16,
                             kind="ExternalOutput" if DEBUG else "Internal").ap()

    S = Sched()
    banks = [nc.alloc_psum_tensor(f"bank{i}", [128, 512], F32).ap() for i in range(8)]

    per = _Arena(nc, SBUF_BASE, SBUF_END)
    ident_bf = per.alloc("ident_bf", [128, 128], BF16)
    mhalf = per.alloc("mhalf", [128, 4], F32)
    stat = per.alloc("stat", [128, 8], F32)
    PH_BASE = per.off

    S.add("sp", _dma(ident_bf, identb_d), writes=["ident_bf"], dma="cst")
    S.add("pool", lambda e: e.memset(mhalf, -0.5), writes=["mhalf"])

    A1 = _Arena(nc, PH_BASE, SBUF_END)
    W = A1.alloc("W", [128, 8, WCOLS], BF16)
    ident_f = A1.alloc("ident_f", [128, 128], F32)
    cmask = A1.alloc("cmask", [128, 128], F32)
    maskneg4 = A1.alloc("maskneg4", [128, 128], F32)
    dqk = A1.alloc("dqk", [128, 12], F32)
    g_mix_b = A1.alloc("g_mix_b", [128, D], F32)
    convml = A1.alloc("convml", [128, 32], F32)
    i4 = A1.alloc("i4", [4, 8], F32)
    bif = A1.alloc("bif", [4, 2], F32)
    bsc = A1.alloc("bsc", [4, 2], F32)
    ones4 = A1.alloc("ones4", [4, 128], F32)
    zeros4 = A1.alloc("zeros4", [4, 128], F32)
    xin = A1.alloc("xin", [128, D], F32)
    u = A1.alloc("u", [128, D], BF16)
    uT = [A1.alloc(f"uT{i}", [128, 8, 128], BF16) for i in range(2)]
    asb = A1.alloc("asb", [128, 8, 131], F32)
    cacc = [A1.alloc(f"cacc{i}", [128, 128], F32) for i in range(2)]
    qTmls = [A1.alloc(f"qTml{i}", [128, 4, 128], BF16) for i in range(2)]
    kTmls = [A1.alloc(f"kTml{i}", [128, 4, 128], BF16) for i in range(2)]
    rX = [A1.alloc("rX0", [128, 512], F32)] * 2
    rM = [A1.alloc(f"rM{i}", [128, 256], F32) for i in range(4)]
    qtok = A1.alloc("qtok", [128, 4, 128], BF16)
    ktok = A1.alloc("ktok", [128, 4, 128], BF16)
    cst = [A1.alloc("cst0", [128, 1024], F32)] * 2
    vmt = [A1.alloc("vmt0", [4, 2, 128], F32)] * 2
    qTrts = [A1.alloc(f"qTrt{i}", [128, 4, 128], BF16) for i in range(2)]
    kTrts = [A1.alloc(f"kTrt{i}", [128, 4, 128], BF16) for i in range(2)]
    kw = A1.alloc("kw", [128, 4, 128], BF16)
    Vmls = [A1.alloc(f"Vml{i}", [128, 4, 257], BF16) for i in range(2)]
    Vrts = [A1.alloc(f"Vrt{i}", [128, 4, 256], BF16) for i in range(2)]
    ogs = [A1.alloc(f"og{i}", [128, D], F32) for i in range(2)]
    ggs = [A1.alloc(f"gg{i}", [128, D], F32) for i in range(2)]
    mixed = A1.alloc("mixed", [128, 2048], BF16)
    Cml_f = A1.alloc("Cml_f", [128, 4, 257], F32)
    Cml_bfs = [A1.alloc(f"Cml_bf{i}", [128, 4, 257], BF16) for i in range(2)]
    Crt_f = A1.alloc("Crt_f", [128, 4, 256], F32)
    Crt_bfs = [A1.alloc(f"Crt_bf{i}", [128, 4, 256], BF16) for i in range(2)]
    WT = A1.alloc("WT", [128, 4, 128], F32)
    PT = A1.alloc("PT", [128, 4, 128], BF16)
    Wint = A1.alloc("Wint", [128, 512], F32)
    qsT = A1.alloc("qsT", [128, 4, 128], BF16)
    hrt = A1.alloc("hrt", [128, 4, 256], F32)
    hraw = A1.alloc("hraw", [128, 4, 257], F32)
    rows = {n: A1.alloc("row_" + n, [4, 128], F32) for n in
            ("li0", "li1", "li", "ef", "sp", "nbcum", "B", "M", "R2", "R3")}
    rhs_bd = A1.alloc("rhs_bd", [4, 4, 128], F32)
    smr = A1.alloc("smr", [4, 16], F32)
    sm = A1.alloc("sm", [128, 64], F32)
    smps = A1.alloc("smps", [128, 20], F32)
    st6 = A1.alloc("st6", [128, 4, 6], F32)
    mv = A1.alloc("mv", [128, 4, 2], F32)

    EX8 = sm[:, 0:8]
    BT = sm[:, 8:12]
    BTS = sm[:, 12:16]
    WSARG = sm[:, 16:20]
    WSRC = sm[:, 20:24]
    DEC = sm[:, 24:28]
    DD = sm[:, 28:32]
    RDEN = sm[:, 32:36]
    T1 = sm[:, 36:40]
    T2 = sm[:, 40:44]
    RSTD = sm[:, 44:48]
    SC = sm[:, 48:52]
    BI = sm[:, 52:56]

    bT = banks[0]
    bT_bf = bT.bitcast(BF16)

    for nm, dst, src in (("ident_f", ident_f, identf_d), ("cmask", cmask, cmask_d), ("maskneg4", maskneg4, mneg_d),
                         ("dqk", dqk, dqk_d), ("g_mix_b", g_mix_b, gmix_d),
                         ("convml", convml, convml_d), ("i4", i4, i4_d), ("bif", bif, bif_d)):
        S.add("sp", _dma(dst, src), writes=[nm], dma="cst")
    for g, (c0, c1) in enumerate(WGROUPS):
        for k in range(8):
            S.add("pool", _dma(W[:, k, c0:c1], win_d[k * 128:(k + 1) * 128, c0:c1]),
                  writes=[("W", g, k)], dma=f"W{g}")

    def wk(col):
        g = _wgroup_of(col)
        return [("W", g, k) for k in range(8)]

    S.add("pool", lambda e: e.memset(ones4, 1.0), writes=["ones4"])
    S.add("pool", lambda e: e.memset(zeros4, 0.0), writes=["zeros4"])
    S.add("pool", lambda e: e.memset(asb, 0.0), writes=[("asb", t) for t in range(8)])
    S.add("pool", lambda e: e.memset(Vmls[0], 1.0), writes=[("Vml", 0, 0), ("Vml", 0, 1)])
    S.add("pool", lambda e: e.memset(Vmls[1], 1.0), writes=[("Vml", 1, 0), ("Vml", 1, 1)])
    S.add("pool", lambda e: e.memset(Cml_f, 0.0), writes=[("Cml_f", h) for h in range(4)])
    S.add("pool", lambda e: e.memset(Cml_bfs[0], 0.0), writes=[("Cml_bf", 0, h) for h in range(4)])
    S.add("pool", lambda e: e.memset(Crt_f, 0.0), writes=[("Crt_f", h) for h in range(4)])
    S.add("pool", lambda e: e.memset(Crt_bfs[0], 0.0), writes=[("Crt_bf", 0, h) for h in range(4)])
    S.add("pool", lambda e: e.memset(smr, 0.0), writes=["mst", "D1", "diagM", "diagD"])
    S.add("pool", lambda e: e.memset(smr[:, 0:1], NEG), writes=["mst"])
    S.add("dve", _ts(bsc[:, 0:1], bif[:, 0:1], 1.0 / GATE_CAP, ALU.mult), reads=["bif"], writes=["bsc0"])
    S.add("dve", _ts(bsc[:, 1:2], bif[:, 1:2], -1.0, ALU.mult), reads=["bif"], writes=["bsc1"])

    MST = smr[:, 0:1]
    D1 = smr[:, 1:2]
    DIAGM = smr[:, 4:8]
    DIAGD = smr[:, 8:12]

    def loads(c):
        s = c % 2
        S.add("sp", _dma(xin, xs_d[c * 128:(c + 1) * 128, :]), writes=["xin"], dma="xin")

    def loads2(c):
        S.add("sp", _dma(cst[0], cs_d[c]), writes=[("cst", 0)], dma="cst0")
        S.add("sp", _dma(vmt[0], vm_d[c]), writes=[("vmt", 0)], dma="vmt0")

    aslot = [0]

    def next_aslot():
        i = aslot[0] % 4
        aslot[0] += 1
        return i

    tmslot = [0]

    def rmsnorm_T(src, gb, dstT, dst_keys, src_key, gkey):
        S.add("act", _act(u, src, AF.Square, accum=stat[:, 0:1]), reads=[src_key], writes=["u", "ss"])
        S.add("dve", _ts(stat[:, 1:2], stat[:, 0:1], 1.0 / D, ALU.mult, EPS, ALU.add), reads=["ss"], writes=["ms"])
        S.add("pool", _tt(stat[:, 2:3], stat[:, 1:2], mhalf[:, 0:1], ALU.pow), reads=["ms", "mhalf"], writes=["rstd"])
        S.add("dve", _stt(u, src, stat[:, 2:3], gb, ALU.mult, ALU.mult), reads=[src_key, "rstd", gkey], writes=["u"])
        S.add("pe", _trs([(bT_bf[:, k * 128:(k + 1) * 128], u[:, k * 128:(k + 1) * 128]) for k in range(8)], ident_bf),
              reads=["u", "ident_bf"], writes=[("bk", 0)])
        S.add("act", _act(dstT, bT_bf.rearrange("p (k t) -> p k t", k=8), AF.Copy), reads=[("bk", 0)], writes=dst_keys)


    fmb = [0]

    def chunk(c, full):
        rp, wp = c % 2, (c + 1) % 2
        qTml, kTml, qTrt, kTrt = qTmls[rp], kTmls[rp], qTrts[rp], kTrts[rp]
        og, gg = ogs[rp], ggs[rp]
        Vml, Vrt = Vmls[rp], Vrts[rp]
        Cml_bf, Crt_bf = Cml_bfs[rp], Crt_bfs[rp]
        Cml_bfw, Crt_bfw = Cml_bfs[wp], Crt_bfs[wp]
        if os.environ.get("MK_FAKE2"):
            S.fake_par = c % int(os.environ["MK_FAKE2"])
        s = c % 2
        uTs = uT[s]
        rmsnorm_T(xin, g_mix_b, uTs, [("uT", s)], "xin", "g_mix_b")
        if c + 1 < NCH:
            loads(c + 1)

        def fm_group(cols_ms):
            b = 1 + fmb[0] % 2
            fmb[0] += 1
            bk = banks[b]
            for sl, (col, m) in enumerate(cols_ms):
                S.add("pe", _mmk(bk[0:m, sl * 128:(sl + 1) * 128], [(W[:, k, col:col + m], uTs[:, k, :]) for k in range(8)]),
                      reads=wk(col) + [("uT", s)], writes=[("bk", b)])
            return bk, ("bk", b)

        for grp in ((0, 1) if full else (1,)):
            bk, bkey = fm_group([((grp * 4 + t) * 128, 128) for t in range(4)])
            S.add("act", _act(asb[:, grp * 4:grp * 4 + 4, 3:131], bk.rearrange("p (a b) -> p a b", a=4), AF.Copy),
                  reads=[bkey], writes=[("asb", grp * 4 + t) for t in range(4)])
            for t in range(grp * 4, grp * 4 + 4):
                acc = cacc[t % 2]
                ak = ("cacc", t % 2)
                S.add("dve", _ts(acc, asb[:, t, 3:131], convml[:, t * 4 + 3:t * 4 + 4], ALU.mult),
                      reads=[("asb", t), "convml"], writes=[ak])
                for kk in (2, 1, 0):
                    S.add("dve", _stt(acc, asb[:, t, kk:kk + 128], convml[:, t * 4 + kk:t * 4 + kk + 1], acc, ALU.mult, ALU.add),
                          reads=[("asb", t), "convml", ak], writes=[ak])
                S.add("pool", _cp(asb[:, t, 0:3], asb[:, t, 128:131]), reads=[("asb", t)], writes=[("asb", t)])
                if t < 4:
                    S.add("act", _act(qTml[:, t, :], acc, AF.Silu), reads=[ak], writes=[("qTml", rp, t)])
                else:
                    S.add("act", _act(kTml[:, t - 4, :], acc, AF.Silu), reads=[ak], writes=[("kTml", rp, t - 4)])
        bk, gkey = fm_group([(1024, 4), (1028, 4)])
        gi_reg = bk[0:4, 0:128]
        gf_reg = bk[0:4, 128:256]
        R = rows
        vs = vmt[s]
        S.add("act", _act(R["li0"], gi_reg, AF.Tanh, bias=bsc[:, 0:1], scale=1.0 / GATE_CAP), reads=[gkey, "bsc0"], writes=["li0"])
        S.add("act", _act(R["ef"], gf_reg, AF.Exp, bias=bsc[:, 1:2], scale=-1.0), reads=[gkey, "bsc1"], writes=["ef"])
        S.add("dve", _stt(R["li1"], R["li0"], GATE_CAP, vs[:, 0, :], ALU.mult, ALU.mult), reads=["li0", ("vmt", 0)], writes=["li1"])
        S.add("dve", _tt(R["li"], R["li1"], vs[:, 1, :], ALU.add), reads=["li1", ("vmt", 0)], writes=["li"])
        S.add("act", _act(R["sp"], R["ef"], AF.Ln, bias=1.0), reads=["ef"], writes=["sp"])
        S.add("dve", _scan(R["nbcum"], R["sp"], zeros4, 0.0, ALU.add, ALU.add), reads=["sp", "zeros4"], writes=["nbcum"])
        S.add("dve", _tt(R["B"], R["li"], R["nbcum"], ALU.add), reads=["li", "nbcum"], writes=["B"])
        S.add("dve", _scan(R["M"], R["B"], R["B"], MST, ALU.max, ALU.max), reads=["B", "mst"], writes=["M"])
        S.add("dve", _ts(DIAGM, i4[:, 0:4], R["M"][:, 127:128], ALU.mult), reads=["i4", "M"], writes=["diagM"])
        S.add("dve", _tt(D1, MST, R["M"][:, 127:128], ALU.subtract), reads=["mst", "M"], writes=["D1"])
        S.add("dve", _ts(DIAGD, i4[:, 0:4], D1, ALU.mult), reads=["i4", "D1"], writes=["diagD"])
        if full:
            S.add("dve", _tt(R["R2"], R["nbcum"], R["M"], ALU.subtract), reads=["nbcum", "M"], writes=["R2"])
            S.add("dve", _ts(R["R2"], R["R2"], 80.0, ALU.min), reads=["R2"], writes=["R2"])
            S.add("dve", _ts(R["R3"], R["M"], MST, ALU.subtract, -1.0, ALU.mult), reads=["M", "mst"], writes=["R3"])
            for h in range(4):
                S.add("dve", _ts(rhs_bd[:, h, :], R["M"], i4[:, 4 + h:5 + h], ALU.mult), reads=["M", "i4"], writes=[("rhs_bd", h)])
        S.add("dve", _tt(MST, R["M"][:, 127:128], R["nbcum"][:, 127:128], ALU.subtract), reads=["M", "nbcum"], writes=["mst"])

        sb_ = 5
        SMP = banks[sb_][:, 0:32]
        skey = ("bk", sb_)

        def smp_mm(e):
            ins = e.matmul(SMP[:, 0:4], lhsT=R["B"], rhs=i4[:, 0:4], start=True, stop=True)
            if full:
                e.matmul(SMP[:, 4:8], lhsT=R["R2"], rhs=i4[:, 0:4], start=True, stop=True)
                e.matmul(SMP[:, 8:12], lhsT=R["R3"], rhs=i4[:, 0:4], start=True, stop=True)
            e.matmul(SMP[:, 12:16], lhsT=ones4, rhs=DIAGM, start=True, stop=True)
            ins = e.matmul(SMP[:, 16:20], lhsT=ones4, rhs=DIAGD, start=True, stop=True)
            return ins
        S.add("pe", smp_mm, reads=["B", "R2", "R3", "i4", "ones4", "diagM", "diagD"], writes=[skey])
        if full:
            S.add("act", _act(smps, SMP[:, 0:20], AF.Copy), reads=[skey], writes=["smps"])
        else:
            S.add("act", _act(smps[:, 0:4], SMP[:, 0:4], AF.Copy), reads=[skey], writes=["smps"])
            S.add("act", _act(smps[:, 12:20], SMP[:, 12:20], AF.Copy), reads=[skey], writes=["smps"])
        if full:
            S.add("act", _act(EX8, smps[:, 4:12], AF.Exp), reads=["smps"], writes=["ex8"])
        S.add("dve", _tt(WSARG, smps[:, 0:4], smps[:, 12:16], ALU.subtract), reads=["smps"], writes=["wsarg"])
        S.add("act", _act(WSRC, WSARG, AF.Exp, bias=float(np.log(S_ML))), reads=["wsarg"], writes=["wsrc"])
        S.add("act", _act(DEC, smps[:, 16:20], AF.Exp), reads=["smps"], writes=["dec"])
        b5 = banks[5]
        k5 = ("bk", 5)
        if full:
            S.add("dve", _ts(BTS, smps[:, 0:4], float(np.log(S_ML)), ALU.add), reads=["smps"], writes=["BTS"])
            S.add("pe", _mmk(b5, [(ones4, rhs_bd.rearrange("p h t -> p (h t)")), (ident_f, maskneg4.unsqueeze(1).broadcast_to([128, 4, 128]))]),
                  reads=["ones4", "ident_f", "maskneg4"] + [("rhs_bd", h) for h in range(4)], writes=[k5])
            for h in range(4):
                S.add("act", _act(WT[:, h, :], b5[:, h * 128:(h + 1) * 128], AF.Exp, bias=BTS[:, h:h + 1]),
                      reads=[k5, "BTS"], writes=[("WT", h)])
            for h in range(4):
                S.add("dve", _ts(rhs_bd[:, h, :], R["R3"], i4[:, h:h + 1], ALU.mult), reads=["R3", "i4"], writes=[("rhs_bd", h)])
            S.add("pe", _mm(b5, ones4, rhs_bd.rearrange("p h t -> p (h t)")),
                  reads=["ones4"] + [("rhs_bd", h) for h in range(4)], writes=[k5])
            S.add("act", _act(Wint, b5, AF.Exp), reads=[k5], writes=["Wint"])
            S.add("dve", _tt(qsT.rearrange("p h t -> p (h t)"), qTml.rearrange("p h t -> p (h t)"), Wint, ALU.mult),
                  reads=["Wint"] + [("qTml", rp, h) for h in range(4)], writes=["qsT"])

        def tm_tile(col):
            b = 3 + tmslot[0] % 2
            tmslot[0] += 1
            S.add("pe", _mmk(banks[b], [(uTs[:, k, :], W[:, k, col:col + 512]) for k in range(8)]),
                  reads=wk(col) + [("uT", s)], writes=[("bk", b)])
            return banks[b], ("bk", b)

        for hh in range(2):
            reg, rk = tm_tile(1032 + hh * 512)
            S.add("act", _act(Vml[:, 2 * hh:2 * hh + 2, 0:256], reg.rearrange("p (a b) -> p a b", a=2), AF.Copy),
                  reads=[rk], writes=[("Vml", rp, hh)])
        for hh in range(2):
            reg, rk = tm_tile(3080 + hh * 512)
            S.add("dve", _cp(Vrt[:, 2 * hh:2 * hh + 2, :], reg.rearrange("p (a b) -> p a b", a=2)),
                  reads=[rk], writes=[("Vrt", rp, hh)])
        if full:
            for hh in range(2):
                reg, rk = tm_tile(2056 + hh * 512)
                S.add("act", _act(og[:, hh * 512:(hh + 1) * 512], reg, AF.Tanh, scale=0.5), reads=[rk], writes=[("og", rp, hh)])
            for hh in range(2):
                reg, rk = tm_tile(4104 + hh * 512)
                S.add("act", _act(gg[:, hh * 512:(hh + 1) * 512], reg, AF.Silu), reads=[rk], writes=[("gg", rp, hh)])

        def rotary(col, xi, qk, dst, dkey):
            reg, rk = tm_tile(col)
            X = rX[xi]
            xk = ("rX", 0)
            S.add("act", _act(X, reg, AF.Copy), reads=[rk], writes=[xk])
            Xv = X.rearrange("p (h a t) -> p h a t", h=4, a=2)
            Tc = cst[0][:, (qk * 2) * 256:(qk * 2 + 1) * 256].rearrange("p (h t) -> p h t", h=4)
            Ts = cst[0][:, (qk * 2 + 1) * 256:(qk * 2 + 2) * 256].rearrange("p (h t) -> p h t", h=4)
            Mv = [m.rearrange("p (h t) -> p h t", h=4) for m in rM]
            Dv = dst.rearrange("p h (a t) -> p h a t", a=2)
            S.add("dve", _tt(Mv[0], Xv[:, :, 0, :], Tc, ALU.mult), reads=[xk, ("cst", 0)], writes=[("rM", 0)])
            S.add("dve", _tt(Mv[1], Xv[:, :, 1, :], Ts, ALU.mult), reads=[xk, ("cst", 0)], writes=[("rM", 1)])
            S.add("pool", _tt(Mv[2], Xv[:, :, 0, :], Ts, ALU.mult), reads=[xk, ("cst", 0)], writes=[("rM", 2)])
            S.add("pool", _tt(Mv[3], Xv[:, :, 1, :], Tc, ALU.mult), reads=[xk, ("cst", 0)], writes=[("rM", 3)])
            S.add("dve", _tt(Dv[:, :, 0, :], Mv[0], Mv[1], ALU.subtract), reads=[("rM", 0), ("rM", 1)], writes=[(dkey, 0)])
            S.add("pool", _tt(Dv[:, :, 1, :], Mv[2], Mv[3], ALU.add), reads=[("rM", 2), ("rM", 3)], writes=[(dkey, 1)])

        rotary(5640, 0, 1, ktok, "ktok")
        if full and not os.environ.get("MK_X1"):
            rotary(5128, 1, 0, qtok, "qtok")
        if c + 1 < NCH:
            loads2(c + 1)

        ob = [6]

        def next_ob():
            b = 6 + ob[0] % 2
            ob[0] += 1
            return banks[b], ("bk", b)

        kb_, kbk = next_ob()
        b0bf = kb_.bitcast(BF16)
        S.add("pe", _trs([(b0bf[:, t * 128:(t + 1) * 128], kTml[:, t, :]) for t in range(4)], ident_bf),
              reads=[("kTml", rp, h) for h in range(4)] + ["ident_bf"], writes=[kbk])
        for t in range(4):
            S.add("act", _act(kw[:, t, :], b0bf[:, t * 128:(t + 1) * 128], AF.Copy, scale=WSRC[:, t:t + 1]),
                  reads=[kbk, "wsrc"], writes=[("kw", t)])
        if full:
            qb_, qbk = next_ob()
            qbbf = qb_.bitcast(BF16)
            S.add("pe", _trs([(qbbf[:, h * 128:(h + 1) * 128], ktok[:, h, :]) for h in range(4)] +
                             [(qbbf[:, (4 + h) * 128:(5 + h) * 128], qtok[:, h, :]) for h in range(4)], ident_bf),
                  reads=[("ktok", 0), ("ktok", 1), ("qtok", 0), ("qtok", 1), "ident_bf"], writes=[qbk])
            S.add("act", _act(kTrt, qbbf[:, 0:512].rearrange("p (h t) -> p h t", h=4), AF.Copy), reads=[qbk],
                  writes=[("kTrt", rp, h) for h in range(4)])
            S.add("act", _act(qTrt, qbbf[:, 512:1024].rearrange("p (h t) -> p h t", h=4), AF.Copy), reads=[qbk],
                  writes=[("qTrt", rp, h) for h in range(4)])

        for h in range(4):
            bo, ko = next_ob()
            S.add("pe", _mm(bo[:, 0:257], kw[:, h, :], Vml[:, h, :]), reads=[("kw", h), ("Vml", rp, h // 2)], writes=[ko])
            S.add("dve", _stt(Cml_f[:, h, :], Cml_f[:, h, :], DEC[:, h:h + 1], bo[:, 0:257], ALU.mult, ALU.add),
                  reads=[("Cml_f", h), "dec", ko], writes=[("Cml_f", h)])
            S.add("pool", _cp(Cml_bfw[:, h, :], Cml_f[:, h, :]), reads=[("Cml_f", h)], writes=[("Cml_bf", wp, h)])
        for pr in range(2):
            bo, ko = next_ob()
            for hh in range(2):
                h = 2 * pr + hh
                S.add("pe", _mm(bo[:, hh * 256:(hh + 1) * 256], ktok[:, h, :], Vrt[:, h, :]), reads=[("ktok", 0), ("ktok", 1), ("Vrt", rp, pr)], writes=[ko])
            for hh in range(2):
                h = 2 * pr + hh
                S.add("dve", _stt(Crt_f[:, h, :], Crt_f[:, h, :], CD[h], bo[:, hh * 256:(hh + 1) * 256], ALU.mult, ALU.add),
                      reads=[("Crt_f", h), ko], writes=[("Crt_f", h)])
            for hh in range(2):
                h = 2 * pr + hh
                S.add("pool", _ts(Crt_bfw[:, h, :], Crt_f[:, h, :], CD[h], ALU.mult, 0.0, ALU.add),
                      reads=[("Crt_f", h)], writes=[("Crt_bf", wp, h)])
        if full:
            S.add("pe", lambda e: [e.matmul(b5[:, h * 128:(h + 1) * 128], lhsT=kTml[:, h, :], rhs=qTml[:, h, :], start=True, stop=True)
                                   for h in range(4)][-1],
                  reads=[("kTml", rp, h) for h in range(4)] + [("qTml", rp, h) for h in range(4)], writes=[k5])
            S.add("dve", _tt(PT.rearrange("p h t -> p (h t)"), b5, WT.rearrange("p h t -> p (h t)"), ALU.mult),
                  reads=[k5] + [("WT", h) for h in range(4)], writes=["PT"])
            for h in range(4):
                bo, ko = next_ob()
                S.add("pe", _mmk(bo[:, 0:257], [(PT[:, h, :], Vml[:, h, :]), (qsT[:, h, :], Cml_bf[:, h, :])]),
                      reads=["PT", ("Vml", rp, h // 2), "qsT", ("Cml_bf", rp, h)], writes=[ko])
                S.add("act", _act(hraw[:, h, :], bo[:, 0:257], AF.Copy), reads=[ko], writes=[("hraw", h)])
                S.add("dve", lambda e, h=h: e.bn_stats(out=st6[:, h, :], in_=hraw[:, h, 0:256]), reads=[("hraw", h)], writes=[("st6", h)])
                S.add("dve", lambda e, h=h: e.bn_aggr(out=mv[:, h, :], in_=st6[:, h, :]), reads=[("st6", h)], writes=[("mv", h)])
            allh = [("hraw", h) for h in range(4)]
            allmv = [("mv", h) for h in range(4)]
            S.add("dve", _ts(T2, hraw[:, :, 256], -1.0, ALU.mult), reads=allh, writes=["t2"])
            S.add("dve", _tt(DD, T2, hraw[:, :, 256], ALU.max), reads=allh + ["t2"], writes=["dd"])
            S.add("dve", _tt(DD, DD, EX8[:, 0:4], ALU.max), reads=["dd", "ex8"], writes=["dd"])
            S.add("dve", lambda e: e.reciprocal(out=RDEN, in_=DD), reads=["dd"], writes=["rden"])
            S.add("dve", _tt(T1, RDEN, RDEN, ALU.mult), reads=["rden"], writes=["t1"])
            S.add("dve", _tt(T2, T1, mv[:, :, 1], ALU.mult), reads=["t1"] + allmv, writes=["t2"])
            S.add("dve", _ts(T1, T2, EPS, ALU.add), reads=["t2"], writes=["t1"])
            S.add("pool", _tt(RSTD, T1, mhalf, ALU.pow), reads=["t1", "mhalf"], writes=["rstdh"])
            S.add("dve", _tt(SC, RDEN, RSTD, ALU.mult), reads=["rden", "rstdh"], writes=["sc"])
            S.add("dve", _stt(BI, mv[:, :, 0], -1.0, SC, ALU.mult, ALU.mult), reads=allmv + ["sc"], writes=["bi"])
            for h in range(4):
                S.add("act", _act(hraw[:, h, 0:256], hraw[:, h, 0:256], AF.Identity, bias=BI[:, h:h + 1], scale=SC[:, h:h + 1]),
                      reads=[("hraw", h), "sc", "bi"], writes=[("hraw", h)])
            S.add("dve", _stt(mixed[:, 0:1024].rearrange("p (h v) -> p h v", h=4), og.rearrange("p (h v) -> p h v", h=4), 1.0,
                              hraw[:, :, 0:256], ALU.add, ALU.mult),
                  reads=allh + [("og", rp, 0), ("og", rp, 1)], writes=[("mixed", 0)])
            S.add("pe", lambda e: [e.matmul(b5[:, h * 128:(h + 1) * 128], lhsT=kTrt[:, h, :], rhs=qTrt[:, h, :], start=True, stop=True)
                                   for h in range(4)][-1],
                  reads=[("kTrt", rp, h) for h in range(4)] + [("qTrt", rp, h) for h in range(4)], writes=[k5])
            S.add("dve", _tt(PT, b5.rearrange("p (h t) -> p h t", h=4), cmask.unsqueeze(1).broadcast_to([128, 4, 128]), ALU.mult),
                  reads=[k5, "cmask"], writes=["PT"])
            for pr in range(2):
                bo, ko = next_ob()
                for hh in range(2):
                    h = 2 * pr + hh
                    S.add("pe", _mmk(bo[:, hh * 256:(hh + 1) * 256], [(PT[:, h, :], Vrt[:, h, :]), (qTrt[:, h, :], Crt_bf[:, h, :])]),
                          reads=["PT", ("Vrt", rp, pr), ("qTrt", rp, h), ("Crt_bf", rp, h)], writes=[ko])
                S.add("act", _act(hrt[:, 2 * pr:2 * pr + 2, :], bo.rearrange("p (a b) -> p a b", a=2), AF.Copy),
                      reads=[ko], writes=[("hrt", 2 * pr), ("hrt", 2 * pr + 1)])
                for hh in range(2):
                    h = 2 * pr + hh
                    S.add("dve", lambda e, h=h: e.bn_stats(out=st6[:, h, :], in_=hrt[:, h, :]), reads=[("hrt", h)], writes=[("st6", h)])
                    S.add("dve", lambda e, h=h: e.bn_aggr(out=mv[:, h, :], in_=st6[:, h, :]), reads=[("st6", h)], writes=[("mv", h)])
            allr = [("hrt", h) for h in range(4)]
            S.add("dve", _ts(T1, mv[:, :, 1], EPS, ALU.add), reads=allmv, writes=["t1"])
            S.add("pool", _tt(RSTD, T1, mhalf, ALU.pow), reads=["t1", "mhalf"], writes=["rstdh"])
            S.add("dve", _stt(BI, mv[:, :, 0], -1.0, RSTD, ALU.mult, ALU.mult), reads=allmv + ["rstdh"], writes=["bi"])
            for h in range(4):
                S.add("act", _act(hrt[:, h, :], hrt[:, h, :], AF.Identity, bias=BI[:, h:h + 1], scale=RSTD[:, h:h + 1]),
                      reads=[("hrt", h), "rstdh", "bi"], writes=[("hrt", h)])
            S.add("dve", _tt(mixed[:, 1024:2048], hrt.rearrange("p h v -> p (h v)"), gg, ALU.mult),
                  reads=allr + [("gg", rp, 0), ("gg", rp, 1)], writes=[("mixed", 1)])
        if full:
            f = c - NCH_P
            S.add("act", _dma(mixed_d[f * 128:(f + 1) * 128, :], mixed), reads=[("mixed", 0), ("mixed", 1)],
                  writes=[("mixed_d", f)], dma="mxst")


    loads(0)
    loads2(0)
    _stop = int(os.environ.get("MK_STOP_AFTER", str(NCH)))
    for c in range(min(NCH, _stop)):
        chunk(c, c >= NCH_P)
    if _stop < 100 and os.environ.get("MK_STOP_AFTER"):
        S.fake_par = None
        S.add("sp", lambda e: e.nop(), reads=S.all_keys())
        with ExitStack() as es:
            sems = {e: es.enter_context(nc.semaphore("s_" + e)) for e in ENGS}
            dsems = {k: es.enter_context(nc.semaphore("d_" + k)) for k in S.dma_counts}
            block = es.enter_context(nc.Block())
            S.emit(block, sems, dsems)
        return nc

    S.fake_par = None
    A3 = _Arena(nc, PH_BASE, SBUF_END)
    Wup = A3.alloc("Wup", [128, 8, DFF], BF16)
    Wout = A3.alloc("Wout", [128, 16, D], BF16)
    xh = A3.alloc("xh", [128, 4, D], F32)
    gcol = A3.alloc("gcol", [128, 16], F32)
    assert A3.off <= PH_BASE + 8 * WCOLS * 2, "early F3 weights must alias W only"
    allW = [("W", g, k) for g in range(len(WGROUPS)) for k in range(8)]
    S.add("pool", _dma(Wup[:, 0, :], wup_d[0:128, :]), writes=[("Wup", 0)] + allW, dma="Wup")
    for k in range(1, 8):
        S.add("pool", _dma(Wup[:, k, :], wup_d[k * 128:(k + 1) * 128, :]), reads=[("Wup", 0)], writes=[("Wup", k)], dma="Wup")
    S.add("sp", _dma(gcol, gcol_d), reads=[("Wup", 0)], writes=["gcol"], dma="gcol")
    S.add("dve", _ts(gcol[:, 0:8], gcol[:, 0:8], 0.5, ALU.mult), reads=["gcol"], writes=["gcol"])
    for k in range(16):
        sl = k % 4
        S.add("sp", _dma(xh[:, sl, :], wout_d[k * 128:(k + 1) * 128, :]), reads=[("Wup", 0)], writes=[("xh", sl)], dma=f"xh{sl}")
        S.add("act" if k % 2 else "dve",
              (_act(Wout[:, k, :], xh[:, sl, :], AF.Copy, scale=gcol[:, k:k + 1]) if k % 2 else
               _ts(Wout[:, k, :], xh[:, sl, :], gcol[:, k:k + 1], ALU.mult)),
              reads=[("xh", sl), "gcol", ("Wup", 0)], writes=[("Wout", k)])
    S.barrier(skip=("Wup", "Wout", "W"))
    NWD = 10
    Wgt = A3.alloc("Wgt", [128, 8, DFF], BF16)
    Wring = A3.alloc("Wring", [128, NWD, D], BF16)
    mtoks = [A3.alloc(f"mtok{i}", [128, 2048], BF16) for i in range(2)]
    mixedT = A3.alloc("mixedT", [128, 16, 256], BF16)
    u2Ts = [A3.alloc(f"u2T{i}", [128, 8, 256], BF16) for i in range(2)]
    u3s = [A3.alloc(f"u3_{i}", [128, D], BF16) for i in range(2)]
    NB3 = int(os.environ.get("MK_NB3", "4"))
    asb3s = [A3.alloc(f"asb3_{i}", [128, 258], F32) for i in range(NB3)]
    acc3 = [A3.alloc(f"acc3_{i}", [128, 256], F32) for i in range(NB3)]
    actT = [A3.alloc(f"actT{i}", [128, 256], BF16) for i in range(NB3)]
    halo = A3.alloc("halo", [128, NJ, 2], F32)
    convff = A3.alloc("convff", [128, NJ * 3], F32)
    g_ffn_b = A3.alloc("g_ffn_b", [128, D], F32)
    g_fin_b = A3.alloc("g_fin_b", [128, D], F32)
    stat2 = A3.alloc("stat2", [128, 16], F32)

    print("[kernel] sbuf A1 end", A1.off, "A3 end", A3.off, "limit", SBUF_END)
    S.add("sp", _dma(convff, convff_d), writes=["convff"], dma="cst")
    S.add("sp", _dma(g_ffn_b, gffn_d), writes=["g_ffn_b"], dma="cst")
    S.add("sp", _dma(g_fin_b, gfin_d), writes=["g_fin_b"], dma="cst")
    S.add("pool", lambda e: e.memset(halo, 0.0), writes=[("halo", j) for j in range(NJ)])
    for k in range(8):
        S.add("pool", _dma(Wgt[:, k, :], wgate_d[k * 128:(k + 1) * 128, :]), writes=[("Wgt", k)], dma="Wgt")

    b_acc = [banks[0], banks[1], banks[2], banks[3]]
    ybk = None
    agrot = [0]
    NAG = int(os.environ.get("MK_NAG", "3"))
    b_y = [banks[4 + NAG], banks[7]] if NAG < 3 else [banks[7], banks[7]]
    ybk = [4 + NAG, 7] if NAG < 3 else [7, 7]
    wdc = [0]
    strot = [0]
    mtc = [0]

    def rms_small(src, skey):
        i = strot[0] % 4
        strot[0] += 1
        return stat2[:, 4 * i:4 * i + 1], stat2[:, 4 * i + 1:4 * i + 2], stat2[:, 4 * i + 2:4 * i + 3], i

    def f3_pre(f0, ntt, is_halo, bp):
        T = ntt * 128
        u2T = u2Ts[bp]
        xsl = [bp * 2 + tt for tt in range(ntt)]
        for tt in range(ntt):
            f = f0 + tt
            p0 = (NCH_P + f) * 128
            S.add("sp", _dma(xh[:, xsl[tt], :], xs_d[p0:p0 + 128, :]), writes=[("xh", xsl[tt])], dma=f"xh{xsl[tt]}")
        for tt in range(ntt):
            f = f0 + tt
            mi = mtc[0] % 2
            mtc[0] += 1
            mtok = mtoks[mi]
            S.add("sp", _dma(mtok, mixed_d[f * 128:(f + 1) * 128, :]), reads=[("mixed_d", f)], writes=[("mtok", mi)], dma=f"mtok{mi}")
            for half in range(2):
                yb = b_y[half].bitcast(BF16)
                S.add("pe", _trs([(yb[:, kk * 128:(kk + 1) * 128], mtok[:, (half * 8 + kk) * 128:(half * 8 + kk + 1) * 128])
                                  for kk in range(8)], ident_bf), reads=[("mtok", mi), "ident_bf"], writes=[("bk", ybk[half])])
                S.add("act" if half else "dve",
                      (_act(mixedT[:, half * 8:half * 8 + 8, tt * 128:(tt + 1) * 128], yb.rearrange("p (k t) -> p k t", k=8), AF.Copy)
                       if half else _cp(mixedT[:, half * 8:half * 8 + 8, tt * 128:(tt + 1) * 128], yb.rearrange("p (k t) -> p k t", k=8))),
                      reads=[("bk", ybk[half])], writes=[("mixedT", tt, half)])
        for tt in range(ntt):
            xv = xh[:, xsl[tt], :]
            for half in range(2):
                S.add("pe", _mmk(b_y[half], [(mixedT[:, k, tt * 128:(tt + 1) * 128], Wout[:, k, half * 512:(half + 1) * 512])
                                             for k in range(16)]),
                      reads=[("mixedT", tt, 0), ("mixedT", tt, 1)] + [("Wout", k) for k in range(16)], writes=[("bk", ybk[half])])
                S.add("dve", _tt(xv[:, half * 512:(half + 1) * 512], b_y[half], xv[:, half * 512:(half + 1) * 512], ALU.add),
                      reads=[("bk", ybk[half]), ("xh", xsl[tt])], writes=[("xh", xsl[tt])])
        for tt in range(ntt):
            xv = xh[:, xsl[tt], :]
            xk = ("xh", xsl[tt])
            ss, ms, rs, si = rms_small(None, None)
            u3 = u3s[tt % 2]
            uk = ("u3", tt % 2)
            S.add("act", _act(u3, xv, AF.Square, accum=ss), reads=[xk], writes=[uk, ("ss", si)])
            S.add("dve", _ts(ms, ss, 1.0 / D, ALU.mult, EPS, ALU.add), reads=[("ss", si)], writes=[("ms", si)])
            S.add("pool", _tt(rs, ms, mhalf[:, 0:1], ALU.pow), reads=[("ms", si), "mhalf"], writes=[("rs", si)])
            S.add("dve", _stt(u3, xv, rs, g_ffn_b, ALU.mult, ALU.mult), reads=[xk, ("rs", si), "g_ffn_b"], writes=[uk])
            yb = b_y[tt % 2].bitcast(BF16)
            S.add("pe", _trs([(yb[:, k * 128:(k + 1) * 128], u3[:, k * 128:(k + 1) * 128]) for k in range(8)], ident_bf),
                  reads=[uk, "ident_bf"], writes=[("bk", ybk[tt % 2])])
            S.add("act", _act(u2T[:, :, tt * 128:(tt + 1) * 128], yb.rearrange("p (k t) -> p k t", k=8), AF.Copy),
                  reads=[("bk", ybk[tt % 2])], writes=[("u2T", bp, tt)])

    def f3_main(f0, ntt, is_halo, bp):
        T = ntt * 128
        u2T = u2Ts[bp]
        xsl = [bp * 2 + tt for tt in range(ntt)]
        u2k = [("u2T", bp, tt) for tt in range(ntt)]
        for j in range(NJ):
            par = agrot[0] % NB3
            agb = 4 + (agrot[0] % NAG)
            agrot[0] += 1
            aT = banks[agb][:, 0:T]
            gT = banks[agb][:, 256:256 + T]
            if not is_halo:
                ws = wdc[0] % NWD
                wdc[0] += 1
                S.add("pool", _dma(Wring[:, ws, :], wdn_d[j * 128:(j + 1) * 128, :]), writes=[("Wring", ws)], dma=f"Wd{ws}")
            S.add("pe", _mmk(aT, [(Wup[:, k, j * 128:(j + 1) * 128], u2T[:, k, 0:T]) for k in range(8)]),
                  reads=u2k + [("Wup", k) for k in range(8)], writes=[("bk", agb)])
            if not is_halo:
                S.add("pe", _mmk(gT, [(Wgt[:, k, j * 128:(j + 1) * 128], u2T[:, k, 0:T]) for k in range(8)]),
                      reads=u2k + [("Wgt", k) for k in range(8)], writes=[("bk", agb)])
            asb3 = asb3s[par]
            S.add("pool", _cp(asb3[:, 0:2], halo[:, j, :]), reads=[("halo", j)], writes=[("asb3h", par)])
            S.add("act", _act(asb3[:, 2:2 + T], aT, AF.Copy), reads=[("bk", agb)], writes=[("asb3", par)])
            S.add("pool", _cp(halo[:, j, :], asb3[:, T:T + 2]), reads=[("asb3", par)], writes=[("halo", j)])
            if is_halo:
                continue
            acc = acc3[par][:, 0:T]
            ak = ("acc3", par)
            S.add("dve", _ts(acc, asb3[:, 2:2 + T], convff[:, j * 3 + 2:j * 3 + 3], ALU.mult), reads=[("asb3", par), "convff"], writes=[ak])
            S.add("dve", _stt(acc, asb3[:, 1:1 + T], convff[:, j * 3 + 1:j * 3 + 2], acc, ALU.mult, ALU.add),
                  reads=[("asb3", par), ("asb3h", par), "convff", ak], writes=[ak])
            S.add("dve", _stt(acc, asb3[:, 0:T], convff[:, j * 3:j * 3 + 1], acc, ALU.mult, ALU.add),
                  reads=[("asb3", par), ("asb3h", par), "convff", ak], writes=[ak])
            S.add("act", _act(acc, acc, AF.Silu), reads=[ak], writes=[ak])
            at = actT[par][:, 0:T]
            S.add("dve", _tt(at, acc, gT, ALU.mult), reads=[ak, ("bk", agb)], writes=[("actT", par)])
            for tt in range(ntt):
                for half in range(2):
                    S.add("pe", _mm(b_acc[tt * 2 + half], at[:, tt * 128:(tt + 1) * 128], Wring[:, ws, half * 512:(half + 1) * 512],
                                    start=(j == 0), stop=(j == NJ - 1)),
                          reads=[("actT", par), ("Wring", ws)], writes=[("bk", tt * 2 + half)])
        if is_halo:
            return
        for tt in range(ntt):
            f = f0 + tt
            xv = xh[:, xsl[tt], :]
            xk = ("xh", xsl[tt])
            for half in range(2):
                S.add("dve", _tt(xv[:, half * 512:(half + 1) * 512], b_acc[tt * 2 + half], xv[:, half * 512:(half + 1) * 512], ALU.add),
                      reads=[("bk", tt * 2 + half), xk], writes=[xk])
            ss, ms, rs, si = rms_small(None, None)
            u3 = u3s[tt % 2]
            uk = ("u3", tt % 2)
            S.add("act", _act(u3, xv, AF.Square, accum=ss), reads=[xk], writes=[uk, ("ss", si)])
            S.add("dve", _ts(ms, ss, 1.0 / D, ALU.mult, EPS, ALU.add), reads=[("ss", si)], writes=[("ms", si)])
            S.add("pool", _tt(rs, ms, mhalf[:, 0:1], ALU.pow), reads=[("ms", si), "mhalf"], writes=[("rs", si)])
            S.add("dve", _stt(xv, xv, rs, g_fin_b, ALU.mult, ALU.mult), reads=[xk, ("rs", si), "g_fin_b"], writes=[xk])
            S.add("act", _dma(out_d[(f - 1) * 128:f * 128, :], xv), reads=[xk], writes=[("out", f)], dma=f"ost{xsl[tt]}")

    blocks = [(0, 1, True, 1)] + [(1 + 2 * b, 2, False, b % 2) for b in range(8)]
    f3_pre(*blocks[0])
    for bi, blk in enumerate(blocks):
        if bi + 1 < len(blocks):
            f3_pre(*blocks[bi + 1])
        f3_main(*blk)

    fin = S.add("sp", lambda e: e.nop(), reads=[("out", f) for f in range(1, NCH_F)])

    with ExitStack() as es:
        sems = {e: es.enter_context(nc.semaphore("s_" + e)) for e in ENGS}
        dsems = {k: es.enter_context(nc.semaphore("d_" + k)) for k in S.dma_counts}
        block = es.enter_context(nc.Block())
        S.emit(block, sems, dsems, reorder=bool(int(os.environ.get("MK_REORDER", "1"))))
        print("[kernel] est_ns", getattr(S, "est_ns", None), "ops", len(S.ops))
    return nc


def _host_consts():
    idx = np.arange(128, dtype=np.float64)
    dqk = np.zeros((128, 12), np.float32)
    for h in range(4):
        dqk[:, h] = np.exp(LOG_GAMMA[h] * (idx + 1.0))
        dqk[:, 4 + h] = S_ML * np.exp(-LOG_GAMMA[h] * (idx + 1.0))
        dqk[:, 8 + h] = S_ML * np.exp(-LOG_GAMMA[h] * (idx + 1.0)) * np.exp(LOG_GAMMA[h] * CH)
    jj, ii = np.meshgrid(np.arange(128), np.arange(128), indexing="ij")
    cm = (jj <= ii).astype(np.float32)
    mneg = np.where(jj <= ii, 0.0, NEG).astype(np.float32)
    i4 = np.concatenate([np.eye(4), -np.eye(4)], axis=1).astype(np.float32)
    return dict(dqk=dqk, cmask=cm, maskneg4=mneg,
                ident_bf=np.eye(128).astype(ml_dtypes.bfloat16), ident_f=np.eye(128, dtype=np.float32), i4=i4)


def _rope_tables(n_null):
    p = np.arange(NPOS, dtype=np.float64)
    pos = np.where(p >= n_null, 48.0 + (p - n_null), 0.0)
    inv = 10000.0 ** (-np.arange(0, 128, 2, dtype=np.float64) / 128.0)
    ang = pos[:, None] * inv[None, :]
    cosr = np.cos(ang).reshape(NCH, 128, 1, 64)
    sinr = np.sin(ang).reshape(NCH, 128, 1, 64)
    idx = np.arange(128, dtype=np.float64)
    dq = np.stack([np.exp(LOG_GAMMA[h] * (idx + 1.0)) for h in range(4)], axis=1)[None, :, :, None]
    dk = np.stack([S_ML * np.exp(-LOG_GAMMA[h] * (idx + 1.0)) for h in range(4)], axis=1)[None, :, :, None]
    tab = np.stack([cosr * dq, sinr * dq, cosr * dk, sinr * dk], axis=2)
    tab = np.ascontiguousarray(tab.reshape(NCH, 128, 1024)).astype(np.float32)
    valid = (p >= n_null).astype(np.float32).reshape(NCH, 128)
    vm = np.stack([valid, (valid - 1.0) * 1e30], axis=1)
    vm = np.ascontiguousarray(np.broadcast_to(vm[:, None], (NCH, 4, 2, 128))).astype(np.float32)
    return tab, vm


def _prep_inputs(inputs):
    f = lambda k: np.asarray(inputs[k], dtype=np.float32)
    x = f("x")
    meta = f("meta_tokens")
    w_in = f("w_in")[0]
    sizes = [512, 512, 1024, 1024, 4, 4, 512, 512, 1024, 1024]
    offs = np.cumsum(sizes)[:-1]
    ml_q, ml_k, ml_v, ml_o, ml_i, ml_f, rt_q, rt_k, rt_v, rt_g = np.split(w_in, offs, axis=1)

    def swap(cols):
        return cols.reshape(D, 4, 2, 64)[:, :, ::-1, :].reshape(D, 512)

    w_in_r = np.ascontiguousarray(np.concatenate(
        [ml_q, ml_k, ml_i, ml_f, ml_v, ml_o, rt_v, rt_g, rt_q, rt_k], axis=1))
    assert w_in_r.shape == (D, WCOLS)
    convml = f("ml_conv_w")[0]
    convw_ml = np.ascontiguousarray(convml.reshape(4, 8, 128).transpose(2, 1, 0).reshape(128, 32))
    convff = f("ffn_conv_w")[0]
    convw_ffn = np.ascontiguousarray(convff.reshape(3, NJ, 128).transpose(2, 1, 0).reshape(128, NJ * 3))
    b_if = np.ascontiguousarray(np.stack([f("ml_b_i")[0], f("ml_b_f")[0]], axis=1))
    gcat = np.concatenate([f("ml_norm_g")[0], f("rt_norm_g")[0]])
    gcol = np.ascontiguousarray(gcat.reshape(16, 128).T)
    bc = lambda v: np.ascontiguousarray(np.broadcast_to(v[None, :], (128, D)))
    common = dict(w_in_r=w_in_r, w_out=f("w_out")[0], w_up=f("w_up")[0], w_gate=f("w_gate")[0], w_down=f("w_down")[0],
                  convw_ml=convw_ml, convw_ffn=convw_ffn, b_if=b_if, gcol=gcol,
                  g_mix_b=bc(f("norm_mix_g")[0]), g_ffn_b=bc(f("norm_ffn_g")[0]), g_fin_b=bc(f("norm_final_g")))
    common.update(_host_consts())
    tabs = [_rope_tables(2160), _rope_tables(112)]
    in_maps = []
    for core in range(8):
        b, t = core // 2, core % 2
        xs = np.zeros((NPOS, D), np.float32)
        if t == 0:
            xs[2160:2176] = meta
            xs[2176:] = x[b, 0:2048]
        else:
            xs[112:128] = meta
            xs[128:] = x[b]
        m = dict(common)
        m["xs"] = xs
        m["cs_tab"], m["vm_tab"] = tabs[t]
        in_maps.append(m)
    return in_maps


_NC_CACHE = {}


def kernel(**inputs):
    in_maps = _prep_inputs(inputs)
    if "nc" not in _NC_CACHE:
        _NC_CACHE["nc"] = build_program()
    nc = _NC_CACHE["nc"]
    res = run_bass_kernel_spmd(nc, in_maps, core_ids=list(range(8)))
    out = np.zeros((4, 4096, D), np.float32)
    for core in range(8):
        b, t = core // 2, core % 2
        out[b, t * 2048:(t + 1) * 2048] = res.results[core]["out"]
    if DEBUG:
        kernel.debug = [res.results[c]["mixed_d"] for c in range(8)]
    return out
```

```python
import os
from contextlib import ExitStack

import numpy as np
import ml_dtypes

import concourse.bass as bass
import concourse.mybir as mybir
from concourse.bass_utils import run_bass_kernel_spmd

F32 = mybir.dt.float32
BF16 = mybir.dt.bfloat16
ALU = mybir.AluOpType
AF = mybir.ActivationFunctionType

NCH_P = 16
NCH_F = 17
NCH = NCH_P + NCH_F
CH = 128
NPOS = NCH * CH
D = 1024
DFF = 2816
NJ = DFF // 128
WCOLS = 6152
EPS = 1e-6
NEG = -1e30
GATE_CAP = 15.0
SBUF_BASE = 16640
SBUF_END = 229376
S_ML = 128.0 ** -0.5
LOG_GAMMA = [float(np.log1p(-2.0 ** (-(5.0 + h)))) for h in range(4)]
CD = [float(np.exp(lg * CH)) for lg in LOG_GAMMA]

ENGS = ("pe", "act", "dve", "pool", "sp")
DEBUG = bool(int(os.environ.get("MK_DEBUG", "0")))


class _Op:
    __slots__ = ("eng", "fn", "deps", "dma_sem", "dma_cnt", "sig", "idx", "need_sig", "wk")

    def __init__(self, eng, fn):
        self.eng = eng
        self.fn = fn
        self.deps = []
        self.dma_sem = None
        self.dma_cnt = 0
        self.sig = 0
        self.need_sig = False


class Sched:
    def __init__(self):
        self.ops = []
        self.last_w = {}
        self.readers = {}
        self.dma_counts = {}
        self.fence = None

    fake_par = None
    FAKE_KEEP = ("bk", "W", "Cml_f", "Cml_bf", "Crt_f", "Crt_bf", "asb", "mixed_d", "cst", "vmt", "uT")

    def _fk(self, keys):
        if self.fake_par is None:
            return keys
        out = []
        for k in keys:
            base = k[0] if isinstance(k, tuple) else k
            if base == "bk" and os.environ.get("MK_FAKEBK"):
                out.append((k, self.fake_par))
                continue
            if base in self.FAKE_KEEP or base in ("mst", "ident_bf", "mhalf", "i4", "ones4", "zeros4", "convml", "dqk", "cmask",
                                                   "maskneg4", "ident_f", "g_mix_b", "bsc0", "bsc1", "bif"):
                out.append(k)
            else:
                out.append((k, self.fake_par))
        return out

    def add(self, eng, fn, reads=(), writes=(), dma=None):
        reads = self._fk(reads)
        writes = self._fk(writes)
        op = _Op(eng, fn)
        op.wk = list(writes)[:2]
        op.idx = len(self.ops)
        deps = {}

        def dep(o, raw):
            val = self.dma_counts[o.dma_sem] if o.dma_sem is not None else 0
            if o.idx in deps:
                if raw and not deps[o.idx][1]:
                    deps[o.idx] = (o, True, val)
            else:
                deps[o.idx] = (o, raw, val)

        for r in reads:
            w = self.last_w.get(r)
            if w is not None:
                dep(w, True)
        for k in writes:
            w = self.last_w.get(k)
            if w is not None:
                dep(w, False)
            for rd in self.readers.get(k, ()):
                dep(rd, False)
        if self.fence is not None:
            dep(self.fence[eng], False)
        op.deps = list(deps.values())
        for r in reads:
            self.readers.setdefault(r, []).append(op)
        for k in writes:
            self.last_w[k] = op
            self.readers[k] = []
        if dma is not None:
            op.dma_sem = dma
            self.dma_counts[dma] = self.dma_counts.get(dma, 0) + 16
            op.dma_cnt = self.dma_counts[dma]
        self.ops.append(op)
        return op

    def all_keys(self):
        return list(set(self.last_w.keys()) | set(self.readers.keys()))

    def barrier(self, skip=()):
        keys = [k for k in self.all_keys() if not (isinstance(k, tuple) and k[0] in skip)]
        fence = {}
        for e in ENGS:
            fence[e] = self.add(e, lambda eng: eng.nop(), writes=keys)
        self.fence = fence

    @staticmethod
    def _skip(d, op, raw):
        if d.dma_sem is not None or op.dma_sem is not None:
            return False
        if d.eng != op.eng:
            return False
        return d.eng == "pe"

    def schedule(self, window=int(os.environ.get("MK_WIN", "200")), lat=float(os.environ.get("MK_LAT", "800"))):
        ops = self.ops
        n = len(ops)
        ndeps = [0] * n
        users = [[] for _ in range(n)]
        for op in ops:
            ndeps[op.idx] = len(op.deps)
            for (d, raw, val) in op.deps:
                users[d.idx].append(op.idx)
        fin = [0.0] * n
        ready_t = [0.0] * n
        prio_cp = os.environ.get("MK_PRIO", "cp") == "cp"
        tail = [0.0] * n
        if prio_cp:
            for op in reversed(ops):
                c = getattr(op.fn, "cost", 150.0)
                t = 0.0
                tl = lat * float(os.environ.get("MK_TAILF", "1.0"))
                for u in users[op.idx]:
                    if tail[u] + tl > t:
                        t = tail[u] + tl
                tail[op.idx] = t + c * float(os.environ.get("MK_COSTF", "1.0"))
        pend = {e: [op.idx for op in ops if op.eng == e] for e in ENGS}
        pos = {e: 0 for e in ENGS}
        done = [False] * n
        efree = {e: 0.0 for e in ENGS}
        order = {e: [] for e in ENGS}
        cur_tab = [None]
        nsw = [0]
        TABSW = float(os.environ.get("MK_TABSW", "1000"))
        remaining = n
        while remaining:
            best = None
            for e in ENGS:
                lst = pend[e]
                i = pos[e]
                while i < len(lst) and done[lst[i]]:
                    i += 1
                pos[e] = i
                if i >= len(lst):
                    continue
                w = 1 if e == "sp" else window
                seen = 0
                j = i
                cand = None
                dma_blocked = False
                while j < len(lst) and seen < w:
                    k = lst[j]
                    j += 1
                    if done[k]:
                        continue
                    seen += 1
                    isdma = ops[k].dma_sem is not None
                    if isdma and dma_blocked:
                        continue
                    if isdma:
                        dma_blocked = True
                    if ndeps[k] > 0:
                        continue
                    st = max(ready_t[k], efree[e])
                    if e == "act" and TABSW > 0.0:
                        tb = getattr(ops[k].fn, "tab", None)
                        if tb is not None and tb != cur_tab[0]:
                            st = st + TABSW
                    if prio_cp:
                        key = (st, -tail[k], k)
                        if cand is None or key < cand:
                            cand = key
                    else:
                        key = (st, 0.0, k)
                        if cand is None or key < cand:
                            cand = key
                            if ready_t[k] <= efree[e]:
                                break
                if cand is not None and (best is None or cand < best[0]):
                    best = (cand, e)
            assert best is not None, "scheduler deadlock"
            (st, _pr, k), e = best
            op = ops[k]
            c = getattr(op.fn, "cost", 150.0)
            if e == "act":
                tb = getattr(op.fn, "tab", None)
                if tb is not None and tb != cur_tab[0]:
                    cur_tab[0] = tb
                    nsw[0] += 1
            if op.dma_sem is not None:
                efree[e] = st + 60.0
                fin[k] = st + c
            else:
                efree[e] = st + c
                fin[k] = st + c
            done[k] = True
            remaining -= 1
            order[e].append(op)
            for u in users[k]:
                ndeps[u] -= 1
                t = fin[k] + lat
                if t > ready_t[u]:
                    ready_t[u] = t
        self.est_ns = max(fin) if n else 0.0
        self.n_tabsw = nsw[0]
        if os.environ.get("MK_CRIT"):
            dist = [0.0] * n
            pred = [-1] * n
            for op in ops:
                c = getattr(op.fn, "cost", 150.0)
                best_t, best_p = 0.0, -1
                for (d, raw, val) in op.deps:
                    t = dist[d.idx] + lat
                    if t > best_t:
                        best_t, best_p = t, d.idx
                dist[op.idx] = best_t + c
                pred[op.idx] = best_p
            lim = self.fence["pe"].idx if self.fence else n
            k = max(range(lim), key=lambda i: dist[i])
            print("[crit] dependency-only critical path before fence:", round(dist[k] / 1000), "us")
            path = []
            while k >= 0:
                path.append(k)
                k = pred[k]
            path.reverse()
            import collections
            cnt = collections.Counter(ops[i].eng for i in path)
            print("[crit] path len", len(path), dict(cnt))
            self.crit_path = path
            mid = int(len(path) * float(os.environ.get("MK_CRITPOS", "0.5")))
            for i in path[mid:mid + 70]:
                print("[crit]  ", ops[i].eng, ops[i].wk, round(getattr(ops[i].fn, "cost", 150.0)))
        if os.environ.get("MK_SCHED_DBG"):
            B = 100000.0
            nb = int(self.est_ns // B) + 1
            busy = {e: [0.0] * nb for e in ENGS}
            for op in ops:
                c = getattr(op.fn, "cost", 150.0)
                if op.dma_sem is not None:
                    continue
                b = int((fin[op.idx] - c) // B)
                busy[op.eng][b] += c
            for b in range(nb):
                print(f"[sched] {b*100:6d}us " + " ".join(f"{e}:{busy[e][b]/B*100:5.1f}%" for e in ENGS if e != "sp"))
            if self.fence:
                print("[sched] fence done at", {e: round(fin[o.idx] / 1000) for e, o in self.fence.items()})
        return order

    def eval_order(self, order, lat):
        ops = self.ops
        fin = {}
        pos = {e: 0 for e in ENGS}
        efree = {e: 0.0 for e in ENGS}
        ecur = [None]
        esw = [0]
        self._esw = esw
        remaining = sum(len(v) for v in order.values())
        while remaining:
            progressed = False
            for e in ENGS:
                while pos[e] < len(order[e]):
                    op = order[e][pos[e]]
                    if any(d.idx not in fin for (d, raw, val) in op.deps):
                        break
                    rt = max([fin[d.idx] + lat for (d, raw, val) in op.deps] + [0.0])
                    st = max(rt, efree[e])
                    c = getattr(op.fn, "cost", 150.0)
                    if e == "act":
                        tb = getattr(op.fn, "tab", None)
                        if tb is not None and tb != ecur[0]:
                            ecur[0] = tb
                            st += 1283.0
                            esw[0] += 1
                    if op.dma_sem is not None:
                        efree[e] = st + 60.0
                    else:
                        efree[e] = st + c
                    fin[op.idx] = st + c
                    pos[e] += 1
                    remaining -= 1
                    progressed = True
            assert progressed, "order deadlock"
        return max(fin.values())

    def emit(self, block, sems, dma_sems, reorder=True):
        for op in self.ops:
            for (d, raw, val) in op.deps:
                if d.dma_sem is None and not self._skip(d, op, raw):
                    d.need_sig = True
        if reorder:
            per_eng = self.schedule()
            if os.environ.get("MK_EVAL_LAT"):
                print("[kernel] eval fixed order @lat", os.environ["MK_EVAL_LAT"], self.eval_order(per_eng, float(os.environ["MK_EVAL_LAT"])), "tab switches", self._esw[0])
        else:
            per_eng = {e: [] for e in ENGS}
            for op in self.ops:
                per_eng[op.eng].append(op)
        cnt = {e: 0 for e in ENGS}
        for e in ENGS:
            for op in per_eng[e]:
                if op.dma_sem is None and op.need_sig:
                    cnt[e] += 1
                    op.sig = cnt[e]
        handles = {"pe": "tensor", "act": "scalar", "dve": "vector", "pool": "gpsimd", "sp": "sync"}

        def body(eng_name):
            def _f(eng):
                waited = {}
                for op in per_eng[eng_name]:
                    for (d, raw, val) in op.deps:
                        if d.dma_sem is not None:
                            key = ("dma", d.dma_sem)
                            sem = dma_sems[d.dma_sem]
                        else:
                            if self._skip(d, op, raw):
                                continue
                            key = d.eng
                            val = d.sig
                            sem = sems[d.eng]
                        if waited.get(key, 0) >= val:
                            continue
                        waited[key] = val
                        eng.wait_ge(sem, val)
                    ins = op.fn(eng)
                    if op.dma_sem is not None:
                        ins.then_inc(dma_sems[op.dma_sem], 16)
                    elif op.need_sig:
                        ins.then_inc(sems[eng_name], 1)
            return _f

        for e in ENGS:
            if per_eng[e]:
                getattr(block, handles[e])(body(e))


def _mmcost(lhsT, rhs):
    n = rhs.free_size()
    c = max(lhsT.free_size() / 1.2, n / 2.37, 30.0)
    if rhs.dtype == F32:
        c *= 4.0
    return c


def _fsz(ap):
    return ap.free_size()


def _mm(out, lhsT, rhs, start=True, stop=True):
    f = lambda e: e.matmul(out, lhsT=lhsT, rhs=rhs, start=start, stop=stop)
    f.cost = _mmcost(lhsT, rhs)
    return f


def _mmk(out, pairs):
    def f(e):
        n = len(pairs)
        ins = None
        for i, (l, r) in enumerate(pairs):
            ins = e.matmul(out, lhsT=l, rhs=r, start=(i == 0), stop=(i == n - 1))
        return ins
    f.cost = sum(_mmcost(l, r) for (l, r) in pairs)
    return f


def _trs(items, ident):
    def f(e):
        ins = None
        for (o, i) in items:
            ins = e.transpose(out=o, in_=i, identity=ident)
        return ins
    f.cost = 120.0 * len(items)
    return f


def _act(out, in_, func, bias=None, scale=None, accum=None):
    def f(e):
        kw = {}
        if bias is not None:
            kw["bias"] = bias
        if scale is not None:
            kw["scale"] = scale
        if accum is not None:
            kw["accum_out"] = accum
        return e.activation(out=out, in_=in_, func=func, **kw)
    f.cost = (224.0 + _fsz(out)) / 1.2
    f.tab = {AF.Silu: "S", AF.Tanh: "S", AF.Exp: "E", AF.Ln: "E"}.get(func)
    return f


def _ts(out, in0, s1, op0, s2=None, op1=None):
    def f(e):
        if op1 is None:
            return e.tensor_scalar(out=out, in0=in0, scalar1=s1, scalar2=None, op0=op0)
        return e.tensor_scalar(out=out, in0=in0, scalar1=s1, scalar2=s2, op0=op0, op1=op1)
    f.cost = (100.0 + _fsz(out)) / 0.96
    return f


def _tt(out, in0, in1, op):
    f = lambda e: e.tensor_tensor(out=out, in0=in0, in1=in1, op=op)
    f.cost = (100.0 + _fsz(out)) / 0.96
    return f


def _stt(out, in0, scalar, in1, op0, op1):
    f = lambda e: e.scalar_tensor_tensor(out=out, in0=in0, scalar=scalar, in1=in1, op0=op0, op1=op1)
    f.cost = (100.0 + _fsz(out)) / 0.96
    return f


def _cp(out, in_):
    f = lambda e: e.tensor_copy(out=out, in_=in_)
    f.cost = (100.0 + _fsz(out)) / 0.96
    return f


def _dma(out, in_):
    f = lambda e: e.dma_start(out=out, in_=in_)
    f.cost = 2000.0 + 128.0 * _fsz(out) * 4 / 200.0
    return f


def _scan(out, d0, d1, init, op0, op1):
    f = lambda e: e.tensor_tensor_scan(out=out, data0=d0, data1=d1, initial=init, op0=op0, op1=op1)
    f.cost = (100.0 + 2 * _fsz(out)) / 0.96
    return f


class _Arena:
    def __init__(self, nc, base, end):
        self.nc = nc
        self.off = base
        self.end = end

    def alloc(self, name, shape, dt):
        size = 1
        for s in shape[1:]:
            size *= s
        size *= 2 if dt == BF16 else 4
        off = (self.off + 31) // 32 * 32
        assert off + size <= self.end, f"SBUF overflow at {name}: {off + size} > {self.end}"
        t = self.nc.alloc_sbuf_tensor_at(name, list(shape), dt, offset=off)
        self.off = off + size
        return t.ap()


WGROUPS = [(1024, 2056), (512, 1024), (5640, 6152), (3080, 4104), (0, 512), (5128, 5640), (2056, 3080), (4104, 5128)]


def _wgroup_of(col):
    for g, (a, b) in enumerate(WGROUPS):
        if a <= col < b:
            return g
    raise ValueError(col)


def build_program():
    nc = bass.Bass("TRN2", target_bir_lowering=False)

    def din(name, shape, dt=F32):
        return nc.dram_tensor(name, list(shape), dt, kind="ExternalInput").ap()

    xs_d = din("xs", [NPOS, D])
    win_d = din("w_in_r", [D, WCOLS])
    wout_d = din("w_out", [2048, D])
    wup_d = din("w_up", [D, DFF])
    wgate_d = din("w_gate", [D, DFF])
    wdn_d = din("w_down", [DFF, D])
    cs_d = din("cs_tab", [NCH, 128, 1024])
    vm_d = din("vm_tab", [NCH, 4, 2, 128])
    dqk_d = din("dqk", [128, 12])
    cmask_d = din("cmask", [128, 128])
    mneg_d = din("maskneg4", [128, 128])
    identb_d = din("ident_bf", [128, 128], BF16)
    identf_d = din("ident_f", [128, 128])
    i4_d = din("i4", [4, 8])
    convml_d = din("convw_ml", [128, 32])
    convff_d = din("convw_ffn", [128, NJ * 3])
    bif_d = din("b_if", [4, 2])
    gcol_d = din("gcol", [128, 16])
    gmix_d = din("g_mix_b", [128, D])
    gffn_d = din("g_ffn_b", [128, D])
    gfin_d = din("g_fin_b", [128, D])
    out_d = nc.dram_tensor("out", [2048, D], F32, kind="ExternalOutput").ap()
    mixed_d = nc.dram_tensor("mixed_d", [NCH_F * CH, 2048], BF16,
                             kind="ExternalOutput" if DEBUG else "Internal").ap()

    S = Sched()
    banks = [nc.alloc_psum_tensor(f"bank{i}", [128, 512], F32).ap() for i in range(8)]

    per = _Arena(nc, SBUF_BASE, SBUF_END)
    ident_bf = per.alloc("ident_bf", [128, 128], BF16)
    mhalf = per.alloc("mhalf", [128, 4], F32)
    stat = per.alloc("stat", [128, 8], F32)
    PH_BASE = per.off

    S.add("sp", _dma(ident_bf, identb_d), writes=["ident_bf"], dma="cst")
    S.add("pool", lambda e: e.memset(mhalf, -0.5), writes=["mhalf"])

    A1 = _Arena(nc, PH_BASE, SBUF_END)
    W = A1.alloc("W", [128, 8, WCOLS], BF16)
    ident_f = A1.alloc("ident_f", [128, 128], F32)
    cmask = A1.alloc("cmask", [128, 128], F32)
    maskneg4 = A1.alloc("maskneg4", [128, 128], F32)
    dqk = A1.alloc("dqk", [128, 12], F32)
    g_mix_b = A1.alloc("g_mix_b", [128, D], F32)
    convml = A1.alloc("convml", [128, 32], F32)
    i4 = A1.alloc("i4", [4, 8], F32)
    bif = A1.alloc("bif", [4, 2], F32)
    bsc = A1.alloc("bsc", [4, 2], F32)
    ones4 = A1.alloc("ones4", [4, 128], F32)
    zeros4 = A1.alloc("zeros4", [4, 128], F32)
    xin = A1.alloc("xin", [128, D], F32)
    u = A1.alloc("u", [128, D], BF16)
    uT = [A1.alloc(f"uT{i}", [128, 8, 128], BF16) for i in range(2)]
    asb = A1.alloc("asb", [128, 8, 131], F32)
    cacc = [A1.alloc(f"cacc{i}", [128, 128], F32) for i in range(2)]
    qTmls = [A1.alloc(f"qTml{i}", [128, 4, 128], BF16) for i in range(2)]
    kTmls = [A1.alloc(f"kTml{i}", [128, 4, 128], BF16) for i in range(2)]
    rX = [A1.alloc("rX0", [128, 512], F32)] * 2
    rM = [A1.alloc(f"rM{i}", [128, 256], F32) for i in range(4)]
    qtok = A1.alloc("qtok", [128, 4, 128], BF16)
    ktok = A1.alloc("ktok", [128, 4, 128], BF16)
    cst = [A1.alloc("cst0", [128, 1024], F32)] * 2
    vmt = [A1.alloc("vmt0", [4, 2, 128], F32)] * 2
    qTrts = [A1.alloc(f"qTrt{i}", [128, 4, 128], BF16) for i in range(2)]
    kTrts = [A1.alloc(f"kTrt{i}", [128, 4, 128], BF16) for i in range(2)]
    kw = A1.alloc("kw", [128, 4, 128], BF16)
    Vmls = [A1.alloc(f"Vml{i}", [128, 4, 257], BF16) for i in range(2)]
    Vrts = [A1.alloc(f"Vrt{i}", [128, 4, 256], BF16) for i in range(2)]
    ogs = [A1.alloc(f"og{i}", [128, D], F32) for i in range(2)]
    ggs = [A1.alloc(f"gg{i}", [128, D], F32) for i in range(2)]
    mixed = A1.alloc("mixed", [128, 2048], BF16)
    Cml_f = A1.alloc("Cml_f", [128, 4, 257], F32)
    Cml_bfs = [A1.alloc(f"Cml_bf{i}", [128, 4, 257], BF16) for i in range(2)]
    Crt_f = A1.alloc("Crt_f", [128, 4, 256], F32)
    Crt_bfs = [A1.alloc(f"Crt_bf{i}", [128, 4, 256], BF16) for i in range(2)]
    WT = A1.alloc("WT", [128, 4, 128], F32)
    PT = A1.alloc("PT", [128, 4, 128], BF16)
    Wint = A1.alloc("Wint", [128, 512], F32)
    qsT = A1.alloc("qsT", [128, 4, 128], BF16)
    hrt = A1.alloc("hrt", [128, 4, 256], F32)
    hraw = A1.alloc("hraw", [128, 4, 257], F32)
    rows = {n: A1.alloc("row_" + n, [4, 128], F32) for n in
            ("li0", "li1", "li", "ef", "sp", "nbcum", "B", "M", "R2", "R3")}
    rhs_bd = A1.alloc("rhs_bd", [4, 4, 128], F32)
    smr = A1.alloc("smr", [4, 16], F32)
    sm = A1.alloc("sm", [128, 64], F32)
    smps = A1.alloc("smps", [128, 20], F32)
    st6 = A1.alloc("st6", [128, 4, 6], F32)
    mv = A1.alloc("mv", [128, 4, 2], F32)

    EX8 = sm[:, 0:8]
    BT = sm[:, 8:12]
    BTS = sm[:, 12:16]
    WSARG = sm[:, 16:20]
    WSRC = sm[:, 20:24]
    DEC = sm[:, 24:28]
    DD = sm[:, 28:32]
    RDEN = sm[:, 32:36]
    T1 = sm[:, 36:40]
    T2 = sm[:, 40:44]
    RSTD = sm[:, 44:48]
    SC = sm[:, 48:52]
    BI = sm[:, 52:56]

    bT = banks[0]
    bT_bf = bT.bitcast(BF16)

    for nm, dst, src in (("ident_f", ident_f, identf_d), ("cmask", cmask, cmask_d), ("maskneg4", maskneg4, mneg_d),
                         ("dqk", dqk, dqk_d), ("g_mix_b", g_mix_b, gmix_d),
                         ("convml", convml, convml_d), ("i4", i4, i4_d), ("bif", bif, bif_d)):
        S.add("sp", _dma(dst, src), writes=[nm], dma="cst")
    first4 = [("W", g, k) for g in range(4) for k in range(8)]
    for g, (c0, c1) in enumerate(WGROUPS):
        for k in range(8):
            S.add("pool", _dma(W[:, k, c0:c1], win_d[k * 128:(k + 1) * 128, c0:c1]),
                  reads=(first4 if g >= 4 else []), writes=[("W", g, k)], dma=f"W{g}")

    def wk(col):
        g = _wgroup_of(col)
        return [("W", g, k) for k in range(8)]

    S.add("pool", lambda e: e.memset(ones4, 1.0), writes=["ones4"])
    S.add("pool", lambda e: e.memset(zeros4, 0.0), writes=["zeros4"])
    S.add("pool", lambda e: e.memset(asb, 0.0), writes=[("asb", t) for t in range(8)])
    S.add("pool", lambda e: e.memset(Vmls[0], 1.0), writes=[("Vml", 0, 0), ("Vml", 0, 1)])
    S.add("pool", lambda e: e.memset(Vmls[1], 1.0), writes=[("Vml", 1, 0), ("Vml", 1, 1)])
    S.add("pool", lambda e: e.memset(Cml_f, 0.0), writes=[("Cml_f", h) for h in range(4)])
    S.add("pool", lambda e: e.memset(Cml_bfs[0], 0.0), writes=[("Cml_bf", 0, h) for h in range(4)])
    S.add("pool", lambda e: e.memset(Crt_f, 0.0), writes=[("Crt_f", h) for h in range(4)])
    S.add("pool", lambda e: e.memset(Crt_bfs[0], 0.0), writes=[("Crt_bf", 0, h) for h in range(4)])
    S.add("pool", lambda e: e.memset(smr, 0.0), writes=["mst", "D1", "diagM", "diagD"])
    S.add("pool", lambda e: e.memset(smr[:, 0:1], NEG), writes=["mst"])
    S.add("dve", _ts(bsc[:, 0:1], bif[:, 0:1], 1.0 / GATE_CAP, ALU.mult), reads=["bif"], writes=["bsc0"])
    S.add("dve", _ts(bsc[:, 1:2], bif[:, 1:2], -1.0, ALU.mult), reads=["bif"], writes=["bsc1"])

    MST = smr[:, 0:1]
    D1 = smr[:, 1:2]
    DIAGM = smr[:, 4:8]
    DIAGD = smr[:, 8:12]

    def loads(c):
        s = c % 2
        S.add("sp", _dma(xin, xs_d[c * 128:(c + 1) * 128, :]), writes=["xin"], dma="xin")

    def loads2(c):
        S.add("sp", _dma(cst[0], cs_d[c]), writes=[("cst", 0)], dma="cst0")
        S.add("sp", _dma(vmt[0], vm_d[c]), writes=[("vmt", 0)], dma="vmt0")

    aslot = [0]

    def next_aslot():
        i = aslot[0] % 4
        aslot[0] += 1
        return i

    tmslot = [0]

    def rmsnorm_T(src, gb, dstT, dst_keys, src_key, gkey):
        S.add("act", _act(u, src, AF.Square, accum=stat[:, 0:1]), reads=[src_key], writes=["u", "ss"])
        S.add("dve", _ts(stat[:, 1:2], stat[:, 0:1], 1.0 / D, ALU.mult, EPS, ALU.add), reads=["ss"], writes=["ms"])
        S.add("pool", _tt(stat[:, 2:3], stat[:, 1:2], mhalf[:, 0:1], ALU.pow), reads=["ms", "mhalf"], writes=["rstd"])
        S.add("dve", _stt(u, src, stat[:, 2:3], gb, ALU.mult, ALU.mult), reads=[src_key, "rstd", gkey], writes=["u"])
        S.add("pe", _trs([(bT_bf[:, k * 128:(k + 1) * 128], u[:, k * 128:(k + 1) * 128]) for k in range(8)], ident_bf),
              reads=["u", "ident_bf"], writes=[("bk", 0)])
        S.add("act", _act(dstT, bT_bf.rearrange("p (k t) -> p k t", k=8), AF.Copy), reads=[("bk", 0)], writes=dst_keys)


    fmb = [0]

    def chunk(c, full):
        rp, wp = c % 2, (c + 1) % 2
        qTml, kTml, qTrt, kTrt = qTmls[rp], kTmls[rp], qTrts[rp], kTrts[rp]
        og, gg = ogs[rp], ggs[rp]
        Vml, Vrt = Vmls[rp], Vrts[rp]
        Cml_bf, Crt_bf = Cml_bfs[rp], Crt_bfs[rp]
        Cml_bfw, Crt_bfw = Cml_bfs[wp], Crt_bfs[wp]
        if os.environ.get("MK_FAKE2"):
            S.fake_par = c % int(os.environ["MK_FAKE2"])
        s = c % 2
        uTs = uT[s]
        rmsnorm_T(xin, g_mix_b, uTs, [("uT", s)], "xin", "g_mix_b")
        if c + 1 < NCH:
            loads(c + 1)

        def fm_group(cols_ms):
            b = 1 + fmb[0] % 2
            fmb[0] += 1
            bk = banks[b]
            for sl, (col, m) in enumerate(cols_ms):
                S.add("pe", _mmk(bk[0:m, sl * 128:(sl + 1) * 128], [(W[:, k, col:col + m], uTs[:, k, :]) for k in range(8)]),
                      reads=wk(col) + [("uT", s)], writes=[("bk", b)])
            return bk, ("bk", b)

        for grp in ((0, 1) if full else (1,)):
            bk, bkey = fm_group([((grp * 4 + t) * 128, 128) for t in range(4)])
            S.add("act", _act(asb[:, grp * 4:grp * 4 + 4, 3:131], bk.rearrange("p (a b) -> p a b", a=4), AF.Copy),
                  reads=[bkey], writes=[("asb", grp * 4 + t) for t in range(4)])
            for t in range(grp * 4, grp * 4 + 4):
                acc = cacc[t % 2]
                ak = ("cacc", t % 2)
                S.add("dve", _ts(acc, asb[:, t, 3:131], convml[:, t * 4 + 3:t * 4 + 4], ALU.mult),
                      reads=[("asb", t), "convml"], writes=[ak])
                for kk in (2, 1, 0):
                    S.add("dve", _stt(acc, asb[:, t, kk:kk + 128], convml[:, t * 4 + kk:t * 4 + kk + 1], acc, ALU.mult, ALU.add),
                          reads=[("asb", t), "convml", ak], writes=[ak])
                S.add("pool", _cp(asb[:, t, 0:3], asb[:, t, 128:131]), reads=[("asb", t)], writes=[("asb", t)])
                if t < 4:
                    S.add("act", _act(qTml[:, t, :], acc, AF.Silu), reads=[ak], writes=[("qTml", rp, t)])
                else:
                    S.add("act", _act(kTml[:, t - 4, :], acc, AF.Silu), reads=[ak], writes=[("kTml", rp, t - 4)])
        bk, gkey = fm_group([(1024, 4), (1028, 4)])
        gi_reg = bk[0:4, 0:128]
        gf_reg = bk[0:4, 128:256]
        R = rows
        vs = vmt[s]
        S.add("act", _act(R["li0"], gi_reg, AF.Tanh, bias=bsc[:, 0:1], scale=1.0 / GATE_CAP), reads=[gkey, "bsc0"], writes=["li0"])
        S.add("act", _act(R["ef"], gf_reg, AF.Exp, bias=bsc[:, 1:2], scale=-1.0), reads=[gkey, "bsc1"], writes=["ef"])
        S.add("dve", _stt(R["li1"], R["li0"], GATE_CAP, vs[:, 0, :], ALU.mult, ALU.mult), reads=["li0", ("vmt", 0)], writes=["li1"])
        S.add("dve", _tt(R["li"], R["li1"], vs[:, 1, :], ALU.add), reads=["li1", ("vmt", 0)], writes=["li"])
        S.add("act", _act(R["sp"], R["ef"], AF.Ln, bias=1.0), reads=["ef"], writes=["sp"])
        S.add("dve", _scan(R["nbcum"], R["sp"], zeros4, 0.0, ALU.add, ALU.add), reads=["sp", "zeros4"], writes=["nbcum"])
        S.add("dve", _tt(R["B"], R["li"], R["nbcum"], ALU.add), reads=["li", "nbcum"], writes=["B"])
        S.add("dve", _scan(R["M"], R["B"], R["B"], MST, ALU.max, ALU.max), reads=["B", "mst"], writes=["M"])
        S.add("dve", _ts(DIAGM, i4[:, 0:4], R["M"][:, 127:128], ALU.mult), reads=["i4", "M"], writes=["diagM"])
        S.add("dve", _tt(D1, MST, R["M"][:, 127:128], ALU.subtract), reads=["mst", "M"], writes=["D1"])
        S.add("dve", _ts(DIAGD, i4[:, 0:4], D1, ALU.mult), reads=["i4", "D1"], writes=["diagD"])
        if full:
            S.add("dve", _tt(R["R2"], R["nbcum"], R["M"], ALU.subtract), reads=["nbcum", "M"], writes=["R2"])
            S.add("dve", _ts(R["R2"], R["R2"], 80.0, ALU.min), reads=["R2"], writes=["R2"])
            S.add("dve", _ts(R["R3"], R["M"], MST, ALU.subtract, -1.0, ALU.mult), reads=["M", "mst"], writes=["R3"])
            for h in range(4):
                S.add("dve", _ts(rhs_bd[:, h, :], R["M"], i4[:, 4 + h:5 + h], ALU.mult), reads=["M", "i4"], writes=[("rhs_bd", h)])
        S.add("dve", _tt(MST, R["M"][:, 127:128], R["nbcum"][:, 127:128], ALU.subtract), reads=["M", "nbcum"], writes=["mst"])

        sb_ = 5
        SMP = banks[sb_][:, 0:32]
        skey = ("bk", sb_)

        def smp_mm(e):
            ins = e.matmul(SMP[:, 0:4], lhsT=R["B"], rhs=i4[:, 0:4], start=True, stop=True)
            if full:
                e.matmul(SMP[:, 4:8], lhsT=R["R2"], rhs=i4[:, 0:4], start=True, stop=True)
                e.matmul(SMP[:, 8:12], lhsT=R["R3"], rhs=i4[:, 0:4], start=True, stop=True)
            e.matmul(SMP[:, 12:16], lhsT=ones4, rhs=DIAGM, start=True, stop=True)
            ins = e.matmul(SMP[:, 16:20], lhsT=ones4, rhs=DIAGD, start=True, stop=True)
            return ins
        S.add("pe", smp_mm, reads=["B", "R2", "R3", "i4", "ones4", "diagM", "diagD"], writes=[skey])
        if full:
            S.add("act", _act(smps, SMP[:, 0:20], AF.Copy), reads=[skey], writes=["smps"])
        else:
            S.add("act", _act(smps[:, 0:4], SMP[:, 0:4], AF.Copy), reads=[skey], writes=["smps"])
            S.add("act", _act(smps[:, 12:20], SMP[:, 12:20], AF.Copy), reads=[skey], writes=["smps"])
        if full:
            S.add("act", _act(EX8, smps[:, 4:12], AF.Exp), reads=["smps"], writes=["ex8"])
        S.add("dve", _tt(WSARG, smps[:, 0:4], smps[:, 12:16], ALU.subtract), reads=["smps"], writes=["wsarg"])
        S.add("act", _act(WSRC, WSARG, AF.Exp, bias=float(np.log(S_ML))), reads=["wsarg"], writes=["wsrc"])
        S.add("act", _act(DEC, smps[:, 16:20], AF.Exp), reads=["smps"], writes=["dec"])
        b5 = banks[5]
        k5 = ("bk", 5)
        if full:
            S.add("dve", _ts(BTS, smps[:, 0:4], float(np.log(S_ML)), ALU.add), reads=["smps"], writes=["BTS"])
            S.add("pe", _mmk(b5, [(ones4, rhs_bd.rearrange("p h t -> p (h t)")), (ident_f, maskneg4.unsqueeze(1).broadcast_to([128, 4, 128]))]),
                  reads=["ones4", "ident_f", "maskneg4"] + [("rhs_bd", h) for h in range(4)], writes=[k5])
            for h in range(4):
                S.add("act", _act(WT[:, h, :], b5[:, h * 128:(h + 1) * 128], AF.Exp, bias=BTS[:, h:h + 1]),
                      reads=[k5, "BTS"], writes=[("WT", h)])
            for h in range(4):
                S.add("dve", _ts(rhs_bd[:, h, :], R["R3"], i4[:, h:h + 1], ALU.mult), reads=["R3", "i4"], writes=[("rhs_bd", h)])
            S.add("pe", _mm(b5, ones4, rhs_bd.rearrange("p h t -> p (h t)")),
                  reads=["ones4"] + [("rhs_bd", h) for h in range(4)], writes=[k5])
            S.add("act", _act(Wint, b5, AF.Exp), reads=[k5], writes=["Wint"])
            S.add("dve", _tt(qsT.rearrange("p h t -> p (h t)"), qTml.rearrange("p h t -> p (h t)"), Wint, ALU.mult),
                  reads=["Wint"] + [("qTml", rp, h) for h in range(4)], writes=["qsT"])

        def tm_tile(col):
            b = 3 + tmslot[0] % 2
            tmslot[0] += 1
            S.add("pe", _mmk(banks[b], [(uTs[:, k, :], W[:, k, col:col + 512]) for k in range(8)]),
                  reads=wk(col) + [("uT", s)], writes=[("bk", b)])
            return banks[b], ("bk", b)

        for hh in range(2):
            reg, rk = tm_tile(1032 + hh * 512)
            S.add("act", _act(Vml[:, 2 * hh:2 * hh + 2, 0:256], reg.rearrange("p (a b) -> p a b", a=2), AF.Copy),
                  reads=[rk], writes=[("Vml", rp, hh)])
        for hh in range(2):
            reg, rk = tm_tile(3080 + hh * 512)
            S.add("dve", _cp(Vrt[:, 2 * hh:2 * hh + 2, :], reg.rearrange("p (a b) -> p a b", a=2)),
                  reads=[rk], writes=[("Vrt", rp, hh)])
        if full:
            for hh in range(2):
                reg, rk = tm_tile(2056 + hh * 512)
                S.add("act", _act(og[:, hh * 512:(hh + 1) * 512], reg, AF.Tanh, scale=0.5), reads=[rk], writes=[("og", rp, hh)])
            for hh in range(2):
                reg, rk = tm_tile(4104 + hh * 512)
                S.add("act", _act(gg[:, hh * 512:(hh + 1) * 512], reg, AF.Silu), reads=[rk], writes=[("gg", rp, hh)])

        def rotary(col, xi, qk, dst, dkey):
            reg, rk = tm_tile(col)
            X = rX[xi]
            xk = ("rX", 0)
            S.add("act", _act(X, reg, AF.Copy), reads=[rk], writes=[xk])
            Xv = X.rearrange("p (h a t) -> p h a t", h=4, a=2)
            Tc = cst[0][:, (qk * 2) * 256:(qk * 2 + 1) * 256].rearrange("p (h t) -> p h t", h=4)
            Ts = cst[0][:, (qk * 2 + 1) * 256:(qk * 2 + 2) * 256].rearrange("p (h t) -> p h t", h=4)
            Mv = [m.rearrange("p (h t) -> p h t", h=4) for m in rM]
            Dv = dst.rearrange("p h (a t) -> p h a t", a=2)
            S.add("dve", _tt(Mv[0], Xv[:, :, 0, :], Tc, ALU.mult), reads=[xk, ("cst", 0)], writes=[("rM", 0)])
            S.add("dve", _tt(Mv[1], Xv[:, :, 1, :], Ts, ALU.mult), reads=[xk, ("cst", 0)], writes=[("rM", 1)])
            S.add("pool", _tt(Mv[2], Xv[:, :, 0, :], Ts, ALU.mult), reads=[xk, ("cst", 0)], writes=[("rM", 2)])
            S.add("pool", _tt(Mv[3], Xv[:, :, 1, :], Tc, ALU.mult), reads=[xk, ("cst", 0)], writes=[("rM", 3)])
            S.add("dve", _tt(Dv[:, :, 0, :], Mv[0], Mv[1], ALU.subtract), reads=[("rM", 0), ("rM", 1)], writes=[(dkey, 0)])
            S.add("pool", _tt(Dv[:, :, 1, :], Mv[2], Mv[3], ALU.add), reads=[("rM", 2), ("rM", 3)], writes=[(dkey, 1)])

        rotary(5640, 0, 1, ktok, "ktok")
        if full and not os.environ.get("MK_X1"):
            rotary(5128, 1, 0, qtok, "qtok")
        if c + 1 < NCH:
            loads2(c + 1)

        ob = [6]

        def next_ob():
            b = 6 + ob[0] % 2
            ob[0] += 1
            return banks[b], ("bk", b)

        kb_, kbk = next_ob()
        b0bf = kb_.bitcast(BF16)
        S.add("pe", _trs([(b0bf[:, t * 128:(t + 1) * 128], kTml[:, t, :]) for t in range(4)], ident_bf),
              reads=[("kTml", rp, h) for h in range(4)] + ["ident_bf"], writes=[kbk])
        for t in range(4):
            S.add("act", _act(kw[:, t, :], b0bf[:, t * 128:(t + 1) * 128], AF.Copy, scale=WSRC[:, t:t + 1]),
                  reads=[kbk, "wsrc"], writes=[("kw", t)])
        if full:
            qb_, qbk = next_ob()
            qbbf = qb_.bitcast(BF16)
            S.add("pe", _trs([(qbbf[:, h * 128:(h + 1) * 128], ktok[:, h, :]) for h in range(4)] +
                             [(qbbf[:, (4 + h) * 128:(5 + h) * 128], qtok[:, h, :]) for h in range(4)], ident_bf),
                  reads=[("ktok", 0), ("ktok", 1), ("qtok", 0), ("qtok", 1), "ident_bf"], writes=[qbk])
            S.add("act", _act(kTrt, qbbf[:, 0:512].rearrange("p (h t) -> p h t", h=4), AF.Copy), reads=[qbk],
                  writes=[("kTrt", rp, h) for h in range(4)])
            S.add("act", _act(qTrt, qbbf[:, 512:1024].rearrange("p (h t) -> p h t", h=4), AF.Copy), reads=[qbk],
                  writes=[("qTrt", rp, h) for h in range(4)])

        for h in range(4):
            bo, ko = next_ob()
            S.add("pe", _mm(bo[:, 0:257], kw[:, h, :], Vml[:, h, :]), reads=[("kw", h), ("Vml", rp, h // 2)], writes=[ko])
            S.add("dve", _stt(Cml_f[:, h, :], Cml_f[:, h, :], DEC[:, h:h + 1], bo[:, 0:257], ALU.mult, ALU.add),
                  reads=[("Cml_f", h), "dec", ko], writes=[("Cml_f", h)])
            S.add("pool", _cp(Cml_bfw[:, h, :], Cml_f[:, h, :]), reads=[("Cml_f", h)], writes=[("Cml_bf", wp, h)])
        for pr in range(2):
            bo, ko = next_ob()
            for hh in range(2):
                h = 2 * pr + hh
                S.add("pe", _mm(bo[:, hh * 256:(hh + 1) * 256], ktok[:, h, :], Vrt[:, h, :]), reads=[("ktok", 0), ("ktok", 1), ("Vrt", rp, pr)], writes=[ko])
            for hh in range(2):
                h = 2 * pr + hh
                S.add("dve", _stt(Crt_f[:, h, :], Crt_f[:, h, :], CD[h], bo[:, hh * 256:(hh + 1) * 256], ALU.mult, ALU.add),
                      reads=[("Crt_f", h), ko], writes=[("Crt_f", h)])
            for hh in range(2):
                h = 2 * pr + hh
                S.add("pool", _ts(Crt_bfw[:, h, :], Crt_f[:, h, :], CD[h], ALU.mult, 0.0, ALU.add),
                      reads=[("Crt_f", h)], writes=[("Crt_bf", wp, h)])
        if full:
            S.add("pe", lambda e: [e.matmul(b5[:, h * 128:(h + 1) * 128], lhsT=kTml[:, h, :], rhs=qTml[:, h, :], start=True, stop=True)
                                   for h in range(4)][-1],
                  reads=[("kTml", rp, h) for h in range(4)] + [("qTml", rp, h) for h in range(4)], writes=[k5])
            S.add("dve", _tt(PT.rearrange("p h t -> p (h t)"), b5, WT.rearrange("p h t -> p (h t)"), ALU.mult),
                  reads=[k5] + [("WT", h) for h in range(4)], writes=["PT"])
            for h in range(4):
                bo, ko = next_ob()
                S.add("pe", _mmk(bo[:, 0:257], [(PT[:, h, :], Vml[:, h, :]), (qsT[:, h, :], Cml_bf[:, h, :])]),
                      reads=["PT", ("Vml", rp, h // 2), "qsT", ("Cml_bf", rp, h)], writes=[ko])
                S.add("act", _act(hraw[:, h, :], bo[:, 0:257], AF.Copy), reads=[ko], writes=[("hraw", h)])
                S.add("dve", lambda e, h=h: e.bn_stats(out=st6[:, h, :], in_=hraw[:, h, 0:256]), reads=[("hraw", h)], writes=[("st6", h)])
                S.add("dve", lambda e, h=h: e.bn_aggr(out=mv[:, h, :], in_=st6[:, h, :]), reads=[("st6", h)], writes=[("mv", h)])
            allh = [("hraw", h) for h in range(4)]
            allmv = [("mv", h) for h in range(4)]
            S.add("dve", _ts(T2, hraw[:, :, 256], -1.0, ALU.mult), reads=allh, writes=["t2"])
            S.add("dve", _tt(DD, T2, hraw[:, :, 256], ALU.max), reads=allh + ["t2"], writes=["dd"])
            S.add("dve", _tt(DD, DD, EX8[:, 0:4], ALU.max), reads=["dd", "ex8"], writes=["dd"])
            S.add("dve", lambda e: e.reciprocal(out=RDEN, in_=DD), reads=["dd"], writes=["rden"])
            S.add("dve", _tt(T1, RDEN, RDEN, ALU.mult), reads=["rden"], writes=["t1"])
            S.add("dve", _tt(T2, T1, mv[:, :, 1], ALU.mult), reads=["t1"] + allmv, writes=["t2"])
            S.add("dve", _ts(T1, T2, EPS, ALU.add), reads=["t2"], writes=["t1"])
            S.add("pool", _tt(RSTD, T1, mhalf, ALU.pow), reads=["t1", "mhalf"], writes=["rstdh"])
            S.add("dve", _tt(SC, RDEN, RSTD, ALU.mult), reads=["rden", "rstdh"], writes=["sc"])
            S.add("dve", _stt(BI, mv[:, :, 0], -1.0, SC, ALU.mult, ALU.mult), reads=allmv + ["sc"], writes=["bi"])
            for h in range(4):
                S.add("act", _act(hraw[:, h, 0:256], hraw[:, h, 0:256], AF.Identity, bias=BI[:, h:h + 1], scale=SC[:, h:h + 1]),
                      reads=[("hraw", h), "sc", "bi"], writes=[("hraw", h)])
            S.add("dve", _stt(mixed[:, 0:1024].rearrange("p (h v) -> p h v", h=4), og.rearrange("p (h v) -> p h v", h=4), 1.0,
                              hraw[:, :, 0:256], ALU.add, ALU.mult),
                  reads=allh + [("og", rp, 0), ("og", rp, 1)], writes=[("mixed", 0)])
            S.add("pe", lambda e: [e.matmul(b5[:, h * 128:(h + 1) * 128], lhsT=kTrt[:, h, :], rhs=qTrt[:, h, :], start=True, stop=True)
                                   for h in range(4)][-1],
                  reads=[("kTrt", rp, h) for h in range(4)] + [("qTrt", rp, h) for h in range(4)], writes=[k5])
            S.add("dve", _tt(PT, b5.rearrange("p (h t) -> p h t", h=4), cmask.unsqueeze(1).broadcast_to([128, 4, 128]), ALU.mult),
                  reads=[k5, "cmask"], writes=["PT"])
            for pr in range(2):
                bo, ko = next_ob()
                for hh in range(2):
                    h = 2 * pr + hh
                    S.add("pe", _mmk(bo[:, hh * 256:(hh + 1) * 256], [(PT[:, h, :], Vrt[:, h, :]), (qTrt[:, h, :], Crt_bf[:, h, :])]),
                          reads=["PT", ("Vrt", rp, pr), ("qTrt", rp, h), ("Crt_bf", rp, h)], writes=[ko])
                S.add("act", _act(hrt[:, 2 * pr:2 * pr + 2, :], bo.rearrange("p (a b) -> p a b", a=2), AF.Copy),
                      reads=[ko], writes=[("hrt", 2 * pr), ("hrt", 2 * pr + 1)])
                for hh in range(2):
                    h = 2 * pr + hh
                    S.add("dve", lambda e, h=h: e.bn_stats(out=st6[:, h, :], in_=hrt[:, h, :]), reads=[("hrt", h)], writes=[("st6", h)])
                    S.add("dve", lambda e, h=h: e.bn_aggr(out=mv[:, h, :], in_=st6[:, h, :]), reads=[("st6", h)], writes=[("mv", h)])
            allr = [("hrt", h) for h in range(4)]
            S.add("dve", _ts(T1, mv[:, :, 1], EPS, ALU.add), reads=allmv, writes=["t1"])
            S.add("pool", _tt(RSTD, T1, mhalf, ALU.pow), reads=["t1", "mhalf"], writes=["rstdh"])
            S.add("dve", _stt(BI, mv[:, :, 0], -1.0, RSTD, ALU.mult, ALU.mult), reads=allmv + ["rstdh"], writes=["bi"])
            for h in range(4):
                S.add("act", _act(hrt[:, h, :], hrt[:, h, :], AF.Identity, bias=BI[:, h:h + 1], scale=RSTD[:, h:h + 1]),
                      reads=[("hrt", h), "rstdh", "bi"], writes=[("hrt", h)])
            S.add("dve", _tt(mixed[:, 1024:2048], hrt.rearrange("p h v -> p (h v)"), gg, ALU.mult),
                  reads=allr + [("gg", rp, 0), ("gg", rp, 1)], writes=[("mixed", 1)])
        if full:
            f = c - NCH_P
            S.add("act", _dma(mixed_d[f * 128:(f + 1) * 128, :], mixed), reads=[("mixed", 0), ("mixed", 1)],
                  writes=[("mixed_d", f)], dma="mxst")


    loads(0)
    loads2(0)
    _stop = int(os.environ.get("MK_STOP_AFTER", str(NCH)))
    for c in range(min(NCH, _stop)):
        chunk(c, c >= NCH_P)
    if _stop < 100 and os.environ.get("MK_STOP_AFTER"):
        S.fake_par = None
        S.add("sp", lambda e: e.nop(), reads=S.all_keys())
        with ExitStack() as es:
            sems = {e: es.enter_context(nc.semaphore("s_" + e)) for e in ENGS}
            dsems = {k: es.enter_context(nc.semaphore("d_" + k)) for k in S.dma_counts}
            block = es.enter_context(nc.Block())
            S.emit(block, sems, dsems)
        return nc

    S.fake_par = None
    A3 = _Arena(nc, PH_BASE, SBUF_END)
    Wup = A3.alloc("Wup", [128, 8, DFF], BF16)
    Wout = A3.alloc("Wout", [128, 16, D], BF16)
    xh = A3.alloc("xh", [128, 4, D], F32)
    gcol = A3.alloc("gcol", [128, 16], F32)
    assert A3.off <= PH_BASE + 8 * WCOLS * 2, "early F3 weights must alias W only"
    allW = [("W", g, k) for g in range(len(WGROUPS)) for k in range(8)]
    S.add("pool", _dma(Wup[:, 0, :], wup_d[0:128, :]), writes=[("Wup", 0)] + allW, dma="Wup")
    for k in range(1, 8):
        S.add("pool", _dma(Wup[:, k, :], wup_d[k * 128:(k + 1) * 128, :]), reads=[("Wup", 0)], writes=[("Wup", k)], dma="Wup")
    S.add("sp", _dma(gcol, gcol_d), reads=[("Wup", 0)], writes=["gcol"], dma="gcol")
    S.add("dve", _ts(gcol[:, 0:8], gcol[:, 0:8], 0.5, ALU.mult), reads=["gcol"], writes=["gcol"])
    for k in range(16):
        sl = k % 4
        S.add("sp", _dma(xh[:, sl, :], wout_d[k * 128:(k + 1) * 128, :]), reads=[("Wup", 0)], writes=[("xh", sl)], dma=f"xh{sl}")
        S.add("act" if k % 2 else "dve",
              (_act(Wout[:, k, :], xh[:, sl, :], AF.Copy, scale=gcol[:, k:k + 1]) if k % 2 else
               _ts(Wout[:, k, :], xh[:, sl, :], gcol[:, k:k + 1], ALU.mult)),
              reads=[("xh", sl), "gcol", ("Wup", 0)], writes=[("Wout", k)])
    S.barrier(skip=("Wup", "Wout", "W"))
    NWD = 10
    Wgt = A3.alloc("Wgt", [128, 8, DFF], BF16)
    Wring = A3.alloc("Wring", [128, NWD, D], BF16)
    mtoks = [A3.alloc(f"mtok{i}", [128, 2048], BF16) for i in range(2)]
    mixedT = A3.alloc("mixedT", [128, 16, 256], BF16)
    u2Ts = [A3.alloc(f"u2T{i}", [128, 8, 256], BF16) for i in range(2)]
    u3s = [A3.alloc(f"u3_{i}", [128, D], BF16) for i in range(2)]
    NB3 = int(os.environ.get("MK_NB3", "4"))
    asb3s = [A3.alloc(f"asb3_{i}", [128, 258], F32) for i in range(NB3)]
    acc3 = [A3.alloc(f"acc3_{i}", [128, 256], F32) for i in range(NB3)]
    actT = [A3.alloc(f"actT{i}", [128, 256], BF16) for i in range(NB3)]
    halo = A3.alloc("halo", [128, NJ, 2], F32)
    convff = A3.alloc("convff", [128, NJ * 3], F32)
    g_ffn_b = A3.alloc("g_ffn_b", [128, D], F32)
    g_fin_b = A3.alloc("g_fin_b", [128, D], F32)
    stat2 = A3.alloc("stat2", [128, 16], F32)

    print("[kernel] sbuf A1 end", A1.off, "A3 end", A3.off, "limit", SBUF_END)
    S.add("sp", _dma(convff, convff_d), writes=["convff"], dma="cst")
    S.add("sp", _dma(g_ffn_b, gffn_d), writes=["g_ffn_b"], dma="cst")
    S.add("sp", _dma(g_fin_b, gfin_d), writes=["g_fin_b"], dma="cst")
    S.add("pool", lambda e: e.memset(halo, 0.0), writes=[("halo", j) for j in range(NJ)])
    for k in range(8):
        S.add("pool", _dma(Wgt[:, k, :], wgate_d[k * 128:(k + 1) * 128, :]), writes=[("Wgt", k)], dma="Wgt")

    b_acc = [banks[0], banks[1], banks[2], banks[3]]
    ybk = None
    agrot = [0]
    NAG = int(os.environ.get("MK_NAG", "3"))
    b_y = [banks[4 + NAG], banks[7]] if NAG < 3 else [banks[7], banks[7]]
    ybk = [4 + NAG, 7] if NAG < 3 else [7, 7]
    wdc = [0]
    strot = [0]
    mtc = [0]

    def rms_small(src, skey):
        i = strot[0] % 4
        strot[0] += 1
        return stat2[:, 4 * i:4 * i + 1], stat2[:, 4 * i + 1:4 * i + 2], stat2[:, 4 * i + 2:4 * i + 3], i

    def f3_pre(f0, ntt, is_halo, bp):
        T = ntt * 128
        u2T = u2Ts[bp]
        xsl = [bp * 2 + tt for tt in range(ntt)]
        for tt in range(ntt):
            f = f0 + tt
            p0 = (NCH_P + f) * 128
            S.add("sp", _dma(xh[:, xsl[tt], :], xs_d[p0:p0 + 128, :]), writes=[("xh", xsl[tt])], dma=f"xh{xsl[tt]}")
        for tt in range(ntt):
            f = f0 + tt
            mi = mtc[0] % 2
            mtc[0] += 1
            mtok = mtoks[mi]
            S.add("sp", _dma(mtok, mixed_d[f * 128:(f + 1) * 128, :]), reads=[("mixed_d", f)], writes=[("mtok", mi)], dma=f"mtok{mi}")
            for half in range(2):
                yb = b_y[half].bitcast(BF16)
                S.add("pe", _trs([(yb[:, kk * 128:(kk + 1) * 128], mtok[:, (half * 8 + kk) * 128:(half * 8 + kk + 1) * 128])
                                  for kk in range(8)], ident_bf), reads=[("mtok", mi), "ident_bf"], writes=[("bk", ybk[half])])
                S.add("act" if half else "dve",
                      (_act(mixedT[:, half * 8:half * 8 + 8, tt * 128:(tt + 1) * 128], yb.rearrange("p (k t) -> p k t", k=8), AF.Copy)
                       if half else _cp(mixedT[:, half * 8:half * 8 + 8, tt * 128:(tt + 1) * 128], yb.rearrange("p (k t) -> p k t", k=8))),
                      reads=[("bk", ybk[half])], writes=[("mixedT", tt, half)])
        for tt in range(ntt):
            xv = xh[:, xsl[tt], :]
            for half in range(2):
                S.add("pe", _mmk(b_y[half], [(mixedT[:, k, tt * 128:(tt + 1) * 128], Wout[:, k, half * 512:(half + 1) * 512])
                                             for k in range(16)]),
                      reads=[("mixedT", tt, 0), ("mixedT", tt, 1)] + [("Wout", k) for k in range(16)], writes=[("bk", ybk[half])])
                S.add("dve", _tt(xv[:, half * 512:(half + 1) * 512], b_y[half], xv[:, half * 512:(half + 1) * 512], ALU.add),
                      reads=[("bk", ybk[half]), ("xh", xsl[tt])], writes=[("xh", xsl[tt])])
        for tt in range(ntt):
            xv = xh[:, xsl[tt], :]
            xk = ("xh", xsl[tt])
            ss, ms, rs, si = rms_small(None, None)
            u3 = u3s[tt % 2]
            uk = ("u3", tt % 2)
            S.add("act", _act(u3, xv, AF.Square, accum=ss), reads=[xk], writes=[uk, ("ss", si)])
            S.add("dve", _ts(ms, ss, 1.0 / D, ALU.mult, EPS, ALU.add), reads=[("ss", si)], writes=[("ms", si)])
            S.add("pool", _tt(rs, ms, mhalf[:, 0:1], ALU.pow), reads=[("ms", si), "mhalf"], writes=[("rs", si)])
            S.add("dve", _stt(u3, xv, rs, g_ffn_b, ALU.mult, ALU.mult), reads=[xk, ("rs", si), "g_ffn_b"], writes=[uk])
            yb = b_y[tt % 2].bitcast(BF16)
            S.add("pe", _trs([(yb[:, k * 128:(k + 1) * 128], u3[:, k * 128:(k + 1) * 128]) for k in range(8)], ident_bf),
                  reads=[uk, "ident_bf"], writes=[("bk", ybk[tt % 2])])
            S.add("act", _act(u2T[:, :, tt * 128:(tt + 1) * 128], yb.rearrange("p (k t) -> p k t", k=8), AF.Copy),
                  reads=[("bk", ybk[tt % 2])], writes=[("u2T", bp, tt)])

    def f3_main(f0, ntt, is_halo, bp):
        T = ntt * 128
        u2T = u2Ts[bp]
        xsl = [bp * 2 + tt for tt in range(ntt)]
        u2k = [("u2T", bp, tt) for tt in range(ntt)]
        for j in range(NJ):
            par = agrot[0] % NB3
            agb = 4 + (agrot[0] % NAG)
            agrot[0] += 1
            aT = banks[agb][:, 0:T]
            gT = banks[agb][:, 256:256 + T]
            if not is_halo:
                ws = wdc[0] % NWD
                wdc[0] += 1
                S.add("pool", _dma(Wring[:, ws, :], wdn_d[j * 128:(j + 1) * 128, :]), writes=[("Wring", ws)], dma=f"Wd{ws}")
            S.add("pe", _mmk(aT, [(Wup[:, k, j * 128:(j + 1) * 128], u2T[:, k, 0:T]) for k in range(8)]),
                  reads=u2k + [("Wup", k) for k in range(8)], writes=[("bk", agb)])
            if not is_halo:
                S.add("pe", _mmk(gT, [(Wgt[:, k, j * 128:(j + 1) * 128], u2T[:, k, 0:T]) for k in range(8)]),
                      reads=u2k + [("Wgt", k) for k in range(8)], writes=[("bk", agb)])
            asb3 = asb3s[par]
            S.add("pool", _cp(asb3[:, 0:2], halo[:, j, :]), reads=[("halo", j)], writes=[("asb3h", par)])
            S.add("act", _act(asb3[:, 2:2 + T], aT, AF.Copy), reads=[("bk", agb)], writes=[("asb3", par)])
            S.add("pool", _cp(halo[:, j, :], asb3[:, T:T + 2]), reads=[("asb3", par)], writes=[("halo", j)])
            if is_halo:
                continue
            acc = acc3[par][:, 0:T]
            ak = ("acc3", par)
            S.add("dve", _ts(acc, asb3[:, 2:2 + T], convff[:, j * 3 + 2:j * 3 + 3], ALU.mult), reads=[("asb3", par), "convff"], writes=[ak])
            S.add("dve", _stt(acc, asb3[:, 1:1 + T], convff[:, j * 3 + 1:j * 3 + 2], acc, ALU.mult, ALU.add),
                  reads=[("asb3", par), ("asb3h", par), "convff", ak], writes=[ak])
            S.add("dve", _stt(acc, asb3[:, 0:T], convff[:, j * 3:j * 3 + 1], acc, ALU.mult, ALU.add),
                  reads=[("asb3", par), ("asb3h", par), "convff", ak], writes=[ak])
            S.add("act", _act(acc, acc, AF.Silu), reads=[ak], writes=[ak])
            at = actT[par][:, 0:T]
            S.add("dve", _tt(at, acc, gT, ALU.mult), reads=[ak, ("bk", agb)], writes=[("actT", par)])
            for tt in range(ntt):
                for half in range(2):
                    S.add("pe", _mm(b_acc[tt * 2 + half], at[:, tt * 128:(tt + 1) * 128], Wring[:, ws, half * 512:(half + 1) * 512],
                                    start=(j == 0), stop=(j == NJ - 1)),
                          reads=[("actT", par), ("Wring", ws)], writes=[("bk", tt * 2 + half)])
        if is_halo:
            return
        for tt in range(ntt):
            f = f0 + tt
            xv = xh[:, xsl[tt], :]
            xk = ("xh", xsl[tt])
            for half in range(2):
                S.add("dve", _tt(xv[:, half * 512:(half + 1) * 512], b_acc[tt * 2 + half], xv[:, half * 512:(half + 1) * 512], ALU.add),
                      reads=[("bk", tt * 2 + half), xk], writes=[xk])
            ss, ms, rs, si = rms_small(None, None)
            u3 = u3s[tt % 2]
            uk = ("u3", tt % 2)
            S.add("act", _act(u3, xv, AF.Square, accum=ss), reads=[xk], writes=[uk, ("ss", si)])
            S.add("dve", _ts(ms, ss, 1.0 / D, ALU.mult, EPS, ALU.add), reads=[("ss", si)], writes=[("ms", si)])
            S.add("pool", _tt(rs, ms, mhalf[:, 0:1], ALU.pow), reads=[("ms", si), "mhalf"], writes=[("rs", si)])
            S.add("dve", _stt(xv, xv, rs, g_fin_b, ALU.mult, ALU.mult), reads=[xk, ("rs", si), "g_fin_b"], writes=[xk])
            S.add("act", _dma(out_d[(f - 1) * 128:f * 128, :], xv), reads=[xk], writes=[("out", f)], dma=f"ost{xsl[tt]}")

    blocks = [(0, 1, True, 1)] + [(1 + 2 * b, 2, False, b % 2) for b in range(8)]
    f3_pre(*blocks[0])
    for bi, blk in enumerate(blocks):
        if bi + 1 < len(blocks):
            f3_pre(*blocks[bi + 1])
        f3_main(*blk)

    fin = S.add("sp", lambda e: e.nop(), reads=[("out", f) for f in range(1, NCH_F)])

    with ExitStack() as es:
        sems = {e: es.enter_context(nc.semaphore("s_" + e)) for e in ENGS}
        dsems = {k: es.enter_context(nc.semaphore("d_" + k)) for k in S.dma_counts}
        block = es.enter_context(nc.Block())
        S.emit(block, sems, dsems, reorder=bool(int(os.environ.get("MK_REORDER", "1"))))
        print("[kernel] est_ns", getattr(S, "est_ns", None), "ops", len(S.ops))
    return nc


def _host_consts():
    idx = np.arange(128, dtype=np.float64)
    dqk = np.zeros((128, 12), np.float32)
    for h in range(4):
        dqk[:, h] = np.exp(LOG_GAMMA[h] * (idx + 1.0))
        dqk[:, 4 + h] = S_ML * np.exp(-LOG_GAMMA[h] * (idx + 1.0))
        dqk[:, 8 + h] = S_ML * np.exp(-LOG_GAMMA[h] * (idx + 1.0)) * np.exp(LOG_GAMMA[h] * CH)
    jj, ii = np.meshgrid(np.arange(128), np.arange(128), indexing="ij")
    cm = (jj <= ii).astype(np.float32)
    mneg = np.where(jj <= ii, 0.0, NEG).astype(np.float32)
    i4 = np.concatenate([np.eye(4), -np.eye(4)], axis=1).astype(np.float32)
    return dict(dqk=dqk, cmask=cm, maskneg4=mneg,
                ident_bf=np.eye(128).astype(ml_dtypes.bfloat16), ident_f=np.eye(128, dtype=np.float32), i4=i4)


def _rope_tables(n_null):
    p = np.arange(NPOS, dtype=np.float64)
    pos = np.where(p >= n_null, 48.0 + (p - n_null), 0.0)
    inv = 10000.0 ** (-np.arange(0, 128, 2, dtype=np.float64) / 128.0)
    ang = pos[:, None] * inv[None, :]
    cosr = np.cos(ang).reshape(NCH, 128, 1, 64)
    sinr = np.sin(ang).reshape(NCH, 128, 1, 64)
    idx = np.arange(128, dtype=np.float64)
    dq = np.stack([np.exp(LOG_GAMMA[h] * (idx + 1.0)) for h in range(4)], axis=1)[None, :, :, None]
    dk = np.stack([S_ML * np.exp(-LOG_GAMMA[h] * (idx + 1.0)) for h in range(4)], axis=1)[None, :, :, None]
    tab = np.stack([cosr * dq, sinr * dq, cosr * dk, sinr * dk], axis=2)
    tab = np.ascontiguousarray(tab.reshape(NCH, 128, 1024)).astype(np.float32)
    valid = (p >= n_null).astype(np.float32).reshape(NCH, 128)
    vm = np.stack([valid, (valid - 1.0) * 1e30], axis=1)
    vm = np.ascontiguousarray(np.broadcast_to(vm[:, None], (NCH, 4, 2, 128))).astype(np.float32)
    return tab, vm


def _prep_inputs(inputs):
    f = lambda k: np.asarray(inputs[k], dtype=np.float32)
    x = f("x")
    meta = f("meta_tokens")
    w_in = f("w_in")[0]
    sizes = [512, 512, 1024, 1024, 4, 4, 512, 512, 1024, 1024]
    offs = np.cumsum(sizes)[:-1]
    ml_q, ml_k, ml_v, ml_o, ml_i, ml_f, rt_q, rt_k, rt_v, rt_g = np.split(w_in, offs, axis=1)

    def swap(cols):
        return cols.reshape(D, 4, 2, 64)[:, :, ::-1, :].reshape(D, 512)

    w_in_r = np.ascontiguousarray(np.concatenate(
        [ml_q, ml_k, ml_i, ml_f, ml_v, ml_o, rt_v, rt_g, rt_q, rt_k], axis=1))
    assert w_in_r.shape == (D, WCOLS)
    convml = f("ml_conv_w")[0]
    convw_ml = np.ascontiguousarray(convml.reshape(4, 8, 128).transpose(2, 1, 0).reshape(128, 32))
    convff = f("ffn_conv_w")[0]
    convw_ffn = np.ascontiguousarray(convff.reshape(3, NJ, 128).transpose(2, 1, 0).reshape(128, NJ * 3))
    b_if = np.ascontiguousarray(np.stack([f("ml_b_i")[0], f("ml_b_f")[0]], axis=1))
    gcat = np.concatenate([f("ml_norm_g")[0], f("rt_norm_g")[0]])
    gcol = np.ascontiguousarray(gcat.reshape(16, 128).T)
    bc = lambda v: np.ascontiguousarray(np.broadcast_to(v[None, :], (128, D)))
    common = dict(w_in_r=w_in_r, w_out=f("w_out")[0], w_up=f("w_up")[0], w_gate=f("w_gate")[0], w_down=f("w_down")[0],
                  convw_ml=convw_ml, convw_ffn=convw_ffn, b_if=b_if, gcol=gcol,
                  g_mix_b=bc(f("norm_mix_g")[0]), g_ffn_b=bc(f("norm_ffn_g")[0]), g_fin_b=bc(f("norm_final_g")))
    common.update(_host_consts())
    tabs = [_rope_tables(2160), _rope_tables(112)]
    in_maps = []
    for core in range(8):
        b, t = core // 2, core % 2
        xs = np.zeros((NPOS, D), np.float32)
        if t == 0:
            xs[2160:2176] = meta
            xs[2176:] = x[b, 0:2048]
        else:
            xs[112:128] = meta
            xs[128:] = x[b]
        m = dict(common)
        m["xs"] = xs
        m["cs_tab"], m["vm_tab"] = tabs[t]
        in_maps.append(m)
    return in_maps


_NC_CACHE = {}


def kernel(**inputs):
    in_maps = _prep_inputs(inputs)
    if "nc" not in _NC_CACHE:
        _NC_CACHE["nc"] = build_program()
    nc = _NC_CACHE["nc"]
    res = run_bass_kernel_spmd(nc, in_maps, core_ids=list(range(8)))
    out = np.zeros((4, 4096, D), np.float32)
    for core in range(8):
        b, t = core // 2, core % 2
        out[b, t * 2048:(t + 1) * 2048] = res.results[core]["out"]
    if DEBUG:
        kernel.debug = [res.results[c]["mixed_d"] for c in range(8)]
    return out
```
